# Optimizing a Trainium2 kernel written in Bass

```python
import math
import jax, jax.numpy as jnp
from jax import lax
import numpy as np

D_MODEL = 1024
BATCH = 16
SEQ = 2048
DEPTH = 1
DEC_BATCH = 2
DEC_SEQ = 16384
PAST_LEN = 128

GRID_W = 64
NA_HEADS = 8
NA_HEAD_DIM = 64
NA_WIDTH = NA_HEADS * NA_HEAD_DIM
NA_ROWS = 8
NA_COLS = 16
NA_QBLOCK = 16
NA_SPAN = 32
N_COL_BLOCKS = GRID_W // NA_QBLOCK
GLA_HEADS = 4
GLA_DK = 128
GLA_DV = 128
GLA_KDIM = GLA_HEADS * GLA_DK
GLA_VDIM = GLA_HEADS * GLA_DV
GLA_GATE_RANK = 16
GLA_GATE_NORM = 16.0
GLA_CHUNK = 64
D_FF = 2816
CONV_WIDTH = 3
EPS = 1e-6
IN_SIZES = (NA_WIDTH, NA_WIDTH, NA_WIDTH, GLA_KDIM, GLA_KDIM, GLA_VDIM, GLA_VDIM,
            GLA_GATE_RANK, GLA_GATE_RANK, D_MODEL, D_MODEL)
IN_COLS = sum(IN_SIZES)
IN_SPLITS = tuple(int(s) for s in np.cumsum(IN_SIZES)[:-1])

kernel_name = "hybrid_natten_bigla_convffn_encoder"


def rmsnorm(x, w):
    xf = x.astype(jnp.float32)
    y = xf * lax.rsqrt(jnp.mean(xf * xf, axis=-1, keepdims=True) + EPS) * w.astype(jnp.float32)
    return y.astype(x.dtype)


def head_rmsnorm_f32(x, w):
    xf = x.astype(jnp.float32)
    return xf * lax.rsqrt(jnp.mean(xf * xf, axis=-1, keepdims=True) + EPS) * w.astype(jnp.float32)


def na_column_tables():
    col = np.arange(GRID_W)
    c0 = np.clip(col - NA_COLS // 2, 0, GRID_W - NA_COLS)
    blk_start = np.clip(np.arange(N_COL_BLOCKS) * NA_QBLOCK - NA_COLS // 2, 0, GRID_W - NA_SPAN)
    key_cols = blk_start[:, None] + np.arange(NA_SPAN)
    q_cols = (np.arange(N_COL_BLOCKS) * NA_QBLOCK)[:, None] + np.arange(NA_QBLOCK)
    start_q = c0[q_cols]
    mask = (key_cols[:, None, :] >= start_q[:, :, None]) & (key_cols[:, None, :] < start_q[:, :, None] + NA_COLS)
    dc_idx = np.clip(key_cols[:, None, :] - q_cols[:, :, None] + NA_COLS - 1, 0, 2 * NA_COLS - 2)
    return key_cols, mask, dc_idx


def neighborhood_attention(q, k, v, rpb):
    B, T, H, hd = q.shape
    rows = T // GRID_W
    kr = min(NA_ROWS, rows)
    qg = q.reshape(B, rows, GRID_W, H, hd)
    kg = k.reshape(B, rows, GRID_W, H, hd)
    vg = v.reshape(B, rows, GRID_W, H, hd)
    key_cols, mask_np, dc_np = na_column_tables()
    col_mask = jnp.asarray(mask_np)[None, None, :, :, None, :]
    dc_idx = jnp.asarray(dc_np)
    rpb = rpb.astype(jnp.float32)
    scale = hd ** -0.5

    def row_step(r):
        r0 = jnp.clip(r - kr // 2, 0, rows - kr)
        k_rows = lax.dynamic_slice_in_dim(kg, r0, kr, axis=1)
        v_rows = lax.dynamic_slice_in_dim(vg, r0, kr, axis=1)
        k_blk = k_rows[:, :, key_cols]
        v_blk = v_rows[:, :, key_cols]
        q_row = lax.dynamic_index_in_dim(qg, r, axis=1, keepdims=False)
        q_row = q_row.reshape(B, N_COL_BLOCKS, NA_QBLOCK, H, hd)
        s = jnp.einsum('bjqhd,brjkhd->bhjqrk', q_row, k_blk) * scale
        dr_idx = r0 + jnp.arange(kr) - r + NA_ROWS - 1
        bias = rpb[:, dr_idx[:, None, None, None], dc_idx[None]]
        s = s + bias.transpose(0, 2, 3, 1, 4)[None]
        s = jnp.where(col_mask, s, -jnp.inf)
        p = jax.nn.softmax(s.reshape(B, H, N_COL_BLOCKS, NA_QBLOCK, kr * NA_SPAN), axis=-1)
        p = p.reshape(B, H, N_COL_BLOCKS, NA_QBLOCK, kr, NA_SPAN)
        o = jnp.einsum('bhjqrk,brjkhd->bjqhd', p, v_blk)
        return o.reshape(B, GRID_W, H, hd)

    out = lax.map(row_step, jnp.arange(rows))
    return out.transpose(1, 0, 2, 3, 4).reshape(B, T, H, hd)


def gla_chunked(q, k, v, g, inclusive):
    B, T, H, DK = q.shape
    DV = v.shape[-1]
    n = T // GLA_CHUNK

    def to_chunks(a):
        return a.astype(jnp.float32).reshape(B, n, GLA_CHUNK, H, a.shape[-1]).transpose(1, 0, 3, 2, 4)

    qc, kc, vc, gc = to_chunks(q), to_chunks(k), to_chunks(v), to_chunks(g)
    idx = jnp.arange(GLA_CHUNK)
    mask = (idx[:, None] >= idx[None, :]) if inclusive else (idx[:, None] > idx[None, :])

    def step(S, inp):
        qi, ki, vi, gi = inp
        b = jnp.cumsum(gi, axis=2)
        o_inter = jnp.einsum('bhcd,bhde->bhce', qi * jnp.exp(b), S)
        diff = b[:, :, :, None, :] - b[:, :, None, :, :]
        decay = jnp.exp(jnp.where(mask[:, :, None], diff, -jnp.inf))
        a = jnp.einsum('bhid,bhjd,bhijd->bhij', qi, ki, decay)
        o_intra = jnp.einsum('bhij,bhje->bhie', a, vi)
        b_last = b[:, :, -1:, :]
        S = jnp.exp(b_last[:, :, 0, :, None]) * S + jnp.einsum('bhjd,bhje->bhde', ki * jnp.exp(b_last - b), vi)
        return S, o_inter + o_intra

    S0 = jnp.zeros((B, H, DK, DV), jnp.float32)
    _, o = lax.scan(step, S0, (qc, kc, vc, gc))
    return o.transpose(1, 0, 3, 2, 4).reshape(B, T, H, DV)


def depthwise_conv(u, w, b):
    C = u.shape[-1]
    y = lax.conv_general_dilated(u, w[:, None, :].astype(u.dtype), window_strides=(1,),
                                 padding=((CONV_WIDTH // 2, CONV_WIDTH // 2),),
                                 dimension_numbers=('NWC', 'WIO', 'NWC'), feature_group_count=C)
    return y + b.astype(u.dtype)


def encoder_layer(x, norm1_w, w_in, qn_w, kn_w, rpb, w_a2_f, b_a_f, w_a2_b, b_a_b, gla_norm_w,
                  w_na_proj, w_gla_proj, w_out, norm2_w, w_up, conv_w, conv_b, w_down):
    B, T, _ = x.shape
    dt = x.dtype
    xn = rmsnorm(x, norm1_w)
    proj = xn @ w_in
    (q_na, k_na, v_na, q_g, k_g, v_g, og_g, lr_f, lr_b, gate_na, gate_gla) = jnp.split(proj, IN_SPLITS, axis=-1)

    qh = head_rmsnorm_f32(q_na.reshape(B, T, NA_HEADS, NA_HEAD_DIM), qn_w)
    kh = head_rmsnorm_f32(k_na.reshape(B, T, NA_HEADS, NA_HEAD_DIM), kn_w)
    vh = v_na.reshape(B, T, NA_HEADS, NA_HEAD_DIM).astype(jnp.float32)
    o_na = neighborhood_attention(qh, kh, vh, rpb).reshape(B, T, NA_WIDTH).astype(dt)
    y_na = o_na @ w_na_proj

    qg = q_g.reshape(B, T, GLA_HEADS, GLA_DK).astype(jnp.float32) * (GLA_DK ** -0.5)
    kg = k_g.reshape(B, T, GLA_HEADS, GLA_DK)
    vg = v_g.reshape(B, T, GLA_HEADS, GLA_DV)
    g_f = jax.nn.log_sigmoid((lr_f @ w_a2_f + b_a_f).astype(jnp.float32)) / GLA_GATE_NORM
    g_b = jax.nn.log_sigmoid((lr_b @ w_a2_b + b_a_b).astype(jnp.float32)) / GLA_GATE_NORM
    g_f = g_f.reshape(B, T, GLA_HEADS, GLA_DK)
    g_b = g_b.reshape(B, T, GLA_HEADS, GLA_DK)
    o_f = gla_chunked(qg, kg, vg, g_f, inclusive=True)
    o_b = jnp.flip(gla_chunked(jnp.flip(qg, 1), jnp.flip(kg, 1), jnp.flip(vg, 1), jnp.flip(g_b, 1),
                               inclusive=False), 1)
    o_g = head_rmsnorm_f32(o_f + o_b, gla_norm_w).reshape(B, T, GLA_VDIM)
    o_g = (o_g * jax.nn.silu(og_g.astype(jnp.float32))).astype(dt)
    y_gla = o_g @ w_gla_proj

    mix = (jax.nn.sigmoid(gate_na.astype(jnp.float32)) * y_na.astype(jnp.float32)
           + jax.nn.sigmoid(gate_gla.astype(jnp.float32)) * y_gla.astype(jnp.float32)).astype(dt)
    h = x + mix @ w_out

    hn = rmsnorm(h, norm2_w)
    u = depthwise_conv(hn @ w_up, conv_w, conv_b)
    a, b = jnp.split(u, 2, axis=-1)
    f = (jax.nn.gelu(a.astype(jnp.float32)) * b.astype(jnp.float32)).astype(dt)
    return h + f @ w_down


def setup_inputs(seed: int = 0) -> dict:
    key = jax.random.key(seed)
    ks = jax.random.split(key, 24)
    f32 = jnp.float32

    def nrm(k, shape, scale):
        return jax.random.normal(k, shape, f32) * scale

    return {
        "x_prompt": nrm(ks[0], (BATCH, SEQ, D_MODEL), 1.0),
        "x_sample": nrm(ks[1], (DEC_BATCH, DEC_SEQ, D_MODEL), 1.0),
        "norm1_w": 1.0 + nrm(ks[2], (DEPTH, D_MODEL), 0.02),
        "w_in": nrm(ks[3], (DEPTH, D_MODEL, IN_COLS), D_MODEL ** -0.5),
        "qn_w": 1.0 + nrm(ks[4], (DEPTH, NA_HEAD_DIM), 0.02),
        "kn_w": 1.0 + nrm(ks[5], (DEPTH, NA_HEAD_DIM), 0.02),
        "rpb": nrm(ks[6], (DEPTH, NA_HEADS, 2 * NA_ROWS - 1, 2 * NA_COLS - 1), 0.02),
        "w_a2_f": nrm(ks[7], (DEPTH, GLA_GATE_RANK, GLA_KDIM), GLA_GATE_RANK ** -0.5),
        "b_a_f": nrm(ks[8], (DEPTH, GLA_KDIM), 0.01),
        "w_a2_b": nrm(ks[9], (DEPTH, GLA_GATE_RANK, GLA_KDIM), GLA_GATE_RANK ** -0.5),
        "b_a_b": nrm(ks[10], (DEPTH, GLA_KDIM), 0.01),
        "gla_norm_w": 1.0 + nrm(ks[11], (DEPTH, GLA_DV), 0.02),
        "w_na_proj": nrm(ks[12], (DEPTH, NA_WIDTH, D_MODEL), NA_WIDTH ** -0.5),
        "w_gla_proj": nrm(ks[13], (DEPTH, GLA_VDIM, D_MODEL), GLA_VDIM ** -0.5),
        "w_out": nrm(ks[14], (DEPTH, D_MODEL, D_MODEL), D_MODEL ** -0.5),
        "norm2_w": 1.0 + nrm(ks[15], (DEPTH, D_MODEL), 0.02),
        "w_up": nrm(ks[16], (DEPTH, D_MODEL, 2 * D_FF), D_MODEL ** -0.5),
        "conv_w": nrm(ks[17], (DEPTH, CONV_WIDTH, 2 * D_FF), CONV_WIDTH ** -0.5),
        "conv_b": nrm(ks[18], (DEPTH, 2 * D_FF), 0.01),
        "w_down": nrm(ks[19], (DEPTH, D_FF, D_MODEL), D_FF ** -0.5),
    }


def reference(x_prompt, x_sample, norm1_w, w_in, qn_w, kn_w, rpb, w_a2_f, b_a_f, w_a2_b, b_a_b,
              gla_norm_w, w_na_proj, w_gla_proj, w_out, norm2_w, w_up, conv_w, conv_b, w_down):
    def trunk(x):
        for l in range(DEPTH):
            x = encoder_layer(x, norm1_w[l], w_in[l], qn_w[l], kn_w[l], rpb[l], w_a2_f[l], b_a_f[l],
                              w_a2_b[l], b_a_b[l], gla_norm_w[l], w_na_proj[l], w_gla_proj[l], w_out[l],
                              norm2_w[l], w_up[l], conv_w[l], conv_b[l], w_down[l])
        return x

    y_prompt = trunk(x_prompt)
    y_sample = trunk(x_sample)
    return (y_prompt, y_sample)
```

```python
import numpy as np
import concourse.bass as bass
import concourse.mybir as mybir
from concourse.bass_utils import run_bass_kernel_spmd
from contextlib import ExitStack

F32 = mybir.dt.float32
BF16 = mybir.dt.bfloat16
AF = mybir.ActivationFunctionType
ALU = mybir.AluOpType
EPS = 1e-6
NCORES = 8
D = 1024
TR = 1152
TK = 1664
NEG = -30000.0
EDGE_TOP = (0, 1, 2)
EDGE_BOT = (7, 8)
C_WBC1, C_WBC2, C_GNW, C_UINC, C_ULT, C_UGT, C_CW, C_CB = 0, 1024, 2048, 2560, 2688, 2816, 2944, 3076
C_QNW, C_KNW, C_LNQ, C_NEG, C_MASK = 3120, 3121, 3122, 3123, 3124
NCF = 3136
B_ID, B_OB, B_ONE, B_MF, B_MB = 0, 128, 256, 384, 512
NCB = 640


class Buf:
    __slots__ = ("w", "rs")

    def __init__(self):
        self.w = None
        self.rs = []


class Rec:
    ENGS = ("pe", "act", "dve", "pool", "sp")

    def __init__(self):
        self.ops = []
        self.bufs = {}
        self.bar = None
        self.last = {}
        self.dmas = []

    def B(self, name):
        b = self.bufs.get(name)
        if b is None:
            b = self.bufs[name] = Buf()
        return b

    def add(self, eng, fn, r=(), w=(), dma=False, stream=None):
        deps = set()
        for n in r:
            b = self.B(n)
            if b.w is not None:
                deps.add(b.w)
        for n in w:
            b = self.B(n)
            if b.w is not None:
                deps.add(b.w)
            deps.update(b.rs)
        if self.bar is not None:
            deps.add(self.bar)
        i = len(self.ops)
        self.ops.append(dict(eng=eng, fn=fn, deps=deps, dma=dma, stream=stream, sig=False, ord=0))
        for n in r:
            self.B(n).rs.append(i)
        for n in w:
            b = self.B(n)
            b.w = i
            b.rs = []
        if dma:
            self.dmas.append(i)
        else:
            self.last[eng] = i
        return i

    def barrier(self, fn):
        deps = set(self.last.values()) | set(self.dmas)
        if self.bar is not None:
            deps.add(self.bar)
        i = len(self.ops)
        self.ops.append(dict(eng="dve", fn=fn, deps=deps, dma=False, stream=None, sig=False, ord=0))
        self.bar = i
        self.last["dve"] = i
        self.dmas = []
        return i

    def pe(self, fn, r=(), w=()):
        return self.add("pe", fn, r, w)

    def act(self, fn, r=(), w=()):
        return self.add("act", fn, r, w)

    def dve(self, fn, r=(), w=()):
        return self.add("dve", fn, r, w)

    def pool(self, fn, r=(), w=()):
        return self.add("pool", fn, r, w)

    def dma(self, q, fn, r=(), w=(), stream=None):
        return self.add(q, fn, r, w, dma=True, stream=stream)

    def emit(self, nc, final_wait_ops=()):
        ops = self.ops
        ops.append(dict(eng="sp", fn=None, deps=set(final_wait_ops), dma=False, stream=None, sig=False, ord=0))
        for o in ops:
            for d in o["deps"]:
                od = ops[d]
                if od["dma"]:
                    continue
                if od["eng"] == "pe" and o["eng"] == "pe" and not o["dma"]:
                    continue
                od["sig"] = True
        cnt = {}
        for o in ops:
            if o["dma"]:
                k = ("s", o["stream"])
                cnt[k] = cnt.get(k, 0) + 1
                o["ord"] = cnt[k]
            elif o["sig"]:
                k = ("e", o["eng"])
                cnt[k] = cnt.get(k, 0) + 1
                o["ord"] = cnt[k]
        with ExitStack() as es:
            sem = {}
            for k in cnt:
                sem[k] = es.enter_context(nc.semaphore("sem_%s_%s" % k))
            block = es.enter_context(nc.Block())
            by_eng = {e: [o for o in ops if o["eng"] == e] for e in self.ENGS}

            def run(engh, ename):
                waited = {}
                for o in by_eng[ename]:
                    need = {}
                    for d in o["deps"]:
                        od = ops[d]
                        if od["dma"]:
                            k = ("s", od["stream"])
                            v = 16 * od["ord"]
                        else:
                            if od["eng"] == "pe" and ename == "pe" and not o["dma"]:
                                continue
                            k = ("e", od["eng"])
                            v = od["ord"]
                        if v > need.get(k, 0):
                            need[k] = v
                    for k, v in need.items():
                        if waited.get(k, 0) >= v:
                            continue
                        waited[k] = v
                        engh.wait_ge(sem[k], v)
                    if o["fn"] is None:
                        continue
                    ins = o["fn"](engh)
                    if o["dma"]:
                        ins.then_inc(sem[("s", o["stream"])], 16)
                    elif o["sig"]:
                        ins.then_inc(sem[("e", ename)], 1)

            if by_eng["sp"]:
                block.sync(lambda e: run(e, "sp"))
            if by_eng["pool"]:
                block.gpsimd(lambda e: run(e, "pool"))
            if by_eng["act"]:
                block.scalar(lambda e: run(e, "act"))
            if by_eng["dve"]:
                block.vector(lambda e: run(e, "dve"))
            if by_eng["pe"]:
                block.tensor(lambda e: run(e, "pe"))
        return len(ops), cnt


def blocks(T):
    out = []
    t = 0
    while t < T:
        n = min(512, T - t)
        out.append((t, n))
        t += n
    return out


def kv_list(i, edge):
    if not edge:
        return list(range(i, i + 5))
    return {0: list(range(0, 7)), 1: list(range(1, 7)), 2: list(range(2, 7)),
            7: list(range(6, 12)), 8: list(range(6, 13))}[i]


class K:
    def __init__(self, nc, es, dbg=None):
        self.nc = nc
        self.es = es
        self.R = Rec()
        self.dbg = dbg
        self.dumps = []
        self.stores = []
        self.cnt = {}
        self.arena = es.enter_context(nc.sbuf_tensor("arena", [128, 106000], BF16))
        self.top = 0
        self.ps = [es.enter_context(nc.psum_tensor(f"ps{i}", [128, 512], F32)) for i in range(6)]
        self.pb = [es.enter_context(nc.psum_tensor(f"pb{i}", [128, 1024], BF16)) for i in range(2)]

    def alloc(self, n, dt=BF16):
        if dt == F32:
            n2 = 2 * n
        else:
            n2 = n
        self.top = (self.top + 1) // 2 * 2
        a = self.arena[:, self.top:self.top + n2]
        self.top += n2
        assert self.top <= 106000, self.top
        return a.bitcast(F32) if dt == F32 else a

    def rot(self, key, n):
        c = self.cnt.get(key, 0)
        self.cnt[key] = c + 1
        return c % n

    def ps_new(self):
        i = self.rot("ps", 6)
        return self.ps[i][:], f"ps{i}"

    def pt_new(self):
        i = self.rot("pt", 2)
        return self.pb[i][:], f"pb{i}"

    def MM(self, out, lhsT, rhs, start, stop, r, w):
        self.R.pe(lambda e: e.matmul(out, lhsT=lhsT, rhs=rhs, start=start, stop=stop), r, w)

    def TR_(self, out, in_, r, w):
        ident = self.ident
        self.R.pe(lambda e: e.transpose(out=out, in_=in_, identity=ident), list(r) + ["cb"], w)

    def ACT(self, out, in_, func, r, w, scale=None, bias=None, accum=None):
        kw = {}
        if scale is not None:
            kw["scale"] = scale
        if bias is not None:
            kw["bias"] = bias
        if accum is not None:
            kw["accum_out"] = accum
        self.R.act(lambda e: e.activation(out=out, in_=in_, func=func, **kw), r, w)

    def ACOPY(self, out, in_, r, w):
        self.R.act(lambda e: e.copy(out=out, in_=in_), r, w)

    def AMUL(self, out, in_, c, r, w):
        self.R.act(lambda e: e.mul(out=out, in_=in_, mul=c), r, w)

    def DCOPY(self, out, in_, r, w):
        self.R.dve(lambda e: e.tensor_copy(out=out, in_=in_), r, w)

    def TS(self, out, in0, s1, s2, op0, op1, r, w):
        if op1 is None:
            self.R.dve(lambda e: e.tensor_scalar(out=out, in0=in0, scalar1=s1, scalar2=None, op0=op0), r, w)
        else:
            self.R.dve(lambda e: e.tensor_scalar(out=out, in0=in0, scalar1=s1, scalar2=s2, op0=op0, op1=op1), r, w)

    def TT(self, out, in0, in1, op, r, w):
        self.R.dve(lambda e: e.tensor_tensor(out=out, in0=in0, in1=in1, op=op), r, w)

    def STT(self, out, in0, scalar, in1, op0, op1, r, w):
        self.R.dve(lambda e: e.scalar_tensor_tensor(out=out, in0=in0, scalar=scalar, in1=in1, op0=op0, op1=op1), r, w)

    def LOAD(self, q, out, in_, w, stream, r=()):
        return self.R.dma(q, lambda e: e.dma_start(out=out, in_=in_), r=r, w=w, stream=stream)

    def STORE(self, q, out, in_, r, stream, w=()):
        i = self.R.dma(q, lambda e: e.dma_start(out=out, in_=in_), r=r, w=w, stream=stream)
        return i

    def dump(self, name, ap, shape, dt, rname):
        if self.dbg is None or name not in self.dbg or getattr(self, "cur_unit", None) != getattr(self, "dbg_unit", None):
            return
        d = self.nc.dram_tensor("dbg_" + name, list(shape), dt, kind="ExternalOutput").ap()
        self.stores.append(self.STORE("sp", d, ap, [rname], "dbg_" + name))
        self.dumps.append(name)

    def barrier(self):
        bt = self.bartile
        self.R.barrier(lambda e: e.memset(bt, 0.0))

    def setup(self, T):
        self.T = T
        A = self.alloc
        self.cf = A(NCF, F32)
        self.cb = A(NCB)
        self.ident = self.cb[:, B_ID:B_ID + 128]
        self.onesblk = self.cb[:, B_OB:B_OB + 128]
        self.ones = self.cb[:, B_ONE:B_ONE + 128]
        self.Mf = self.cb[:, B_MF:B_MF + 128]
        self.Mb = self.cb[:, B_MB:B_MB + 128]
        self.wa2 = [A(512), A(512)]
        self.ba = [A(512), A(512)]
        self.wlr = A(256).rearrange("p (k n) -> p k n", k=8)
        self.slabI = A(5120)
        self.S = {"f": A(512, F32), "b": A(512, F32)}
        self.Sbf = A(512)
        self.Ssave = [A(512, F32) for _ in range(4)]
        self.xs = [A(1024, F32) for _ in range(2)]
        self.jk = [A(1024) for _ in range(2)]
        self.xb = [A(1024) for _ in range(2)]
        self.xTt = [A(1024).rearrange("p (k n) -> p k n", k=8) for _ in range(2)]
        self.st = [A(8, F32) for _ in range(8)]
        self.wsl = [A(4096) for _ in range(4)]
        self.PT = [A(896) for _ in range(2)]
        self.ona = [A(512) for _ in range(2)]
        self.bartile = A(2, F32)
        self.tmp0 = self.top = (self.top + 1) // 2 * 2
        self.tf = [A(512, F32) for _ in range(8)]
        self.tb = [A(512) for _ in range(10)]
        assert self.top - self.tmp0 == 13312
        self.base = self.top
        L = self.LOAD
        L("sp", self.cf, T["cf"][:, :], ["cf"], "cf")
        L("pool", self.cb, T["cb"][:, :], ["cb"], "cb")
        for d, nm in enumerate(("f", "b")):
            L("pool", self.wa2[d][0:16, :], T["w_a2_" + nm][:, :], ["wa2"], "wa2" + nm)
            L("pool", self.ba[d][0:1, :], T["b_a_" + nm][:, :], ["ba"], "ba" + nm)
        L("pool", self.wlr, T["w_in"][:, 3584:3616].rearrange("(k p) n -> p k n", p=128), ["wlr"], "wlr")
        L("pool", self.slabI, T["slabI"][:, :], ["slabI"], "slabI")

    def tF(self):
        i = self.rot("tf", 8)
        return self.tf[i], f"tf{i}"

    def tB(self):
        i = self.rot("tb", 10)
        return self.tb[i], f"tb{i}"

    def stt_(self):
        i = self.rot("st", 8)
        return self.st[i], f"st{i}"

    def wload(self, si, cols_ap, ncols, kchunks=8):
        sl, nm = self.wsl[si], f"wsl{si}"
        v = sl[:, 0:kchunks * ncols].rearrange("p (k n) -> p k n", k=kchunks)
        self.LOAD("pool", v, cols_ap.rearrange("(k p) n -> p k n", p=128), [nm], nm)
        return v, nm

    def xload(self, src_rows):
        i = self.rot("xs", 2)
        self.LOAD("sp", self.xs[i], src_rows, [f"xs{i}"], f"xs{i}")
        return self.xs[i], f"xs{i}"

    def norm_T(self, src, sname, wbc_off, dst3, dname, maskcol=None):
        j = self.rot("jk", 2)
        jk, jn = self.jk[j], f"jk{j}"
        xb, xn = self.xb[j], f"xb{j}"
        st, sn = self.stt_()
        cf = self.cf
        self.ACT(jk, src, AF.Square, [sname], [jn, sn], accum=st[:, 0:1])
        self.TS(st[:, 1:2], st[:, 0:1], 1.0 / D, EPS, ALU.mult, ALU.add, [sn], [sn])
        self.ACT(st[:, 2:3], st[:, 1:2], AF.Ln, [sn], [sn])
        self.ACT(st[:, 3:4], st[:, 2:3], AF.Exp, [sn], [sn], scale=-0.5)
        if maskcol is not None:
            self.TT(st[:, 3:4], st[:, 3:4], cf[:, maskcol:maskcol + 1], ALU.mult, [sn, "cf"], [sn])
        self.STT(xb, src, st[:, 3:4], cf[:, wbc_off:wbc_off + D], ALU.mult, ALU.mult, [sname, sn, "cf"], [xn])
        pt, pn = self.pt_new()
        for kc in range(8):
            self.TR_(pt[:, kc * 128:(kc + 1) * 128], xb[:, kc * 128:(kc + 1) * 128], [xn], [pn])
        src3 = pt.rearrange("p (a b) -> p a b", a=8)
        self.DCOPY(dst3[:, 0:4, :], src3[:, 0:4, :], [pn], [dname])
        self.ACOPY(dst3[:, 4:8, :], src3[:, 4:8, :], [pn], [dname])

    def gates(self, lrT, lrn, d):
        ps, pn = self.ps_new()
        self.MM(ps, lrT, self.wa2[d][0:16, :], True, False, [lrn, "wa2"], [pn])
        self.MM(ps, self.ones[0:1, :], self.ba[d][0:1, :], False, True, ["cb", "ba"], [pn])
        gp, gn = self.tF()
        self.ACT(gp, ps, AF.Exp, [pn], [gn], scale=-1.0)
        self.R.dve(lambda e: e.tensor_scalar_add(out=gp, in0=gp, scalar1=1.0), [gn], [gn])
        self.ACT(gp, gp, AF.Ln, [gn], [gn])
        return gp, gn

    def tok_kv(self, xT3, xname, wk, wkn, wv, wvn):
        psk, pkn = self.ps_new()
        for kc in range(8):
            self.MM(psk, xT3[:, kc, :], wk[:, kc, :], kc == 0, kc == 7, [xname, wkn], [pkn])
        psv, pvn = self.ps_new()
        for kc in range(8):
            self.MM(psv, xT3[:, kc, :], wv[:, kc, :], kc == 0, kc == 7, [xname, wvn], [pvn])
        vt, vn = self.tB()
        self.ACOPY(vt, psv, [pvn], [vn])
        return psk, pkn, vt, vn

    def state_prep(self, gp, gn, psk, pkn, d):
        cf = self.cf
        U = cf[:, C_UGT:C_UGT + 128] if d == "f" else cf[:, C_ULT:C_ULT + 128]
        psc, pcn = self.ps_new()
        self.MM(psc, U, gp, True, True, ["cf", gn], [pcn])
        Ec, en = self.tF()
        self.ACT(Ec, psc, AF.Exp, [pcn], [en])
        kt, kn = self.tB()
        self.TT(kt, psk, Ec, ALU.mult, [pkn, en], [kn])
        pse, pen = self.ps_new()
        for hh in range(4):
            self.MM(pse[:, hh:hh + 1], gp[:, hh * 128:(hh + 1) * 128], cf[:, C_NEG:C_NEG + 1], True, True,
                    [gn, "cf"], [pen])
        st, sn = self.stt_()
        self.ACT(st[:, 0:4], pse[:, 0:4], AF.Exp, [pen], [sn])
        return kt, kn, st, sn

    def state_apply(self, kt, kn, st, sn, vt, vn, d, snap=None, snapn=None):
        S, Sn = self.S[d], "S" + d
        psd, pdn = self.ps_new()
        for hh in range(4):
            sl = slice(hh * 128, (hh + 1) * 128)
            self.MM(psd[:, sl], kt[:, sl], vt[:, sl], True, True, [kn, vn], [pdn])
        S3 = S.rearrange("p (h v) -> p h v", h=4)
        self.TT(S3, S3, st[:, 0:4].rearrange("p (h o) -> p h o", o=1).to_broadcast([128, 4, 128]), ALU.mult,
                [Sn, sn], [Sn])
        if snap is not None:
            self.ACOPY(snap, S, [Sn], [snapn])
        self.TT(S, S, psd, ALU.add, [Sn, pdn], [Sn])

    def state_update(self, gp, gn, psk, pkn, vt, vn, d, snap=None, snapn=None):
        kt, kn, st, sn = self.state_prep(gp, gn, psk, pkn, d)
        self.state_apply(kt, kn, st, sn, vt, vn, d, snap, snapn)

    def zero_state(self, d):
        S = self.S[d]
        self.R.dve(lambda e: e.memset(S, 0.0), [], ["S" + d])

    def state_scan(self, xsrc, taus, d, wk, wkn, wv, wvn, snaps=None, saves=None):
        di = 0 if d == "f" else 1
        for n_, tau in enumerate(taus):
            xs, xn = self.xload(xsrc[tau * 128:(tau + 1) * 128, :])
            j = self.rot("xTt", 2)
            xT3, xTn = self.xTt[j], f"xTt{j}"
            self.norm_T(xs, xn, C_WBC1, xT3, xTn)
            psk, pkn, vt, vn = self.tok_kv(xT3, xTn, wk, wkn, wv, wvn)
            psl, pln = self.ps_new()
            for kc in range(8):
                self.MM(psl[0:16, 0:128], self.wlr[:, kc, 16 * di:16 * di + 16], xT3[:, kc, :], kc == 0, kc == 7,
                        ["wlr", xTn], [pln])
            lrt, lrn = self.tB()
            self.DCOPY(lrt[0:16, 0:128], psl[0:16, 0:128], [pln], [lrn])
            gp, gn = self.gates(lrt[0:16, 0:128], lrn, di)
            sp = None if snaps is None else snaps[n_]
            self.state_update(gp, gn, psk, pkn, vt, vn, d, sp[0] if sp else None, sp[1] if sp else None)
            if saves and tau in saves:
                dst, dn = saves[tau]
                S = self.S[d]
                self.R.dve(lambda e, dst=dst, S=S: e.tensor_copy(out=dst, in_=S), ["S" + d], [dn])

    def unit(self, xsrc, kv0, ut, yout, hscr, Sb_init, uname):
        R, T, cf = self.R, self.T, self.cf
        A = self.alloc
        self.top = self.base
        self.cur_unit = uname
        w_in = T["w_in"]
        xnT = A(8 * TK).rearrange("p (k n) -> p k n", k=8)
        for t in range(13):
            xs, xn = self.xload(xsrc[(kv0 + t) * 128:(kv0 + t + 1) * 128, :])
            self.norm_T(xs, xn, C_WBC1, xnT[:, :, t * 128:(t + 1) * 128], f"xnT{t}")
        self.dump("xnT", xnT, [128, 8, TK], BF16, "xnT12")

        def xr(t0, n, off):
            a = (off + t0) // 128
            b = (off + t0 + n - 1) // 128
            return [f"xnT{t}" for t in range(a, b + 1)]

        m1 = self.top
        KT = A(4 * TK).rearrange("p (k n) -> p k n", k=4)
        QT = A(4 * TR).rearrange("p (k n) -> p k n", k=4)
        V = A(13 * 8 * 65)
        V4 = V.rearrange("p (t h d) -> p t h d", t=13, h=8)
        onaT = A(4 * TR).rearrange("p (k n) -> p k n", k=4)

        def headnorm(ps, pn, n, nwc, biasc, dst, dname):
            sq, sqn = self.tB()
            self.ACT(sq[:, :n], ps[:, :n], AF.Square, [pn], [sqn])
            ps2, p2n = self.ps_new()
            self.MM(ps2[:, :n], self.onesblk, sq[:, :n], True, True, ["cb", sqn], [p2n])
            r1, r1n = self.tF()
            self.TS(r1[:, :n], ps2[:, :n], 1.0 / 64, EPS, ALU.mult, ALU.add, [p2n], [r1n])
            self.ACT(r1[:, :n], r1[:, :n], AF.Ln, [r1n], [r1n])
            if biasc is None:
                self.ACT(r1[:, :n], r1[:, :n], AF.Exp, [r1n], [r1n], scale=-0.5)
            else:
                self.ACT(r1[:, :n], r1[:, :n], AF.Exp, [r1n, "cf"], [r1n], scale=-0.5, bias=cf[:, biasc:biasc + 1])
            self.STT(dst, ps[:, :n], cf[:, nwc:nwc + 1], r1[:, :n], ALU.mult, ALU.mult, [pn, r1n, "cf"], [dname])

        w, wn = self.wload(0, w_in[:, 512:1024], 512)
        for p in range(4):
            for (t0, n) in blocks(TK):
                ps, pn = self.ps_new()
                for kc in range(8):
                    self.MM(ps[:, :n], w[:, kc, p * 128:(p + 1) * 128], xnT[:, kc, t0:t0 + n], kc == 0, kc == 7,
                            xr(t0, n, 0) + [wn], [pn])
                headnorm(ps, pn, n, C_KNW, None, KT[:, p, t0:t0 + n], "KT")
        w, wn = self.wload(2, w_in[:, 0:512], 512)
        for p in range(4):
            for (t0, n) in blocks(TR):
                ps, pn = self.ps_new()
                for kc in range(8):
                    self.MM(ps[:, :n], w[:, kc, p * 128:(p + 1) * 128], xnT[:, kc, 256 + t0:256 + t0 + n], kc == 0,
                            kc == 7, xr(t0, n, 256) + [wn], [pn])
                headnorm(ps, pn, n, C_QNW, C_LNQ, QT[:, p, t0:t0 + n], "QT")
        w, wn = self.wload(0, w_in[:, 1024:1536], 512)
        R.pool(lambda e: e.memset(V, 1.0), [], ["V"])
        for t in range(13):
            ps, pn = self.ps_new()
            for kc in range(8):
                self.MM(ps, xnT[:, kc, t * 128:(t + 1) * 128], w[:, kc, :], kc == 0, kc == 7, [f"xnT{t}", wn], [pn])
            self.ACOPY(V4[:, t, :, 0:64], ps.rearrange("p (h d) -> p h d", h=8), [pn], ["V"])
        self.dump("KT", KT, [128, 4, TK], BF16, "KT")
        self.dump("QT", QT, [128, 4, TR], BF16, "QT")
        self.dump("V", V, [128, 13 * 8 * 65], BF16, "V")

        for i in range(9):
            edge_src = None
            if i in EDGE_TOP and ut["top"] is not None:
                edge_src = ut["top"][EDGE_TOP.index(i)]
            if i in EDGE_BOT and ut["bot"] is not None:
                edge_src = ut["bot"][EDGE_BOT.index(i)]
            kvs = kv_list(i, edge_src is not None)
            if edge_src is not None:
                self.LOAD("pool", self.wsl[0][:, 0:4096], edge_src[:, 0:4096], ["wsl0"], "wsl0")
                self.LOAD("pool", self.wsl[2][:, 0:3072], edge_src[:, 4096:7168], ["wsl2"], "wsl2")
            nk = len(kvs)
            oj = self.rot("ona", 2)
            ona, onan = self.ona[oj], f"ona{oj}"
            for h in range(8):
                p, bp = h // 2, 64 * (h % 2)
                psA, pAn = self.ps_new()
                psB, pBn = (self.ps_new() if nk > 4 else (None, None))
                for a, t in enumerate(kvs):
                    dst, dn = (psA, pAn) if a < 4 else (psB, pBn)
                    dst = dst[:, (a % 4) * 128:(a % 4 + 1) * 128]
                    self.MM(dst, KT[bp:bp + 64, p, t * 128:(t + 1) * 128], QT[bp:bp + 64, p, i * 128:(i + 1) * 128],
                            True, False, ["KT", "QT"], [dn])
                    if edge_src is None:
                        sl_ap, sl_n = self.slabI[:, (a * 8 + h) * 128:(a * 8 + h + 1) * 128], "slabI"
                    elif a < 4:
                        sl_ap, sl_n = self.wsl[0][:, (a * 8 + h) * 128:(a * 8 + h + 1) * 128], "wsl0"
                    else:
                        sl_ap, sl_n = self.wsl[2][:, ((a - 4) * 8 + h) * 128:((a - 4) * 8 + h + 1) * 128], "wsl2"
                    self.MM(dst, self.ident, sl_ap, False, True, ["cb", sl_n], [dn])
                pj = self.rot("PT", 2)
                PT, PTn = self.PT[pj], f"PT{pj}"
                na = min(4, nk)
                self.ACT(PT[:, 0:na * 128], psA[:, 0:na * 128], AF.Exp, [pAn], [PTn])
                if nk > 4:
                    self.ACT(PT[:, 512:512 + (nk - 4) * 128], psB[:, 0:(nk - 4) * 128], AF.Exp, [pBn], [PTn])
                pso, pon = self.ps_new()
                for a, t in enumerate(kvs):
                    self.MM(pso[:, 0:65], PT[:, a * 128:(a + 1) * 128], V4[:, t, h, :], a == 0, a == nk - 1,
                            [PTn, "V"], [pon])
                st, sn = self.stt_()
                R.dve(lambda e, st=st, pso=pso: e.reciprocal(out=st[:, 0:1], in_=pso[:, 64:65]), [pon], [sn])
                self.TS(ona[:, h * 64:(h + 1) * 64], pso[:, 0:64], st[:, 0:1], None, ALU.mult, None, [pon, sn], [onan])
            pt, pn = self.pt_new()
            for c in range(4):
                self.TR_(pt[:, c * 128:(c + 1) * 128], ona[:, c * 128:(c + 1) * 128], [onan], [pn])
            self.ACOPY(onaT[:, :, i * 128:(i + 1) * 128], pt[:, 0:512].rearrange("p (a b) -> p a b", a=4), [pn], ["onaT"])
        self.dump("onaT", onaT, [128, 4, TR], BF16, "onaT")

        self.barrier()
        self.top = m1
        onaT2 = A(4 * TR).rearrange("p (k n) -> p k n", k=4)
        self.ACOPY(onaT2, onaT, ["onaT"], ["onaT2"])
        self.barrier()
        onaT = onaT2
        ogT = A(4 * TR).rearrange("p (k n) -> p k n", k=4)
        m_gla = self.top
        qgT = A(4 * TR).rearrange("p (k n) -> p k n", k=4)
        kgT = A(4 * TR).rearrange("p (k n) -> p k n", k=4)
        lr = [self.wsl[2][:, 0:TR], self.wsl[2][:, TR:2 * TR]]
        sog = A(9 * 512).rearrange("p (t n) -> p t n", t=9)
        snap = A(9 * 512).rearrange("p (t n) -> p t n", t=9)
        w, wn = self.wload(0, w_in[:, 1536:2048], 512)
        for hh in range(4):
            for (t0, n) in blocks(TR):
                ps, pn = self.ps_new()
                for kc in range(8):
                    self.MM(ps[:, :n], w[:, kc, hh * 128:(hh + 1) * 128], xnT[:, kc, 256 + t0:256 + t0 + n], kc == 0,
                            kc == 7, xr(t0, n, 256) + [wn], [pn])
                self.AMUL(qgT[:, hh, t0:t0 + n], ps[:, :n], 128 ** -0.5, [pn], ["qgT"])
        wk, wkn = self.wload(1, w_in[:, 2048:2560], 512)
        for hh in range(4):
            for (t0, n) in blocks(TR):
                ps, pn = self.ps_new()
                for kc in range(8):
                    self.MM(ps[:, :n], wk[:, kc, hh * 128:(hh + 1) * 128], xnT[:, kc, 256 + t0:256 + t0 + n], kc == 0,
                            kc == 7, xr(t0, n, 256) + [wkn], [pn])
                self.DCOPY(kgT[:, hh, t0:t0 + n], ps[:, :n], [pn], ["kgT"])
        for d in range(2):
            for (t0, n) in blocks(TR):
                ps, pn = self.ps_new()
                for kc in range(8):
                    self.MM(ps[0:16, :n], self.wlr[:, kc, 16 * d:16 * d + 16], xnT[:, kc, 256 + t0:256 + t0 + n],
                            kc == 0, kc == 7, xr(t0, n, 256) + ["wlr"], [pn])
                self.ACOPY(lr[d][0:16, t0:t0 + n], ps[0:16, :n], [pn], [f"lr{d}", "wsl2"])
        w, wn = self.wload(0, w_in[:, 3072:3584], 512)
        for i in range(9):
            ps, pn = self.ps_new()
            for kc in range(8):
                self.MM(ps, xnT[:, kc, (i + 2) * 128:(i + 3) * 128], w[:, kc, :], kc == 0, kc == 7,
                        [f"xnT{i + 2}", wn], [pn])
            tg, tgn = self.tF()
            self.ACT(tg, ps, AF.Tanh, [pn], [tgn], scale=0.5)
            self.STT(tg, tg, 1.0, ps, ALU.add, ALU.mult, [tgn, pn], [tgn])
            self.TT(sog[:, i, :], tg, cf[:, C_GNW:C_GNW + 512], ALU.mult, [tgn, "cf"], ["sog"])
        wv, wvn = self.wload(3, w_in[:, 2560:3072], 512)

        if Sb_init is None:
            self.zero_state("b")
        else:
            Sb = self.S["b"]
            R.dve(lambda e, Sb=Sb, src=Sb_init[0]: e.tensor_copy(out=Sb, in_=src), [Sb_init[1]], ["Sb"])
        for i in reversed(range(9)):
            xT3 = xnT[:, :, (i + 2) * 128:(i + 3) * 128]
            psk, pkn, vt, vn = self.tok_kv(xT3, f"xnT{i + 2}", wk, wkn, wv, wvn)
            gp, gn = self.gates(lr[1][0:16, i * 128:(i + 1) * 128], "lr1", 1)
            self.state_update(gp, gn, psk, pkn, vt, vn, "b", snap[:, i, :], "snap")

        Sf = self.S["f"]
        Sf3 = Sf.rearrange("p (h v) -> p h v", h=4)
        for i in range(9):
            if i == 8:
                exp_ = self.Sfx
                R.dve(lambda e, exp_=exp_, Sf=Sf: e.tensor_copy(out=exp_, in_=Sf), ["Sf"], ["Sfx"])
            xT3 = xnT[:, :, (i + 2) * 128:(i + 3) * 128]
            psk, pkn, vt, vn = self.tok_kv(xT3, f"xnT{i + 2}", wk, wkn, wv, wvn)
            qts, As = [], []
            gpf = self.gates(lr[0][0:16, i * 128:(i + 1) * 128], "lr0", 0)
            ktf, ktfn, stf, stfn = self.state_prep(gpf[0], gpf[1], psk, pkn, "f")
            for d, dn_ in enumerate(("f", "b")):
                if d == 0:
                    gp, gn = gpf
                else:
                    gp, gn = self.gates(lr[d][0:16, i * 128:(i + 1) * 128], f"lr{d}", d)
                Ufm = cf[:, C_UINC:C_UINC + 128] if d == 0 else cf[:, C_ULT:C_ULT + 128]
                psp, ppn = self.ps_new()
                for hh in range(4):
                    sl = slice(hh * 128, (hh + 1) * 128)
                    self.MM(psp[:, sl], gp[:, sl], Ufm, True, True, [gn, "cf"], [ppn])
                Eq, eqn = self.tF()
                Ek, ekn = self.tF()
                self.ACT(Eq, psp, AF.Exp, [ppn], [eqn], scale=(1.0 if d == 0 else -1.0))
                self.ACT(Ek, psp, AF.Exp, [ppn], [ekn], scale=(-1.0 if d == 0 else 1.0))
                qt, qn = self.tB()
                kt2, k2n = self.tB()
                q3 = qt.rearrange("p (h t) -> p h t", h=4)
                k3 = kt2.rearrange("p (h t) -> p h t", h=4)
                self.TT(q3, qgT[:, :, i * 128:(i + 1) * 128], Eq.rearrange("p (h t) -> p h t", h=4), ALU.mult,
                        ["qgT", eqn], [qn])
                self.TT(k3, kgT[:, :, i * 128:(i + 1) * 128], Ek.rearrange("p (h t) -> p h t", h=4), ALU.mult,
                        ["kgT", ekn], [k2n])
                psa, pan = self.ps_new()
                for hh in range(4):
                    sl = slice(hh * 128, (hh + 1) * 128)
                    self.MM(psa[:, sl], kt2[:, sl], qt[:, sl], True, True, [k2n, qn], [pan])
                Ad, adn = self.tB()
                Mm = self.Mf if d == 0 else self.Mb
                self.TT(Ad.rearrange("p (h t) -> p h t", h=4), psa.rearrange("p (h t) -> p h t", h=4),
                        Mm.rearrange("p (o t) -> p o t", o=1).to_broadcast([128, 4, 128]), ALU.mult, [pan, "cb"], [adn])
                qts.append((qt, qn))
                As.append((Ad, adn))
            self.ACOPY(self.Sbf, Sf, ["Sf"], ["Sbf"])
            pso, pon = self.ps_new()
            for hh in range(4):
                sl = slice(hh * 128, (hh + 1) * 128)
                self.MM(pso[:, sl], qts[0][0][:, sl], self.Sbf[:, sl], True, False, [qts[0][1], "Sbf"], [pon])
                self.MM(pso[:, sl], As[0][0][:, sl], vt[:, sl], False, False, [As[0][1], vn], [pon])
                self.MM(pso[:, sl], qts[1][0][:, sl], snap[:, i, sl], False, False, [qts[1][1], "snap"], [pon])
                self.MM(pso[:, sl], As[1][0][:, sl], vt[:, sl], False, True, [As[1][1], vn], [pon])
            self.state_apply(ktf, ktfn, stf, stfn, vt, vn, "f")
            st, sn = self.stt_()
            jj = self.rot("jk", 2)
            for hh in range(4):
                sl = slice(hh * 128, (hh + 1) * 128)
                self.ACT(self.jk[jj][:, sl], pso[:, sl], AF.Square, [pon], [f"jk{jj}", sn], accum=st[:, hh:hh + 1])
            self.TS(st[:, 4:8], st[:, 0:4], 1.0 / 128, EPS, ALU.mult, ALU.add, [sn], [sn])
            self.ACT(st[:, 4:8], st[:, 4:8], AF.Ln, [sn], [sn])
            self.ACT(st[:, 4:8], st[:, 4:8], AF.Exp, [sn], [sn], scale=-0.5)
            self.TS(st[:, 4:8], st[:, 4:8], 0.5, None, ALU.mult, None, [sn], [sn])
            og, ogn = self.tF()
            self.TT(og.rearrange("p (h t) -> p h t", h=4), pso.rearrange("p (h t) -> p h t", h=4),
                    st[:, 4:8].rearrange("p (h o) -> p h o", o=1).to_broadcast([128, 4, 128]), ALU.mult, [pon, sn], [ogn])
            ogb, obn = self.tB()
            self.TT(ogb, og, sog[:, i, :], ALU.mult, [ogn, "sog"], [obn])
            pt, pn = self.pt_new()
            for c in range(4):
                self.TR_(pt[:, c * 128:(c + 1) * 128], ogb[:, c * 128:(c + 1) * 128], [obn], [pn])
            self.ACOPY(ogT[:, :, i * 128:(i + 1) * 128], pt[:, 0:512].rearrange("p (a b) -> p a b", a=4), [pn], ["ogT"])
        self.dump("ogT", ogT, [128, 4, TR], BF16, "ogT")

        self.barrier()
        self.top = m_gla
        mixT = A(8 * TR).rearrange("p (k n) -> p k n", k=8)
        wna, wnan = self.wload(0, T["w_na_proj"][:, :], 1024, kchunks=4)
        wgl, wgln = self.wload(1, T["w_gla_proj"][:, :], 1024, kchunks=4)
        for hp in range(2):
            g1, g1n = self.wload(2, w_in[:, 3616 + 512 * hp:3616 + 512 * hp + 512], 512)
            g2, g2n = self.wload(3, w_in[:, 4640 + 512 * hp:4640 + 512 * hp + 512], 512)
            for c4 in range(4):
                c = hp * 4 + c4
                for (t0, n) in blocks(TR):
                    ps1, p1n = self.ps_new()
                    for kk in range(4):
                        self.MM(ps1[:, :n], wna[:, kk, c * 128:(c + 1) * 128], onaT[:, kk, t0:t0 + n], kk == 0, kk == 3,
                                [wnan, "onaT2"], [p1n])
                    ps2, p2n = self.ps_new()
                    for kk in range(4):
                        self.MM(ps2[:, :n], wgl[:, kk, c * 128:(c + 1) * 128], ogT[:, kk, t0:t0 + n], kk == 0, kk == 3,
                                [wgln, "ogT"], [p2n])
                    ps3, p3n = self.ps_new()
                    for kc in range(8):
                        self.MM(ps3[:, :n], g1[:, kc, c4 * 128:(c4 + 1) * 128], xnT[:, kc, 256 + t0:256 + t0 + n],
                                kc == 0, kc == 7, xr(t0, n, 256) + [g1n], [p3n])
                    ps4, p4n = self.ps_new()
                    for kc in range(8):
                        self.MM(ps4[:, :n], g2[:, kc, c4 * 128:(c4 + 1) * 128], xnT[:, kc, 256 + t0:256 + t0 + n],
                                kc == 0, kc == 7, xr(t0, n, 256) + [g2n], [p4n])
                    t1, t1n = self.tF()
                    t2, t2n = self.tF()
                    self.ACT(t1[:, :n], ps3[:, :n], AF.Tanh, [p3n], [t1n], scale=0.5)
                    self.ACT(t2[:, :n], ps4[:, :n], AF.Tanh, [p4n], [t2n], scale=0.5)
                    self.STT(t1[:, :n], t1[:, :n], 1.0, ps1[:, :n], ALU.add, ALU.mult, [t1n, p1n], [t1n])
                    self.STT(t2[:, :n], t2[:, :n], 1.0, ps2[:, :n], ALU.add, ALU.mult, [t2n, p2n], [t2n])
                    self.TT(mixT[:, c, t0:t0 + n], t1[:, :n], t2[:, :n], ALU.add, [t1n, t2n], ["mixT"])
        self.dump("mixT", mixT, [128, 8, TR], BF16, "mixT")
        hnT = A(8 * TR).rearrange("p (k n) -> p k n", k=8)
        wo = []
        for half in range(2):
            wo.append(self.wload(half, T["w_out"][:, half * 512:(half + 1) * 512], 512))
        for i in range(9):
            xs, xn = self.xload(xsrc[(kv0 + 2 + i) * 128:(kv0 + 3 + i) * 128, :])
            for half in range(2):
                ps, pn = self.ps_new()
                for kc in range(8):
                    self.MM(ps, mixT[:, kc, i * 128:(i + 1) * 128], wo[half][0][:, kc, :], kc == 0, kc == 7,
                            ["mixT", wo[half][1]], [pn])
                sl = slice(half * 512, (half + 1) * 512)
                self.STT(xs[:, sl], ps, 0.5, xs[:, sl], ALU.mult, ALU.add, [pn, xn], [xn])
            self.STORE("sp", hscr[i * 128:(i + 1) * 128, :], xs, [xn], "hst_" + xn, w=[f"hscr{i}"])
            mc = None
            if i == 0:
                mc = C_MASK + 2 * ut["mcol"]
            if i == 8:
                mc = C_MASK + 2 * ut["mcol"] + 1
            self.norm_T(xs, xn, C_WBC2, hnT[:, :, i * 128:(i + 1) * 128], f"hnT{i}", maskcol=mc)
        self.dump("hnT", hnT, [128, 8, TR], BF16, "hnT8")

        self.barrier()
        self.top = self.base
        hnT2 = A(8 * TR).rearrange("p (k n) -> p k n", k=8)
        self.ACOPY(hnT2[:, 0:4, :], hnT[:, 0:4, :], [f"hnT{i}" for i in range(9)], ["hnT2"])
        self.DCOPY(hnT2[:, 4:8, :], hnT[:, 4:8, :], [f"hnT{i}" for i in range(9)], ["hnT2"])
        self.barrier()
        hnT = hnT2
        fT = A(22 * 1024).rearrange("p (j n) -> p j n", j=22)
        tmp = self.arena[:, self.tmp0:self.tmp0 + 13312]
        u = [tmp[:, 0:2304].bitcast(F32), tmp[:, 2304:4608].bitcast(F32)]
        cv = [tmp[:, 4608:6656].bitcast(F32), tmp[:, 6656:8704].bitcast(F32)]
        gq = tmp[:, 8704:10752].bitcast(F32)
        w_up = T["w_up"]
        for j in range(22):
            si_ = j % 4
            sl_, wn = self.wsl[si_], f"wsl{si_}"
            wv2 = sl_[:, 0:2048].rearrange("p (k n) -> p k n", k=8)
            self.LOAD("pool", wv2[:, :, 0:128], w_up[:, j * 128:(j + 1) * 128].rearrange("(k p) n -> p k n", p=128),
                      [wn], wn + "a")
            self.LOAD("pool", wv2[:, :, 128:256],
                      w_up[:, 2816 + j * 128:2816 + (j + 1) * 128].rearrange("(k p) n -> p k n", p=128), [wn], wn + "b")
            for ab in range(2):
                for (t0, n) in blocks(TR):
                    ps, pn = self.ps_new()
                    for kc in range(8):
                        self.MM(ps[:, :n], wv2[:, kc, ab * 128:(ab + 1) * 128], hnT[:, kc, t0:t0 + n], kc == 0, kc == 7,
                                [wn, "hnT2"], [pn])
                    self.ACOPY(u[ab][:, t0:t0 + n], ps[:, :n], [pn], [f"u{ab}"])
                jj = ab * 22 + j
                cw = lambda tap, jj=jj: cf[:, C_CW + jj * 3 + tap:C_CW + jj * 3 + tap + 1]
                self.ACT(cv[ab], u[ab][:, 63:1087], AF.Identity, [f"u{ab}", "cf"], [f"cv{ab}"], scale=cw(0),
                         bias=cf[:, C_CB + jj:C_CB + jj + 1])
                self.STT(cv[ab], u[ab][:, 64:1088], cw(1), cv[ab], ALU.mult, ALU.add, [f"u{ab}", "cf", f"cv{ab}"],
                         [f"cv{ab}"])
                self.STT(cv[ab], u[ab][:, 65:1089], cw(2), cv[ab], ALU.mult, ALU.add, [f"u{ab}", "cf", f"cv{ab}"],
                         [f"cv{ab}"])
            self.ACT(gq, cv[0], AF.Square, ["cv0"], ["gq"])
            self.TS(gq, gq, 0.044715, 1.0, ALU.mult, ALU.add, ["gq"], ["gq"])
            self.TT(gq, gq, cv[0], ALU.mult, ["gq", "cv0"], ["gq"])
            self.ACT(gq, gq, AF.Tanh, ["gq"], ["gq"], scale=0.7978845608028654)
            self.STT(gq, gq, 1.0, cv[0], ALU.add, ALU.mult, ["gq", "cv0"], ["gq"])
            self.TT(fT[:, j, :], gq, cv[1], ALU.mult, ["gq", "cv1"], ["fT"])
        self.dump("fT", fT, [128, 22, 1024], BF16, "fT")

        self.barrier()
        wdA = self.arena[:, self.base:self.base + 9 * 1024].rearrange("p (j n) -> p j n", j=9)
        wdB = self.arena[:, self.tmp0:self.tmp0 + 13 * 1024].rearrange("p (j n) -> p j n", j=13)
        wdr = T["w_down"].rearrange("(j p) n -> p j n", p=128)
        for g in range(0, 9, 3):
            self.LOAD("pool", wdA[:, g:g + 3, :], wdr[:, g:g + 3, :], ["wd"], f"wdA{g}")
        for g in range(0, 13, 4):
            ge = min(13, g + 4)
            self.LOAD("pool", wdB[:, g:ge, :], wdr[:, 9 + g:9 + ge, :], ["wd"], f"wdB{g}")
        wdj = lambda j: (wdA[:, j] if j < 9 else wdB[:, j - 9])
        for i8 in range(8):
            hi_ = self.rot("xs", 2)
            hs, hn_ = self.xs[hi_], f"xs{hi_}"
            self.LOAD("sp", hs, hscr[64 + i8 * 128:64 + (i8 + 1) * 128, :], [hn_], hn_,
                      r=[f"hscr{i8}", f"hscr{i8 + 1}"])
            for half in range(2):
                ps, pn = self.ps_new()
                for j in range(22):
                    self.MM(ps, fT[:, j, i8 * 128:(i8 + 1) * 128], wdj(j)[:, half * 512:(half + 1) * 512], j == 0,
                            j == 21, ["fT", "wd"], [pn])
                sl = slice(half * 512, (half + 1) * 512)
                self.STT(hs[:, sl], ps, 0.5, hs[:, sl], ALU.mult, ALU.add, [pn, hn_], [hn_])
            self.stores.append(self.STORE("sp", yout[i8 * 128:(i8 + 1) * 128, :], hs, [hn_], "yst_" + hn_))
        self.barrier()

    def sequence(self, xsrc, n_units, tau0, pre, tau_max, uts, yout, hscr):
        T = self.T
        w_in = T["w_in"]
        self.zero_state("f")
        self.zero_state("b")
        post = list(range(tau_max, tau0 + 8, -1))
        saves = {}
        inits = [None] * n_units
        for m in range(n_units):
            need = tau0 + 8 * m + 9
            if need <= tau_max:
                saves[need] = (self.Ssave[m], f"Ssave{m}")
                inits[m] = (self.Ssave[m], f"Ssave{m}")
        if pre or post:
            wk, wkn = self.wload(1, w_in[:, 2048:2560], 512)
            wv, wvn = self.wload(3, w_in[:, 2560:3072], 512)
            if pre:
                self.state_scan(xsrc, pre, "f", wk, wkn, wv, wvn)
            if post:
                self.state_scan(xsrc, post, "b", wk, wkn, wv, wvn, saves=saves)
        self.barrier()
        for m in range(n_units):
            self.unit(xsrc, tau0 + 8 * m - 2, uts[m], yout[m * 1024:(m + 1) * 1024, :], hscr, inits[m], f"u{m}")
            Sf, Sfx = self.S["f"], self.Sfx
            self.R.dve(lambda e, Sf=Sf, Sfx=Sfx: e.tensor_copy(out=Sf, in_=Sfx), ["Sfx"], ["Sf"])
            self.barrier()


def build_nc(dbg=None, only_prompt_units=None, xs_tiles=225):
    nc = bass.Bass("TRN2", target_bir_lowering=False)
    dt = lambda name, shape, kind="ExternalInput": nc.dram_tensor(name, list(shape), F32, kind=kind).ap()
    T = {
        "xp": dt("xp", [2, 21 * 128, D]),
        "xs": dt("xs", [xs_tiles * 128, D]),
        "cf": dt("cf", [128, NCF]),
        "cb": dt("cb", [128, NCB]),
        "slabI": dt("slabI", [128, 5120]),
        "sT_p": dt("sT_p", [3, 128, 7168]),
        "sB_p": dt("sB_p", [2, 128, 7168]),
        "sT_s": dt("sT_s", [3, 128, 7168]),
        "sB_s": dt("sB_s", [2, 128, 7168]),
        "w_in": dt("w_in", [D, 5664]),
        "w_a2_f": dt("w_a2_f", [16, 512]),
        "b_a_f": dt("b_a_f", [1, 512]),
        "w_a2_b": dt("w_a2_b", [16, 512]),
        "b_a_b": dt("b_a_b", [1, 512]),
        "w_na_proj": dt("w_na_proj", [512, D]),
        "w_gla_proj": dt("w_gla_proj", [512, D]),
        "w_out": dt("w_out", [D, D]),
        "w_up": dt("w_up", [D, 5632]),
        "w_down": dt("w_down", [2816, D]),
    }
    yp = dt("yp", [2, 2048, D], "ExternalOutput")
    ys = dt("ys", [4096, D], "ExternalOutput")
    hscr = dt("hscr", [TR, D], "Internal")
    with ExitStack() as es:
        k = K(nc, es, dbg)
        k.setup(T)
        k.Sfx = k.alloc(512, F32)
        k.base = k.top
        sT_p = [T["sT_p"][e] for e in range(3)]
        sB_p = [T["sB_p"][e] for e in range(2)]
        sT_s = [T["sT_s"][e] for e in range(3)]
        sB_s = [T["sB_s"][e] for e in range(2)]
        ut_p = [dict(top=sT_p, bot=None, mcol=0), dict(top=None, bot=sB_p, mcol=1)]
        ut_s = [dict(top=sT_s, bot=None, mcol=2), dict(top=None, bot=None, mcol=3),
                dict(top=None, bot=None, mcol=4), dict(top=None, bot=sB_s, mcol=5)]
        if only_prompt_units is not None:
            k.dbg_unit = f"u{only_prompt_units - 1}"
            k.sequence(T["xp"][0], only_prompt_units, 2, [], 18, ut_p, yp[0], hscr)
        else:
            for b in range(2):
                k.sequence(T["xp"][b], 2, 2, [], 18, ut_p, yp[b], hscr)
            k.sequence(T["xs"], 4, 96, list(range(0, 96)), 224, ut_s, ys, hscr)
        n, cnt = k.R.emit(nc, k.stores)
        k.nops = n
    return nc, k


def make_slab(rpb, i, kvs, lo, hi):
    H = rpb.shape[0]
    out = np.full((128, 7, H, 128), NEG, np.float32)
    kc = np.arange(64)
    qc = np.arange(64)
    c0 = np.clip(qc - 8, 0, 48)
    colv = (kc[:, None] >= c0[None, :]) & (kc[:, None] < c0[None, :] + 16)
    dcv = np.clip(kc[:, None] - qc[None, :] + 15, 0, 30)
    for a, t in enumerate(kvs):
        for rr in range(2):
            rk = 2 * t - 5 + rr
            for qq in range(2):
                rq = 2 * i - 1 + qq
                if lo <= rq < hi:
                    r0 = min(max(rq - 4, lo), hi - 8)
                else:
                    r0 = rq - 4
                if not (r0 <= rk < r0 + 8):
                    continue
                dr = rk - rq + 7
                blk = np.where(colv[None], rpb[:, dr][:, dcv], NEG)
                out[rr * 64:(rr + 1) * 64, a, :, qq * 64:(qq + 1) * 64] = blk.transpose(1, 0, 2)
    return out


def host_consts(norm1_w, norm2_w, gla_norm_w, qn_w, kn_w, conv_w, conv_b, masks):
    cf = np.zeros((128, NCF), np.float32)
    cf[:, C_WBC1:C_WBC1 + D] = norm1_w[None, :]
    cf[:, C_WBC2:C_WBC2 + D] = norm2_w[None, :]
    cf[:, C_GNW:C_GNW + 512] = np.tile(gla_norm_w, 4)[None, :]
    idx = np.arange(128)
    cf[:, C_UINC:C_UINC + 128] = (idx[:, None] <= idx[None, :]) * (-1.0 / 16)
    cf[:, C_ULT:C_ULT + 128] = (idx[:, None] < idx[None, :]) * (-1.0 / 16)
    cf[:, C_UGT:C_UGT + 128] = (idx[:, None] > idx[None, :]) * (-1.0 / 16)
    cw = conv_w.reshape(3, 44, 128)
    cf[:, C_CW:C_CW + 132] = cw.transpose(2, 1, 0).reshape(128, 132)
    cf[:, C_CB:C_CB + 44] = conv_b.reshape(44, 128).T
    cf[:, C_QNW] = np.tile(qn_w, 2)
    cf[:, C_KNW] = np.tile(kn_w, 2)
    cf[:, C_LNQ] = np.float32(np.log(0.125))
    cf[:, C_NEG] = -1.0 / 16
    for m, (top_real, bot_real) in enumerate(masks):
        cf[:, C_MASK + 2 * m] = 1.0
        cf[0:64, C_MASK + 2 * m] = 1.0 if top_real else 0.0
        cf[:, C_MASK + 2 * m + 1] = 1.0
        cf[64:128, C_MASK + 2 * m + 1] = 1.0 if bot_real else 0.0
    cb = np.zeros((128, NCB), np.float32)
    cb[:, B_ID:B_ID + 128] = np.eye(128)
    cb[0:64, B_OB:B_OB + 64] = 1.0
    cb[64:128, B_OB + 64:B_OB + 128] = 1.0
    cb[:, B_ONE:B_ONE + 128] = 1.0
    cb[:, B_MF:B_MF + 128] = (idx[:, None] <= idx[None, :])
    cb[:, B_MB:B_MB + 128] = (idx[:, None] > idx[None, :])
    return cf, cb


def make_in_maps(inp):
    f = lambda k: np.asarray(inp[k], np.float32)
    x_prompt, x_sample = f("x_prompt"), f("x_sample")
    rpb = f("rpb")[0]
    w_in = np.ascontiguousarray(f("w_in")[0])
    shared = {
        "w_in": w_in,
        "w_a2_f": np.ascontiguousarray(f("w_a2_f")[0]), "b_a_f": np.ascontiguousarray(f("b_a_f")[0][None, :]),
        "w_a2_b": np.ascontiguousarray(f("w_a2_b")[0]), "b_a_b": np.ascontiguousarray(f("b_a_b")[0][None, :]),
        "w_na_proj": np.ascontiguousarray(f("w_na_proj")[0]), "w_gla_proj": np.ascontiguousarray(f("w_gla_proj")[0]),
        "w_out": np.ascontiguousarray(f("w_out")[0]), "w_up": np.ascontiguousarray(f("w_up")[0]),
        "w_down": np.ascontiguousarray(f("w_down")[0]),
    }
    flat = lambda s: np.ascontiguousarray(s.reshape(128, 7168))
    slabI = np.ascontiguousarray(make_slab(rpb, 3, kv_list(3, False), -100, 100)[:, 0:5].reshape(128, 5120))
    sT_cl = np.stack([flat(make_slab(rpb, i, kv_list(i, True), 0, 32)) for i in EDGE_TOP])
    sT_un = np.stack([flat(make_slab(rpb, i, kv_list(i, True), -100, 100)) for i in EDGE_TOP])
    sB_cl = np.stack([flat(make_slab(rpb, i, kv_list(i, True), -16, 16)) for i in EDGE_BOT])
    sB_un = np.stack([flat(make_slab(rpb, i, kv_list(i, True), -100, 100)) for i in EDGE_BOT])
    shared.update({"slabI": slabI, "sT_p": sT_cl, "sB_p": sB_cl})
    maps = []
    for c in range(NCORES):
        s, j = c // 4, c % 4
        xp = np.zeros((2, 42, 64, D), np.float32)
        for b in range(2):
            xp[b, 5:37] = x_prompt[2 * c + b].reshape(32, 64, D)
        R0 = 64 * j
        xs = np.zeros((450, 64, D), np.float32)
        g0 = R0 - 193
        lo_, hi_ = max(0, g0), min(256, g0 + 450)
        xs[lo_ - g0:hi_ - g0] = x_sample[s].reshape(256, 64, D)[lo_:hi_]
        masks = [(False, True), (True, False)]
        for m in range(4):
            masks.append((R0 + 16 * m - 1 >= 0, R0 + 16 * m + 16 < 256))
        cf, cb = host_consts(f("norm1_w")[0], f("norm2_w")[0], f("gla_norm_w")[0], f("qn_w")[0], f("kn_w")[0],
                             f("conv_w")[0], f("conv_b")[0], masks)
        d = dict(shared)
        d.update({"xp": xp.reshape(2, 21 * 128, D), "xs": xs.reshape(225 * 128, D), "cf": cf, "cb": cb,
                  "sT_s": sT_cl if j == 0 else sT_un, "sB_s": sB_cl if j == 3 else sB_un})
        maps.append(d)
    return maps


def kernel(**inp):
    maps = make_in_maps(inp)
    nc, k = build_nc()
    res = run_bass_kernel_spmd(nc, maps, core_ids=list(range(NCORES)))
    y_prompt = np.empty((16, 2048, D), np.float32)
    y_sample = np.empty((2, 16384, D), np.float32)
    for c in range(NCORES):
        s, j = c // 4, c % 4
        r = res.results[c]
        y_prompt[2 * c:2 * c + 2] = np.asarray(r["yp"], np.float32)
        y_sample[s, 4096 * j:4096 * (j + 1)] = np.asarray(r["ys"], np.float32)
    return (y_prompt, y_sample)
```

```python
import numpy as np
import concourse.bass as bass
import concourse.mybir as mybir
from concourse.bass_utils import run_bass_kernel_spmd
from contextlib import ExitStack

F32 = mybir.dt.float32
BF16 = mybir.dt.bfloat16
AF = mybir.ActivationFunctionType
ALU = mybir.AluOpType
EPS = 1e-6
NCORES = 8
D = 1024
TR = 1152
TK = 1664
NEG = -30000.0
EDGE_TOP = (0, 1, 2)
EDGE_BOT = (7, 8)
C_WBC1, C_WBC2, C_GNW, C_UINC, C_ULT, C_UGT, C_CW, C_CB = 0, 1024, 2048, 2560, 2688, 2816, 2944, 3076
C_QNW, C_KNW, C_LNQ, C_NEG, C_MASK = 3120, 3121, 3122, 3123, 3124
NCF = 3136
B_ID, B_OB, B_ONE, B_MF, B_MB, B_ZERO = 0, 128, 256, 384, 512, 640
NCB = 768


class Buf:
    __slots__ = ("w", "rs")

    def __init__(self):
        self.w = None
        self.rs = []


class Rec:
    ENGS = ("pe", "act", "dve", "pool", "sp")

    def __init__(self):
        self.ops = []
        self.bufs = {}
        self.bar = None
        self.last = {}
        self.dmas = []

    def B(self, name):
        b = self.bufs.get(name)
        if b is None:
            b = self.bufs[name] = Buf()
        return b

    def add(self, eng, fn, r=(), w=(), dma=False, stream=None):
        deps = set()
        for n in r:
            b = self.B(n)
            if b.w is not None:
                deps.add(b.w)
        for n in w:
            b = self.B(n)
            if b.w is not None:
                deps.add(b.w)
            deps.update(b.rs)
        if self.bar is not None:
            deps.add(self.bar)
        i = len(self.ops)
        self.ops.append(dict(eng=eng, fn=fn, deps=deps, dma=dma, stream=stream, sig=False, ord=0))
        for n in r:
            self.B(n).rs.append(i)
        for n in w:
            b = self.B(n)
            b.w = i
            b.rs = []
        if dma:
            self.dmas.append(i)
        else:
            self.last[eng] = i
        return i

    def barrier(self, fn):
        deps = set(self.last.values()) | set(self.dmas)
        if self.bar is not None:
            deps.add(self.bar)
        i = len(self.ops)
        self.ops.append(dict(eng="dve", fn=fn, deps=deps, dma=False, stream=None, sig=False, ord=0))
        self.bar = i
        self.last["dve"] = i
        self.dmas = []
        return i

    def pe(self, fn, r=(), w=()):
        return self.add("pe", fn, r, w)

    def act(self, fn, r=(), w=()):
        return self.add("act", fn, r, w)

    def dve(self, fn, r=(), w=()):
        return self.add("dve", fn, r, w)

    def pool(self, fn, r=(), w=()):
        return self.add("pool", fn, r, w)

    def dma(self, q, fn, r=(), w=(), stream=None):
        return self.add(q, fn, r, w, dma=True, stream=stream)

    def emit(self, nc, final_wait_ops=()):
        ops = self.ops
        ops.append(dict(eng="sp", fn=None, deps=set(final_wait_ops), dma=False, stream=None, sig=False, ord=0))
        for o in ops:
            for d in o["deps"]:
                od = ops[d]
                if od["dma"]:
                    continue
                if od["eng"] == "pe" and o["eng"] == "pe" and not o["dma"]:
                    continue
                od["sig"] = True
        cnt = {}
        for o in ops:
            if o["dma"]:
                k = ("s", o["stream"])
                cnt[k] = cnt.get(k, 0) + 1
                o["ord"] = cnt[k]
            elif o["sig"]:
                k = ("e", o["eng"])
                cnt[k] = cnt.get(k, 0) + 1
                o["ord"] = cnt[k]
        with ExitStack() as es:
            sem = {}
            for k in cnt:
                sem[k] = es.enter_context(nc.semaphore("sem_%s_%s" % k))
            block = es.enter_context(nc.Block())
            by_eng = {e: [o for o in ops if o["eng"] == e] for e in self.ENGS}

            def run(engh, ename):
                waited = {}
                for o in by_eng[ename]:
                    need = {}
                    for d in o["deps"]:
                        od = ops[d]
                        if od["dma"]:
                            k = ("s", od["stream"])
                            v = 16 * od["ord"]
                        else:
                            if od["eng"] == "pe" and ename == "pe" and not o["dma"]:
                                continue
                            k = ("e", od["eng"])
                            v = od["ord"]
                        if v > need.get(k, 0):
                            need[k] = v
                    for k, v in need.items():
                        if waited.get(k, 0) >= v:
                            continue
                        waited[k] = v
                        engh.wait_ge(sem[k], v)
                    if o["fn"] is None:
                        continue
                    ins = o["fn"](engh)
                    if o["dma"]:
                        ins.then_inc(sem[("s", o["stream"])], 16)
                    elif o["sig"]:
                        ins.then_inc(sem[("e", ename)], 1)

            if by_eng["sp"]:
                block.sync(lambda e: run(e, "sp"))
            if by_eng["pool"]:
                block.gpsimd(lambda e: run(e, "pool"))
            if by_eng["act"]:
                block.scalar(lambda e: run(e, "act"))
            if by_eng["dve"]:
                block.vector(lambda e: run(e, "dve"))
            if by_eng["pe"]:
                block.tensor(lambda e: run(e, "pe"))
        return len(ops), cnt


def blocks(T):
    out = []
    t = 0
    while t < T:
        n = min(512, T - t)
        out.append((t, n))
        t += n
    return out


def kv_list(i, edge):
    if not edge:
        return list(range(i, i + 5))
    return {0: list(range(0, 7)), 1: list(range(1, 7)), 2: list(range(2, 7)),
            7: list(range(6, 12)), 8: list(range(6, 13))}[i]


class K:
    def __init__(self, nc, es, dbg=None):
        self.nc = nc
        self.es = es
        self.R = Rec()
        self.dbg = dbg
        self.dumps = []
        self.stores = []
        self.cnt = {}
        self.arena = es.enter_context(nc.sbuf_tensor("arena", [128, 106400], BF16))
        self.top = 0
        self.ps = [es.enter_context(nc.psum_tensor(f"ps{i}", [128, 512], F32)) for i in range(6)]
        self.pb = [es.enter_context(nc.psum_tensor(f"pb{i}", [128, 1024], BF16)) for i in range(2)]

    def alloc(self, n, dt=BF16):
        if dt == F32:
            n2 = 2 * n
        else:
            n2 = n
        self.top = (self.top + 1) // 2 * 2
        a = self.arena[:, self.top:self.top + n2]
        self.top += n2
        assert self.top <= 106400, self.top
        return a.bitcast(F32) if dt == F32 else a

    def rot(self, key, n):
        c = self.cnt.get(key, 0)
        self.cnt[key] = c + 1
        return c % n

    def ps_new(self, bank=None):
        i = self.rot("ps", 6) if bank is None else bank
        return self.ps[i][:], f"ps{i}"

    def pt_new(self):
        i = self.rot("pt", 2)
        return self.pb[i][:], f"pb{i}"

    def MM(self, out, lhsT, rhs, start, stop, r, w):
        self.R.pe(lambda e: e.matmul(out, lhsT=lhsT, rhs=rhs, start=start, stop=stop), r, w)

    def TR_(self, out, in_, r, w):
        ident = self.ident
        self.R.pe(lambda e: e.transpose(out=out, in_=in_, identity=ident), list(r) + ["cb"], w)

    def ACT(self, out, in_, func, r, w, scale=None, bias=None, accum=None):
        kw = {}
        if scale is not None:
            kw["scale"] = scale
        if bias is not None:
            kw["bias"] = bias
        if accum is not None:
            kw["accum_out"] = accum
        self.R.act(lambda e: e.activation(out=out, in_=in_, func=func, **kw), r, w)

    def ACOPY(self, out, in_, r, w):
        self.R.act(lambda e: e.copy(out=out, in_=in_), r, w)

    def AMUL(self, out, in_, c, r, w):
        self.R.act(lambda e: e.mul(out=out, in_=in_, mul=c), r, w)

    def DCOPY(self, out, in_, r, w):
        self.R.dve(lambda e: e.tensor_copy(out=out, in_=in_), r, w)

    def TS(self, out, in0, s1, s2, op0, op1, r, w):
        if op1 is None:
            self.R.dve(lambda e: e.tensor_scalar(out=out, in0=in0, scalar1=s1, scalar2=None, op0=op0), r, w)
        else:
            self.R.dve(lambda e: e.tensor_scalar(out=out, in0=in0, scalar1=s1, scalar2=s2, op0=op0, op1=op1), r, w)

    def TT(self, out, in0, in1, op, r, w):
        self.R.dve(lambda e: e.tensor_tensor(out=out, in0=in0, in1=in1, op=op), r, w)

    def STT(self, out, in0, scalar, in1, op0, op1, r, w):
        self.R.dve(lambda e: e.scalar_tensor_tensor(out=out, in0=in0, scalar=scalar, in1=in1, op0=op0, op1=op1), r, w)

    def LOAD(self, q, out, in_, w, stream, r=()):
        return self.R.dma(q, lambda e: e.dma_start(out=out, in_=in_), r=r, w=w, stream=stream)

    def STORE(self, q, out, in_, r, stream, w=()):
        i = self.R.dma(q, lambda e: e.dma_start(out=out, in_=in_), r=r, w=w, stream=stream)
        return i

    def dump(self, name, ap, shape, dt, rname):
        if self.dbg is None or name not in self.dbg or getattr(self, "cur_unit", None) != getattr(self, "dbg_unit", None):
            return
        d = self.nc.dram_tensor("dbg_" + name, list(shape), dt, kind="ExternalOutput").ap()
        self.stores.append(self.STORE("sp", d, ap, [rname], "dbg_" + name))
        self.dumps.append(name)

    def barrier(self):
        bt = self.bartile
        self.R.barrier(lambda e: e.memset(bt, 0.0))

    def setup(self, T):
        self.T = T
        A = self.alloc
        self.cf = A(NCF, F32)
        self.cb = A(NCB)
        self.ident = self.cb[:, B_ID:B_ID + 128]
        self.onesblk = self.cb[:, B_OB:B_OB + 128]
        self.ones = self.cb[:, B_ONE:B_ONE + 128]
        self.Mf = self.cb[:, B_MF:B_MF + 128]
        self.Mb = self.cb[:, B_MB:B_MB + 128]
        self.wa2 = [A(512), A(512)]
        self.ba = [A(512), A(512)]
        self.wlr = A(256).rearrange("p (k n) -> p k n", k=8)
        self.slabI = A(5120)
        self.S = {"f": A(512, F32), "b": A(512, F32)}
        self.Sbf = A(512)
        self.Ssave = [A(512, F32) for _ in range(4)]
        self.xs = [A(1024, F32) for _ in range(2)]
        self.jk = [A(1024) for _ in range(2)]
        self.xb = [A(1024) for _ in range(2)]
        self.xTt = [A(1024).rearrange("p (k n) -> p k n", k=8) for _ in range(2)]
        self.st = [A(8, F32) for _ in range(8)]
        self.wsl = [A(4096) for _ in range(4)]
        self.PT = [A(896) for _ in range(2)]
        self.ona = [A(512) for _ in range(2)]
        self.bartile = A(2, F32)
        self.tmp0 = self.top = (self.top + 1) // 2 * 2
        self.tf = [A(512, F32) for _ in range(8)]
        self.tb = [A(512) for _ in range(10)]
        assert self.top - self.tmp0 == 13312
        self.base = self.top
        L = self.LOAD
        L("sp", self.cf, T["cf"][:, :], ["cf"], "cf")
        L("pool", self.cb, T["cb"][:, :], ["cb"], "cb")
        for d, nm in enumerate(("f", "b")):
            L("pool", self.wa2[d][0:16, :], T["w_a2_" + nm][:, :], ["wa2"], "wa2" + nm)
            L("pool", self.ba[d][0:1, :], T["b_a_" + nm][:, :], ["ba"], "ba" + nm)
        L("pool", self.wlr, T["w_in"][:, 3584:3616].rearrange("(k p) n -> p k n", p=128), ["wlr"], "wlr")

    def tF(self):
        i = self.rot("tf", 8)
        return self.tf[i], f"tf{i}"

    def tB(self):
        i = self.rot("tb", 10)
        return self.tb[i], f"tb{i}"

    def stt_(self):
        i = self.rot("st", 8)
        return self.st[i], f"st{i}"

    def wload(self, si, cols_ap, ncols, kchunks=8):
        sl, nm = self.wsl[si], f"wsl{si}"
        v = sl[:, 0:kchunks * ncols].rearrange("p (k n) -> p k n", k=kchunks)
        if getattr(self, "nowload", False) and self.cnt.get("wl%d" % si, 0) > 0:
            return v, nm
        self.cnt["wl%d" % si] = 1
        self.LOAD("pool", v, cols_ap.rearrange("(k p) n -> p k n", p=128), [nm], nm)
        return v, nm

    def xload(self, src_rows):
        i = self.rot("xs", 2)
        self.LOAD("sp", self.xs[i], src_rows, [f"xs{i}"], f"xs{i}")
        return self.xs[i], f"xs{i}"

    def norm_A(self, src, sname, wbc_off, maskcol=None):
        j = self.rot("jk", 2)
        jk, jn = self.jk[j], f"jk{j}"
        xb, xn = self.xb[j], f"xb{j}"
        st, sn = self.stt_()
        cf = self.cf
        self.ACT(jk, src, AF.Square, [sname], [jn, sn], accum=st[:, 0:1])
        self.TS(st[:, 1:2], st[:, 0:1], 1.0 / D, EPS, ALU.mult, ALU.add, [sn], [sn])
        self.ACT(st[:, 2:3], st[:, 1:2], AF.Ln, [sn], [sn])
        self.ACT(st[:, 3:4], st[:, 2:3], AF.Exp, [sn], [sn], scale=-0.5)
        if maskcol is not None:
            self.TT(st[:, 3:4], st[:, 3:4], cf[:, maskcol:maskcol + 1], ALU.mult, [sn, "cf"], [sn])
        self.STT(xb, src, st[:, 3:4], cf[:, wbc_off:wbc_off + D], ALU.mult, ALU.mult, [sname, sn, "cf"], [xn])
        return xb, xn

    def norm_B(self, xb, xn, dst3, dname):
        pt, pn = self.pt_new()
        for kc in range(8):
            self.TR_(pt[:, kc * 128:(kc + 1) * 128], xb[:, kc * 128:(kc + 1) * 128], [xn], [pn])
        src3 = pt.rearrange("p (a b) -> p a b", a=8)
        self.DCOPY(dst3[:, 0:4, :], src3[:, 0:4, :], [pn], [dname])
        self.ACOPY(dst3[:, 4:8, :], src3[:, 4:8, :], [pn], [dname])

    def norm_T(self, src, sname, wbc_off, dst3, dname, maskcol=None):
        xb, xn = self.norm_A(src, sname, wbc_off, maskcol)
        self.norm_B(xb, xn, dst3, dname)

    def gates(self, lrT, lrn, d, bank=None):
        ps, pn = self.ps_new(bank)
        self.MM(ps, lrT, self.wa2[d][0:16, :], True, False, [lrn, "wa2"], [pn])
        self.MM(ps, self.ones[0:1, :], self.ba[d][0:1, :], False, True, ["cb", "ba"], [pn])
        gp, gn = self.tF()
        self.ACT(gp, ps, AF.Exp, [pn], [gn], scale=-1.0)
        self.R.dve(lambda e: e.tensor_scalar_add(out=gp, in0=gp, scalar1=1.0), [gn], [gn])
        self.ACT(gp, gp, AF.Ln, [gn], [gn])
        return gp, gn

    def tok_kv(self, xT3, xname, wk, wkn, wv, wvn):
        psk, pkn = self.ps_new()
        for kc in range(8):
            self.MM(psk, xT3[:, kc, :], wk[:, kc, :], kc == 0, kc == 7, [xname, wkn], [pkn])
        psv, pvn = self.ps_new()
        for kc in range(8):
            self.MM(psv, xT3[:, kc, :], wv[:, kc, :], kc == 0, kc == 7, [xname, wvn], [pvn])
        vt, vn = self.tB()
        self.ACOPY(vt, psv, [pvn], [vn])
        return psk, pkn, vt, vn

    def state_prep(self, gp, gn, psk, pkn, d, banks=(None, None)):
        cf = self.cf
        U = cf[:, C_UGT:C_UGT + 128] if d == "f" else cf[:, C_ULT:C_ULT + 128]
        psc, pcn = self.ps_new(banks[0])
        self.MM(psc, U, gp, True, True, ["cf", gn], [pcn])
        Ec, en = self.tF()
        self.ACT(Ec, psc, AF.Exp, [pcn], [en])
        kt, kn = self.tB()
        self.TT(kt, psk, Ec, ALU.mult, [pkn, en], [kn])
        pse, pen = self.ps_new(banks[1])
        for hh in range(4):
            self.MM(pse[:, hh:hh + 1], gp[:, hh * 128:(hh + 1) * 128], cf[:, C_NEG:C_NEG + 1], True, True,
                    [gn, "cf"], [pen])
        st, sn = self.stt_()
        self.ACT(st[:, 0:4], pse[:, 0:4], AF.Exp, [pen], [sn])
        return kt, kn, st, sn

    def state_apply(self, kt, kn, st, sn, vt, vn, d, snap=None, snapn=None, bank=None):
        S, Sn = self.S[d], "S" + d
        psd, pdn = self.ps_new(bank)
        for hh in range(4):
            sl = slice(hh * 128, (hh + 1) * 128)
            self.MM(psd[:, sl], kt[:, sl], vt[:, sl], True, True, [kn, vn], [pdn])
        S3 = S.rearrange("p (h v) -> p h v", h=4)
        self.TT(S3, S3, st[:, 0:4].rearrange("p (h o) -> p h o", o=1).to_broadcast([128, 4, 128]), ALU.mult,
                [Sn, sn], [Sn])
        if snap is not None:
            self.ACOPY(snap, S, [Sn], [snapn])
        self.TT(S, S, psd, ALU.add, [Sn, pdn], [Sn])

    def state_update(self, gp, gn, psk, pkn, vt, vn, d, snap=None, snapn=None):
        kt, kn, st, sn = self.state_prep(gp, gn, psk, pkn, d)
        self.state_apply(kt, kn, st, sn, vt, vn, d, snap, snapn)

    def zero_state(self, d):
        S = self.S[d]
        self.R.dve(lambda e: e.memset(S, 0.0), [], ["S" + d])

    def scan_pipe(self, d, n, p1, wk, wkn, wv, wvn, lr_of=None, snap_of=None, save_after=None, p1a=None):
        di = 0 if d == "f" else 1
        s1, s2, s3 = {}, {}, {}
        ksb_pool = [self.slabI[:, j * 1024:(j + 1) * 1024].bitcast(F32) for j in range(3)]

        def P2(i):
            xT3, xn = s1.pop(i)
            psk, pkn = self.ps_new(0)
            for kc in range(8):
                self.MM(psk, xT3[:, kc, :], wk[:, kc, :], kc == 0, kc == 7, [xn, wkn], [pkn])
            j = self.rot("ksb", 3)
            ksb, ksn = ksb_pool[j], f"ksb{j}"
            self.ACOPY(ksb, psk, [pkn], [ksn])
            psv, pvn = self.ps_new(1)
            for kc in range(8):
                self.MM(psv, xT3[:, kc, :], wv[:, kc, :], kc == 0, kc == 7, [xn, wvn], [pvn])
            vt, vn = self.tB()
            self.DCOPY(vt, psv, [pvn], [vn])
            if lr_of is None:
                psl, pln = self.ps_new(2)
                for kc in range(8):
                    self.MM(psl[0:16, 0:128], self.wlr[:, kc, 16 * di:16 * di + 16], xT3[:, kc, :], kc == 0, kc == 7,
                            ["wlr", xn], [pln])
                lrt, lrn = self.tB()
                self.DCOPY(lrt[0:16, 0:128], psl[0:16, 0:128], [pln], [lrn])
                lr = (lrt[0:16, 0:128], lrn)
            else:
                lr = lr_of(i)
            s2[i] = (ksb, ksn, vt, vn, lr)

        def P3a(i):
            ksb, ksn, vt, vn, lr = s2.pop(i)
            gp, gn = self.gates(lr[0], lr[1], di, bank=3)
            s3[i] = (ksb, ksn, vt, vn, gp, gn)

        def P3b(i):
            ksb, ksn, vt, vn, gp, gn = s3.pop(i)
            kt, kn, st, sn = self.state_prep(gp, gn, ksb, ksn, d, banks=(4, 5))
            sp = snap_of(i) if snap_of else None
            self.state_apply(kt, kn, st, sn, vt, vn, d, sp[0] if sp else None, sp[1] if sp else None, bank=5)
            if save_after and i in save_after:
                dst, dn = save_after[i]
                S = self.S[d]
                self.R.dve(lambda e, dst=dst, S=S: e.tensor_copy(out=dst, in_=S), ["S" + d], [dn])

        s0 = {}
        for it in range(n + 4):
            if 0 <= it - 4 < n:
                P3b(it - 4)
            if 0 <= it - 3 < n:
                P3a(it - 3)
            if 0 <= it - 2 < n:
                P2(it - 2)
            if 0 <= it - 1 < n:
                s1[it - 1] = p1(it - 1, s0.pop(it - 1, None))
            if it < n and p1a is not None:
                s0[it] = p1a(it)

    def state_scan(self, xsrc, taus, d, wk, wkn, wv, wvn, saves=None):
        def p1a(i):
            tau = taus[i]
            xs, xn = self.xload(xsrc[tau * 128:(tau + 1) * 128, :])
            return self.norm_A(xs, xn, C_WBC1)

        def p1(i, tok):
            j = self.rot("xTt", 2)
            xT3, xTn = self.xTt[j], f"xTt{j}"
            self.norm_B(tok[0], tok[1], xT3, xTn)
            return xT3, xTn
        sa = None
        if saves:
            sa = {i: saves[t] for i, t in enumerate(taus) if t in saves}
        self.scan_pipe(d, len(taus), p1, wk, wkn, wv, wvn, save_after=sa, p1a=p1a)

    def unit(self, xsrc, kv0, ut, yout, hscr, Sb_init, uname):
        R, T, cf = self.R, self.T, self.cf
        A = self.alloc
        self.top = self.base
        self.cur_unit = uname
        w_in = T["w_in"]
        xnT = A(8 * TK).rearrange("p (k n) -> p k n", k=8)
        self.LOAD("pool", self.slabI, T["slabI"][:, :], ["slabI"] + [f"ksb{j}" for j in range(3)], "slabI")
        for t in range(13):
            xs, xn = self.xload(xsrc[(kv0 + t) * 128:(kv0 + t + 1) * 128, :])
            self.norm_T(xs, xn, C_WBC1, xnT[:, :, t * 128:(t + 1) * 128], f"xnT{t}")
        self.dump("xnT", xnT, [128, 8, TK], BF16, "xnT12")
        if getattr(self, "stop_after", None) == "U1":
            self.R.dve(lambda e: e.memset(self.Sfx, 0.0), [], ["Sfx"])
            self.barrier()
            return

        def xr(t0, n, off):
            a = (off + t0) // 128
            b = (off + t0 + n - 1) // 128
            return [f"xnT{t}" for t in range(a, b + 1)]

        m1 = self.top
        KT = A(4 * TK).rearrange("p (k n) -> p k n", k=4)
        QT = A(4 * TR).rearrange("p (k n) -> p k n", k=4)
        V = A(13 * 8 * 65)
        V4 = V.rearrange("p (t h d) -> p t h d", t=13, h=8)
        onaT = A(4 * TR).rearrange("p (k n) -> p k n", k=4)

        def headnorm(ps, pn, n, nwc, biasc, dst, dname):
            sq, sqn = self.tB()
            self.ACT(sq[:, :n], ps[:, :n], AF.Square, [pn], [sqn])
            ps2, p2n = self.ps_new()
            self.MM(ps2[:, :n], self.onesblk, sq[:, :n], True, True, ["cb", sqn], [p2n])
            r1, r1n = self.tF()
            self.TS(r1[:, :n], ps2[:, :n], 1.0 / 64, EPS, ALU.mult, ALU.add, [p2n], [r1n])
            self.ACT(r1[:, :n], r1[:, :n], AF.Ln, [r1n], [r1n])
            if biasc is None:
                self.ACT(r1[:, :n], r1[:, :n], AF.Exp, [r1n], [r1n], scale=-0.5)
            else:
                self.ACT(r1[:, :n], r1[:, :n], AF.Exp, [r1n, "cf"], [r1n], scale=-0.5, bias=cf[:, biasc:biasc + 1])
            self.STT(dst, ps[:, :n], cf[:, nwc:nwc + 1], r1[:, :n], ALU.mult, ALU.mult, [pn, r1n, "cf"], [dname])

        w, wn = self.wload(0, w_in[:, 512:1024], 512)
        for p in range(4):
            for (t0, n) in blocks(TK):
                ps, pn = self.ps_new()
                for kc in range(8):
                    self.MM(ps[:, :n], w[:, kc, p * 128:(p + 1) * 128], xnT[:, kc, t0:t0 + n], kc == 0, kc == 7,
                            xr(t0, n, 0) + [wn], [pn])
                headnorm(ps, pn, n, C_KNW, None, KT[:, p, t0:t0 + n], "KT")
        w, wn = self.wload(2, w_in[:, 0:512], 512)
        for p in range(4):
            for (t0, n) in blocks(TR):
                ps, pn = self.ps_new()
                for kc in range(8):
                    self.MM(ps[:, :n], w[:, kc, p * 128:(p + 1) * 128], xnT[:, kc, 256 + t0:256 + t0 + n], kc == 0,
                            kc == 7, xr(t0, n, 256) + [wn], [pn])
                headnorm(ps, pn, n, C_QNW, C_LNQ, QT[:, p, t0:t0 + n], "QT")
        w, wn = self.wload(0, w_in[:, 1024:1536], 512)
        R.pool(lambda e: e.memset(V, 1.0), [], ["V"])
        for t in range(13):
            ps, pn = self.ps_new()
            for kc in range(8):
                self.MM(ps, xnT[:, kc, t * 128:(t + 1) * 128], w[:, kc, :], kc == 0, kc == 7, [f"xnT{t}", wn], [pn])
            self.ACOPY(V4[:, t, :, 0:64], ps.rearrange("p (h d) -> p h d", h=8), [pn], ["V"])
        self.dump("KT", KT, [128, 4, TK], BF16, "KT")
        self.dump("QT", QT, [128, 4, TR], BF16, "QT")
        self.dump("V", V, [128, 13 * 8 * 65], BF16, "V")
        if getattr(self, "stop_after", None) == "U2":
            self.R.dve(lambda e: e.memset(self.Sfx, 0.0), [], ["Sfx"])
            self.barrier()
            return

        onaAll = A(9 * 512).rearrange("p (i n) -> p i n", i=9)
        edge_qps = []
        if ut["top"] is not None:
            edge_qps += [(i, ut["top"][EDGE_TOP.index(i)]) for i in EDGE_TOP]
        if ut["bot"] is not None:
            edge_qps += [(i, ut["bot"][EDGE_BOT.index(i)]) for i in EDGE_BOT]
        eidx = {i: n_ for n_, (i, _) in enumerate(edge_qps)}
        kvl = {i: kv_list(i, i in eidx) for i in range(9)}
        for h in range(8):
            p, bp = h // 2, 64 * (h % 2)
            es_i = 0 if h % 2 == 0 else 2
            es, esn = self.wsl[es_i], f"wsl{es_i}"
            for n_, (i, src) in enumerate(edge_qps):
                self.LOAD("pool", es[:, n_ * 896:(n_ + 1) * 896], src[:, h * 896:(h + 1) * 896], [esn], f"{esn}_{n_}")
            OX, OXn = self.ps_new(4)
            OY, OYn = self.ps_new(5)
            zer = self.cb[:, B_ZERO:B_ZERO + 128]
            self.MM(OX[:, 0:455], zer, self.slabI[:, 0:455], True, False, ["cb", "slabI"], [OXn])
            self.MM(OY[:, 0:130], zer, self.slabI[:, 0:130], True, False, ["cb", "slabI"], [OYn])
            lastX = max(kvl[i][-1] for i in range(7))
            lastY = max(kvl[i][-1] for i in (7, 8))
            for t in range(13):
                qs = [i for i in range(9) if t in kvl[i]]
                if not qs:
                    continue
                assert qs == list(range(qs[0], qs[-1] + 1)) and len(qs) <= 7
                bA, bB = (0, 1) if t % 2 == 0 else (2, 3)
                psA, pAn = self.ps_new(bA)
                psB, pBn = self.ps_new(bB)
                qA, qB = qs[:4], qs[4:]
                for (qq, ps_, pn_) in ((qA, psA, pAn), (qB, psB, pBn)):
                    if not qq:
                        continue
                    nq = len(qq)
                    c = 0
                    while c < nq:
                        i = qq[c]
                        if i in eidx:
                            c2 = c + 1
                            a_ = kvl[i].index(t)
                            blk = eidx[i] * 896 + a_ * 128
                            b_ap, b_n = es[:, blk:blk + 128], esn
                        else:
                            c2 = c
                            while c2 < nq and qq[c2] not in eidx:
                                c2 += 1
                            blk = (h * 5 + (i - t + 4)) * 128
                            b_ap, b_n = self.slabI[:, blk:blk + (c2 - c) * 128], "slabI"
                        self.MM(ps_[:, c * 128:c2 * 128], KT[bp:bp + 64, p, t * 128:(t + 1) * 128],
                                QT[bp:bp + 64, p, qq[c] * 128:(qq[c2 - 1] + 1) * 128], True, False, ["KT", "QT"], [pn_])
                        self.MM(ps_[:, c * 128:c2 * 128], self.ident, b_ap, False, True, ["cb", b_n], [pn_])
                        c = c2
                pj = self.rot("PT", 2)
                PT, PTn = self.PT[pj], f"PT{pj}"
                self.ACT(PT[:, 0:len(qA) * 128], psA[:, 0:len(qA) * 128], AF.Exp, [pAn], [PTn])
                if qB:
                    self.ACT(PT[:, 512:512 + len(qB) * 128], psB[:, 0:len(qB) * 128], AF.Exp, [pBn], [PTn])
                for c, i in enumerate(qs):
                    if i < 7:
                        od, odn = OX[:, i * 65:(i + 1) * 65], OXn
                    else:
                        od, odn = OY[:, (i - 7) * 65:(i - 6) * 65], OYn
                    is_last = (c == len(qs) - 1 and t == lastY) if i >= 7 else \
                        (t == lastX and i == max(q for q in qs if q < 7))
                    self.MM(od, PT[:, c * 128:(c + 1) * 128], V4[:, t, h, :], False, is_last, [PTn, "V"], [odn])
            for (O_, On_, i0, ni) in ((OX, OXn, 0, 7), (OY, OYn, 7, 2)):
                O3 = O_[:, 0:ni * 65].rearrange("p (i d) -> p i d", i=ni)
                st, sn = self.stt_()
                R.dve(lambda e, st=st, O3=O3, ni=ni: e.reciprocal(out=st[:, 0:ni], in_=O3[:, :, 64]), [On_], [sn])
                self.TT(onaAll[:, i0:i0 + ni, h * 64:(h + 1) * 64], O3[:, :, 0:64],
                        st[:, 0:ni].rearrange("p (i o) -> p i o", o=1).to_broadcast([128, ni, 64]), ALU.mult,
                        [On_, sn], ["onaAll"])
        for i in range(9):
            pt, pn = self.pt_new()
            for c in range(4):
                self.TR_(pt[:, c * 128:(c + 1) * 128], onaAll[:, i, c * 128:(c + 1) * 128], ["onaAll"], [pn])
            self.ACOPY(onaT[:, :, i * 128:(i + 1) * 128], pt[:, 0:512].rearrange("p (a b) -> p a b", a=4), [pn], ["onaT"])
        self.dump("onaT", onaT, [128, 4, TR], BF16, "onaT")
        if getattr(self, "stop_after", None) == "U3":
            self.R.dve(lambda e: e.memset(self.Sfx, 0.0), [], ["Sfx"])
            self.barrier()
            return

        self.barrier()
        self.top = m1
        onaT2 = A(4 * TR).rearrange("p (k n) -> p k n", k=4)
        self.ACOPY(onaT2, onaT, ["onaT"], ["onaT2"])
        self.barrier()
        onaT = onaT2
        ogT = A(4 * TR).rearrange("p (k n) -> p k n", k=4)
        m_gla = self.top
        qgT = A(4 * TR).rearrange("p (k n) -> p k n", k=4)
        kgT = A(4 * TR).rearrange("p (k n) -> p k n", k=4)
        lr = [self.wsl[2][:, 0:TR], self.wsl[2][:, TR:2 * TR]]
        sog = A(9 * 512).rearrange("p (t n) -> p t n", t=9)
        snap = A(9 * 512).rearrange("p (t n) -> p t n", t=9)
        w, wn = self.wload(0, w_in[:, 1536:2048], 512)
        for hh in range(4):
            for (t0, n) in blocks(TR):
                ps, pn = self.ps_new()
                for kc in range(8):
                    self.MM(ps[:, :n], w[:, kc, hh * 128:(hh + 1) * 128], xnT[:, kc, 256 + t0:256 + t0 + n], kc == 0,
                            kc == 7, xr(t0, n, 256) + [wn], [pn])
                self.AMUL(qgT[:, hh, t0:t0 + n], ps[:, :n], 128 ** -0.5, [pn], ["qgT"])
        wk, wkn = self.wload(1, w_in[:, 2048:2560], 512)
        for hh in range(4):
            for (t0, n) in blocks(TR):
                ps, pn = self.ps_new()
                for kc in range(8):
                    self.MM(ps[:, :n], wk[:, kc, hh * 128:(hh + 1) * 128], xnT[:, kc, 256 + t0:256 + t0 + n], kc == 0,
                            kc == 7, xr(t0, n, 256) + [wkn], [pn])
                self.DCOPY(kgT[:, hh, t0:t0 + n], ps[:, :n], [pn], ["kgT"])
        for d in range(2):
            for (t0, n) in blocks(TR):
                ps, pn = self.ps_new()
                for kc in range(8):
                    self.MM(ps[0:16, :n], self.wlr[:, kc, 16 * d:16 * d + 16], xnT[:, kc, 256 + t0:256 + t0 + n],
                            kc == 0, kc == 7, xr(t0, n, 256) + ["wlr"], [pn])
                self.ACOPY(lr[d][0:16, t0:t0 + n], ps[0:16, :n], [pn], [f"lr{d}", "wsl2"])
        w, wn = self.wload(0, w_in[:, 3072:3584], 512)
        for i in range(9):
            ps, pn = self.ps_new()
            for kc in range(8):
                self.MM(ps, xnT[:, kc, (i + 2) * 128:(i + 3) * 128], w[:, kc, :], kc == 0, kc == 7,
                        [f"xnT{i + 2}", wn], [pn])
            tg, tgn = self.tF()
            self.ACT(tg, ps, AF.Tanh, [pn], [tgn], scale=0.5)
            self.STT(tg, tg, 1.0, ps, ALU.add, ALU.mult, [tgn, pn], [tgn])
            self.TT(sog[:, i, :], tg, cf[:, C_GNW:C_GNW + 512], ALU.mult, [tgn, "cf"], ["sog"])
        wv, wvn = self.wload(3, w_in[:, 2560:3072], 512)
        if getattr(self, "stop_after", None) == "U5":
            self.R.dve(lambda e: e.memset(self.Sfx, 0.0), [], ["Sfx"])
            self.barrier()
            return

        if Sb_init is None:
            self.zero_state("b")
        else:
            Sb = self.S["b"]
            R.dve(lambda e, Sb=Sb, src=Sb_init[0]: e.tensor_copy(out=Sb, in_=src), [Sb_init[1]], ["Sb"])
        order6 = list(reversed(range(9)))
        self.scan_pipe("b", 9,
                       lambda n_, tok=None: (xnT[:, :, (order6[n_] + 2) * 128:(order6[n_] + 3) * 128], f"xnT{order6[n_] + 2}"),
                       wk, wkn, wv, wvn,
                       lr_of=lambda n_: (lr[1][0:16, order6[n_] * 128:(order6[n_] + 1) * 128], "lr1"),
                       snap_of=lambda n_: (snap[:, order6[n_], :], "snap"))
        Sf = self.S["f"]
        Sf3 = Sf.rearrange("p (h v) -> p h v", h=4)
        for i in range(9):
            if i == 8:
                exp_ = self.Sfx
                R.dve(lambda e, exp_=exp_, Sf=Sf: e.tensor_copy(out=exp_, in_=Sf), ["Sf"], ["Sfx"])
            xT3 = xnT[:, :, (i + 2) * 128:(i + 3) * 128]
            psk, pkn, vt, vn = self.tok_kv(xT3, f"xnT{i + 2}", wk, wkn, wv, wvn)
            qts, As = [], []
            gpf = self.gates(lr[0][0:16, i * 128:(i + 1) * 128], "lr0", 0)
            ktf, ktfn, stf, stfn = self.state_prep(gpf[0], gpf[1], psk, pkn, "f")
            for d, dn_ in enumerate(("f", "b")):
                if d == 0:
                    gp, gn = gpf
                else:
                    gp, gn = self.gates(lr[d][0:16, i * 128:(i + 1) * 128], f"lr{d}", d)
                Ufm = cf[:, C_UINC:C_UINC + 128] if d == 0 else cf[:, C_ULT:C_ULT + 128]
                psp, ppn = self.ps_new()
                for hh in range(4):
                    sl = slice(hh * 128, (hh + 1) * 128)
                    self.MM(psp[:, sl], gp[:, sl], Ufm, True, True, [gn, "cf"], [ppn])
                Eq, eqn = self.tF()
                Ek, ekn = self.tF()
                self.ACT(Eq, psp, AF.Exp, [ppn], [eqn], scale=(1.0 if d == 0 else -1.0))
                self.ACT(Ek, psp, AF.Exp, [ppn], [ekn], scale=(-1.0 if d == 0 else 1.0))
                qt, qn = self.tB()
                kt2, k2n = self.tB()
                q3 = qt.rearrange("p (h t) -> p h t", h=4)
                k3 = kt2.rearrange("p (h t) -> p h t", h=4)
                self.TT(q3, qgT[:, :, i * 128:(i + 1) * 128], Eq.rearrange("p (h t) -> p h t", h=4), ALU.mult,
                        ["qgT", eqn], [qn])
                self.TT(k3, kgT[:, :, i * 128:(i + 1) * 128], Ek.rearrange("p (h t) -> p h t", h=4), ALU.mult,
                        ["kgT", ekn], [k2n])
                psa, pan = self.ps_new()
                for hh in range(4):
                    sl = slice(hh * 128, (hh + 1) * 128)
                    self.MM(psa[:, sl], kt2[:, sl], qt[:, sl], True, True, [k2n, qn], [pan])
                Ad, adn = self.tB()
                Mm = self.Mf if d == 0 else self.Mb
                self.TT(Ad.rearrange("p (h t) -> p h t", h=4), psa.rearrange("p (h t) -> p h t", h=4),
                        Mm.rearrange("p (o t) -> p o t", o=1).to_broadcast([128, 4, 128]), ALU.mult, [pan, "cb"], [adn])
                qts.append((qt, qn))
                As.append((Ad, adn))
            self.ACOPY(self.Sbf, Sf, ["Sf"], ["Sbf"])
            pso, pon = self.ps_new()
            for hh in range(4):
                sl = slice(hh * 128, (hh + 1) * 128)
                self.MM(pso[:, sl], qts[0][0][:, sl], self.Sbf[:, sl], True, False, [qts[0][1], "Sbf"], [pon])
                self.MM(pso[:, sl], As[0][0][:, sl], vt[:, sl], False, False, [As[0][1], vn], [pon])
                self.MM(pso[:, sl], qts[1][0][:, sl], snap[:, i, sl], False, False, [qts[1][1], "snap"], [pon])
                self.MM(pso[:, sl], As[1][0][:, sl], vt[:, sl], False, True, [As[1][1], vn], [pon])
            self.state_apply(ktf, ktfn, stf, stfn, vt, vn, "f")
            st, sn = self.stt_()
            jj = self.rot("jk", 2)
            for hh in range(4):
                sl = slice(hh * 128, (hh + 1) * 128)
                self.ACT(self.jk[jj][:, sl], pso[:, sl], AF.Square, [pon], [f"jk{jj}", sn], accum=st[:, hh:hh + 1])
            self.TS(st[:, 4:8], st[:, 0:4], 1.0 / 128, EPS, ALU.mult, ALU.add, [sn], [sn])
            self.ACT(st[:, 4:8], st[:, 4:8], AF.Ln, [sn], [sn])
            self.ACT(st[:, 4:8], st[:, 4:8], AF.Exp, [sn], [sn], scale=-0.5)
            self.TS(st[:, 4:8], st[:, 4:8], 0.5, None, ALU.mult, None, [sn], [sn])
            og, ogn = self.tF()
            self.TT(og.rearrange("p (h t) -> p h t", h=4), pso.rearrange("p (h t) -> p h t", h=4),
                    st[:, 4:8].rearrange("p (h o) -> p h o", o=1).to_broadcast([128, 4, 128]), ALU.mult, [pon, sn], [ogn])
            ogb, obn = self.tB()
            self.TT(ogb, og, sog[:, i, :], ALU.mult, [ogn, "sog"], [obn])
            pt, pn = self.pt_new()
            for c in range(4):
                self.TR_(pt[:, c * 128:(c + 1) * 128], ogb[:, c * 128:(c + 1) * 128], [obn], [pn])
            self.ACOPY(ogT[:, :, i * 128:(i + 1) * 128], pt[:, 0:512].rearrange("p (a b) -> p a b", a=4), [pn], ["ogT"])
        self.dump("ogT", ogT, [128, 4, TR], BF16, "ogT")
        if getattr(self, "stop_after", None) == "U7":
            self.R.dve(lambda e: e.memset(self.Sfx, 0.0), [], ["Sfx"])
            self.barrier()
            return

        self.barrier()
        self.top = m_gla
        mixT = A(8 * TR).rearrange("p (k n) -> p k n", k=8)
        wna, wnan = self.wload(0, T["w_na_proj"][:, :], 1024, kchunks=4)
        wgl, wgln = self.wload(1, T["w_gla_proj"][:, :], 1024, kchunks=4)
        for hp in range(2):
            g1, g1n = self.wload(2, w_in[:, 3616 + 512 * hp:3616 + 512 * hp + 512], 512)
            g2, g2n = self.wload(3, w_in[:, 4640 + 512 * hp:4640 + 512 * hp + 512], 512)
            for c4 in range(4):
                c = hp * 4 + c4
                for (t0, n) in blocks(TR):
                    ps1, p1n = self.ps_new()
                    for kk in range(4):
                        self.MM(ps1[:, :n], wna[:, kk, c * 128:(c + 1) * 128], onaT[:, kk, t0:t0 + n], kk == 0, kk == 3,
                                [wnan, "onaT2"], [p1n])
                    ps2, p2n = self.ps_new()
                    for kk in range(4):
                        self.MM(ps2[:, :n], wgl[:, kk, c * 128:(c + 1) * 128], ogT[:, kk, t0:t0 + n], kk == 0, kk == 3,
                                [wgln, "ogT"], [p2n])
                    ps3, p3n = self.ps_new()
                    for kc in range(8):
                        self.MM(ps3[:, :n], g1[:, kc, c4 * 128:(c4 + 1) * 128], xnT[:, kc, 256 + t0:256 + t0 + n],
                                kc == 0, kc == 7, xr(t0, n, 256) + [g1n], [p3n])
                    ps4, p4n = self.ps_new()
                    for kc in range(8):
                        self.MM(ps4[:, :n], g2[:, kc, c4 * 128:(c4 + 1) * 128], xnT[:, kc, 256 + t0:256 + t0 + n],
                                kc == 0, kc == 7, xr(t0, n, 256) + [g2n], [p4n])
                    t1, t1n = self.tF()
                    t2, t2n = self.tF()
                    self.ACT(t1[:, :n], ps3[:, :n], AF.Tanh, [p3n], [t1n], scale=0.5)
                    self.ACT(t2[:, :n], ps4[:, :n], AF.Tanh, [p4n], [t2n], scale=0.5)
                    self.STT(t1[:, :n], t1[:, :n], 1.0, ps1[:, :n], ALU.add, ALU.mult, [t1n, p1n], [t1n])
                    self.STT(t2[:, :n], t2[:, :n], 1.0, ps2[:, :n], ALU.add, ALU.mult, [t2n, p2n], [t2n])
                    self.TT(mixT[:, c, t0:t0 + n], t1[:, :n], t2[:, :n], ALU.add, [t1n, t2n], ["mixT"])
        self.dump("mixT", mixT, [128, 8, TR], BF16, "mixT")
        if getattr(self, "stop_after", None) == "U8a":
            self.R.dve(lambda e: e.memset(self.Sfx, 0.0), [], ["Sfx"])
            self.barrier()
            return
        hnT = A(8 * TR).rearrange("p (k n) -> p k n", k=8)
        wo = []
        for half in range(2):
            wo.append(self.wload(half, T["w_out"][:, half * 512:(half + 1) * 512], 512))
        for i in range(9):
            xs, xn = self.xload(xsrc[(kv0 + 2 + i) * 128:(kv0 + 3 + i) * 128, :])
            for half in range(2):
                ps, pn = self.ps_new()
                for kc in range(8):
                    self.MM(ps, mixT[:, kc, i * 128:(i + 1) * 128], wo[half][0][:, kc, :], kc == 0, kc == 7,
                            ["mixT", wo[half][1]], [pn])
                sl = slice(half * 512, (half + 1) * 512)
                self.STT(xs[:, sl], ps, 0.5, xs[:, sl], ALU.mult, ALU.add, [pn, xn], [xn])
            self.STORE("sp", hscr[i * 128:(i + 1) * 128, :], xs, [xn], "hst_" + xn, w=[f"hscr{i}"])
            mc = None
            if i == 0:
                mc = C_MASK + 2 * ut["mcol"]
            if i == 8:
                mc = C_MASK + 2 * ut["mcol"] + 1
            self.norm_T(xs, xn, C_WBC2, hnT[:, :, i * 128:(i + 1) * 128], f"hnT{i}", maskcol=mc)
        self.dump("hnT", hnT, [128, 8, TR], BF16, "hnT8")
        if getattr(self, "stop_after", None) == "U8b":
            self.R.dve(lambda e: e.memset(self.Sfx, 0.0), [], ["Sfx"])
            self.barrier()
            return

        self.barrier()
        self.top = self.base
        hnT2 = A(8 * TR).rearrange("p (k n) -> p k n", k=8)
        self.ACOPY(hnT2[:, 0:4, :], hnT[:, 0:4, :], [f"hnT{i}" for i in range(9)], ["hnT2"])
        self.DCOPY(hnT2[:, 4:8, :], hnT[:, 4:8, :], [f"hnT{i}" for i in range(9)], ["hnT2"])
        self.barrier()
        hnT = hnT2
        fT = A(22 * 1024).rearrange("p (j n) -> p j n", j=22)
        tmp = self.arena[:, self.tmp0:self.tmp0 + 13312]
        u = [self.slabI[:, 0:2304].bitcast(F32), self.slabI[:, 2304:4608].bitcast(F32)]
        cv = [tmp[:, 0:2048].bitcast(F32), tmp[:, 2048:4096].bitcast(F32)]
        gq = tmp[:, 4096:6144].bitcast(F32)
        wdT = tmp[:, 6144:13312].rearrange("p (j n) -> p j n", j=7)
        wdL = tmp[:, 0:6144].rearrange("p (j n) -> p j n", j=6)
        wdC = A(9 * 1024).rearrange("p (j n) -> p j n", j=9)
        wdr = T["w_down"].rearrange("(j p) n -> p j n", p=128)
        w_up = T["w_up"]
        for j in range(22):
            si_ = j % 4
            sl_, wn = self.wsl[si_], f"wsl{si_}"
            wv2 = sl_[:, 0:2048].rearrange("p (k n) -> p k n", k=8)
            self.LOAD("pool", wv2[:, :, 0:128], w_up[:, j * 128:(j + 1) * 128].rearrange("(k p) n -> p k n", p=128),
                      [wn], wn + "a")
            self.LOAD("pool", wv2[:, :, 128:256],
                      w_up[:, 2816 + j * 128:2816 + (j + 1) * 128].rearrange("(k p) n -> p k n", p=128), [wn], wn + "b")
            if j == 3:
                for g in range(0, 9, 3):
                    self.LOAD("pool", wdC[:, g:g + 3, :], wdr[:, g:g + 3, :], ["wdC"], f"wdC{g}")
                self.LOAD("pool", wdT[:, 0:4, :], wdr[:, 9:13, :], ["wdT"], "wdT0")
                self.LOAD("pool", wdT[:, 4:7, :], wdr[:, 13:16, :], ["wdT"], "wdT4")
            for ab in range(2):
                for (t0, n) in blocks(TR):
                    ps, pn = self.ps_new()
                    for kc in range(8):
                        self.MM(ps[:, :n], wv2[:, kc, ab * 128:(ab + 1) * 128], hnT[:, kc, t0:t0 + n], kc == 0, kc == 7,
                                [wn, "hnT2"], [pn])
                    self.ACOPY(u[ab][:, t0:t0 + n], ps[:, :n], [pn], [f"u{ab}"])
                jj = ab * 22 + j
                cw = lambda tap, jj=jj: cf[:, C_CW + jj * 3 + tap:C_CW + jj * 3 + tap + 1]
                self.ACT(cv[ab], u[ab][:, 63:1087], AF.Identity, [f"u{ab}", "cf"], [f"cv{ab}"], scale=cw(0),
                         bias=cf[:, C_CB + jj:C_CB + jj + 1])
                self.STT(cv[ab], u[ab][:, 64:1088], cw(1), cv[ab], ALU.mult, ALU.add, [f"u{ab}", "cf", f"cv{ab}"],
                         [f"cv{ab}"])
                self.STT(cv[ab], u[ab][:, 65:1089], cw(2), cv[ab], ALU.mult, ALU.add, [f"u{ab}", "cf", f"cv{ab}"],
                         [f"cv{ab}"])
            self.ACT(gq, cv[0], AF.Square, ["cv0"], ["gq"])
            self.TS(gq, gq, 0.044715, 1.0, ALU.mult, ALU.add, ["gq"], ["gq"])
            self.TT(gq, gq, cv[0], ALU.mult, ["gq", "cv0"], ["gq"])
            self.ACT(gq, gq, AF.Tanh, ["gq"], ["gq"], scale=0.7978845608028654)
            self.STT(gq, gq, 1.0, cv[0], ALU.add, ALU.mult, ["gq", "cv0"], ["gq"])
            self.TT(fT[:, j, :], gq, cv[1], ALU.mult, ["gq", "cv1"], ["fT"])
        self.dump("fT", fT, [128, 22, 1024], BF16, "fT")
        if getattr(self, "stop_after", None) == "U10":
            self.R.dve(lambda e: e.memset(self.Sfx, 0.0), [], ["Sfx"])
            self.barrier()
            return

        self.barrier()
        self.LOAD("pool", wdL[:, 0:3, :], wdr[:, 16:19, :], ["wdL"], "wdL0")
        self.LOAD("pool", wdL[:, 3:6, :], wdr[:, 19:22, :], ["wdL"], "wdL3")

        def wdj(j):
            if j < 9:
                return wdC[:, j], "wdC"
            if j < 16:
                return wdT[:, j - 9], "wdT"
            return wdL[:, j - 16], "wdL"
        for i8 in range(8):
            hi_ = self.rot("xs", 2)
            hs, hn_ = self.xs[hi_], f"xs{hi_}"
            self.LOAD("sp", hs, hscr[64 + i8 * 128:64 + (i8 + 1) * 128, :], [hn_], hn_,
                      r=[f"hscr{i8}", f"hscr{i8 + 1}"])
            for half in range(2):
                ps, pn = self.ps_new()
                for j in range(22):
                    self.MM(ps, fT[:, j, i8 * 128:(i8 + 1) * 128], wdj(j)[0][:, half * 512:(half + 1) * 512], j == 0,
                            j == 21, ["fT", wdj(j)[1]], [pn])
                sl = slice(half * 512, (half + 1) * 512)
                self.STT(hs[:, sl], ps, 0.5, hs[:, sl], ALU.mult, ALU.add, [pn, hn_], [hn_])
            self.stores.append(self.STORE("sp", yout[i8 * 128:(i8 + 1) * 128, :], hs, [hn_], "yst_" + hn_))
        self.barrier()

    def sequence(self, xsrc, n_units, tau0, pre, tau_max, uts, yout, hscr):
        T = self.T
        w_in = T["w_in"]
        self.zero_state("f")
        self.zero_state("b")
        post = list(range(tau_max, tau0 + 8, -1))
        saves = {}
        inits = [None] * n_units
        for m in range(n_units):
            need = tau0 + 8 * m + 9
            if need <= tau_max:
                saves[need] = (self.Ssave[m], f"Ssave{m}")
                inits[m] = (self.Ssave[m], f"Ssave{m}")
        if pre or post:
            wk, wkn = self.wload(1, w_in[:, 2048:2560], 512)
            wv, wvn = self.wload(3, w_in[:, 2560:3072], 512)
            if pre:
                self.state_scan(xsrc, pre, "f", wk, wkn, wv, wvn)
            if post:
                self.state_scan(xsrc, post, "b", wk, wkn, wv, wvn, saves=saves)
        self.barrier()
        for m in range(n_units):
            self.unit(xsrc, tau0 + 8 * m - 2, uts[m], yout[m * 1024:(m + 1) * 1024, :], hscr, inits[m], f"u{m}")
            Sf, Sfx = self.S["f"], self.Sfx
            self.R.dve(lambda e, Sf=Sf, Sfx=Sfx: e.tensor_copy(out=Sf, in_=Sfx), ["Sfx"], ["Sf"])
            self.barrier()


def build_nc(dbg=None, only_prompt_units=None, xs_tiles=225, variant=None):
    nc = bass.Bass("TRN2", target_bir_lowering=False)
    dt = lambda name, shape, kind="ExternalInput": nc.dram_tensor(name, list(shape), F32, kind=kind).ap()
    T = {
        "xp": dt("xp", [2, 21 * 128, D]),
        "xs": dt("xs", [xs_tiles * 128, D]),
        "cf": dt("cf", [128, NCF]),
        "cb": dt("cb", [128, NCB]),
        "slabI": dt("slabI", [128, 5120]),
        "sT_p": dt("sT_p", [3, 128, 7168]),
        "sB_p": dt("sB_p", [2, 128, 7168]),
        "sT_s": dt("sT_s", [3, 128, 7168]),
        "sB_s": dt("sB_s", [2, 128, 7168]),
        "w_in": dt("w_in", [D, 5664]),
        "w_a2_f": dt("w_a2_f", [16, 512]),
        "b_a_f": dt("b_a_f", [1, 512]),
        "w_a2_b": dt("w_a2_b", [16, 512]),
        "b_a_b": dt("b_a_b", [1, 512]),
        "w_na_proj": dt("w_na_proj", [512, D]),
        "w_gla_proj": dt("w_gla_proj", [512, D]),
        "w_out": dt("w_out", [D, D]),
        "w_up": dt("w_up", [D, 5632]),
        "w_down": dt("w_down", [2816, D]),
    }
    yp = dt("yp", [2, 2048, D], "ExternalOutput")
    ys = dt("ys", [4096, D], "ExternalOutput")
    hscr = dt("hscr", [TR, D], "Internal")
    with ExitStack() as es:
        k = K(nc, es, dbg)
        k.setup(T)
        k.Sfx = k.alloc(512, F32)
        k.base = k.top
        sT_p = [T["sT_p"][e] for e in range(3)]
        sB_p = [T["sB_p"][e] for e in range(2)]
        sT_s = [T["sT_s"][e] for e in range(3)]
        sB_s = [T["sB_s"][e] for e in range(2)]
        ut_p = [dict(top=sT_p, bot=None, mcol=0), dict(top=None, bot=sB_p, mcol=1)]
        ut_s = [dict(top=sT_s, bot=None, mcol=2), dict(top=None, bot=None, mcol=3),
                dict(top=None, bot=None, mcol=4), dict(top=None, bot=sB_s, mcol=5)]
        if only_prompt_units is not None:
            k.dbg_unit = f"u{only_prompt_units - 1}"
            k.sequence(T["xp"][0], only_prompt_units, 2, [], 18, ut_p, yp[0], hscr)
        elif variant == "prompts":
            for b in range(2):
                k.sequence(T["xp"][b], 2, 2, [], 18, ut_p, yp[b], hscr)
        elif variant == "scans":
            k.sequence(T["xs"], 0, 96, list(range(0, 96)), 224, ut_s, ys, hscr)
        else:
            for b in range(2):
                k.sequence(T["xp"][b], 2, 2, [], 18, ut_p, yp[b], hscr)
            k.sequence(T["xs"], 4, 96, list(range(0, 96)), 224, ut_s, ys, hscr)
        n, cnt = k.R.emit(nc, k.stores)
        k.nops = n
    return nc, k


def make_slab(rpb, i, kvs, lo, hi):
    H = rpb.shape[0]
    out = np.full((128, 7, H, 128), NEG, np.float32)
    kc = np.arange(64)
    qc = np.arange(64)
    c0 = np.clip(qc - 8, 0, 48)
    colv = (kc[:, None] >= c0[None, :]) & (kc[:, None] < c0[None, :] + 16)
    dcv = np.clip(kc[:, None] - qc[None, :] + 15, 0, 30)
    for a, t in enumerate(kvs):
        for rr in range(2):
            rk = 2 * t - 5 + rr
            for qq in range(2):
                rq = 2 * i - 1 + qq
                if lo <= rq < hi:
                    r0 = min(max(rq - 4, lo), hi - 8)
                else:
                    r0 = rq - 4
                if not (r0 <= rk < r0 + 8):
                    continue
                dr = rk - rq + 7
                blk = np.where(colv[None], rpb[:, dr][:, dcv], NEG)
                out[rr * 64:(rr + 1) * 64, a, :, qq * 64:(qq + 1) * 64] = blk.transpose(1, 0, 2)
    return out


def host_consts(norm1_w, norm2_w, gla_norm_w, qn_w, kn_w, conv_w, conv_b, masks):
    cf = np.zeros((128, NCF), np.float32)
    cf[:, C_WBC1:C_WBC1 + D] = norm1_w[None, :]
    cf[:, C_WBC2:C_WBC2 + D] = norm2_w[None, :]
    cf[:, C_GNW:C_GNW + 512] = np.tile(gla_norm_w, 4)[None, :]
    idx = np.arange(128)
    cf[:, C_UINC:C_UINC + 128] = (idx[:, None] <= idx[None, :]) * (-1.0 / 16)
    cf[:, C_ULT:C_ULT + 128] = (idx[:, None] < idx[None, :]) * (-1.0 / 16)
    cf[:, C_UGT:C_UGT + 128] = (idx[:, None] > idx[None, :]) * (-1.0 / 16)
    cw = conv_w.reshape(3, 44, 128)
    cf[:, C_CW:C_CW + 132] = cw.transpose(2, 1, 0).reshape(128, 132)
    cf[:, C_CB:C_CB + 44] = conv_b.reshape(44, 128).T
    cf[:, C_QNW] = np.tile(qn_w, 2)
    cf[:, C_KNW] = np.tile(kn_w, 2)
    cf[:, C_LNQ] = np.float32(np.log(0.125))
    cf[:, C_NEG] = -1.0 / 16
    for m, (top_real, bot_real) in enumerate(masks):
        cf[:, C_MASK + 2 * m] = 1.0
        cf[0:64, C_MASK + 2 * m] = 1.0 if top_real else 0.0
        cf[:, C_MASK + 2 * m + 1] = 1.0
        cf[64:128, C_MASK + 2 * m + 1] = 1.0 if bot_real else 0.0
    cb = np.zeros((128, NCB), np.float32)
    cb[:, B_ID:B_ID + 128] = np.eye(128)
    cb[0:64, B_OB:B_OB + 64] = 1.0
    cb[64:128, B_OB + 64:B_OB + 128] = 1.0
    cb[:, B_ONE:B_ONE + 128] = 1.0
    cb[:, B_MF:B_MF + 128] = (idx[:, None] <= idx[None, :])
    cb[:, B_MB:B_MB + 128] = (idx[:, None] > idx[None, :])
    return cf, cb


def make_in_maps(inp):
    f = lambda k: np.asarray(inp[k], np.float32)
    x_prompt, x_sample = f("x_prompt"), f("x_sample")
    rpb = f("rpb")[0]
    w_in = np.ascontiguousarray(f("w_in")[0])
    shared = {
        "w_in": w_in,
        "w_a2_f": np.ascontiguousarray(f("w_a2_f")[0]), "b_a_f": np.ascontiguousarray(f("b_a_f")[0][None, :]),
        "w_a2_b": np.ascontiguousarray(f("w_a2_b")[0]), "b_a_b": np.ascontiguousarray(f("b_a_b")[0][None, :]),
        "w_na_proj": np.ascontiguousarray(f("w_na_proj")[0]), "w_gla_proj": np.ascontiguousarray(f("w_gla_proj")[0]),
        "w_out": np.ascontiguousarray(f("w_out")[0]), "w_up": np.ascontiguousarray(f("w_up")[0]),
        "w_down": np.ascontiguousarray(f("w_down")[0]),
    }
    flat = lambda s: np.ascontiguousarray(s.transpose(0, 2, 1, 3).reshape(128, 7168))
    slabI = make_slab(rpb, 3, kv_list(3, False), -100, 100)[:, 0:5]
    slabI = np.ascontiguousarray(slabI[:, ::-1].transpose(0, 2, 1, 3).reshape(128, 5120))
    sT_cl = np.stack([flat(make_slab(rpb, i, kv_list(i, True), 0, 32)) for i in EDGE_TOP])
    sT_un = np.stack([flat(make_slab(rpb, i, kv_list(i, True), -100, 100)) for i in EDGE_TOP])
    sB_cl = np.stack([flat(make_slab(rpb, i, kv_list(i, True), -16, 16)) for i in EDGE_BOT])
    sB_un = np.stack([flat(make_slab(rpb, i, kv_list(i, True), -100, 100)) for i in EDGE_BOT])
    shared.update({"slabI": slabI, "sT_p": sT_cl, "sB_p": sB_cl})
    maps = []
    for c in range(NCORES):
        s, j = c // 4, c % 4
        xp = np.zeros((2, 42, 64, D), np.float32)
        for b in range(2):
            xp[b, 5:37] = x_prompt[2 * c + b].reshape(32, 64, D)
        R0 = 64 * j
        xs = np.zeros((450, 64, D), np.float32)
        g0 = R0 - 193
        lo_, hi_ = max(0, g0), min(256, g0 + 450)
        xs[lo_ - g0:hi_ - g0] = x_sample[s].reshape(256, 64, D)[lo_:hi_]
        masks = [(False, True), (True, False)]
        for m in range(4):
            masks.append((R0 + 16 * m - 1 >= 0, R0 + 16 * m + 16 < 256))
        cf, cb = host_consts(f("norm1_w")[0], f("norm2_w")[0], f("gla_norm_w")[0], f("qn_w")[0], f("kn_w")[0],
                             f("conv_w")[0], f("conv_b")[0], masks)
        d = dict(shared)
        d.update({"xp": xp.reshape(2, 21 * 128, D), "xs": xs.reshape(225 * 128, D), "cf": cf, "cb": cb,
                  "sT_s": sT_cl if j == 0 else sT_un, "sB_s": sB_cl if j == 3 else sB_un})
        maps.append(d)
    return maps


def kernel(**inp):
    maps = make_in_maps(inp)
    nc, k = build_nc()
    res = run_bass_kernel_spmd(nc, maps, core_ids=list(range(NCORES)))
    y_prompt = np.empty((16, 2048, D), np.float32)
    y_sample = np.empty((2, 16384, D), np.float32)
    for c in range(NCORES):
        s, j = c // 4, c % 4
        r = res.results[c]
        y_prompt[2 * c:2 * c + 2] = np.asarray(r["yp"], np.float32)
        y_sample[s, 4096 * j:4096 * (j + 1)] = np.asarray(r["ys"], np.float32)
    return (y_prompt, y_sample)
```

```python
import numpy as np
import concourse.bass as bass
import concourse.mybir as mybir
from concourse.bass_utils import run_bass_kernel_spmd
from contextlib import ExitStack

F32 = mybir.dt.float32
BF16 = mybir.dt.bfloat16
AF = mybir.ActivationFunctionType
ALU = mybir.AluOpType
EPS = 1e-6
NCORES = 8
D = 1024
TR = 1152
TK = 1664
NEG = -30000.0
EDGE_TOP = (0, 1, 2)
EDGE_BOT = (7, 8)
C_WBC1, C_WBC2, C_GNW, C_UINC, C_ULT, C_UGT, C_CW, C_CB = 0, 1024, 2048, 2560, 2688, 2816, 2944, 3076
C_QNW, C_KNW, C_LNQ, C_NEG, C_MASK, C_EPS = 3120, 3121, 3122, 3123, 3124, 3136
NCF = 3138
B_ID, B_OB, B_ONE, B_MF, B_MB, B_ZERO = 0, 128, 256, 384, 512, 640
NCB = 768


class Buf:
    __slots__ = ("w", "rs")

    def __init__(self):
        self.w = None
        self.rs = []


class Rec:
    ENGS = ("pe", "act", "dve", "pool", "sp")

    def __init__(self):
        self.ops = []
        self.bufs = {}
        self.bar = None
        self.last = {}
        self.dmas = []

    def B(self, name):
        b = self.bufs.get(name)
        if b is None:
            b = self.bufs[name] = Buf()
        return b

    def add(self, eng, fn, r=(), w=(), dma=False, stream=None):
        deps = set()
        for n in r:
            b = self.B(n)
            if b.w is not None:
                deps.add(b.w)
        for n in w:
            b = self.B(n)
            if b.w is not None:
                deps.add(b.w)
            deps.update(b.rs)
        if self.bar is not None:
            deps.add(self.bar)
        i = len(self.ops)
        self.ops.append(dict(eng=eng, fn=fn, deps=deps, dma=dma, stream=stream, sig=False, ord=0))
        for n in r:
            self.B(n).rs.append(i)
        for n in w:
            b = self.B(n)
            b.w = i
            b.rs = []
        if dma:
            self.dmas.append(i)
        else:
            self.last[eng] = i
        return i

    def barrier(self, fn):
        deps = set(self.last.values()) | set(self.dmas)
        if self.bar is not None:
            deps.add(self.bar)
        i = len(self.ops)
        self.ops.append(dict(eng="dve", fn=fn, deps=deps, dma=False, stream=None, sig=False, ord=0))
        self.bar = i
        self.last["dve"] = i
        self.dmas = []
        return i

    def pe(self, fn, r=(), w=()):
        return self.add("pe", fn, r, w)

    def act(self, fn, r=(), w=()):
        return self.add("act", fn, r, w)

    def dve(self, fn, r=(), w=()):
        return self.add("dve", fn, r, w)

    def pool(self, fn, r=(), w=()):
        return self.add("pool", fn, r, w)

    def dma(self, q, fn, r=(), w=(), stream=None):
        return self.add(q, fn, r, w, dma=True, stream=stream)

    def emit(self, nc, final_wait_ops=()):
        ops = self.ops
        ops.append(dict(eng="sp", fn=None, deps=set(final_wait_ops), dma=False, stream=None, sig=False, ord=0))
        for o in ops:
            for d in o["deps"]:
                od = ops[d]
                if od["dma"]:
                    continue
                if od["eng"] == "pe" and o["eng"] == "pe" and not o["dma"]:
                    continue
                od["sig"] = True
        cnt = {}
        for o in ops:
            if o["dma"]:
                k = ("s", o["stream"])
                cnt[k] = cnt.get(k, 0) + 1
                o["ord"] = cnt[k]
            elif o["sig"]:
                k = ("e", o["eng"])
                cnt[k] = cnt.get(k, 0) + 1
                o["ord"] = cnt[k]
        with ExitStack() as es:
            sem = {}
            for k in cnt:
                sem[k] = es.enter_context(nc.semaphore("sem_%s_%s" % k))
            block = es.enter_context(nc.Block())
            by_eng = {e: [o for o in ops if o["eng"] == e] for e in self.ENGS}

            def run(engh, ename):
                waited = {}
                for o in by_eng[ename]:
                    need = {}
                    for d in o["deps"]:
                        od = ops[d]
                        if od["dma"]:
                            k = ("s", od["stream"])
                            v = 16 * od["ord"]
                        else:
                            if od["eng"] == "pe" and ename == "pe" and not o["dma"]:
                                continue
                            k = ("e", od["eng"])
                            v = od["ord"]
                        if v > need.get(k, 0):
                            need[k] = v
                    for k, v in need.items():
                        if waited.get(k, 0) >= v:
                            continue
                        waited[k] = v
                        engh.wait_ge(sem[k], v)
                    if o["fn"] is None:
                        continue
                    ins = o["fn"](engh)
                    if o["dma"]:
                        ins.then_inc(sem[("s", o["stream"])], 16)
                    elif o["sig"]:
                        ins.then_inc(sem[("e", ename)], 1)

            if by_eng["sp"]:
                block.sync(lambda e: run(e, "sp"))
            if by_eng["pool"]:
                block.gpsimd(lambda e: run(e, "pool"))
            if by_eng["act"]:
                block.scalar(lambda e: run(e, "act"))
            if by_eng["dve"]:
                block.vector(lambda e: run(e, "dve"))
            if by_eng["pe"]:
                block.tensor(lambda e: run(e, "pe"))
        return len(ops), cnt


def blocks(T):
    out = []
    t = 0
    while t < T:
        n = min(512, T - t)
        out.append((t, n))
        t += n
    return out


def kv_list(i, edge):
    if not edge:
        return list(range(i, i + 5))
    return {0: list(range(0, 7)), 1: list(range(1, 7)), 2: list(range(2, 7)),
            7: list(range(6, 12)), 8: list(range(6, 13))}[i]


class K:
    def __init__(self, nc, es, dbg=None):
        self.nc = nc
        self.es = es
        self.R = Rec()
        self.dbg = dbg
        self.dumps = []
        self.stores = []
        self.cnt = {}
        self.arena = es.enter_context(nc.sbuf_tensor("arena", [128, 106400], BF16))
        self.top = 0
        self.ps = [es.enter_context(nc.psum_tensor(f"ps{i}", [128, 512], F32)) for i in range(6)]
        self.pb = [es.enter_context(nc.psum_tensor(f"pb{i}", [128, 1024], BF16)) for i in range(2)]

    def alloc(self, n, dt=BF16):
        if dt == F32:
            n2 = 2 * n
        else:
            n2 = n
        self.top = (self.top + 1) // 2 * 2
        a = self.arena[:, self.top:self.top + n2]
        self.top += n2
        assert self.top <= 106400, self.top
        return a.bitcast(F32) if dt == F32 else a

    def rot(self, key, n):
        c = self.cnt.get(key, 0)
        self.cnt[key] = c + 1
        return c % n

    def ps_new(self, bank=None):
        i = self.rot("ps", 6) if bank is None else bank
        return self.ps[i][:], f"ps{i}"

    def pt_new(self):
        i = self.rot("pt", 2)
        return self.pb[i][:], f"pb{i}"

    def MM(self, out, lhsT, rhs, start, stop, r, w):
        self.R.pe(lambda e: e.matmul(out, lhsT=lhsT, rhs=rhs, start=start, stop=stop), r, w)

    def TR_(self, out, in_, r, w):
        ident = self.ident
        self.R.pe(lambda e: e.transpose(out=out, in_=in_, identity=ident), list(r) + ["cb"], w)

    def ACT(self, out, in_, func, r, w, scale=None, bias=None, accum=None):
        kw = {}
        if scale is not None:
            kw["scale"] = scale
        if bias is not None:
            kw["bias"] = bias
        if accum is not None:
            kw["accum_out"] = accum
        self.R.act(lambda e: e.activation(out=out, in_=in_, func=func, **kw), r, w)

    def ACOPY(self, out, in_, r, w):
        self.R.act(lambda e: e.copy(out=out, in_=in_), r, w)

    def AMUL(self, out, in_, c, r, w):
        self.R.act(lambda e: e.mul(out=out, in_=in_, mul=c), r, w)

    def DCOPY(self, out, in_, r, w):
        self.R.dve(lambda e: e.tensor_copy(out=out, in_=in_), r, w)

    def TS(self, out, in0, s1, s2, op0, op1, r, w):
        if op1 is None:
            self.R.dve(lambda e: e.tensor_scalar(out=out, in0=in0, scalar1=s1, scalar2=None, op0=op0), r, w)
        else:
            self.R.dve(lambda e: e.tensor_scalar(out=out, in0=in0, scalar1=s1, scalar2=s2, op0=op0, op1=op1), r, w)

    def TT(self, out, in0, in1, op, r, w):
        self.R.dve(lambda e: e.tensor_tensor(out=out, in0=in0, in1=in1, op=op), r, w)

    def STT(self, out, in0, scalar, in1, op0, op1, r, w):
        self.R.dve(lambda e: e.scalar_tensor_tensor(out=out, in0=in0, scalar=scalar, in1=in1, op0=op0, op1=op1), r, w)

    def LOAD(self, q, out, in_, w, stream, r=()):
        return self.R.dma(q, lambda e: e.dma_start(out=out, in_=in_), r=r, w=w, stream=stream)

    def STORE(self, q, out, in_, r, stream, w=()):
        i = self.R.dma(q, lambda e: e.dma_start(out=out, in_=in_), r=r, w=w, stream=stream)
        return i

    def dump(self, name, ap, shape, dt, rname):
        if self.dbg is None or name not in self.dbg or getattr(self, "cur_unit", None) != getattr(self, "dbg_unit", None):
            return
        d = self.nc.dram_tensor("dbg_" + name, list(shape), dt, kind="ExternalOutput").ap()
        self.stores.append(self.STORE("sp", d, ap, [rname], "dbg_" + name))
        self.dumps.append(name)

    def barrier(self):
        bt = self.bartile
        self.R.barrier(lambda e: e.memset(bt, 0.0))

    def setup(self, T):
        self.T = T
        A = self.alloc
        self.cf = A(NCF, F32)
        self.cb = A(NCB)
        self.ident = self.cb[:, B_ID:B_ID + 128]
        self.onesblk = self.cb[:, B_OB:B_OB + 128]
        self.ones = self.cb[:, B_ONE:B_ONE + 128]
        self.Mf = self.cb[:, B_MF:B_MF + 128]
        self.Mb = self.cb[:, B_MB:B_MB + 128]
        self.wa2 = [A(512), A(512)]
        self.ba = [A(512), A(512)]
        self.wlr = A(256).rearrange("p (k n) -> p k n", k=8)
        self.slabI = A(5120)
        self.S = {"f": A(512, F32), "b": A(512, F32)}
        self.Sbf = A(512)
        self.Ssave = [A(512, F32) for _ in range(4)]
        self.xs = [A(1024, F32) for _ in range(2)]
        self.jk = [A(1024) for _ in range(2)]
        self.xb = [A(1024) for _ in range(2)]
        self.xTt = [A(1024).rearrange("p (k n) -> p k n", k=8) for _ in range(2)]
        self.st = [A(8, F32) for _ in range(8)]
        self.wsl = [A(4096) for _ in range(4)]
        self.PT = [A(896) for _ in range(2)]
        self.ona = [A(512) for _ in range(2)]
        self.bartile = A(2, F32)
        self.tmp0 = self.top = (self.top + 1) // 2 * 2
        self.tf = [A(512, F32) for _ in range(8)]
        self.tb = [A(512) for _ in range(10)]
        assert self.top - self.tmp0 == 13312
        self.base = self.top
        L = self.LOAD
        L("sp", self.cf, T["cf"][:, :], ["cf"], "cf")
        L("pool", self.cb, T["cb"][:, :], ["cb"], "cb")
        for d, nm in enumerate(("f", "b")):
            L("pool", self.wa2[d][0:16, :], T["w_a2_" + nm][:, :], ["wa2"], "wa2" + nm)
            L("pool", self.ba[d][0:1, :], T["b_a_" + nm][:, :], ["ba"], "ba" + nm)
        L("pool", self.wlr, T["w_in"][:, 3584:3616].rearrange("(k p) n -> p k n", p=128), ["wlr"], "wlr")

    def tF(self):
        i = self.rot("tf", 8)
        return self.tf[i], f"tf{i}"

    def tB(self):
        i = self.rot("tb", 10)
        return self.tb[i], f"tb{i}"

    def stt_(self):
        i = self.rot("st", 8)
        return self.st[i], f"st{i}"

    def wload(self, si, cols_ap, ncols, kchunks=8):
        sl, nm = self.wsl[si], f"wsl{si}"
        v = sl[:, 0:kchunks * ncols].rearrange("p (k n) -> p k n", k=kchunks)
        if getattr(self, "nowload", False) and self.cnt.get("wl%d" % si, 0) > 0:
            return v, nm
        self.cnt["wl%d" % si] = 1
        self.LOAD("pool", v, cols_ap.rearrange("(k p) n -> p k n", p=128), [nm], nm)
        return v, nm

    def xload(self, src_rows):
        i = self.rot("xs", 2)
        self.LOAD("sp", self.xs[i], src_rows, [f"xs{i}"], f"xs{i}")
        return self.xs[i], f"xs{i}"

    def norm_A(self, src, sname, wbc_off, maskcol=None):
        j = self.rot("jk", 2)
        jk, jn = self.jk[j], f"jk{j}"
        xb, xn = self.xb[j], f"xb{j}"
        st, sn = self.stt_()
        cf = self.cf
        self.ACT(jk, src, AF.Square, [sname], [jn, sn], accum=st[:, 0:1])
        self.ACT(st[:, 2:3], st[:, 0:1], AF.Ln, [sn, "cf"], [sn], scale=1.0 / D, bias=cf[:, C_EPS:C_EPS + 1])
        self.ACT(st[:, 3:4], st[:, 2:3], AF.Exp, [sn], [sn], scale=-0.5)
        if maskcol is not None:
            self.TT(st[:, 3:4], st[:, 3:4], cf[:, maskcol:maskcol + 1], ALU.mult, [sn, "cf"], [sn])
        self.STT(xb, src, st[:, 3:4], cf[:, wbc_off:wbc_off + D], ALU.mult, ALU.mult, [sname, sn, "cf"], [xn])
        return xb, xn

    def norm_B(self, xb, xn, dst3, dname):
        pt, pn = self.pt_new()
        for kc in range(8):
            self.TR_(pt[:, kc * 128:(kc + 1) * 128], xb[:, kc * 128:(kc + 1) * 128], [xn], [pn])
        src3 = pt.rearrange("p (a b) -> p a b", a=8)
        self.DCOPY(dst3[:, 0:4, :], src3[:, 0:4, :], [pn], [dname])
        self.ACOPY(dst3[:, 4:8, :], src3[:, 4:8, :], [pn], [dname])

    def norm_T(self, src, sname, wbc_off, dst3, dname, maskcol=None):
        xb, xn = self.norm_A(src, sname, wbc_off, maskcol)
        self.norm_B(xb, xn, dst3, dname)

    def gates(self, lrT, lrn, d, bank=None):
        ps, pn = self.ps_new(bank)
        self.MM(ps, lrT, self.wa2[d][0:16, :], True, False, [lrn, "wa2"], [pn])
        self.MM(ps, self.ones[0:1, :], self.ba[d][0:1, :], False, True, ["cb", "ba"], [pn])
        gp, gn = self.tF()
        self.ACT(gp, ps, AF.Exp, [pn], [gn], scale=-1.0)
        self.R.dve(lambda e: e.tensor_scalar_add(out=gp, in0=gp, scalar1=1.0), [gn], [gn])
        self.ACT(gp, gp, AF.Ln, [gn], [gn])
        return gp, gn

    def tok_kv(self, xT3, xname, wk, wkn, wv, wvn):
        psk, pkn = self.ps_new()
        for kc in range(8):
            self.MM(psk, xT3[:, kc, :], wk[:, kc, :], kc == 0, kc == 7, [xname, wkn], [pkn])
        psv, pvn = self.ps_new()
        for kc in range(8):
            self.MM(psv, xT3[:, kc, :], wv[:, kc, :], kc == 0, kc == 7, [xname, wvn], [pvn])
        vt, vn = self.tB()
        self.ACOPY(vt, psv, [pvn], [vn])
        return psk, pkn, vt, vn

    def state_prep(self, gp, gn, psk, pkn, d, banks=(None, None)):
        cf = self.cf
        U = cf[:, C_UGT:C_UGT + 128] if d == "f" else cf[:, C_ULT:C_ULT + 128]
        psc, pcn = self.ps_new(banks[0])
        self.MM(psc, U, gp, True, True, ["cf", gn], [pcn])
        Ec, en = self.tF()
        self.ACT(Ec, psc, AF.Exp, [pcn], [en])
        kt, kn = self.tB()
        self.TT(kt, psk, Ec, ALU.mult, [pkn, en], [kn])
        pse, pen = self.ps_new(banks[1])
        for hh in range(4):
            self.MM(pse[:, hh:hh + 1], gp[:, hh * 128:(hh + 1) * 128], cf[:, C_NEG:C_NEG + 1], True, True,
                    [gn, "cf"], [pen])
        st, sn = self.stt_()
        self.ACT(st[:, 0:4], pse[:, 0:4], AF.Exp, [pen], [sn])
        return kt, kn, st, sn

    def state_apply(self, kt, kn, st, sn, vt, vn, d, snap=None, snapn=None, bank=None):
        S, Sn = self.S[d], "S" + d
        psd, pdn = self.ps_new(bank)
        for hh in range(4):
            sl = slice(hh * 128, (hh + 1) * 128)
            self.MM(psd[:, sl], kt[:, sl], vt[:, sl], True, True, [kn, vn], [pdn])
        S3 = S.rearrange("p (h v) -> p h v", h=4)
        self.TT(S3, S3, st[:, 0:4].rearrange("p (h o) -> p h o", o=1).to_broadcast([128, 4, 128]), ALU.mult,
                [Sn, sn], [Sn])
        if snap is not None:
            self.ACOPY(snap, S, [Sn], [snapn])
        self.TT(S, S, psd, ALU.add, [Sn, pdn], [Sn])

    def state_update(self, gp, gn, psk, pkn, vt, vn, d, snap=None, snapn=None):
        kt, kn, st, sn = self.state_prep(gp, gn, psk, pkn, d)
        self.state_apply(kt, kn, st, sn, vt, vn, d, snap, snapn)

    def zero_state(self, d):
        S = self.S[d]
        self.R.dve(lambda e: e.memset(S, 0.0), [], ["S" + d])

    def scan_pipe(self, d, n, p1, wk, wkn, wv, wvn, lr_of=None, snap_of=None, save_after=None, p1a=None):
        di = 0 if d == "f" else 1
        s1, s2, s3 = {}, {}, {}
        ksb_pool = [self.slabI[:, j * 1024:(j + 1) * 1024].bitcast(F32) for j in range(3)]

        def P2(i):
            xT3, xn = s1.pop(i)
            psk, pkn = self.ps_new(0)
            for kc in range(8):
                self.MM(psk, xT3[:, kc, :], wk[:, kc, :], kc == 0, kc == 7, [xn, wkn], [pkn])
            j = self.rot("ksb", 3)
            ksb, ksn = ksb_pool[j], f"ksb{j}"
            self.ACOPY(ksb, psk, [pkn], [ksn])
            psv, pvn = self.ps_new(1)
            for kc in range(8):
                self.MM(psv, xT3[:, kc, :], wv[:, kc, :], kc == 0, kc == 7, [xn, wvn], [pvn])
            vt, vn = self.tB()
            self.DCOPY(vt, psv, [pvn], [vn])
            if lr_of is None:
                psl, pln = self.ps_new(2)
                for kc in range(8):
                    self.MM(psl[0:16, 0:128], self.wlr[:, kc, 16 * di:16 * di + 16], xT3[:, kc, :], kc == 0, kc == 7,
                            ["wlr", xn], [pln])
                lrt, lrn = self.tB()
                self.DCOPY(lrt[0:16, 0:128], psl[0:16, 0:128], [pln], [lrn])
                lr = (lrt[0:16, 0:128], lrn)
            else:
                lr = lr_of(i)
            s2[i] = (ksb, ksn, vt, vn, lr)

        def P3a(i):
            ksb, ksn, vt, vn, lr = s2.pop(i)
            gp, gn = self.gates(lr[0], lr[1], di, bank=3)
            s3[i] = (ksb, ksn, vt, vn, gp, gn)

        def P3b(i):
            ksb, ksn, vt, vn, gp, gn = s3.pop(i)
            kt, kn, st, sn = self.state_prep(gp, gn, ksb, ksn, d, banks=(4, 5))
            sp = snap_of(i) if snap_of else None
            self.state_apply(kt, kn, st, sn, vt, vn, d, sp[0] if sp else None, sp[1] if sp else None, bank=5)
            if save_after and i in save_after:
                dst, dn = save_after[i]
                S = self.S[d]
                self.R.dve(lambda e, dst=dst, S=S: e.tensor_copy(out=dst, in_=S), ["S" + d], [dn])

        s0 = {}
        for it in range(n + 4):
            if 0 <= it - 4 < n:
                P3b(it - 4)
            if 0 <= it - 3 < n:
                P3a(it - 3)
            if 0 <= it - 2 < n:
                P2(it - 2)
            if 0 <= it - 1 < n:
                s1[it - 1] = p1(it - 1, s0.pop(it - 1, None))
            if it < n and p1a is not None:
                s0[it] = p1a(it)

    def state_scan(self, xsrc, taus, d, wk, wkn, wv, wvn, saves=None):
        def p1a(i):
            tau = taus[i]
            xs, xn = self.xload(xsrc[tau * 128:(tau + 1) * 128, :])
            return self.norm_A(xs, xn, C_WBC1)

        def p1(i, tok):
            j = self.rot("xTt", 2)
            xT3, xTn = self.xTt[j], f"xTt{j}"
            self.norm_B(tok[0], tok[1], xT3, xTn)
            return xT3, xTn
        sa = None
        if saves:
            sa = {i: saves[t] for i, t in enumerate(taus) if t in saves}
        self.scan_pipe(d, len(taus), p1, wk, wkn, wv, wvn, save_after=sa, p1a=p1a)

    def unit(self, xsrc, kv0, ut, yout, hscr, Sb_init, uname):
        R, T, cf = self.R, self.T, self.cf
        A = self.alloc
        self.top = self.base
        self.cur_unit = uname
        w_in = T["w_in"]
        xnT = A(8 * TK).rearrange("p (k n) -> p k n", k=8)
        self.LOAD("pool", self.slabI, T["slabI"][:, :], ["slabI"] + [f"ksb{j}" for j in range(3)], "slabI")
        prev = None
        for t in range(14):
            cur = None
            if t < 13:
                xs, xn = self.xload(xsrc[(kv0 + t) * 128:(kv0 + t + 1) * 128, :])
                cur = self.norm_A(xs, xn, C_WBC1)
            if prev is not None:
                self.norm_B(prev[0], prev[1], xnT[:, :, (t - 1) * 128:t * 128], f"xnT{t - 1}")
            prev = cur
        self.dump("xnT", xnT, [128, 8, TK], BF16, "xnT12")
        if getattr(self, "stop_after", None) == "U1":
            self.R.dve(lambda e: e.memset(self.Sfx, 0.0), [], ["Sfx"])
            self.barrier()
            return

        def xr(t0, n, off):
            a = (off + t0) // 128
            b = (off + t0 + n - 1) // 128
            return [f"xnT{t}" for t in range(a, b + 1)]

        m1 = self.top
        KT = A(4 * TK).rearrange("p (k n) -> p k n", k=4)
        QT = A(4 * TR).rearrange("p (k n) -> p k n", k=4)
        V = A(13 * 8 * 65)
        V4 = V.rearrange("p (t h d) -> p t h d", t=13, h=8)
        onaT = A(4 * TR).rearrange("p (k n) -> p k n", k=4)

        hn_pending = []

        def hn_finish():
            while hn_pending:
                (ps, pn, n, nwc, biasc, dst, dname, sq, sqn) = hn_pending.pop(0)
                ps2, p2n = self.ps_new()
                self.MM(ps2[:, :n], self.onesblk, sq[:, :n], True, True, ["cb", sqn], [p2n])
                r1, r1n = self.tF()
                self.ACT(r1[:, :n], ps2[:, :n], AF.Ln, [p2n, "cf"], [r1n], scale=1.0 / 64, bias=cf[:, C_EPS:C_EPS + 1])
                if biasc is None:
                    self.ACT(r1[:, :n], r1[:, :n], AF.Exp, [r1n], [r1n], scale=-0.5)
                else:
                    self.ACT(r1[:, :n], r1[:, :n], AF.Exp, [r1n, "cf"], [r1n], scale=-0.5, bias=cf[:, biasc:biasc + 1])
                self.STT(dst, ps[:, :n], cf[:, nwc:nwc + 1], r1[:, :n], ALU.mult, ALU.mult, [pn, r1n, "cf"], [dname])

        def headnorm(ps, pn, n, nwc, biasc, dst, dname):
            sq, sqn = self.tB()
            self.ACT(sq[:, :n], ps[:, :n], AF.Square, [pn], [sqn])
            hn_pending.append((ps, pn, n, nwc, biasc, dst, dname, sq, sqn))

        w, wn = self.wload(0, w_in[:, 512:1024], 512)
        for p in range(4):
            for (t0, n) in blocks(TK):
                ps, pn = self.ps_new()
                for kc in range(8):
                    self.MM(ps[:, :n], w[:, kc, p * 128:(p + 1) * 128], xnT[:, kc, t0:t0 + n], kc == 0, kc == 7,
                            xr(t0, n, 0) + [wn], [pn])
                hn_finish()
                headnorm(ps, pn, n, C_KNW, None, KT[:, p, t0:t0 + n], "KT")
        w, wn = self.wload(2, w_in[:, 0:512], 512)
        for p in range(4):
            for (t0, n) in blocks(TR):
                ps, pn = self.ps_new()
                for kc in range(8):
                    self.MM(ps[:, :n], w[:, kc, p * 128:(p + 1) * 128], xnT[:, kc, 256 + t0:256 + t0 + n], kc == 0,
                            kc == 7, xr(t0, n, 256) + [wn], [pn])
                hn_finish()
                headnorm(ps, pn, n, C_QNW, C_LNQ, QT[:, p, t0:t0 + n], "QT")
        w, wn = self.wload(0, w_in[:, 1024:1536], 512)
        R.pool(lambda e: e.memset(V, 1.0), [], ["V"])
        hn_first_v = True
        for t in range(13):
            ps, pn = self.ps_new()
            for kc in range(8):
                self.MM(ps, xnT[:, kc, t * 128:(t + 1) * 128], w[:, kc, :], kc == 0, kc == 7, [f"xnT{t}", wn], [pn])
            if hn_first_v:
                hn_finish()
                hn_first_v = False
            self.ACOPY(V4[:, t, :, 0:64], ps.rearrange("p (h d) -> p h d", h=8), [pn], ["V"])
        self.dump("KT", KT, [128, 4, TK], BF16, "KT")
        self.dump("QT", QT, [128, 4, TR], BF16, "QT")
        self.dump("V", V, [128, 13 * 8 * 65], BF16, "V")
        if getattr(self, "stop_after", None) == "U2":
            self.R.dve(lambda e: e.memset(self.Sfx, 0.0), [], ["Sfx"])
            self.barrier()
            return

        onaAll = A(9 * 512).rearrange("p (i n) -> p i n", i=9)
        edge_qps = []
        if ut["top"] is not None:
            edge_qps += [(i, ut["top"][EDGE_TOP.index(i)]) for i in EDGE_TOP]
        if ut["bot"] is not None:
            edge_qps += [(i, ut["bot"][EDGE_BOT.index(i)]) for i in EDGE_BOT]
        eidx = {i: n_ for n_, (i, _) in enumerate(edge_qps)}
        kvl = {i: kv_list(i, i in eidx) for i in range(9)}
        for h in range(8):
            p, bp = h // 2, 64 * (h % 2)
            es_i = 0 if h % 2 == 0 else 2
            es, esn = self.wsl[es_i], f"wsl{es_i}"
            for n_, (i, src) in enumerate(edge_qps):
                self.LOAD("pool", es[:, n_ * 896:(n_ + 1) * 896], src[:, h * 896:(h + 1) * 896], [esn], f"{esn}_{n_}")
            OX, OXn = self.ps_new(4)
            OY, OYn = self.ps_new(5)
            zer = self.cb[:, B_ZERO:B_ZERO + 128]
            self.MM(OX[:, 0:455], zer, self.slabI[:, 0:455], True, False, ["cb", "slabI"], [OXn])
            self.MM(OY[:, 0:130], zer, self.slabI[:, 0:130], True, False, ["cb", "slabI"], [OYn])
            lastX = max(kvl[i][-1] for i in range(7))
            lastY = max(kvl[i][-1] for i in (7, 8))
            def S1(t):
                qs = [i for i in range(9) if t in kvl[i]]
                if not qs:
                    return None
                assert qs == list(range(qs[0], qs[-1] + 1)) and len(qs) <= 7
                bA, bB = (0, 1) if t % 2 == 0 else (2, 3)
                psA, pAn = self.ps_new(bA)
                psB, pBn = self.ps_new(bB)
                qA, qB = qs[:4], qs[4:]
                for (qq, ps_, pn_) in ((qA, psA, pAn), (qB, psB, pBn)):
                    if not qq:
                        continue
                    nq = len(qq)
                    c = 0
                    while c < nq:
                        i = qq[c]
                        if i in eidx:
                            c2 = c + 1
                            a_ = kvl[i].index(t)
                            blk = eidx[i] * 896 + a_ * 128
                            b_ap, b_n = es[:, blk:blk + 128], esn
                        else:
                            c2 = c
                            while c2 < nq and qq[c2] not in eidx:
                                c2 += 1
                            blk = (h * 5 + (i - t + 4)) * 128
                            b_ap, b_n = self.slabI[:, blk:blk + (c2 - c) * 128], "slabI"
                        self.MM(ps_[:, c * 128:c2 * 128], KT[bp:bp + 64, p, t * 128:(t + 1) * 128],
                                QT[bp:bp + 64, p, qq[c] * 128:(qq[c2 - 1] + 1) * 128], True, False, ["KT", "QT"], [pn_])
                        self.MM(ps_[:, c * 128:c2 * 128], self.ident, b_ap, False, True, ["cb", b_n], [pn_])
                        c = c2
                pj = self.rot("PT", 2)
                PT, PTn = self.PT[pj], f"PT{pj}"
                self.ACT(PT[:, 0:len(qA) * 128], psA[:, 0:len(qA) * 128], AF.Exp, [pAn], [PTn])
                if qB:
                    self.ACT(PT[:, 512:512 + len(qB) * 128], psB[:, 0:len(qB) * 128], AF.Exp, [pBn], [PTn])
                return (qs, PT, PTn)

            def S2(t, tok):
                qs, PT, PTn = tok
                for c, i in enumerate(qs):
                    if i < 7:
                        od, odn = OX[:, i * 65:(i + 1) * 65], OXn
                    else:
                        od, odn = OY[:, (i - 7) * 65:(i - 6) * 65], OYn
                    is_last = (c == len(qs) - 1 and t == lastY) if i >= 7 else \
                        (t == lastX and i == max(q for q in qs if q < 7))
                    self.MM(od, PT[:, c * 128:(c + 1) * 128], V4[:, t, h, :], False, is_last, [PTn, "V"], [odn])

            tok_prev = S1(0)
            for t in range(13):
                tok_next = S1(t + 1) if t + 1 < 13 else None
                if tok_prev is not None:
                    S2(t, tok_prev)
                tok_prev = tok_next
            for (O_, On_, i0, ni) in ((OX, OXn, 0, 7), (OY, OYn, 7, 2)):
                O3 = O_[:, 0:ni * 65].rearrange("p (i d) -> p i d", i=ni)
                st, sn = self.stt_()
                R.dve(lambda e, st=st, O3=O3, ni=ni: e.reciprocal(out=st[:, 0:ni], in_=O3[:, :, 64]), [On_], [sn])
                self.TT(onaAll[:, i0:i0 + ni, h * 64:(h + 1) * 64], O3[:, :, 0:64],
                        st[:, 0:ni].rearrange("p (i o) -> p i o", o=1).to_broadcast([128, ni, 64]), ALU.mult,
                        [On_, sn], ["onaAll"])
        for i in range(9):
            pt, pn = self.pt_new()
            for c in range(4):
                self.TR_(pt[:, c * 128:(c + 1) * 128], onaAll[:, i, c * 128:(c + 1) * 128], ["onaAll"], [pn])
            self.ACOPY(onaT[:, :, i * 128:(i + 1) * 128], pt[:, 0:512].rearrange("p (a b) -> p a b", a=4), [pn], ["onaT"])
        self.dump("onaT", onaT, [128, 4, TR], BF16, "onaT")
        if getattr(self, "stop_after", None) == "U3":
            self.R.dve(lambda e: e.memset(self.Sfx, 0.0), [], ["Sfx"])
            self.barrier()
            return

        self.barrier()
        self.top = m1
        onaT2 = A(4 * TR).rearrange("p (k n) -> p k n", k=4)
        self.ACOPY(onaT2, onaT, ["onaT"], ["onaT2"])
        self.barrier()
        onaT = onaT2
        ogT = A(4 * TR).rearrange("p (k n) -> p k n", k=4)
        m_gla = self.top
        qgT = A(4 * TR).rearrange("p (k n) -> p k n", k=4)
        kgT = A(4 * TR).rearrange("p (k n) -> p k n", k=4)
        lr = [self.wsl[2][:, 0:TR], self.wsl[2][:, TR:2 * TR]]
        sog = A(9 * 512).rearrange("p (t n) -> p t n", t=9)
        snap = A(9 * 512).rearrange("p (t n) -> p t n", t=9)
        w, wn = self.wload(0, w_in[:, 1536:2048], 512)
        for hh in range(4):
            for (t0, n) in blocks(TR):
                ps, pn = self.ps_new()
                for kc in range(8):
                    self.MM(ps[:, :n], w[:, kc, hh * 128:(hh + 1) * 128], xnT[:, kc, 256 + t0:256 + t0 + n], kc == 0,
                            kc == 7, xr(t0, n, 256) + [wn], [pn])
                self.AMUL(qgT[:, hh, t0:t0 + n], ps[:, :n], 128 ** -0.5, [pn], ["qgT"])
        wk, wkn = self.wload(1, w_in[:, 2048:2560], 512)
        for hh in range(4):
            for (t0, n) in blocks(TR):
                ps, pn = self.ps_new()
                for kc in range(8):
                    self.MM(ps[:, :n], wk[:, kc, hh * 128:(hh + 1) * 128], xnT[:, kc, 256 + t0:256 + t0 + n], kc == 0,
                            kc == 7, xr(t0, n, 256) + [wkn], [pn])
                self.DCOPY(kgT[:, hh, t0:t0 + n], ps[:, :n], [pn], ["kgT"])
        for d in range(2):
            for (t0, n) in blocks(TR):
                ps, pn = self.ps_new()
                for kc in range(8):
                    self.MM(ps[0:16, :n], self.wlr[:, kc, 16 * d:16 * d + 16], xnT[:, kc, 256 + t0:256 + t0 + n],
                            kc == 0, kc == 7, xr(t0, n, 256) + ["wlr"], [pn])
                self.ACOPY(lr[d][0:16, t0:t0 + n], ps[0:16, :n], [pn], [f"lr{d}", "wsl2"])
        w, wn = self.wload(0, w_in[:, 3072:3584], 512)
        for i in range(9):
            ps, pn = self.ps_new()
            for kc in range(8):
                self.MM(ps, xnT[:, kc, (i + 2) * 128:(i + 3) * 128], w[:, kc, :], kc == 0, kc == 7,
                        [f"xnT{i + 2}", wn], [pn])
            tg, tgn = self.tF()
            self.ACT(tg, ps, AF.Tanh, [pn], [tgn], scale=0.5)
            self.STT(tg, tg, 1.0, ps, ALU.add, ALU.mult, [tgn, pn], [tgn])
            self.TT(sog[:, i, :], tg, cf[:, C_GNW:C_GNW + 512], ALU.mult, [tgn, "cf"], ["sog"])
        wv, wvn = self.wload(3, w_in[:, 2560:3072], 512)
        if getattr(self, "stop_after", None) == "U5":
            self.R.dve(lambda e: e.memset(self.Sfx, 0.0), [], ["Sfx"])
            self.barrier()
            return

        if Sb_init is None:
            self.zero_state("b")
        else:
            Sb = self.S["b"]
            R.dve(lambda e, Sb=Sb, src=Sb_init[0]: e.tensor_copy(out=Sb, in_=src), [Sb_init[1]], ["Sb"])
        order6 = list(reversed(range(9)))
        self.scan_pipe("b", 9,
                       lambda n_, tok=None: (xnT[:, :, (order6[n_] + 2) * 128:(order6[n_] + 3) * 128], f"xnT{order6[n_] + 2}"),
                       wk, wkn, wv, wvn,
                       lr_of=lambda n_: (lr[1][0:16, order6[n_] * 128:(order6[n_] + 1) * 128], "lr1"),
                       snap_of=lambda n_: (snap[:, order6[n_], :], "snap"))
        self.barrier()
        Sf = self.S["f"]
        ded = [[self.slabI[:, (par * 4 + q) * 512:(par * 4 + q + 1) * 512] for q in range(4)] for par in range(2)]
        m1out = {}

        def M1(i):
            par = i % 2
            xT3 = xnT[:, :, (i + 2) * 128:(i + 3) * 128]
            psk, pkn, vt, vn = self.tok_kv(xT3, f"xnT{i + 2}", wk, wkn, wv, wvn)
            qts, As = [], []
            gpf = self.gates(lr[0][0:16, i * 128:(i + 1) * 128], "lr0", 0)
            ktf, ktfn, stf, stfn = self.state_prep(gpf[0], gpf[1], psk, pkn, "f")
            for d in range(2):
                if d == 0:
                    gp, gn = gpf
                else:
                    gp, gn = self.gates(lr[d][0:16, i * 128:(i + 1) * 128], f"lr{d}", d)
                Ufm = cf[:, C_UINC:C_UINC + 128] if d == 0 else cf[:, C_ULT:C_ULT + 128]
                psp, ppn = self.ps_new()
                for hh in range(4):
                    sl = slice(hh * 128, (hh + 1) * 128)
                    self.MM(psp[:, sl], gp[:, sl], Ufm, True, True, [gn, "cf"], [ppn])
                Eq, eqn = self.tF()
                Ek, ekn = self.tF()
                self.ACT(Eq, psp, AF.Exp, [ppn], [eqn], scale=(1.0 if d == 0 else -1.0))
                self.ACT(Ek, psp, AF.Exp, [ppn], [ekn], scale=(-1.0 if d == 0 else 1.0))
                qt, qn = ded[par][d], f"ded{par}{d}"
                kt2, k2n = self.tB()
                q3 = qt.rearrange("p (h t) -> p h t", h=4)
                k3 = kt2.rearrange("p (h t) -> p h t", h=4)
                self.TT(q3, qgT[:, :, i * 128:(i + 1) * 128], Eq.rearrange("p (h t) -> p h t", h=4), ALU.mult,
                        ["qgT", eqn], [qn])
                self.TT(k3, kgT[:, :, i * 128:(i + 1) * 128], Ek.rearrange("p (h t) -> p h t", h=4), ALU.mult,
                        ["kgT", ekn], [k2n])
                psa, pan = self.ps_new()
                for hh in range(4):
                    sl = slice(hh * 128, (hh + 1) * 128)
                    self.MM(psa[:, sl], kt2[:, sl], qt[:, sl], True, True, [k2n, qn], [pan])
                Ad, adn = ded[par][2 + d], f"ded{par}{2 + d}"
                Mm = self.Mf if d == 0 else self.Mb
                self.TT(Ad.rearrange("p (h t) -> p h t", h=4), psa.rearrange("p (h t) -> p h t", h=4),
                        Mm.rearrange("p (o t) -> p o t", o=1).to_broadcast([128, 4, 128]), ALU.mult, [pan, "cb"], [adn])
                qts.append((qt, qn))
                As.append((Ad, adn))
            m1out[i] = (vt, vn, ktf, ktfn, stf, stfn, qts, As)

        def M2(i):
            vt, vn, ktf, ktfn, stf, stfn, qts, As = m1out.pop(i)
            if i == 8:
                exp_ = self.Sfx
                R.dve(lambda e, exp_=exp_, Sf=Sf: e.tensor_copy(out=exp_, in_=Sf), ["Sf"], ["Sfx"])
            self.ACOPY(self.Sbf, Sf, ["Sf"], ["Sbf"])
            pso, pon = self.ps_new()
            for hh in range(4):
                sl = slice(hh * 128, (hh + 1) * 128)
                self.MM(pso[:, sl], qts[0][0][:, sl], self.Sbf[:, sl], True, False, [qts[0][1], "Sbf"], [pon])
                self.MM(pso[:, sl], As[0][0][:, sl], vt[:, sl], False, False, [As[0][1], vn], [pon])
                self.MM(pso[:, sl], qts[1][0][:, sl], snap[:, i, sl], False, False, [qts[1][1], "snap"], [pon])
                self.MM(pso[:, sl], As[1][0][:, sl], vt[:, sl], False, True, [As[1][1], vn], [pon])
            self.state_apply(ktf, ktfn, stf, stfn, vt, vn, "f")
            st, sn = self.stt_()
            jj = self.rot("jk", 2)
            for hh in range(4):
                sl = slice(hh * 128, (hh + 1) * 128)
                self.ACT(self.jk[jj][:, sl], pso[:, sl], AF.Square, [pon], [f"jk{jj}", sn], accum=st[:, hh:hh + 1])
            self.TS(st[:, 4:8], st[:, 0:4], 1.0 / 128, EPS, ALU.mult, ALU.add, [sn], [sn])
            self.ACT(st[:, 4:8], st[:, 4:8], AF.Ln, [sn], [sn])
            self.ACT(st[:, 4:8], st[:, 4:8], AF.Exp, [sn], [sn], scale=-0.5)
            self.TS(st[:, 4:8], st[:, 4:8], 0.5, None, ALU.mult, None, [sn], [sn])
            og, ogn = self.tF()
            self.TT(og.rearrange("p (h t) -> p h t", h=4), pso.rearrange("p (h t) -> p h t", h=4),
                    st[:, 4:8].rearrange("p (h o) -> p h o", o=1).to_broadcast([128, 4, 128]), ALU.mult, [pon, sn], [ogn])
            ogb, obn = self.tB()
            self.TT(ogb, og, sog[:, i, :], ALU.mult, [ogn, "sog"], [obn])
            pt, pn = self.pt_new()
            for c in range(4):
                self.TR_(pt[:, c * 128:(c + 1) * 128], ogb[:, c * 128:(c + 1) * 128], [obn], [pn])
            self.ACOPY(ogT[:, :, i * 128:(i + 1) * 128], pt[:, 0:512].rearrange("p (a b) -> p a b", a=4), [pn], ["ogT"])

        M1(0)
        for i in range(9):
            if i + 1 < 9:
                M1(i + 1)
            M2(i)
        self.dump("ogT", ogT, [128, 4, TR], BF16, "ogT")
        if getattr(self, "stop_after", None) == "U7":
            self.R.dve(lambda e: e.memset(self.Sfx, 0.0), [], ["Sfx"])
            self.barrier()
            return

        self.barrier()
        self.top = m_gla
        mixT = A(8 * TR).rearrange("p (k n) -> p k n", k=8)
        wna, wnan = self.wload(0, T["w_na_proj"][:, :], 1024, kchunks=4)
        wgl, wgln = self.wload(1, T["w_gla_proj"][:, :], 1024, kchunks=4)
        for hp in range(2):
            g1, g1n = self.wload(2, w_in[:, 3616 + 512 * hp:3616 + 512 * hp + 512], 512)
            g2, g2n = self.wload(3, w_in[:, 4640 + 512 * hp:4640 + 512 * hp + 512], 512)
            for c4 in range(4):
                c = hp * 4 + c4
                for (t0, n) in blocks(TR):
                    ps1, p1n = self.ps_new()
                    for kk in range(4):
                        self.MM(ps1[:, :n], wna[:, kk, c * 128:(c + 1) * 128], onaT[:, kk, t0:t0 + n], kk == 0, kk == 3,
                                [wnan, "onaT2"], [p1n])
                    ps2, p2n = self.ps_new()
                    for kk in range(4):
                        self.MM(ps2[:, :n], wgl[:, kk, c * 128:(c + 1) * 128], ogT[:, kk, t0:t0 + n], kk == 0, kk == 3,
                                [wgln, "ogT"], [p2n])
                    ps3, p3n = self.ps_new()
                    for kc in range(8):
                        self.MM(ps3[:, :n], g1[:, kc, c4 * 128:(c4 + 1) * 128], xnT[:, kc, 256 + t0:256 + t0 + n],
                                kc == 0, kc == 7, xr(t0, n, 256) + [g1n], [p3n])
                    ps4, p4n = self.ps_new()
                    for kc in range(8):
                        self.MM(ps4[:, :n], g2[:, kc, c4 * 128:(c4 + 1) * 128], xnT[:, kc, 256 + t0:256 + t0 + n],
                                kc == 0, kc == 7, xr(t0, n, 256) + [g2n], [p4n])
                    t1, t1n = self.tF()
                    t2, t2n = self.tF()
                    self.ACT(t1[:, :n], ps3[:, :n], AF.Tanh, [p3n], [t1n], scale=0.5)
                    self.ACT(t2[:, :n], ps4[:, :n], AF.Tanh, [p4n], [t2n], scale=0.5)
                    self.STT(t1[:, :n], t1[:, :n], 1.0, ps1[:, :n], ALU.add, ALU.mult, [t1n, p1n], [t1n])
                    self.STT(t2[:, :n], t2[:, :n], 1.0, ps2[:, :n], ALU.add, ALU.mult, [t2n, p2n], [t2n])
                    self.TT(mixT[:, c, t0:t0 + n], t1[:, :n], t2[:, :n], ALU.add, [t1n, t2n], ["mixT"])
        self.dump("mixT", mixT, [128, 8, TR], BF16, "mixT")
        if getattr(self, "stop_after", None) == "U8a":
            self.R.dve(lambda e: e.memset(self.Sfx, 0.0), [], ["Sfx"])
            self.barrier()
            return
        hnT = A(8 * TR).rearrange("p (k n) -> p k n", k=8)
        wo = []
        for half in range(2):
            wo.append(self.wload(half, T["w_out"][:, half * 512:(half + 1) * 512], 512))
        prevB = None
        for i in range(10):
            curB = None
            if i < 9:
                xs, xn = self.xload(xsrc[(kv0 + 2 + i) * 128:(kv0 + 3 + i) * 128, :])
                for half in range(2):
                    ps, pn = self.ps_new()
                    for kc in range(8):
                        self.MM(ps, mixT[:, kc, i * 128:(i + 1) * 128], wo[half][0][:, kc, :], kc == 0, kc == 7,
                                ["mixT", wo[half][1]], [pn])
                    sl = slice(half * 512, (half + 1) * 512)
                    self.STT(xs[:, sl], ps, 0.5, xs[:, sl], ALU.mult, ALU.add, [pn, xn], [xn])
                self.STORE("sp", hscr[i * 128:(i + 1) * 128, :], xs, [xn], "hst_" + xn, w=[f"hscr{i}"])
                mc = None
                if i == 0:
                    mc = C_MASK + 2 * ut["mcol"]
                if i == 8:
                    mc = C_MASK + 2 * ut["mcol"] + 1
                curB = self.norm_A(xs, xn, C_WBC2, maskcol=mc) + (i,)
            if prevB is not None:
                self.norm_B(prevB[0], prevB[1], hnT[:, :, prevB[2] * 128:(prevB[2] + 1) * 128], f"hnT{prevB[2]}")
            prevB = curB
        self.dump("hnT", hnT, [128, 8, TR], BF16, "hnT8")
        if getattr(self, "stop_after", None) == "U8b":
            self.R.dve(lambda e: e.memset(self.Sfx, 0.0), [], ["Sfx"])
            self.barrier()
            return

        self.barrier()
        self.top = self.base
        hnT2 = A(8 * TR).rearrange("p (k n) -> p k n", k=8)
        self.ACOPY(hnT2[:, 0:4, :], hnT[:, 0:4, :], [f"hnT{i}" for i in range(9)], ["hnT2"])
        self.DCOPY(hnT2[:, 4:8, :], hnT[:, 4:8, :], [f"hnT{i}" for i in range(9)], ["hnT2"])
        self.barrier()
        hnT = hnT2
        fT = A(22 * 1024).rearrange("p (j n) -> p j n", j=22)
        tmp = self.arena[:, self.tmp0:self.tmp0 + 13312]
        u = [self.slabI[:, 0:2304].bitcast(F32), self.slabI[:, 2304:4608].bitcast(F32)]
        cv = [tmp[:, 0:2048].bitcast(F32), tmp[:, 2048:4096].bitcast(F32)]
        gq = tmp[:, 4096:6144].bitcast(F32)
        wdT = tmp[:, 6144:13312].rearrange("p (j n) -> p j n", j=7)
        wdL = tmp[:, 0:6144].rearrange("p (j n) -> p j n", j=6)
        wdC = A(9 * 1024).rearrange("p (j n) -> p j n", j=9)
        wdr = T["w_down"].rearrange("(j p) n -> p j n", p=128)
        w_up = T["w_up"]
        for j in range(22):
            si_ = j % 4
            sl_, wn = self.wsl[si_], f"wsl{si_}"
            wv2 = sl_[:, 0:2048].rearrange("p (k n) -> p k n", k=8)
            self.LOAD("pool", wv2[:, :, 0:128], w_up[:, j * 128:(j + 1) * 128].rearrange("(k p) n -> p k n", p=128),
                      [wn], wn + "a")
            self.LOAD("pool", wv2[:, :, 128:256],
                      w_up[:, 2816 + j * 128:2816 + (j + 1) * 128].rearrange("(k p) n -> p k n", p=128), [wn], wn + "b")
            if j == 3:
                for g in range(0, 9, 3):
                    self.LOAD("pool", wdC[:, g:g + 3, :], wdr[:, g:g + 3, :], ["wdC"], f"wdC{g}")
                self.LOAD("pool", wdT[:, 0:4, :], wdr[:, 9:13, :], ["wdT"], "wdT0")
                self.LOAD("pool", wdT[:, 4:7, :], wdr[:, 13:16, :], ["wdT"], "wdT4")
            for ab in range(2):
                for (t0, n) in blocks(TR):
                    ps, pn = self.ps_new()
                    for kc in range(8):
                        self.MM(ps[:, :n], wv2[:, kc, ab * 128:(ab + 1) * 128], hnT[:, kc, t0:t0 + n], kc == 0, kc == 7,
                                [wn, "hnT2"], [pn])
                    self.ACOPY(u[ab][:, t0:t0 + n], ps[:, :n], [pn], [f"u{ab}"])
                jj = ab * 22 + j
                cw = lambda tap, jj=jj: cf[:, C_CW + jj * 3 + tap:C_CW + jj * 3 + tap + 1]
                self.ACT(cv[ab], u[ab][:, 63:1087], AF.Identity, [f"u{ab}", "cf"], [f"cv{ab}"], scale=cw(0),
                         bias=cf[:, C_CB + jj:C_CB + jj + 1])
                self.STT(cv[ab], u[ab][:, 64:1088], cw(1), cv[ab], ALU.mult, ALU.add, [f"u{ab}", "cf", f"cv{ab}"],
                         [f"cv{ab}"])
                self.STT(cv[ab], u[ab][:, 65:1089], cw(2), cv[ab], ALU.mult, ALU.add, [f"u{ab}", "cf", f"cv{ab}"],
                         [f"cv{ab}"])
            self.ACT(gq, cv[0], AF.Square, ["cv0"], ["gq"])
            self.TS(gq, gq, 0.044715, 1.0, ALU.mult, ALU.add, ["gq"], ["gq"])
            self.TT(gq, gq, cv[0], ALU.mult, ["gq", "cv0"], ["gq"])
            self.ACT(gq, gq, AF.Tanh, ["gq"], ["gq"], scale=0.7978845608028654)
            self.STT(gq, gq, 1.0, cv[0], ALU.add, ALU.mult, ["gq", "cv0"], ["gq"])
            self.TT(fT[:, j, :], gq, cv[1], ALU.mult, ["gq", "cv1"], ["fT"])
        self.dump("fT", fT, [128, 22, 1024], BF16, "fT")
        if getattr(self, "stop_after", None) == "U10":
            self.R.dve(lambda e: e.memset(self.Sfx, 0.0), [], ["Sfx"])
            self.barrier()
            return

        self.barrier()
        self.LOAD("pool", wdL[:, 0:3, :], wdr[:, 16:19, :], ["wdL"], "wdL0")
        self.LOAD("pool", wdL[:, 3:6, :], wdr[:, 19:22, :], ["wdL"], "wdL3")

        def wdj(j):
            if j < 9:
                return wdC[:, j], "wdC"
            if j < 16:
                return wdT[:, j - 9], "wdT"
            return wdL[:, j - 16], "wdL"
        for i8 in range(8):
            hi_ = self.rot("xs", 2)
            hs, hn_ = self.xs[hi_], f"xs{hi_}"
            self.LOAD("sp", hs, hscr[64 + i8 * 128:64 + (i8 + 1) * 128, :], [hn_], hn_,
                      r=[f"hscr{i8}", f"hscr{i8 + 1}"])
            for half in range(2):
                ps, pn = self.ps_new()
                for j in range(22):
                    self.MM(ps, fT[:, j, i8 * 128:(i8 + 1) * 128], wdj(j)[0][:, half * 512:(half + 1) * 512], j == 0,
                            j == 21, ["fT", wdj(j)[1]], [pn])
                sl = slice(half * 512, (half + 1) * 512)
                self.STT(hs[:, sl], ps, 0.5, hs[:, sl], ALU.mult, ALU.add, [pn, hn_], [hn_])
            self.stores.append(self.STORE("sp", yout[i8 * 128:(i8 + 1) * 128, :], hs, [hn_], "yst_" + hn_))
        self.barrier()

    def sequence(self, xsrc, n_units, tau0, pre, tau_max, uts, yout, hscr):
        T = self.T
        w_in = T["w_in"]
        self.zero_state("f")
        self.zero_state("b")
        post = list(range(tau_max, tau0 + 8, -1))
        saves = {}
        inits = [None] * n_units
        for m in range(n_units):
            need = tau0 + 8 * m + 9
            if need <= tau_max:
                saves[need] = (self.Ssave[m], f"Ssave{m}")
                inits[m] = (self.Ssave[m], f"Ssave{m}")
        if pre or post:
            wk, wkn = self.wload(1, w_in[:, 2048:2560], 512)
            wv, wvn = self.wload(3, w_in[:, 2560:3072], 512)
            if pre:
                self.state_scan(xsrc, pre, "f", wk, wkn, wv, wvn)
            if post:
                self.state_scan(xsrc, post, "b", wk, wkn, wv, wvn, saves=saves)
        self.barrier()
        for m in range(n_units):
            self.unit(xsrc, tau0 + 8 * m - 2, uts[m], yout[m * 1024:(m + 1) * 1024, :], hscr, inits[m], f"u{m}")
            Sf, Sfx = self.S["f"], self.Sfx
            self.R.dve(lambda e, Sf=Sf, Sfx=Sfx: e.tensor_copy(out=Sf, in_=Sfx), ["Sfx"], ["Sf"])
            self.barrier()


def build_nc(dbg=None, only_prompt_units=None, xs_tiles=225, variant=None):
    nc = bass.Bass("TRN2", target_bir_lowering=False)
    dt = lambda name, shape, kind="ExternalInput": nc.dram_tensor(name, list(shape), F32, kind=kind).ap()
    T = {
        "xp": dt("xp", [2, 21 * 128, D]),
        "xs": dt("xs", [xs_tiles * 128, D]),
        "cf": dt("cf", [128, NCF]),
        "cb": dt("cb", [128, NCB]),
        "slabI": dt("slabI", [128, 5120]),
        "sT_p": dt("sT_p", [3, 128, 7168]),
        "sB_p": dt("sB_p", [2, 128, 7168]),
        "sT_s": dt("sT_s", [3, 128, 7168]),
        "sB_s": dt("sB_s", [2, 128, 7168]),
        "w_in": dt("w_in", [D, 5664]),
        "w_a2_f": dt("w_a2_f", [16, 512]),
        "b_a_f": dt("b_a_f", [1, 512]),
        "w_a2_b": dt("w_a2_b", [16, 512]),
        "b_a_b": dt("b_a_b", [1, 512]),
        "w_na_proj": dt("w_na_proj", [512, D]),
        "w_gla_proj": dt("w_gla_proj", [512, D]),
        "w_out": dt("w_out", [D, D]),
        "w_up": dt("w_up", [D, 5632]),
        "w_down": dt("w_down", [2816, D]),
    }
    yp = dt("yp", [2, 2048, D], "ExternalOutput")
    ys = dt("ys", [4096, D], "ExternalOutput")
    hscr = dt("hscr", [TR, D], "Internal")
    with ExitStack() as es:
        k = K(nc, es, dbg)
        k.setup(T)
        k.Sfx = k.alloc(512, F32)
        k.base = k.top
        sT_p = [T["sT_p"][e] for e in range(3)]
        sB_p = [T["sB_p"][e] for e in range(2)]
        sT_s = [T["sT_s"][e] for e in range(3)]
        sB_s = [T["sB_s"][e] for e in range(2)]
        ut_p = [dict(top=sT_p, bot=None, mcol=0), dict(top=None, bot=sB_p, mcol=1)]
        ut_s = [dict(top=sT_s, bot=None, mcol=2), dict(top=None, bot=None, mcol=3),
                dict(top=None, bot=None, mcol=4), dict(top=None, bot=sB_s, mcol=5)]
        if only_prompt_units is not None:
            k.dbg_unit = f"u{only_prompt_units - 1}"
            k.sequence(T["xp"][0], only_prompt_units, 2, [], 18, ut_p, yp[0], hscr)
        elif variant == "prompts":
            for b in range(2):
                k.sequence(T["xp"][b], 2, 2, [], 18, ut_p, yp[b], hscr)
        elif variant == "scans":
            k.sequence(T["xs"], 0, 96, list(range(0, 96)), 224, ut_s, ys, hscr)
        else:
            for b in range(2):
                k.sequence(T["xp"][b], 2, 2, [], 18, ut_p, yp[b], hscr)
            k.sequence(T["xs"], 4, 96, list(range(0, 96)), 224, ut_s, ys, hscr)
        n, cnt = k.R.emit(nc, k.stores)
        k.nops = n
    return nc, k


def make_slab(rpb, i, kvs, lo, hi):
    H = rpb.shape[0]
    out = np.full((128, 7, H, 128), NEG, np.float32)
    kc = np.arange(64)
    qc = np.arange(64)
    c0 = np.clip(qc - 8, 0, 48)
    colv = (kc[:, None] >= c0[None, :]) & (kc[:, None] < c0[None, :] + 16)
    dcv = np.clip(kc[:, None] - qc[None, :] + 15, 0, 30)
    for a, t in enumerate(kvs):
        for rr in range(2):
            rk = 2 * t - 5 + rr
            for qq in range(2):
                rq = 2 * i - 1 + qq
                if lo <= rq < hi:
                    r0 = min(max(rq - 4, lo), hi - 8)
                else:
                    r0 = rq - 4
                if not (r0 <= rk < r0 + 8):
                    continue
                dr = rk - rq + 7
                blk = np.where(colv[None], rpb[:, dr][:, dcv], NEG)
                out[rr * 64:(rr + 1) * 64, a, :, qq * 64:(qq + 1) * 64] = blk.transpose(1, 0, 2)
    return out


def host_consts(norm1_w, norm2_w, gla_norm_w, qn_w, kn_w, conv_w, conv_b, masks):
    cf = np.zeros((128, NCF), np.float32)
    cf[:, C_WBC1:C_WBC1 + D] = norm1_w[None, :]
    cf[:, C_WBC2:C_WBC2 + D] = norm2_w[None, :]
    cf[:, C_GNW:C_GNW + 512] = np.tile(gla_norm_w, 4)[None, :]
    idx = np.arange(128)
    cf[:, C_UINC:C_UINC + 128] = (idx[:, None] <= idx[None, :]) * (-1.0 / 16)
    cf[:, C_ULT:C_ULT + 128] = (idx[:, None] < idx[None, :]) * (-1.0 / 16)
    cf[:, C_UGT:C_UGT + 128] = (idx[:, None] > idx[None, :]) * (-1.0 / 16)
    cw = conv_w.reshape(3, 44, 128)
    cf[:, C_CW:C_CW + 132] = cw.transpose(2, 1, 0).reshape(128, 132)
    cf[:, C_CB:C_CB + 44] = conv_b.reshape(44, 128).T
    cf[:, C_QNW] = np.tile(qn_w, 2)
    cf[:, C_KNW] = np.tile(kn_w, 2)
    cf[:, C_LNQ] = np.float32(np.log(0.125))
    cf[:, C_NEG] = -1.0 / 16
    cf[:, C_EPS] = EPS
    for m, (top_real, bot_real) in enumerate(masks):
        cf[:, C_MASK + 2 * m] = 1.0
        cf[0:64, C_MASK + 2 * m] = 1.0 if top_real else 0.0
        cf[:, C_MASK + 2 * m + 1] = 1.0
        cf[64:128, C_MASK + 2 * m + 1] = 1.0 if bot_real else 0.0
    cb = np.zeros((128, NCB), np.float32)
    cb[:, B_ID:B_ID + 128] = np.eye(128)
    cb[0:64, B_OB:B_OB + 64] = 1.0
    cb[64:128, B_OB + 64:B_OB + 128] = 1.0
    cb[:, B_ONE:B_ONE + 128] = 1.0
    cb[:, B_MF:B_MF + 128] = (idx[:, None] <= idx[None, :])
    cb[:, B_MB:B_MB + 128] = (idx[:, None] > idx[None, :])
    return cf, cb


def make_in_maps(inp):
    f = lambda k: np.asarray(inp[k], np.float32)
    x_prompt, x_sample = f("x_prompt"), f("x_sample")
    rpb = f("rpb")[0]
    w_in = np.ascontiguousarray(f("w_in")[0])
    shared = {
        "w_in": w_in,
        "w_a2_f": np.ascontiguousarray(f("w_a2_f")[0]), "b_a_f": np.ascontiguousarray(f("b_a_f")[0][None, :]),
        "w_a2_b": np.ascontiguousarray(f("w_a2_b")[0]), "b_a_b": np.ascontiguousarray(f("b_a_b")[0][None, :]),
        "w_na_proj": np.ascontiguousarray(f("w_na_proj")[0]), "w_gla_proj": np.ascontiguousarray(f("w_gla_proj")[0]),
        "w_out": np.ascontiguousarray(f("w_out")[0]), "w_up": np.ascontiguousarray(f("w_up")[0]),
        "w_down": np.ascontiguousarray(f("w_down")[0]),
    }
    flat = lambda s: np.ascontiguousarray(s.transpose(0, 2, 1, 3).reshape(128, 7168))
    slabI = make_slab(rpb, 3, kv_list(3, False), -100, 100)[:, 0:5]
    slabI = np.ascontiguousarray(slabI[:, ::-1].transpose(0, 2, 1, 3).reshape(128, 5120))
    sT_cl = np.stack([flat(make_slab(rpb, i, kv_list(i, True), 0, 32)) for i in EDGE_TOP])
    sT_un = np.stack([flat(make_slab(rpb, i, kv_list(i, True), -100, 100)) for i in EDGE_TOP])
    sB_cl = np.stack([flat(make_slab(rpb, i, kv_list(i, True), -16, 16)) for i in EDGE_BOT])
    sB_un = np.stack([flat(make_slab(rpb, i, kv_list(i, True), -100, 100)) for i in EDGE_BOT])
    shared.update({"slabI": slabI, "sT_p": sT_cl, "sB_p": sB_cl})
    maps = []
    for c in range(NCORES):
        s, j = c // 4, c % 4
        xp = np.zeros((2, 42, 64, D), np.float32)
        for b in range(2):
            xp[b, 5:37] = x_prompt[2 * c + b].reshape(32, 64, D)
        R0 = 64 * j
        xs = np.zeros((450, 64, D), np.float32)
        g0 = R0 - 193
        lo_, hi_ = max(0, g0), min(256, g0 + 450)
        xs[lo_ - g0:hi_ - g0] = x_sample[s].reshape(256, 64, D)[lo_:hi_]
        masks = [(False, True), (True, False)]
        for m in range(4):
            masks.append((R0 + 16 * m - 1 >= 0, R0 + 16 * m + 16 < 256))
        cf, cb = host_consts(f("norm1_w")[0], f("norm2_w")[0], f("gla_norm_w")[0], f("qn_w")[0], f("kn_w")[0],
                             f("conv_w")[0], f("conv_b")[0], masks)
        d = dict(shared)
        d.update({"xp": xp.reshape(2, 21 * 128, D), "xs": xs.reshape(225 * 128, D), "cf": cf, "cb": cb,
                  "sT_s": sT_cl if j == 0 else sT_un, "sB_s": sB_cl if j == 3 else sB_un})
        maps.append(d)
    return maps


def kernel(**inp):
    maps = make_in_maps(inp)
    nc, k = build_nc()
    res = run_bass_kernel_spmd(nc, maps, core_ids=list(range(NCORES)))
    y_prompt = np.empty((16, 2048, D), np.float32)
    y_sample = np.empty((2, 16384, D), np.float32)
    for c in range(NCORES):
        s, j = c // 4, c % 4
        r = res.results[c]
        y_prompt[2 * c:2 * c + 2] = np.asarray(r["yp"], np.float32)
        y_sample[s, 4096 * j:4096 * (j + 1)] = np.asarray(r["ys"], np.float32)
    return (y_prompt, y_sample)
```

```python
import numpy as np
import concourse.bass as bass
import concourse.mybir as mybir
from concourse.bass_utils import run_bass_kernel_spmd
from contextlib import ExitStack

F32 = mybir.dt.float32
BF16 = mybir.dt.bfloat16
AF = mybir.ActivationFunctionType
ALU = mybir.AluOpType
EPS = 1e-6
NCORES = 8
D = 1024
TR = 1152
TK = 1664
NEG = -30000.0
EDGE_TOP = (0, 1, 2)
EDGE_BOT = (7, 8)
C_WBC1, C_WBC2, C_GNW, C_UINC, C_ULT, C_UGT, C_CW, C_CB = 0, 1024, 2048, 2560, 2688, 2816, 2944, 3076
C_QNW, C_KNW, C_LNQ, C_NEG, C_MASK, C_EPS = 3120, 3121, 3122, 3123, 3124, 3136
NCF = 3138
B_ID, B_OB, B_ONE, B_MF, B_MB, B_ZERO = 0, 128, 256, 384, 512, 640
NCB = 768


class Buf:
    __slots__ = ("w", "rs")

    def __init__(self):
        self.w = None
        self.rs = []


class Rec:
    ENGS = ("pe", "act", "dve", "pool", "sp")

    def __init__(self):
        self.ops = []
        self.bufs = {}
        self.bar = None
        self.last = {}
        self.dmas = []

    def B(self, name):
        b = self.bufs.get(name)
        if b is None:
            b = self.bufs[name] = Buf()
        return b

    def add(self, eng, fn, r=(), w=(), dma=False, stream=None):
        deps = set()
        for n in r:
            b = self.B(n)
            if b.w is not None:
                deps.add(b.w)
        for n in w:
            b = self.B(n)
            if b.w is not None:
                deps.add(b.w)
            deps.update(b.rs)
        if self.bar is not None:
            deps.add(self.bar)
        i = len(self.ops)
        self.ops.append(dict(eng=eng, fn=fn, deps=deps, dma=dma, stream=stream, sig=False, ord=0))
        for n in r:
            self.B(n).rs.append(i)
        for n in w:
            b = self.B(n)
            b.w = i
            b.rs = []
        if dma:
            self.dmas.append(i)
        else:
            self.last[eng] = i
        return i

    def barrier(self, fn):
        deps = set(self.last.values()) | set(self.dmas)
        if self.bar is not None:
            deps.add(self.bar)
        i = len(self.ops)
        self.ops.append(dict(eng="dve", fn=fn, deps=deps, dma=False, stream=None, sig=False, ord=0))
        self.bar = i
        self.last["dve"] = i
        self.dmas = []
        return i

    def pe(self, fn, r=(), w=()):
        return self.add("pe", fn, r, w)

    def act(self, fn, r=(), w=()):
        return self.add("act", fn, r, w)

    def dve(self, fn, r=(), w=()):
        return self.add("dve", fn, r, w)

    def pool(self, fn, r=(), w=()):
        return self.add("pool", fn, r, w)

    def dma(self, q, fn, r=(), w=(), stream=None):
        return self.add(q, fn, r, w, dma=True, stream=stream)

    def emit(self, nc, final_wait_ops=()):
        ops = self.ops
        ops.append(dict(eng="sp", fn=None, deps=set(final_wait_ops), dma=False, stream=None, sig=False, ord=0))
        for o in ops:
            for d in o["deps"]:
                od = ops[d]
                if od["dma"]:
                    continue
                if od["eng"] == "pe" and o["eng"] == "pe" and not o["dma"]:
                    continue
                od["sig"] = True
        cnt = {}
        for o in ops:
            if o["dma"]:
                k = ("s", o["stream"])
                cnt[k] = cnt.get(k, 0) + 1
                o["ord"] = cnt[k]
            elif o["sig"]:
                k = ("e", o["eng"])
                cnt[k] = cnt.get(k, 0) + 1
                o["ord"] = cnt[k]
        with ExitStack() as es:
            sem = {}
            for k in cnt:
                sem[k] = es.enter_context(nc.semaphore("sem_%s_%s" % k))
            block = es.enter_context(nc.Block())
            by_eng = {e: [o for o in ops if o["eng"] == e] for e in self.ENGS}

            def run(engh, ename):
                waited = {}
                for o in by_eng[ename]:
                    need = {}
                    for d in o["deps"]:
                        od = ops[d]
                        if od["dma"]:
                            k = ("s", od["stream"])
                            v = 16 * od["ord"]
                        else:
                            if od["eng"] == "pe" and ename == "pe" and not o["dma"]:
                                continue
                            k = ("e", od["eng"])
                            v = od["ord"]
                        if v > need.get(k, 0):
                            need[k] = v
                    for k, v in need.items():
                        if waited.get(k, 0) >= v:
                            continue
                        waited[k] = v
                        engh.wait_ge(sem[k], v)
                    if o["fn"] is None:
                        continue
                    ins = o["fn"](engh)
                    if o["dma"]:
                        ins.then_inc(sem[("s", o["stream"])], 16)
                    elif o["sig"]:
                        ins.then_inc(sem[("e", ename)], 1)

            if by_eng["sp"]:
                block.sync(lambda e: run(e, "sp"))
            if by_eng["pool"]:
                block.gpsimd(lambda e: run(e, "pool"))
            if by_eng["act"]:
                block.scalar(lambda e: run(e, "act"))
            if by_eng["dve"]:
                block.vector(lambda e: run(e, "dve"))
            if by_eng["pe"]:
                block.tensor(lambda e: run(e, "pe"))
        return len(ops), cnt


def blocks(T):
    out = []
    t = 0
    while t < T:
        n = min(512, T - t)
        out.append((t, n))
        t += n
    return out


def kv_list(i, edge):
    if not edge:
        return list(range(i, i + 5))
    return {0: list(range(0, 7)), 1: list(range(1, 7)), 2: list(range(2, 7)),
            7: list(range(6, 12)), 8: list(range(6, 13))}[i]


class K:
    def __init__(self, nc, es, dbg=None):
        self.nc = nc
        self.es = es
        self.R = Rec()
        self.dbg = dbg
        self.dumps = []
        self.stores = []
        self.cnt = {}
        self.arena = es.enter_context(nc.sbuf_tensor("arena", [128, 106400], BF16))
        self.top = 0
        self.ps = [es.enter_context(nc.psum_tensor(f"ps{i}", [128, 512], F32)) for i in range(6)]
        self.pb = [es.enter_context(nc.psum_tensor(f"pb{i}", [128, 1024], BF16)) for i in range(2)]

    def alloc(self, n, dt=BF16):
        if dt == F32:
            n2 = 2 * n
        else:
            n2 = n
        self.top = (self.top + 1) // 2 * 2
        a = self.arena[:, self.top:self.top + n2]
        self.top += n2
        assert self.top <= 106400, self.top
        return a.bitcast(F32) if dt == F32 else a

    def rot(self, key, n):
        c = self.cnt.get(key, 0)
        self.cnt[key] = c + 1
        return c % n

    def ps_new(self, bank=None):
        i = self.rot("ps", 6) if bank is None else bank
        return self.ps[i][:], f"ps{i}"

    def pt_new(self):
        i = self.rot("pt", 2)
        return self.pb[i][:], f"pb{i}"

    def MM(self, out, lhsT, rhs, start, stop, r, w):
        self.R.pe(lambda e: e.matmul(out, lhsT=lhsT, rhs=rhs, start=start, stop=stop), r, w)

    def TR_(self, out, in_, r, w):
        ident = self.ident
        self.R.pe(lambda e: e.transpose(out=out, in_=in_, identity=ident), list(r) + ["cb"], w)

    def ACT(self, out, in_, func, r, w, scale=None, bias=None, accum=None):
        kw = {}
        if scale is not None:
            kw["scale"] = scale
        if bias is not None:
            kw["bias"] = bias
        if accum is not None:
            kw["accum_out"] = accum
        self.R.act(lambda e: e.activation(out=out, in_=in_, func=func, **kw), r, w)

    def ACOPY(self, out, in_, r, w):
        self.R.act(lambda e: e.copy(out=out, in_=in_), r, w)

    def AMUL(self, out, in_, c, r, w):
        self.R.act(lambda e: e.mul(out=out, in_=in_, mul=c), r, w)

    def DCOPY(self, out, in_, r, w):
        self.R.dve(lambda e: e.tensor_copy(out=out, in_=in_), r, w)

    def TS(self, out, in0, s1, s2, op0, op1, r, w):
        if op1 is None:
            self.R.dve(lambda e: e.tensor_scalar(out=out, in0=in0, scalar1=s1, scalar2=None, op0=op0), r, w)
        else:
            self.R.dve(lambda e: e.tensor_scalar(out=out, in0=in0, scalar1=s1, scalar2=s2, op0=op0, op1=op1), r, w)

    def TT(self, out, in0, in1, op, r, w):
        self.R.dve(lambda e: e.tensor_tensor(out=out, in0=in0, in1=in1, op=op), r, w)

    def STT(self, out, in0, scalar, in1, op0, op1, r, w):
        self.R.dve(lambda e: e.scalar_tensor_tensor(out=out, in0=in0, scalar=scalar, in1=in1, op0=op0, op1=op1), r, w)

    def LOAD(self, q, out, in_, w, stream, r=()):
        return self.R.dma(q, lambda e: e.dma_start(out=out, in_=in_), r=r, w=w, stream=stream)

    def STORE(self, q, out, in_, r, stream, w=()):
        i = self.R.dma(q, lambda e: e.dma_start(out=out, in_=in_), r=r, w=w, stream=stream)
        return i

    def dump(self, name, ap, shape, dt, rname):
        if self.dbg is None or name not in self.dbg or getattr(self, "cur_unit", None) != getattr(self, "dbg_unit", None):
            return
        d = self.nc.dram_tensor("dbg_" + name, list(shape), dt, kind="ExternalOutput").ap()
        self.stores.append(self.STORE("sp", d, ap, [rname], "dbg_" + name))
        self.dumps.append(name)

    def barrier(self):
        bt = self.bartile
        self.R.barrier(lambda e: e.memset(bt, 0.0))

    def setup(self, T):
        self.T = T
        A = self.alloc
        self.cf = A(NCF, F32)
        self.cb = A(NCB)
        self.ident = self.cb[:, B_ID:B_ID + 128]
        self.onesblk = self.cb[:, B_OB:B_OB + 128]
        self.ones = self.cb[:, B_ONE:B_ONE + 128]
        self.Mf = self.cb[:, B_MF:B_MF + 128]
        self.Mb = self.cb[:, B_MB:B_MB + 128]
        self.wa2 = [A(512), A(512)]
        self.ba = [A(512), A(512)]
        self.wlr = A(256).rearrange("p (k n) -> p k n", k=8)
        self.slabI = A(5120)
        self.S = {"f": A(512, F32), "b": A(512, F32)}
        self.Sbf = A(512)
        self.Ssave = [A(512, F32) for _ in range(4)]
        self.xs = [A(1024, F32) for _ in range(2)]
        self.jk = [A(1024) for _ in range(2)]
        self.xb = [A(1024) for _ in range(2)]
        self.xTt = [A(1024).rearrange("p (k n) -> p k n", k=8) for _ in range(2)]
        self.st = [A(8, F32) for _ in range(8)]
        self.wsl = [A(4096) for _ in range(4)]
        self.PT = [A(896) for _ in range(2)]
        self.ona = [A(512) for _ in range(2)]
        self.bartile = A(2, F32)
        self.tmp0 = self.top = (self.top + 1) // 2 * 2
        self.tf = [A(512, F32) for _ in range(8)]
        self.tb = [A(512) for _ in range(10)]
        assert self.top - self.tmp0 == 13312
        self.base = self.top
        L = self.LOAD
        L("sp", self.cf, T["cf"][:, :], ["cf"], "cf")
        L("pool", self.cb, T["cb"][:, :], ["cb"], "cb")
        for d, nm in enumerate(("f", "b")):
            L("pool", self.wa2[d][0:16, :], T["w_a2_" + nm][:, :], ["wa2"], "wa2" + nm)
            L("pool", self.ba[d][0:1, :], T["b_a_" + nm][:, :], ["ba"], "ba" + nm)
        L("pool", self.wlr, T["w_in"][:, 3584:3616].rearrange("(k p) n -> p k n", p=128), ["wlr"], "wlr")

    def tF(self):
        i = self.rot("tf", 8)
        return self.tf[i], f"tf{i}"

    def tB(self):
        i = self.rot("tb", 10)
        return self.tb[i], f"tb{i}"

    def stt_(self):
        i = self.rot("st", 8)
        return self.st[i], f"st{i}"

    def wload(self, si, cols_ap, ncols, kchunks=8):
        sl, nm = self.wsl[si], f"wsl{si}"
        v = sl[:, 0:kchunks * ncols].rearrange("p (k n) -> p k n", k=kchunks)
        if getattr(self, "nowload", False) and self.cnt.get("wl%d" % si, 0) > 0:
            return v, nm
        self.cnt["wl%d" % si] = 1
        self.LOAD("pool", v, cols_ap.rearrange("(k p) n -> p k n", p=128), [nm], nm)
        return v, nm

    def xload(self, src_rows):
        i = self.rot("xs", 2)
        self.LOAD("sp", self.xs[i], src_rows, [f"xs{i}"], f"xs{i}")
        return self.xs[i], f"xs{i}"

    def norm_A(self, src, sname, wbc_off, maskcol=None):
        j = self.rot("jk", 2)
        jk, jn = self.jk[j], f"jk{j}"
        xb, xn = self.xb[j], f"xb{j}"
        st, sn = self.stt_()
        cf = self.cf
        self.ACT(jk, src, AF.Square, [sname], [jn, sn], accum=st[:, 0:1])
        self.ACT(st[:, 2:3], st[:, 0:1], AF.Ln, [sn, "cf"], [sn], scale=1.0 / D, bias=cf[:, C_EPS:C_EPS + 1])
        self.ACT(st[:, 3:4], st[:, 2:3], AF.Exp, [sn], [sn], scale=-0.5)
        if maskcol is not None:
            self.TT(st[:, 3:4], st[:, 3:4], cf[:, maskcol:maskcol + 1], ALU.mult, [sn, "cf"], [sn])
        self.STT(xb, src, st[:, 3:4], cf[:, wbc_off:wbc_off + D], ALU.mult, ALU.mult, [sname, sn, "cf"], [xn])
        return xb, xn

    def norm_B(self, xb, xn, dst3, dname):
        pt, pn = self.pt_new()
        for kc in range(8):
            self.TR_(pt[:, kc * 128:(kc + 1) * 128], xb[:, kc * 128:(kc + 1) * 128], [xn], [pn])
        src3 = pt.rearrange("p (a b) -> p a b", a=8)
        self.DCOPY(dst3[:, 0:4, :], src3[:, 0:4, :], [pn], [dname])
        self.ACOPY(dst3[:, 4:8, :], src3[:, 4:8, :], [pn], [dname])

    def norm_T(self, src, sname, wbc_off, dst3, dname, maskcol=None):
        xb, xn = self.norm_A(src, sname, wbc_off, maskcol)
        self.norm_B(xb, xn, dst3, dname)

    def gates(self, lrT, lrn, d, bank=None):
        ps, pn = self.ps_new(bank)
        self.MM(ps, lrT, self.wa2[d][0:16, :], True, False, [lrn, "wa2"], [pn])
        self.MM(ps, self.ones[0:1, :], self.ba[d][0:1, :], False, True, ["cb", "ba"], [pn])
        gp, gn = self.tF()
        self.ACT(gp, ps, AF.Exp, [pn], [gn], scale=-1.0)
        self.ACT(gp, gp, AF.Ln, [gn, "cb"], [gn], bias=self.ones[:, 0:1])
        return gp, gn

    def tok_kv(self, xT3, xname, wk, wkn, wv, wvn):
        psk, pkn = self.ps_new()
        for kc in range(8):
            self.MM(psk, xT3[:, kc, :], wk[:, kc, :], kc == 0, kc == 7, [xname, wkn], [pkn])
        psv, pvn = self.ps_new()
        for kc in range(8):
            self.MM(psv, xT3[:, kc, :], wv[:, kc, :], kc == 0, kc == 7, [xname, wvn], [pvn])
        vt, vn = self.tB()
        self.ACOPY(vt, psv, [pvn], [vn])
        return psk, pkn, vt, vn

    def state_prep(self, gp, gn, psk, pkn, d, banks=(None, None)):
        cf = self.cf
        U = cf[:, C_UGT:C_UGT + 128] if d == "f" else cf[:, C_ULT:C_ULT + 128]
        psc, pcn = self.ps_new(banks[0])
        self.MM(psc, U, gp, True, True, ["cf", gn], [pcn])
        Ec, en = self.tF()
        self.ACT(Ec, psc, AF.Exp, [pcn], [en])
        kt, kn = self.tB()
        self.TT(kt, psk, Ec, ALU.mult, [pkn, en], [kn])
        pse, pen = self.ps_new(banks[1])
        for hh in range(4):
            self.MM(pse[:, hh:hh + 1], gp[:, hh * 128:(hh + 1) * 128], cf[:, C_NEG:C_NEG + 1], True, True,
                    [gn, "cf"], [pen])
        st, sn = self.stt_()
        self.ACT(st[:, 0:4], pse[:, 0:4], AF.Exp, [pen], [sn])
        return kt, kn, st, sn

    def state_apply(self, kt, kn, st, sn, vt, vn, d, snap=None, snapn=None, bank=None):
        S, Sn = self.S[d], "S" + d
        psd, pdn = self.ps_new(bank)
        for hh in range(4):
            sl = slice(hh * 128, (hh + 1) * 128)
            self.MM(psd[:, sl], kt[:, sl], vt[:, sl], True, True, [kn, vn], [pdn])
        S3 = S.rearrange("p (h v) -> p h v", h=4)
        self.TT(S3, S3, st[:, 0:4].rearrange("p (h o) -> p h o", o=1).to_broadcast([128, 4, 128]), ALU.mult,
                [Sn, sn], [Sn])
        if snap is not None:
            self.ACOPY(snap, S, [Sn], [snapn])
        self.TT(S, S, psd, ALU.add, [Sn, pdn], [Sn])

    def state_update(self, gp, gn, psk, pkn, vt, vn, d, snap=None, snapn=None):
        kt, kn, st, sn = self.state_prep(gp, gn, psk, pkn, d)
        self.state_apply(kt, kn, st, sn, vt, vn, d, snap, snapn)

    def zero_state(self, d):
        S = self.S[d]
        self.R.dve(lambda e: e.memset(S, 0.0), [], ["S" + d])

    def scan_pipe(self, d, n, p1, wk, wkn, wv, wvn, lr_of=None, snap_of=None, save_after=None, p1a=None):
        di = 0 if d == "f" else 1
        s1, s2, s3 = {}, {}, {}
        ksb_pool = [self.slabI[:, j * 1024:(j + 1) * 1024].bitcast(F32) for j in range(3)]

        def P2(i):
            xT3, xn = s1.pop(i)
            psk, pkn = self.ps_new(0)
            for kc in range(8):
                self.MM(psk, xT3[:, kc, :], wk[:, kc, :], kc == 0, kc == 7, [xn, wkn], [pkn])
            j = self.rot("ksb", 3)
            ksb, ksn = ksb_pool[j], f"ksb{j}"
            self.ACOPY(ksb, psk, [pkn], [ksn])
            psv, pvn = self.ps_new(1)
            for kc in range(8):
                self.MM(psv, xT3[:, kc, :], wv[:, kc, :], kc == 0, kc == 7, [xn, wvn], [pvn])
            vt, vn = self.tB()
            self.DCOPY(vt, psv, [pvn], [vn])
            if lr_of is None:
                psl, pln = self.ps_new(2)
                for kc in range(8):
                    self.MM(psl[0:16, 0:128], self.wlr[:, kc, 16 * di:16 * di + 16], xT3[:, kc, :], kc == 0, kc == 7,
                            ["wlr", xn], [pln])
                lrt, lrn = self.tB()
                self.DCOPY(lrt[0:16, 0:128], psl[0:16, 0:128], [pln], [lrn])
                lr = (lrt[0:16, 0:128], lrn)
            else:
                lr = lr_of(i)
            s2[i] = (ksb, ksn, vt, vn, lr)

        def P3a(i):
            ksb, ksn, vt, vn, lr = s2.pop(i)
            gp, gn = self.gates(lr[0], lr[1], di, bank=3)
            s3[i] = (ksb, ksn, vt, vn, gp, gn)

        def P3b(i):
            ksb, ksn, vt, vn, gp, gn = s3.pop(i)
            kt, kn, st, sn = self.state_prep(gp, gn, ksb, ksn, d, banks=(4, 5))
            sp = snap_of(i) if snap_of else None
            self.state_apply(kt, kn, st, sn, vt, vn, d, sp[0] if sp else None, sp[1] if sp else None, bank=5)
            if save_after and i in save_after:
                dst, dn = save_after[i]
                S = self.S[d]
                self.R.dve(lambda e, dst=dst, S=S: e.tensor_copy(out=dst, in_=S), ["S" + d], [dn])

        s0 = {}
        for it in range(n + 4):
            if 0 <= it - 4 < n:
                P3b(it - 4)
            if 0 <= it - 3 < n:
                P3a(it - 3)
            if 0 <= it - 2 < n:
                P2(it - 2)
            if 0 <= it - 1 < n:
                s1[it - 1] = p1(it - 1, s0.pop(it - 1, None))
            if it < n and p1a is not None:
                s0[it] = p1a(it)

    def state_scan(self, xsrc, taus, d, wk, wkn, wv, wvn, saves=None):
        def p1a(i):
            tau = taus[i]
            xs, xn = self.xload(xsrc[tau * 128:(tau + 1) * 128, :])
            return self.norm_A(xs, xn, C_WBC1)

        def p1(i, tok):
            j = self.rot("xTt", 2)
            xT3, xTn = self.xTt[j], f"xTt{j}"
            self.norm_B(tok[0], tok[1], xT3, xTn)
            return xT3, xTn
        sa = None
        if saves:
            sa = {i: saves[t] for i, t in enumerate(taus) if t in saves}
        self.scan_pipe(d, len(taus), p1, wk, wkn, wv, wvn, save_after=sa, p1a=p1a)

    def unit(self, xsrc, kv0, ut, yout, hscr, Sb_init, uname):
        R, T, cf = self.R, self.T, self.cf
        A = self.alloc
        self.top = self.base
        self.cur_unit = uname
        w_in = T["w_in"]
        xnT = A(8 * TK).rearrange("p (k n) -> p k n", k=8)
        self.LOAD("pool", self.slabI, T["slabI"][:, :], ["slabI"] + [f"ksb{j}" for j in range(3)], "slabI")
        prev = None
        for t in range(14):
            cur = None
            if t < 13:
                xs, xn = self.xload(xsrc[(kv0 + t) * 128:(kv0 + t + 1) * 128, :])
                cur = self.norm_A(xs, xn, C_WBC1)
            if prev is not None:
                self.norm_B(prev[0], prev[1], xnT[:, :, (t - 1) * 128:t * 128], f"xnT{t - 1}")
            prev = cur
        self.dump("xnT", xnT, [128, 8, TK], BF16, "xnT12")
        if getattr(self, "stop_after", None) == "U1":
            self.R.dve(lambda e: e.memset(self.Sfx, 0.0), [], ["Sfx"])
            self.barrier()
            return

        def xr(t0, n, off):
            a = (off + t0) // 128
            b = (off + t0 + n - 1) // 128
            return [f"xnT{t}" for t in range(a, b + 1)]

        m1 = self.top
        onaT = A(4 * TR).rearrange("p (k n) -> p k n", k=4)
        KT = A(4 * TK).rearrange("p (k n) -> p k n", k=4)
        QT = A(4 * TR).rearrange("p (k n) -> p k n", k=4)
        V = A(13 * 8 * 65)
        V4 = V.rearrange("p (t h d) -> p t h d", t=13, h=8)

        hn_pending = []

        def hn_finish():
            while hn_pending:
                (ps, pn, n, nwc, biasc, dst, dname, sq, sqn) = hn_pending.pop(0)
                ps2, p2n = self.ps_new()
                self.MM(ps2[:, :n], self.onesblk, sq[:, :n], True, True, ["cb", sqn], [p2n])
                r1, r1n = self.tF()
                self.ACT(r1[:, :n], ps2[:, :n], AF.Ln, [p2n, "cf"], [r1n], scale=1.0 / 64, bias=cf[:, C_EPS:C_EPS + 1])
                if biasc is None:
                    self.ACT(r1[:, :n], r1[:, :n], AF.Exp, [r1n], [r1n], scale=-0.5)
                else:
                    self.ACT(r1[:, :n], r1[:, :n], AF.Exp, [r1n, "cf"], [r1n], scale=-0.5, bias=cf[:, biasc:biasc + 1])
                self.STT(dst, ps[:, :n], cf[:, nwc:nwc + 1], r1[:, :n], ALU.mult, ALU.mult, [pn, r1n, "cf"], [dname])

        def headnorm(ps, pn, n, nwc, biasc, dst, dname):
            sq, sqn = self.tB()
            self.ACT(sq[:, :n], ps[:, :n], AF.Square, [pn], [sqn])
            hn_pending.append((ps, pn, n, nwc, biasc, dst, dname, sq, sqn))

        w, wn = self.wload(0, w_in[:, 512:1024], 512)
        for p in range(4):
            for (t0, n) in blocks(TK):
                ps, pn = self.ps_new()
                for kc in range(8):
                    self.MM(ps[:, :n], w[:, kc, p * 128:(p + 1) * 128], xnT[:, kc, t0:t0 + n], kc == 0, kc == 7,
                            xr(t0, n, 0) + [wn], [pn])
                hn_finish()
                headnorm(ps, pn, n, C_KNW, None, KT[:, p, t0:t0 + n], "KT")
        w, wn = self.wload(2, w_in[:, 0:512], 512)
        for p in range(4):
            for (t0, n) in blocks(TR):
                ps, pn = self.ps_new()
                for kc in range(8):
                    self.MM(ps[:, :n], w[:, kc, p * 128:(p + 1) * 128], xnT[:, kc, 256 + t0:256 + t0 + n], kc == 0,
                            kc == 7, xr(t0, n, 256) + [wn], [pn])
                hn_finish()
                headnorm(ps, pn, n, C_QNW, C_LNQ, QT[:, p, t0:t0 + n], "QT")
        w, wn = self.wload(0, w_in[:, 1024:1536], 512)
        R.pool(lambda e: e.memset(V, 1.0), [], ["V"])
        hn_first_v = True
        for t in range(13):
            ps, pn = self.ps_new()
            for kc in range(8):
                self.MM(ps, xnT[:, kc, t * 128:(t + 1) * 128], w[:, kc, :], kc == 0, kc == 7, [f"xnT{t}", wn], [pn])
            if hn_first_v:
                hn_finish()
                hn_first_v = False
            self.ACOPY(V4[:, t, :, 0:64], ps.rearrange("p (h d) -> p h d", h=8), [pn], ["V"])
        self.dump("KT", KT, [128, 4, TK], BF16, "KT")
        self.dump("QT", QT, [128, 4, TR], BF16, "QT")
        self.dump("V", V, [128, 13 * 8 * 65], BF16, "V")
        if getattr(self, "stop_after", None) == "U2":
            self.R.dve(lambda e: e.memset(self.Sfx, 0.0), [], ["Sfx"])
            self.barrier()
            return

        onaAll = A(9 * 512).rearrange("p (i n) -> p i n", i=9)
        edge_qps = []
        if ut["top"] is not None:
            edge_qps += [(i, ut["top"][EDGE_TOP.index(i)]) for i in EDGE_TOP]
        if ut["bot"] is not None:
            edge_qps += [(i, ut["bot"][EDGE_BOT.index(i)]) for i in EDGE_BOT]
        eidx = {i: n_ for n_, (i, _) in enumerate(edge_qps)}
        kvl = {i: kv_list(i, i in eidx) for i in range(9)}
        for h in range(8):
            p, bp = h // 2, 64 * (h % 2)
            es_i = 0 if h % 2 == 0 else 2
            es, esn = self.wsl[es_i], f"wsl{es_i}"
            for n_, (i, src) in enumerate(edge_qps):
                self.LOAD("pool", es[:, n_ * 896:(n_ + 1) * 896], src[:, h * 896:(h + 1) * 896], [esn], f"{esn}_{n_}")
            OX, OXn = self.ps_new(4)
            OY, OYn = self.ps_new(5)
            zer = self.cb[:, B_ZERO:B_ZERO + 128]
            self.MM(OX[:, 0:455], zer, self.slabI[:, 0:455], True, False, ["cb", "slabI"], [OXn])
            self.MM(OY[:, 0:130], zer, self.slabI[:, 0:130], True, False, ["cb", "slabI"], [OYn])
            lastX = max(kvl[i][-1] for i in range(7))
            lastY = max(kvl[i][-1] for i in (7, 8))
            def S1(t):
                qs = [i for i in range(9) if t in kvl[i]]
                if not qs:
                    return None
                assert qs == list(range(qs[0], qs[-1] + 1)) and len(qs) <= 7
                bA, bB = (0, 1) if t % 2 == 0 else (2, 3)
                psA, pAn = self.ps_new(bA)
                psB, pBn = self.ps_new(bB)
                qA, qB = qs[:4], qs[4:]
                for (qq, ps_, pn_) in ((qA, psA, pAn), (qB, psB, pBn)):
                    if not qq:
                        continue
                    nq = len(qq)
                    c = 0
                    while c < nq:
                        i = qq[c]
                        if i in eidx:
                            c2 = c + 1
                            a_ = kvl[i].index(t)
                            blk = eidx[i] * 896 + a_ * 128
                            b_ap, b_n = es[:, blk:blk + 128], esn
                        else:
                            c2 = c
                            while c2 < nq and qq[c2] not in eidx:
                                c2 += 1
                            blk = (h * 5 + (i - t + 4)) * 128
                            b_ap, b_n = self.slabI[:, blk:blk + (c2 - c) * 128], "slabI"
                        self.MM(ps_[:, c * 128:c2 * 128], KT[bp:bp + 64, p, t * 128:(t + 1) * 128],
                                QT[bp:bp + 64, p, qq[c] * 128:(qq[c2 - 1] + 1) * 128], True, False, ["KT", "QT"], [pn_])
                        self.MM(ps_[:, c * 128:c2 * 128], self.ident, b_ap, False, True, ["cb", b_n], [pn_])
                        c = c2
                pj = self.rot("PT", 2)
                PT, PTn = self.PT[pj], f"PT{pj}"
                self.ACT(PT[:, 0:len(qA) * 128], psA[:, 0:len(qA) * 128], AF.Exp, [pAn], [PTn])
                if qB:
                    self.ACT(PT[:, 512:512 + len(qB) * 128], psB[:, 0:len(qB) * 128], AF.Exp, [pBn], [PTn])
                return (qs, PT, PTn)

            def S2(t, tok):
                qs, PT, PTn = tok
                for c, i in enumerate(qs):
                    if i < 7:
                        od, odn = OX[:, i * 65:(i + 1) * 65], OXn
                    else:
                        od, odn = OY[:, (i - 7) * 65:(i - 6) * 65], OYn
                    is_last = (c == len(qs) - 1 and t == lastY) if i >= 7 else \
                        (t == lastX and i == max(q for q in qs if q < 7))
                    self.MM(od, PT[:, c * 128:(c + 1) * 128], V4[:, t, h, :], False, is_last, [PTn, "V"], [odn])

            tok_prev = S1(0)
            for t in range(13):
                tok_next = S1(t + 1) if t + 1 < 13 else None
                if tok_prev is not None:
                    S2(t, tok_prev)
                tok_prev = tok_next
            for (O_, On_, i0, ni) in ((OX, OXn, 0, 7), (OY, OYn, 7, 2)):
                O3 = O_[:, 0:ni * 65].rearrange("p (i d) -> p i d", i=ni)
                st, sn = self.stt_()
                R.dve(lambda e, st=st, O3=O3, ni=ni: e.reciprocal(out=st[:, 0:ni], in_=O3[:, :, 64]), [On_], [sn])
                self.TT(onaAll[:, i0:i0 + ni, h * 64:(h + 1) * 64], O3[:, :, 0:64],
                        st[:, 0:ni].rearrange("p (i o) -> p i o", o=1).to_broadcast([128, ni, 64]), ALU.mult,
                        [On_, sn], ["onaAll"])
        for i in range(9):
            pt, pn = self.pt_new()
            for c in range(4):
                self.TR_(pt[:, c * 128:(c + 1) * 128], onaAll[:, i, c * 128:(c + 1) * 128], ["onaAll"], [pn])
            self.ACOPY(onaT[:, :, i * 128:(i + 1) * 128], pt[:, 0:512].rearrange("p (a b) -> p a b", a=4), [pn], ["onaT"])
        self.dump("onaT", onaT, [128, 4, TR], BF16, "onaT")
        if getattr(self, "stop_after", None) == "U3":
            self.R.dve(lambda e: e.memset(self.Sfx, 0.0), [], ["Sfx"])
            self.barrier()
            return

        self.barrier()
        self.top = m1 + 4 * TR
        ogT = A(4 * TR).rearrange("p (k n) -> p k n", k=4)
        m_gla = self.top
        qgT = A(4 * TR).rearrange("p (k n) -> p k n", k=4)
        kgT = A(4 * TR).rearrange("p (k n) -> p k n", k=4)
        lr = [self.wsl[2][:, 0:TR], self.wsl[2][:, TR:2 * TR]]
        sog = A(9 * 512).rearrange("p (t n) -> p t n", t=9)
        snap = A(9 * 512).rearrange("p (t n) -> p t n", t=9)
        w, wn = self.wload(0, w_in[:, 1536:2048], 512)
        for hh in range(4):
            for (t0, n) in blocks(TR):
                ps, pn = self.ps_new()
                for kc in range(8):
                    self.MM(ps[:, :n], w[:, kc, hh * 128:(hh + 1) * 128], xnT[:, kc, 256 + t0:256 + t0 + n], kc == 0,
                            kc == 7, xr(t0, n, 256) + [wn], [pn])
                self.AMUL(qgT[:, hh, t0:t0 + n], ps[:, :n], 128 ** -0.5, [pn], ["qgT"])
        wk, wkn = self.wload(1, w_in[:, 2048:2560], 512)
        for hh in range(4):
            for (t0, n) in blocks(TR):
                ps, pn = self.ps_new()
                for kc in range(8):
                    self.MM(ps[:, :n], wk[:, kc, hh * 128:(hh + 1) * 128], xnT[:, kc, 256 + t0:256 + t0 + n], kc == 0,
                            kc == 7, xr(t0, n, 256) + [wkn], [pn])
                self.DCOPY(kgT[:, hh, t0:t0 + n], ps[:, :n], [pn], ["kgT"])
        for d in range(2):
            for (t0, n) in blocks(TR):
                ps, pn = self.ps_new()
                for kc in range(8):
                    self.MM(ps[0:16, :n], self.wlr[:, kc, 16 * d:16 * d + 16], xnT[:, kc, 256 + t0:256 + t0 + n],
                            kc == 0, kc == 7, xr(t0, n, 256) + ["wlr"], [pn])
                self.ACOPY(lr[d][0:16, t0:t0 + n], ps[0:16, :n], [pn], [f"lr{d}", "wsl2"])
        w, wn = self.wload(0, w_in[:, 3072:3584], 512)
        for i in range(9):
            ps, pn = self.ps_new()
            for kc in range(8):
                self.MM(ps, xnT[:, kc, (i + 2) * 128:(i + 3) * 128], w[:, kc, :], kc == 0, kc == 7,
                        [f"xnT{i + 2}", wn], [pn])
            tg, tgn = self.tF()
            self.ACT(tg, ps, AF.Tanh, [pn], [tgn], scale=0.5)
            self.STT(tg, tg, 1.0, ps, ALU.add, ALU.mult, [tgn, pn], [tgn])
            self.TT(sog[:, i, :], tg, cf[:, C_GNW:C_GNW + 512], ALU.mult, [tgn, "cf"], ["sog"])
        wv, wvn = self.wload(3, w_in[:, 2560:3072], 512)
        if getattr(self, "stop_after", None) == "U5":
            self.R.dve(lambda e: e.memset(self.Sfx, 0.0), [], ["Sfx"])
            self.barrier()
            return

        if Sb_init is None:
            self.zero_state("b")
        else:
            Sb = self.S["b"]
            R.dve(lambda e, Sb=Sb, src=Sb_init[0]: e.tensor_copy(out=Sb, in_=src), [Sb_init[1]], ["Sb"])
        order6 = list(reversed(range(9)))
        self.scan_pipe("b", 9,
                       lambda n_, tok=None: (xnT[:, :, (order6[n_] + 2) * 128:(order6[n_] + 3) * 128], f"xnT{order6[n_] + 2}"),
                       wk, wkn, wv, wvn,
                       lr_of=lambda n_: (lr[1][0:16, order6[n_] * 128:(order6[n_] + 1) * 128], "lr1"),
                       snap_of=lambda n_: (snap[:, order6[n_], :], "snap"))
        self.barrier()
        Sf = self.S["f"]
        ded = [[self.slabI[:, (par * 4 + q) * 512:(par * 4 + q + 1) * 512] for q in range(4)] for par in range(2)]
        m1out = {}

        def M1(i):
            par = i % 2
            xT3 = xnT[:, :, (i + 2) * 128:(i + 3) * 128]
            psk, pkn, vt, vn = self.tok_kv(xT3, f"xnT{i + 2}", wk, wkn, wv, wvn)
            qts, As = [], []
            gpf = self.gates(lr[0][0:16, i * 128:(i + 1) * 128], "lr0", 0)
            ktf, ktfn, stf, stfn = self.state_prep(gpf[0], gpf[1], psk, pkn, "f")
            for d in range(2):
                if d == 0:
                    gp, gn = gpf
                else:
                    gp, gn = self.gates(lr[d][0:16, i * 128:(i + 1) * 128], f"lr{d}", d)
                Ufm = cf[:, C_UINC:C_UINC + 128] if d == 0 else cf[:, C_ULT:C_ULT + 128]
                psp, ppn = self.ps_new()
                for hh in range(4):
                    sl = slice(hh * 128, (hh + 1) * 128)
                    self.MM(psp[:, sl], gp[:, sl], Ufm, True, True, [gn, "cf"], [ppn])
                Eq, eqn = self.tF()
                Ek, ekn = self.tF()
                self.ACT(Eq, psp, AF.Exp, [ppn], [eqn], scale=(1.0 if d == 0 else -1.0))
                self.ACT(Ek, psp, AF.Exp, [ppn], [ekn], scale=(-1.0 if d == 0 else 1.0))
                qt, qn = ded[par][d], f"ded{par}{d}"
                kt2, k2n = self.tB()
                q3 = qt.rearrange("p (h t) -> p h t", h=4)
                k3 = kt2.rearrange("p (h t) -> p h t", h=4)
                self.TT(q3, qgT[:, :, i * 128:(i + 1) * 128], Eq.rearrange("p (h t) -> p h t", h=4), ALU.mult,
                        ["qgT", eqn], [qn])
                self.TT(k3, kgT[:, :, i * 128:(i + 1) * 128], Ek.rearrange("p (h t) -> p h t", h=4), ALU.mult,
                        ["kgT", ekn], [k2n])
                psa, pan = self.ps_new()
                for hh in range(4):
                    sl = slice(hh * 128, (hh + 1) * 128)
                    self.MM(psa[:, sl], kt2[:, sl], qt[:, sl], True, True, [k2n, qn], [pan])
                Ad, adn = ded[par][2 + d], f"ded{par}{2 + d}"
                Mm = self.Mf if d == 0 else self.Mb
                self.TT(Ad.rearrange("p (h t) -> p h t", h=4), psa.rearrange("p (h t) -> p h t", h=4),
                        Mm.rearrange("p (o t) -> p o t", o=1).to_broadcast([128, 4, 128]), ALU.mult, [pan, "cb"], [adn])
                qts.append((qt, qn))
                As.append((Ad, adn))
            m1out[i] = (vt, vn, ktf, ktfn, stf, stfn, qts, As)

        def M2(i):
            vt, vn, ktf, ktfn, stf, stfn, qts, As = m1out.pop(i)
            if i == 8:
                exp_ = self.Sfx
                R.dve(lambda e, exp_=exp_, Sf=Sf: e.tensor_copy(out=exp_, in_=Sf), ["Sf"], ["Sfx"])
            self.ACOPY(self.Sbf, Sf, ["Sf"], ["Sbf"])
            pso, pon = self.ps_new()
            for hh in range(4):
                sl = slice(hh * 128, (hh + 1) * 128)
                self.MM(pso[:, sl], qts[0][0][:, sl], self.Sbf[:, sl], True, False, [qts[0][1], "Sbf"], [pon])
                self.MM(pso[:, sl], As[0][0][:, sl], vt[:, sl], False, False, [As[0][1], vn], [pon])
                self.MM(pso[:, sl], qts[1][0][:, sl], snap[:, i, sl], False, False, [qts[1][1], "snap"], [pon])
                self.MM(pso[:, sl], As[1][0][:, sl], vt[:, sl], False, True, [As[1][1], vn], [pon])
            self.state_apply(ktf, ktfn, stf, stfn, vt, vn, "f")
            st, sn = self.stt_()
            jj = self.rot("jk", 2)
            for hh in range(4):
                sl = slice(hh * 128, (hh + 1) * 128)
                self.ACT(self.jk[jj][:, sl], pso[:, sl], AF.Square, [pon], [f"jk{jj}", sn], accum=st[:, hh:hh + 1])
            self.TS(st[:, 4:8], st[:, 0:4], 1.0 / 128, EPS, ALU.mult, ALU.add, [sn], [sn])
            self.ACT(st[:, 4:8], st[:, 4:8], AF.Ln, [sn], [sn])
            self.ACT(st[:, 4:8], st[:, 4:8], AF.Exp, [sn], [sn], scale=-0.5)
            self.TS(st[:, 4:8], st[:, 4:8], 0.5, None, ALU.mult, None, [sn], [sn])
            og, ogn = self.tF()
            self.TT(og.rearrange("p (h t) -> p h t", h=4), pso.rearrange("p (h t) -> p h t", h=4),
                    st[:, 4:8].rearrange("p (h o) -> p h o", o=1).to_broadcast([128, 4, 128]), ALU.mult, [pon, sn], [ogn])
            ogb, obn = self.tB()
            self.TT(ogb, og, sog[:, i, :], ALU.mult, [ogn, "sog"], [obn])
            pt, pn = self.pt_new()
            for c in range(4):
                self.TR_(pt[:, c * 128:(c + 1) * 128], ogb[:, c * 128:(c + 1) * 128], [obn], [pn])
            self.ACOPY(ogT[:, :, i * 128:(i + 1) * 128], pt[:, 0:512].rearrange("p (a b) -> p a b", a=4), [pn], ["ogT"])

        M1(0)
        for i in range(9):
            if i + 1 < 9:
                M1(i + 1)
            M2(i)
        self.dump("ogT", ogT, [128, 4, TR], BF16, "ogT")
        if getattr(self, "stop_after", None) == "U7":
            self.R.dve(lambda e: e.memset(self.Sfx, 0.0), [], ["Sfx"])
            self.barrier()
            return

        self.barrier()
        self.top = m_gla
        mixT = A(8 * TR).rearrange("p (k n) -> p k n", k=8)
        wna, wnan = self.wload(0, T["w_na_proj"][:, :], 1024, kchunks=4)
        wgl, wgln = self.wload(1, T["w_gla_proj"][:, :], 1024, kchunks=4)
        for hp in range(2):
            g1, g1n = self.wload(2, w_in[:, 3616 + 512 * hp:3616 + 512 * hp + 512], 512)
            g2, g2n = self.wload(3, w_in[:, 4640 + 512 * hp:4640 + 512 * hp + 512], 512)
            for c4 in range(4):
                c = hp * 4 + c4
                for (t0, n) in blocks(TR):
                    ps1, p1n = self.ps_new()
                    for kk in range(4):
                        self.MM(ps1[:, :n], wna[:, kk, c * 128:(c + 1) * 128], onaT[:, kk, t0:t0 + n], kk == 0, kk == 3,
                                [wnan, "onaT"], [p1n])
                    ps2, p2n = self.ps_new()
                    for kk in range(4):
                        self.MM(ps2[:, :n], wgl[:, kk, c * 128:(c + 1) * 128], ogT[:, kk, t0:t0 + n], kk == 0, kk == 3,
                                [wgln, "ogT"], [p2n])
                    ps3, p3n = self.ps_new()
                    for kc in range(8):
                        self.MM(ps3[:, :n], g1[:, kc, c4 * 128:(c4 + 1) * 128], xnT[:, kc, 256 + t0:256 + t0 + n],
                                kc == 0, kc == 7, xr(t0, n, 256) + [g1n], [p3n])
                    ps4, p4n = self.ps_new()
                    for kc in range(8):
                        self.MM(ps4[:, :n], g2[:, kc, c4 * 128:(c4 + 1) * 128], xnT[:, kc, 256 + t0:256 + t0 + n],
                                kc == 0, kc == 7, xr(t0, n, 256) + [g2n], [p4n])
                    t1, t1n = self.tF()
                    t2, t2n = self.tF()
                    self.ACT(t1[:, :n], ps3[:, :n], AF.Tanh, [p3n], [t1n], scale=0.5)
                    self.ACT(t2[:, :n], ps4[:, :n], AF.Tanh, [p4n], [t2n], scale=0.5)
                    self.STT(t1[:, :n], t1[:, :n], 1.0, ps1[:, :n], ALU.add, ALU.mult, [t1n, p1n], [t1n])
                    self.STT(t2[:, :n], t2[:, :n], 1.0, ps2[:, :n], ALU.add, ALU.mult, [t2n, p2n], [t2n])
                    self.TT(mixT[:, c, t0:t0 + n], t1[:, :n], t2[:, :n], ALU.add, [t1n, t2n], ["mixT"])
        self.dump("mixT", mixT, [128, 8, TR], BF16, "mixT")
        if getattr(self, "stop_after", None) == "U8a":
            self.R.dve(lambda e: e.memset(self.Sfx, 0.0), [], ["Sfx"])
            self.barrier()
            return
        hnT = A(8 * TR).rearrange("p (k n) -> p k n", k=8)
        wo = []
        for half in range(2):
            wo.append(self.wload(half, T["w_out"][:, half * 512:(half + 1) * 512], 512))
        prevB = None
        for i in range(10):
            curB = None
            if i < 9:
                xs, xn = self.xload(xsrc[(kv0 + 2 + i) * 128:(kv0 + 3 + i) * 128, :])
                for half in range(2):
                    ps, pn = self.ps_new()
                    for kc in range(8):
                        self.MM(ps, mixT[:, kc, i * 128:(i + 1) * 128], wo[half][0][:, kc, :], kc == 0, kc == 7,
                                ["mixT", wo[half][1]], [pn])
                    sl = slice(half * 512, (half + 1) * 512)
                    self.STT(xs[:, sl], ps, 0.5, xs[:, sl], ALU.mult, ALU.add, [pn, xn], [xn])
                self.STORE("sp", hscr[i * 128:(i + 1) * 128, :], xs, [xn], "hst_" + xn, w=[f"hscr{i}"])
                mc = None
                if i == 0:
                    mc = C_MASK + 2 * ut["mcol"]
                if i == 8:
                    mc = C_MASK + 2 * ut["mcol"] + 1
                curB = self.norm_A(xs, xn, C_WBC2, maskcol=mc) + (i,)
            if prevB is not None:
                self.norm_B(prevB[0], prevB[1], hnT[:, :, prevB[2] * 128:(prevB[2] + 1) * 128], f"hnT{prevB[2]}")
            prevB = curB
        self.dump("hnT", hnT, [128, 8, TR], BF16, "hnT8")
        if getattr(self, "stop_after", None) == "U8b":
            self.R.dve(lambda e: e.memset(self.Sfx, 0.0), [], ["Sfx"])
            self.barrier()
            return

        self.barrier()
        hn_names = [f"hnT{i}" for i in range(9)]
        self.top = self.base
        fT = A(22 * 1024).rearrange("p (j n) -> p j n", j=22)
        tmp = self.arena[:, self.tmp0:self.tmp0 + 13312]
        u = [self.slabI[:, 0:2304].bitcast(F32), self.slabI[:, 2304:4608].bitcast(F32)]
        cv = [tmp[:, 0:2048].bitcast(F32), tmp[:, 2048:4096].bitcast(F32)]
        gq = tmp[:, 4096:6144].bitcast(F32)
        wdT = tmp[:, 6144:13312].rearrange("p (j n) -> p j n", j=7)
        wdL = tmp[:, 0:6144].rearrange("p (j n) -> p j n", j=6)
        wdC = A(9 * 1024).rearrange("p (j n) -> p j n", j=9)
        assert self.top <= self.base + 8 * TK + 4 * TR + 4 * TR + 8 * TR, "fT/wdC would overlap hnT"
        wdr = T["w_down"].rearrange("(j p) n -> p j n", p=128)
        w_up = T["w_up"]
        for j in range(22):
            si_ = j % 4
            sl_, wn = self.wsl[si_], f"wsl{si_}"
            wv2 = sl_[:, 0:2048].rearrange("p (k n) -> p k n", k=8)
            self.LOAD("pool", wv2[:, :, 0:128], w_up[:, j * 128:(j + 1) * 128].rearrange("(k p) n -> p k n", p=128),
                      [wn], wn + "a")
            self.LOAD("pool", wv2[:, :, 128:256],
                      w_up[:, 2816 + j * 128:2816 + (j + 1) * 128].rearrange("(k p) n -> p k n", p=128), [wn], wn + "b")
            if j == 3:
                for g in range(0, 9, 3):
                    self.LOAD("pool", wdC[:, g:g + 3, :], wdr[:, g:g + 3, :], ["wdC"], f"wdC{g}")
                self.LOAD("pool", wdT[:, 0:4, :], wdr[:, 9:13, :], ["wdT"], "wdT0")
                self.LOAD("pool", wdT[:, 4:7, :], wdr[:, 13:16, :], ["wdT"], "wdT4")
            for ab in range(2):
                for (t0, n) in blocks(TR):
                    ps, pn = self.ps_new()
                    for kc in range(8):
                        self.MM(ps[:, :n], wv2[:, kc, ab * 128:(ab + 1) * 128], hnT[:, kc, t0:t0 + n], kc == 0, kc == 7,
                                [wn] + hn_names, [pn])
                    self.ACOPY(u[ab][:, t0:t0 + n], ps[:, :n], [pn], [f"u{ab}"])
                jj = ab * 22 + j
                cw = lambda tap, jj=jj: cf[:, C_CW + jj * 3 + tap:C_CW + jj * 3 + tap + 1]
                self.ACT(cv[ab], u[ab][:, 63:1087], AF.Identity, [f"u{ab}", "cf"], [f"cv{ab}"], scale=cw(0),
                         bias=cf[:, C_CB + jj:C_CB + jj + 1])
                self.STT(cv[ab], u[ab][:, 64:1088], cw(1), cv[ab], ALU.mult, ALU.add, [f"u{ab}", "cf", f"cv{ab}"],
                         [f"cv{ab}"])
                self.STT(cv[ab], u[ab][:, 65:1089], cw(2), cv[ab], ALU.mult, ALU.add, [f"u{ab}", "cf", f"cv{ab}"],
                         [f"cv{ab}"])
            self.ACT(gq, cv[0], AF.Square, ["cv0"], ["gq"])
            self.TS(gq, gq, 0.044715, 1.0, ALU.mult, ALU.add, ["gq"], ["gq"])
            self.TT(gq, gq, cv[0], ALU.mult, ["gq", "cv0"], ["gq"])
            self.ACT(gq, gq, AF.Tanh, ["gq"], ["gq"], scale=0.7978845608028654)
            self.STT(gq, gq, 1.0, cv[0], ALU.add, ALU.mult, ["gq", "cv0"], ["gq"])
            self.TT(fT[:, j, :], gq, cv[1], ALU.mult, ["gq", "cv1"], ["fT"])
        self.dump("fT", fT, [128, 22, 1024], BF16, "fT")
        if getattr(self, "stop_after", None) == "U10":
            self.R.dve(lambda e: e.memset(self.Sfx, 0.0), [], ["Sfx"])
            self.barrier()
            return

        self.barrier()
        self.LOAD("pool", wdL[:, 0:3, :], wdr[:, 16:19, :], ["wdL"], "wdL0")
        self.LOAD("pool", wdL[:, 3:6, :], wdr[:, 19:22, :], ["wdL"], "wdL3")

        def wdj(j):
            if j < 9:
                return wdC[:, j], "wdC"
            if j < 16:
                return wdT[:, j - 9], "wdT"
            return wdL[:, j - 16], "wdL"
        for i8 in range(8):
            hi_ = self.rot("xs", 2)
            hs, hn_ = self.xs[hi_], f"xs{hi_}"
            self.LOAD("sp", hs, hscr[64 + i8 * 128:64 + (i8 + 1) * 128, :], [hn_], hn_,
                      r=[f"hscr{i8}", f"hscr{i8 + 1}"])
            for half in range(2):
                ps, pn = self.ps_new()
                for j in range(22):
                    self.MM(ps, fT[:, j, i8 * 128:(i8 + 1) * 128], wdj(j)[0][:, half * 512:(half + 1) * 512], j == 0,
                            j == 21, ["fT", wdj(j)[1]], [pn])
                sl = slice(half * 512, (half + 1) * 512)
                self.STT(hs[:, sl], ps, 0.5, hs[:, sl], ALU.mult, ALU.add, [pn, hn_], [hn_])
            self.stores.append(self.STORE("sp", yout[i8 * 128:(i8 + 1) * 128, :], hs, [hn_], "yst_" + hn_))
        self.barrier()

    def sequence(self, xsrc, n_units, tau0, pre, tau_max, uts, yout, hscr):
        T = self.T
        w_in = T["w_in"]
        self.zero_state("f")
        self.zero_state("b")
        post = list(range(tau_max, tau0 + 8, -1))
        saves = {}
        inits = [None] * n_units
        for m in range(n_units):
            need = tau0 + 8 * m + 9
            if need <= tau_max:
                saves[need] = (self.Ssave[m], f"Ssave{m}")
                inits[m] = (self.Ssave[m], f"Ssave{m}")
        if pre or post:
            wk, wkn = self.wload(1, w_in[:, 2048:2560], 512)
            wv, wvn = self.wload(3, w_in[:, 2560:3072], 512)
            if pre:
                self.state_scan(xsrc, pre, "f", wk, wkn, wv, wvn)
            if post:
                self.state_scan(xsrc, post, "b", wk, wkn, wv, wvn, saves=saves)
        self.barrier()
        for m in range(n_units):
            self.unit(xsrc, tau0 + 8 * m - 2, uts[m], yout[m * 1024:(m + 1) * 1024, :], hscr, inits[m], f"u{m}")
            Sf, Sfx = self.S["f"], self.Sfx
            self.R.dve(lambda e, Sf=Sf, Sfx=Sfx: e.tensor_copy(out=Sf, in_=Sfx), ["Sfx"], ["Sf"])
            self.barrier()


def build_nc(dbg=None, only_prompt_units=None, xs_tiles=225, variant=None):
    nc = bass.Bass("TRN2", target_bir_lowering=False)
    dt = lambda name, shape, kind="ExternalInput": nc.dram_tensor(name, list(shape), F32, kind=kind).ap()
    T = {
        "xp": dt("xp", [2, 21 * 128, D]),
        "xs": dt("xs", [xs_tiles * 128, D]),
        "cf": dt("cf", [128, NCF]),
        "cb": dt("cb", [128, NCB]),
        "slabI": dt("slabI", [128, 5120]),
        "sT_p": dt("sT_p", [3, 128, 7168]),
        "sB_p": dt("sB_p", [2, 128, 7168]),
        "sT_s": dt("sT_s", [3, 128, 7168]),
        "sB_s": dt("sB_s", [2, 128, 7168]),
        "w_in": dt("w_in", [D, 5664]),
        "w_a2_f": dt("w_a2_f", [16, 512]),
        "b_a_f": dt("b_a_f", [1, 512]),
        "w_a2_b": dt("w_a2_b", [16, 512]),
        "b_a_b": dt("b_a_b", [1, 512]),
        "w_na_proj": dt("w_na_proj", [512, D]),
        "w_gla_proj": dt("w_gla_proj", [512, D]),
        "w_out": dt("w_out", [D, D]),
        "w_up": dt("w_up", [D, 5632]),
        "w_down": dt("w_down", [2816, D]),
    }
    yp = dt("yp", [2, 2048, D], "ExternalOutput")
    ys = dt("ys", [4096, D], "ExternalOutput")
    hscr = dt("hscr", [TR, D], "Internal")
    with ExitStack() as es:
        k = K(nc, es, dbg)
        k.setup(T)
        k.Sfx = k.alloc(512, F32)
        k.base = k.top
        sT_p = [T["sT_p"][e] for e in range(3)]
        sB_p = [T["sB_p"][e] for e in range(2)]
        sT_s = [T["sT_s"][e] for e in range(3)]
        sB_s = [T["sB_s"][e] for e in range(2)]
        ut_p = [dict(top=sT_p, bot=None, mcol=0), dict(top=None, bot=sB_p, mcol=1)]
        ut_s = [dict(top=sT_s, bot=None, mcol=2), dict(top=None, bot=None, mcol=3),
                dict(top=None, bot=None, mcol=4), dict(top=None, bot=sB_s, mcol=5)]
        if only_prompt_units is not None:
            k.dbg_unit = f"u{only_prompt_units - 1}"
            k.sequence(T["xp"][0], only_prompt_units, 2, [], 18, ut_p, yp[0], hscr)
        elif variant == "prompts":
            for b in range(2):
                k.sequence(T["xp"][b], 2, 2, [], 18, ut_p, yp[b], hscr)
        elif variant == "scans":
            k.sequence(T["xs"], 0, 96, list(range(0, 96)), 224, ut_s, ys, hscr)
        else:
            for b in range(2):
                k.sequence(T["xp"][b], 2, 2, [], 18, ut_p, yp[b], hscr)
            k.sequence(T["xs"], 4, 96, list(range(0, 96)), 224, ut_s, ys, hscr)
        n, cnt = k.R.emit(nc, k.stores)
        k.nops = n
    return nc, k


def make_slab(rpb, i, kvs, lo, hi):
    H = rpb.shape[0]
    out = np.full((128, 7, H, 128), NEG, np.float32)
    kc = np.arange(64)
    qc = np.arange(64)
    c0 = np.clip(qc - 8, 0, 48)
    colv = (kc[:, None] >= c0[None, :]) & (kc[:, None] < c0[None, :] + 16)
    dcv = np.clip(kc[:, None] - qc[None, :] + 15, 0, 30)
    for a, t in enumerate(kvs):
        for rr in range(2):
            rk = 2 * t - 5 + rr
            for qq in range(2):
                rq = 2 * i - 1 + qq
                if lo <= rq < hi:
                    r0 = min(max(rq - 4, lo), hi - 8)
                else:
                    r0 = rq - 4
                if not (r0 <= rk < r0 + 8):
                    continue
                dr = rk - rq + 7
                blk = np.where(colv[None], rpb[:, dr][:, dcv], NEG)
                out[rr * 64:(rr + 1) * 64, a, :, qq * 64:(qq + 1) * 64] = blk.transpose(1, 0, 2)
    return out


def host_consts(norm1_w, norm2_w, gla_norm_w, qn_w, kn_w, conv_w, conv_b, masks):
    cf = np.zeros((128, NCF), np.float32)
    cf[:, C_WBC1:C_WBC1 + D] = norm1_w[None, :]
    cf[:, C_WBC2:C_WBC2 + D] = norm2_w[None, :]
    cf[:, C_GNW:C_GNW + 512] = np.tile(gla_norm_w, 4)[None, :]
    idx = np.arange(128)
    cf[:, C_UINC:C_UINC + 128] = (idx[:, None] <= idx[None, :]) * (-1.0 / 16)
    cf[:, C_ULT:C_ULT + 128] = (idx[:, None] < idx[None, :]) * (-1.0 / 16)
    cf[:, C_UGT:C_UGT + 128] = (idx[:, None] > idx[None, :]) * (-1.0 / 16)
    cw = conv_w.reshape(3, 44, 128)
    cf[:, C_CW:C_CW + 132] = cw.transpose(2, 1, 0).reshape(128, 132)
    cf[:, C_CB:C_CB + 44] = conv_b.reshape(44, 128).T
    cf[:, C_QNW] = np.tile(qn_w, 2)
    cf[:, C_KNW] = np.tile(kn_w, 2)
    cf[:, C_LNQ] = np.float32(np.log(0.125))
    cf[:, C_NEG] = -1.0 / 16
    cf[:, C_EPS] = EPS
    for m, (top_real, bot_real) in enumerate(masks):
        cf[:, C_MASK + 2 * m] = 1.0
        cf[0:64, C_MASK + 2 * m] = 1.0 if top_real else 0.0
        cf[:, C_MASK + 2 * m + 1] = 1.0
        cf[64:128, C_MASK + 2 * m + 1] = 1.0 if bot_real else 0.0
    cb = np.zeros((128, NCB), np.float32)
    cb[:, B_ID:B_ID + 128] = np.eye(128)
    cb[0:64, B_OB:B_OB + 64] = 1.0
    cb[64:128, B_OB + 64:B_OB + 128] = 1.0
    cb[:, B_ONE:B_ONE + 128] = 1.0
    cb[:, B_MF:B_MF + 128] = (idx[:, None] <= idx[None, :])
    cb[:, B_MB:B_MB + 128] = (idx[:, None] > idx[None, :])
    return cf, cb


def make_in_maps(inp):
    f = lambda k: np.asarray(inp[k], np.float32)
    x_prompt, x_sample = f("x_prompt"), f("x_sample")
    rpb = f("rpb")[0]
    w_in = np.ascontiguousarray(f("w_in")[0])
    shared = {
        "w_in": w_in,
        "w_a2_f": np.ascontiguousarray(f("w_a2_f")[0]), "b_a_f": np.ascontiguousarray(f("b_a_f")[0][None, :]),
        "w_a2_b": np.ascontiguousarray(f("w_a2_b")[0]), "b_a_b": np.ascontiguousarray(f("b_a_b")[0][None, :]),
        "w_na_proj": np.ascontiguousarray(f("w_na_proj")[0]), "w_gla_proj": np.ascontiguousarray(f("w_gla_proj")[0]),
        "w_out": np.ascontiguousarray(f("w_out")[0]), "w_up": np.ascontiguousarray(f("w_up")[0]),
        "w_down": np.ascontiguousarray(f("w_down")[0]),
    }
    flat = lambda s: np.ascontiguousarray(s.transpose(0, 2, 1, 3).reshape(128, 7168))
    slabI = make_slab(rpb, 3, kv_list(3, False), -100, 100)[:, 0:5]
    slabI = np.ascontiguousarray(slabI[:, ::-1].transpose(0, 2, 1, 3).reshape(128, 5120))
    sT_cl = np.stack([flat(make_slab(rpb, i, kv_list(i, True), 0, 32)) for i in EDGE_TOP])
    sT_un = np.stack([flat(make_slab(rpb, i, kv_list(i, True), -100, 100)) for i in EDGE_TOP])
    sB_cl = np.stack([flat(make_slab(rpb, i, kv_list(i, True), -16, 16)) for i in EDGE_BOT])
    sB_un = np.stack([flat(make_slab(rpb, i, kv_list(i, True), -100, 100)) for i in EDGE_BOT])
    shared.update({"slabI": slabI, "sT_p": sT_cl, "sB_p": sB_cl})
    maps = []
    for c in range(NCORES):
        s, j = c // 4, c % 4
        xp = np.zeros((2, 42, 64, D), np.float32)
        for b in range(2):
            xp[b, 5:37] = x_prompt[2 * c + b].reshape(32, 64, D)
        R0 = 64 * j
        xs = np.zeros((450, 64, D), np.float32)
        g0 = R0 - 193
        lo_, hi_ = max(0, g0), min(256, g0 + 450)
        xs[lo_ - g0:hi_ - g0] = x_sample[s].reshape(256, 64, D)[lo_:hi_]
        masks = [(False, True), (True, False)]
        for m in range(4):
            masks.append((R0 + 16 * m - 1 >= 0, R0 + 16 * m + 16 < 256))
        cf, cb = host_consts(f("norm1_w")[0], f("norm2_w")[0], f("gla_norm_w")[0], f("qn_w")[0], f("kn_w")[0],
                             f("conv_w")[0], f("conv_b")[0], masks)
        d = dict(shared)
        d.update({"xp": xp.reshape(2, 21 * 128, D), "xs": xs.reshape(225 * 128, D), "cf": cf, "cb": cb,
                  "sT_s": sT_cl if j == 0 else sT_un, "sB_s": sB_cl if j == 3 else sB_un})
        maps.append(d)
    return maps


def kernel(**inp):
    maps = make_in_maps(inp)
    nc, k = build_nc()
    res = run_bass_kernel_spmd(nc, maps, core_ids=list(range(NCORES)))
    y_prompt = np.empty((16, 2048, D), np.float32)
    y_sample = np.empty((2, 16384, D), np.float32)
    for c in range(NCORES):
        s, j = c // 4, c % 4
        r = res.results[c]
        y_prompt[2 * c:2 * c + 2] = np.asarray(r["yp"], np.float32)
        y_sample[s, 4096 * j:4096 * (j + 1)] = np.asarray(r["ys"], np.float32)
    return (y_prompt, y_sample)
```

```python
import numpy as np
import concourse.bass as bass
import concourse.mybir as mybir
from concourse.bass_utils import run_bass_kernel_spmd
from contextlib import ExitStack

F32 = mybir.dt.float32
BF16 = mybir.dt.bfloat16
AF = mybir.ActivationFunctionType
ALU = mybir.AluOpType
EPS = 1e-6
NCORES = 8
D = 1024
TR = 1152
TK = 1664
NEG = -30000.0
EDGE_TOP = (0, 1, 2)
EDGE_BOT = (7, 8)
C_WBC1, C_WBC2, C_GNW, C_UINC, C_ULT, C_UGT, C_CW, C_CB = 0, 1024, 2048, 2560, 2688, 2816, 2944, 3076
C_QNW, C_KNW, C_LNQ, C_NEG, C_MASK, C_EPS = 3120, 3121, 3122, 3123, 3124, 3136
NCF = 3138
B_ID, B_OB, B_ONE, B_MF, B_MB, B_ZERO = 0, 128, 256, 384, 512, 640
NCB = 768


class Buf:
    __slots__ = ("w", "rs")

    def __init__(self):
        self.w = None
        self.rs = []


class Rec:
    ENGS = ("pe", "act", "dve", "pool", "sp")

    def __init__(self):
        self.ops = []
        self.bufs = {}
        self.bar = None
        self.last = {}
        self.dmas = []

    def B(self, name):
        b = self.bufs.get(name)
        if b is None:
            b = self.bufs[name] = Buf()
        return b

    def add(self, eng, fn, r=(), w=(), dma=False, stream=None):
        deps = set()
        for n in r:
            b = self.B(n)
            if b.w is not None:
                deps.add(b.w)
        for n in w:
            b = self.B(n)
            if b.w is not None:
                deps.add(b.w)
            deps.update(b.rs)
        if self.bar is not None:
            deps.add(self.bar)
        i = len(self.ops)
        self.ops.append(dict(eng=eng, fn=fn, deps=deps, dma=dma, stream=stream, sig=False, ord=0))
        for n in r:
            self.B(n).rs.append(i)
        for n in w:
            b = self.B(n)
            b.w = i
            b.rs = []
        if dma:
            self.dmas.append(i)
        else:
            self.last[eng] = i
        return i

    def barrier(self, fn):
        deps = set(self.last.values()) | set(self.dmas)
        if self.bar is not None:
            deps.add(self.bar)
        i = len(self.ops)
        self.ops.append(dict(eng="dve", fn=fn, deps=deps, dma=False, stream=None, sig=False, ord=0))
        self.bar = i
        self.last["dve"] = i
        self.dmas = []
        return i

    def pe(self, fn, r=(), w=()):
        return self.add("pe", fn, r, w)

    def act(self, fn, r=(), w=()):
        return self.add("act", fn, r, w)

    def dve(self, fn, r=(), w=()):
        return self.add("dve", fn, r, w)

    def pool(self, fn, r=(), w=()):
        return self.add("pool", fn, r, w)

    def dma(self, q, fn, r=(), w=(), stream=None):
        return self.add(q, fn, r, w, dma=True, stream=stream)

    def emit(self, nc, final_wait_ops=()):
        ops = self.ops
        ops.append(dict(eng="sp", fn=None, deps=set(final_wait_ops), dma=False, stream=None, sig=False, ord=0))
        for o in ops:
            for d in o["deps"]:
                od = ops[d]
                if od["dma"]:
                    continue
                if od["eng"] == "pe" and o["eng"] == "pe" and not o["dma"]:
                    continue
                od["sig"] = True
        cnt = {}
        for o in ops:
            if o["dma"]:
                k = ("s", o["stream"])
                cnt[k] = cnt.get(k, 0) + 1
                o["ord"] = cnt[k]
            elif o["sig"]:
                k = ("e", o["eng"])
                cnt[k] = cnt.get(k, 0) + 1
                o["ord"] = cnt[k]
        with ExitStack() as es:
            sem = {}
            for k in cnt:
                sem[k] = es.enter_context(nc.semaphore("sem_%s_%s" % k))
            block = es.enter_context(nc.Block())
            by_eng = {e: [o for o in ops if o["eng"] == e] for e in self.ENGS}

            def run(engh, ename):
                waited = {}
                for o in by_eng[ename]:
                    need = {}
                    for d in o["deps"]:
                        od = ops[d]
                        if od["dma"]:
                            k = ("s", od["stream"])
                            v = 16 * od["ord"]
                        else:
                            if od["eng"] == "pe" and ename == "pe" and not o["dma"]:
                                continue
                            k = ("e", od["eng"])
                            v = od["ord"]
                        if v > need.get(k, 0):
                            need[k] = v
                    for k, v in need.items():
                        if waited.get(k, 0) >= v:
                            continue
                        waited[k] = v
                        engh.wait_ge(sem[k], v)
                    if o["fn"] is None:
                        continue
                    ins = o["fn"](engh)
                    if o["dma"]:
                        ins.then_inc(sem[("s", o["stream"])], 16)
                    elif o["sig"]:
                        ins.then_inc(sem[("e", ename)], 1)

            if by_eng["sp"]:
                block.sync(lambda e: run(e, "sp"))
            if by_eng["pool"]:
                block.gpsimd(lambda e: run(e, "pool"))
            if by_eng["act"]:
                block.scalar(lambda e: run(e, "act"))
            if by_eng["dve"]:
                block.vector(lambda e: run(e, "dve"))
            if by_eng["pe"]:
                block.tensor(lambda e: run(e, "pe"))
        return len(ops), cnt


def blocks(T):
    out = []
    t = 0
    while t < T:
        n = min(512, T - t)
        out.append((t, n))
        t += n
    return out


def kv_list(i, edge):
    if not edge:
        return list(range(i, i + 5))
    return {0: list(range(0, 7)), 1: list(range(1, 7)), 2: list(range(2, 7)),
            7: list(range(6, 12)), 8: list(range(6, 13))}[i]


class K:
    def __init__(self, nc, es, dbg=None):
        self.nc = nc
        self.es = es
        self.R = Rec()
        self.dbg = dbg
        self.dumps = []
        self.stores = []
        self.cnt = {}
        self.arena = es.enter_context(nc.sbuf_tensor("arena", [128, 106400], BF16))
        self.top = 0
        self.ps = [es.enter_context(nc.psum_tensor(f"ps{i}", [128, 512], F32)) for i in range(6)]
        self.pb = [es.enter_context(nc.psum_tensor(f"pb{i}", [128, 1024], BF16)) for i in range(2)]

    def alloc(self, n, dt=BF16):
        if dt == F32:
            n2 = 2 * n
        else:
            n2 = n
        self.top = (self.top + 1) // 2 * 2
        a = self.arena[:, self.top:self.top + n2]
        self.top += n2
        assert self.top <= 106400, self.top
        return a.bitcast(F32) if dt == F32 else a

    def rot(self, key, n):
        c = self.cnt.get(key, 0)
        self.cnt[key] = c + 1
        return c % n

    def ps_new(self, bank=None):
        i = self.rot("ps", 6) if bank is None else bank
        return self.ps[i][:], f"ps{i}"

    def pt_new(self):
        i = self.rot("pt", 2)
        return self.pb[i][:], f"pb{i}"

    def MM(self, out, lhsT, rhs, start, stop, r, w):
        self.R.pe(lambda e: e.matmul(out, lhsT=lhsT, rhs=rhs, start=start, stop=stop), r, w)

    def TR_(self, out, in_, r, w):
        ident = self.ident
        self.R.pe(lambda e: e.transpose(out=out, in_=in_, identity=ident), list(r) + ["cb"], w)

    def ACT(self, out, in_, func, r, w, scale=None, bias=None, accum=None):
        kw = {}
        if scale is not None:
            kw["scale"] = scale
        if bias is not None:
            kw["bias"] = bias
        if accum is not None:
            kw["accum_out"] = accum
        self.R.act(lambda e: e.activation(out=out, in_=in_, func=func, **kw), r, w)

    def ACOPY(self, out, in_, r, w):
        self.R.act(lambda e: e.copy(out=out, in_=in_), r, w)

    def AMUL(self, out, in_, c, r, w):
        self.R.act(lambda e: e.mul(out=out, in_=in_, mul=c), r, w)

    def DCOPY(self, out, in_, r, w):
        self.R.dve(lambda e: e.tensor_copy(out=out, in_=in_), r, w)

    def TS(self, out, in0, s1, s2, op0, op1, r, w):
        if op1 is None:
            self.R.dve(lambda e: e.tensor_scalar(out=out, in0=in0, scalar1=s1, scalar2=None, op0=op0), r, w)
        else:
            self.R.dve(lambda e: e.tensor_scalar(out=out, in0=in0, scalar1=s1, scalar2=s2, op0=op0, op1=op1), r, w)

    def TT(self, out, in0, in1, op, r, w):
        self.R.dve(lambda e: e.tensor_tensor(out=out, in0=in0, in1=in1, op=op), r, w)

    def STT(self, out, in0, scalar, in1, op0, op1, r, w):
        self.R.dve(lambda e: e.scalar_tensor_tensor(out=out, in0=in0, scalar=scalar, in1=in1, op0=op0, op1=op1), r, w)

    def LOAD(self, q, out, in_, w, stream, r=()):
        return self.R.dma(q, lambda e: e.dma_start(out=out, in_=in_), r=r, w=w, stream=stream)

    def STORE(self, q, out, in_, r, stream, w=()):
        i = self.R.dma(q, lambda e: e.dma_start(out=out, in_=in_), r=r, w=w, stream=stream)
        return i

    def dump(self, name, ap, shape, dt, rname):
        if self.dbg is None or name not in self.dbg or getattr(self, "cur_unit", None) != getattr(self, "dbg_unit", None):
            return
        d = self.nc.dram_tensor("dbg_" + name, list(shape), dt, kind="ExternalOutput").ap()
        self.stores.append(self.STORE("sp", d, ap, [rname], "dbg_" + name))
        self.dumps.append(name)

    def barrier(self):
        bt = self.bartile
        self.R.barrier(lambda e: e.memset(bt, 0.0))

    def fence(self, reads, writes):
        bt = self.bartile
        self.R.dve(lambda e: e.memset(bt, 0.0), list(reads), list(writes) + ["bartile"])

    def setup(self, T):
        self.T = T
        A = self.alloc
        self.cf = A(NCF, F32)
        self.cb = A(NCB)
        self.ident = self.cb[:, B_ID:B_ID + 128]
        self.onesblk = self.cb[:, B_OB:B_OB + 128]
        self.ones = self.cb[:, B_ONE:B_ONE + 128]
        self.Mf = self.cb[:, B_MF:B_MF + 128]
        self.Mb = self.cb[:, B_MB:B_MB + 128]
        self.wa2 = [A(512), A(512)]
        self.ba = [A(512), A(512)]
        self.wlr = A(256).rearrange("p (k n) -> p k n", k=8)
        self.slabI = A(5120)
        self.S = {"f": A(512, F32), "b": A(512, F32)}
        self.Sbf = A(512)
        self.Ssave = [A(512, F32) for _ in range(4)]
        self.xs = [A(1024, F32) for _ in range(2)]
        self.jk = [A(1024) for _ in range(2)]
        self.xb = [A(1024) for _ in range(2)]
        self.xTt = [A(1024).rearrange("p (k n) -> p k n", k=8) for _ in range(2)]
        self.st = [A(8, F32) for _ in range(8)]
        self.wsl = [A(4096) for _ in range(4)]
        self.PT = [A(896) for _ in range(2)]
        self.ona = [A(512) for _ in range(2)]
        self.bartile = A(2, F32)
        self.tmp0 = self.top = (self.top + 1) // 2 * 2
        self.tf = [A(512, F32) for _ in range(8)]
        self.tb = [A(512) for _ in range(10)]
        assert self.top - self.tmp0 == 13312
        self.base = self.top
        L = self.LOAD
        L("sp", self.cf, T["cf"][:, :], ["cf"], "cf")
        L("pool", self.cb, T["cb"][:, :], ["cb"], "cb")
        for d, nm in enumerate(("f", "b")):
            L("pool", self.wa2[d][0:16, :], T["w_a2_" + nm][:, :], ["wa2"], "wa2" + nm)
            L("pool", self.ba[d][0:1, :], T["b_a_" + nm][:, :], ["ba"], "ba" + nm)
        L("pool", self.wlr, T["w_in"][:, 3584:3616].rearrange("(k p) n -> p k n", p=128), ["wlr"], "wlr")

    def tF(self):
        i = self.rot("tf", 8)
        return self.tf[i], f"tf{i}"

    def tB(self):
        i = self.rot("tb", 10)
        return self.tb[i], f"tb{i}"

    def stt_(self):
        i = self.rot("st", 8)
        return self.st[i], f"st{i}"

    def wload(self, si, cols_ap, ncols, kchunks=8):
        sl, nm = self.wsl[si], f"wsl{si}"
        v = sl[:, 0:kchunks * ncols].rearrange("p (k n) -> p k n", k=kchunks)
        if getattr(self, "nowload", False) and self.cnt.get("wl%d" % si, 0) > 0:
            return v, nm
        self.cnt["wl%d" % si] = 1
        self.LOAD("pool", v, cols_ap.rearrange("(k p) n -> p k n", p=128), [nm], nm)
        return v, nm

    def xload(self, src_rows):
        i = self.rot("xs", 2)
        self.LOAD("sp", self.xs[i], src_rows, [f"xs{i}"], f"xs{i}")
        return self.xs[i], f"xs{i}"

    def norm_A(self, src, sname, wbc_off, maskcol=None):
        j = self.rot("jk", 2)
        jk, jn = self.jk[j], f"jk{j}"
        xb, xn = self.xb[j], f"xb{j}"
        st, sn = self.stt_()
        cf = self.cf
        self.ACT(jk, src, AF.Square, [sname], [jn, sn], accum=st[:, 0:1])
        self.ACT(st[:, 2:3], st[:, 0:1], AF.Ln, [sn, "cf"], [sn], scale=1.0 / D, bias=cf[:, C_EPS:C_EPS + 1])
        self.ACT(st[:, 3:4], st[:, 2:3], AF.Exp, [sn], [sn], scale=-0.5)
        if maskcol is not None:
            self.TT(st[:, 3:4], st[:, 3:4], cf[:, maskcol:maskcol + 1], ALU.mult, [sn, "cf"], [sn])
        self.STT(xb, src, st[:, 3:4], cf[:, wbc_off:wbc_off + D], ALU.mult, ALU.mult, [sname, sn, "cf"], [xn])
        return xb, xn

    def norm_B(self, xb, xn, dst3, dname):
        pt, pn = self.pt_new()
        for kc in range(8):
            self.TR_(pt[:, kc * 128:(kc + 1) * 128], xb[:, kc * 128:(kc + 1) * 128], [xn], [pn])
        src3 = pt.rearrange("p (a b) -> p a b", a=8)
        self.DCOPY(dst3[:, 0:4, :], src3[:, 0:4, :], [pn], [dname])
        self.ACOPY(dst3[:, 4:8, :], src3[:, 4:8, :], [pn], [dname])

    def norm_T(self, src, sname, wbc_off, dst3, dname, maskcol=None):
        xb, xn = self.norm_A(src, sname, wbc_off, maskcol)
        self.norm_B(xb, xn, dst3, dname)

    def gates(self, lrT, lrn, d, bank=None):
        ps, pn = self.ps_new(bank)
        self.MM(ps, lrT, self.wa2[d][0:16, :], True, False, [lrn, "wa2"], [pn])
        self.MM(ps, self.ones[0:1, :], self.ba[d][0:1, :], False, True, ["cb", "ba"], [pn])
        gp, gn = self.tF()
        self.ACT(gp, ps, AF.Exp, [pn], [gn], scale=-1.0)
        self.ACT(gp, gp, AF.Ln, [gn, "cb"], [gn], bias=self.ones[:, 0:1])
        return gp, gn

    def tok_kv(self, xT3, xname, wk, wkn, wv, wvn):
        psk, pkn = self.ps_new()
        for kc in range(8):
            self.MM(psk, xT3[:, kc, :], wk[:, kc, :], kc == 0, kc == 7, [xname, wkn], [pkn])
        psv, pvn = self.ps_new()
        for kc in range(8):
            self.MM(psv, xT3[:, kc, :], wv[:, kc, :], kc == 0, kc == 7, [xname, wvn], [pvn])
        vt, vn = self.tB()
        self.ACOPY(vt, psv, [pvn], [vn])
        return psk, pkn, vt, vn

    def state_prep(self, gp, gn, psk, pkn, d, banks=(None, None)):
        cf = self.cf
        U = cf[:, C_UGT:C_UGT + 128] if d == "f" else cf[:, C_ULT:C_ULT + 128]
        psc, pcn = self.ps_new(banks[0])
        self.MM(psc, U, gp, True, True, ["cf", gn], [pcn])
        Ec, en = self.tF()
        self.ACT(Ec, psc, AF.Exp, [pcn], [en])
        kt, kn = self.tB()
        self.TT(kt, psk, Ec, ALU.mult, [pkn, en], [kn])
        pse, pen = self.ps_new(banks[1])
        for hh in range(4):
            self.MM(pse[:, hh:hh + 1], gp[:, hh * 128:(hh + 1) * 128], cf[:, C_NEG:C_NEG + 1], True, True,
                    [gn, "cf"], [pen])
        st, sn = self.stt_()
        self.ACT(st[:, 0:4], pse[:, 0:4], AF.Exp, [pen], [sn])
        return kt, kn, st, sn

    def state_apply(self, kt, kn, st, sn, vt, vn, d, snap=None, snapn=None, bank=None):
        S, Sn = self.S[d], "S" + d
        psd, pdn = self.ps_new(bank)
        for hh in range(4):
            sl = slice(hh * 128, (hh + 1) * 128)
            self.MM(psd[:, sl], kt[:, sl], vt[:, sl], True, True, [kn, vn], [pdn])
        S3 = S.rearrange("p (h v) -> p h v", h=4)
        self.TT(S3, S3, st[:, 0:4].rearrange("p (h o) -> p h o", o=1).to_broadcast([128, 4, 128]), ALU.mult,
                [Sn, sn], [Sn])
        if snap is not None:
            self.ACOPY(snap, S, [Sn], [snapn])
        self.TT(S, S, psd, ALU.add, [Sn, pdn], [Sn])

    def state_update(self, gp, gn, psk, pkn, vt, vn, d, snap=None, snapn=None):
        kt, kn, st, sn = self.state_prep(gp, gn, psk, pkn, d)
        self.state_apply(kt, kn, st, sn, vt, vn, d, snap, snapn)

    def zero_state(self, d):
        S = self.S[d]
        self.R.dve(lambda e: e.memset(S, 0.0), [], ["S" + d])

    def scan_pipe(self, d, n, p1, wk, wkn, wv, wvn, lr_of=None, snap_of=None, save_after=None, p1a=None):
        di = 0 if d == "f" else 1
        s1, s2, s3 = {}, {}, {}
        ksb_pool = [self.slabI[:, j * 1024:(j + 1) * 1024].bitcast(F32) for j in range(3)]

        def P2(i):
            xT3, xn = s1.pop(i)
            psk, pkn = self.ps_new(0)
            for kc in range(8):
                self.MM(psk, xT3[:, kc, :], wk[:, kc, :], kc == 0, kc == 7, [xn, wkn], [pkn])
            j = self.rot("ksb", 3)
            ksb, ksn = ksb_pool[j], f"ksb{j}"
            self.ACOPY(ksb, psk, [pkn], [ksn])
            psv, pvn = self.ps_new(1)
            for kc in range(8):
                self.MM(psv, xT3[:, kc, :], wv[:, kc, :], kc == 0, kc == 7, [xn, wvn], [pvn])
            vt, vn = self.tB()
            self.DCOPY(vt, psv, [pvn], [vn])
            if lr_of is None:
                psl, pln = self.ps_new(2)
                for kc in range(8):
                    self.MM(psl[0:16, 0:128], self.wlr[:, kc, 16 * di:16 * di + 16], xT3[:, kc, :], kc == 0, kc == 7,
                            ["wlr", xn], [pln])
                lrt, lrn = self.tB()
                self.DCOPY(lrt[0:16, 0:128], psl[0:16, 0:128], [pln], [lrn])
                lr = (lrt[0:16, 0:128], lrn)
            else:
                lr = lr_of(i)
            s2[i] = (ksb, ksn, vt, vn, lr)

        def P3a(i):
            ksb, ksn, vt, vn, lr = s2.pop(i)
            gp, gn = self.gates(lr[0], lr[1], di, bank=3)
            s3[i] = (ksb, ksn, vt, vn, gp, gn)

        def P3b(i):
            ksb, ksn, vt, vn, gp, gn = s3.pop(i)
            kt, kn, st, sn = self.state_prep(gp, gn, ksb, ksn, d, banks=(4, 5))
            sp = snap_of(i) if snap_of else None
            self.state_apply(kt, kn, st, sn, vt, vn, d, sp[0] if sp else None, sp[1] if sp else None, bank=5)
            if save_after and i in save_after:
                dst, dn = save_after[i]
                S = self.S[d]
                self.R.dve(lambda e, dst=dst, S=S: e.tensor_copy(out=dst, in_=S), ["S" + d], [dn])

        s0 = {}
        for it in range(n + 4):
            if 0 <= it - 4 < n:
                P3b(it - 4)
            if 0 <= it - 3 < n:
                P3a(it - 3)
            if 0 <= it - 2 < n:
                P2(it - 2)
            if 0 <= it - 1 < n:
                s1[it - 1] = p1(it - 1, s0.pop(it - 1, None))
            if it < n and p1a is not None:
                s0[it] = p1a(it)

    def state_scan(self, xsrc, taus, d, wk, wkn, wv, wvn, saves=None):
        def p1a(i):
            tau = taus[i]
            xs, xn = self.xload(xsrc[tau * 128:(tau + 1) * 128, :])
            return self.norm_A(xs, xn, C_WBC1)

        def p1(i, tok):
            j = self.rot("xTt", 2)
            xT3, xTn = self.xTt[j], f"xTt{j}"
            self.norm_B(tok[0], tok[1], xT3, xTn)
            return xT3, xTn
        sa = None
        if saves:
            sa = {i: saves[t] for i, t in enumerate(taus) if t in saves}
        self.scan_pipe(d, len(taus), p1, wk, wkn, wv, wvn, save_after=sa, p1a=p1a)

    def unit(self, xsrc, kv0, ut, yout, hscr, Sb_init, uname):
        R, T, cf = self.R, self.T, self.cf
        A = self.alloc
        self.top = self.base
        self.cur_unit = uname
        w_in = T["w_in"]
        xnT = A(8 * TK).rearrange("p (k n) -> p k n", k=8)
        self.LOAD("pool", self.slabI, T["slabI"][:, :], ["slabI"] + [f"ksb{j}" for j in range(3)], "slabI")
        prev = None
        for t in range(14):
            cur = None
            if t < 13:
                xs, xn = self.xload(xsrc[(kv0 + t) * 128:(kv0 + t + 1) * 128, :])
                cur = self.norm_A(xs, xn, C_WBC1)
            if prev is not None:
                self.norm_B(prev[0], prev[1], xnT[:, :, (t - 1) * 128:t * 128], f"xnT{t - 1}")
            prev = cur
        self.dump("xnT", xnT, [128, 8, TK], BF16, "xnT12")
        if getattr(self, "stop_after", None) == "U1":
            self.R.dve(lambda e: e.memset(self.Sfx, 0.0), [], ["Sfx"])
            self.barrier()
            return

        def xr(t0, n, off):
            a = (off + t0) // 128
            b = (off + t0 + n - 1) // 128
            return [f"xnT{t}" for t in range(a, b + 1)]

        m1 = self.top
        onaT = A(4 * TR).rearrange("p (k n) -> p k n", k=4)
        KT = A(4 * TK).rearrange("p (k n) -> p k n", k=4)
        QT = A(4 * TR).rearrange("p (k n) -> p k n", k=4)
        V = A(13 * 8 * 65)
        V4 = V.rearrange("p (t h d) -> p t h d", t=13, h=8)

        hn_pending = []

        def hn_finish():
            while hn_pending:
                (ps, pn, n, nwc, biasc, dst, dname, sq, sqn) = hn_pending.pop(0)
                ps2, p2n = self.ps_new()
                self.MM(ps2[:, :n], self.onesblk, sq[:, :n], True, True, ["cb", sqn], [p2n])
                r1, r1n = self.tF()
                self.ACT(r1[:, :n], ps2[:, :n], AF.Ln, [p2n, "cf"], [r1n], scale=1.0 / 64, bias=cf[:, C_EPS:C_EPS + 1])
                if biasc is None:
                    self.ACT(r1[:, :n], r1[:, :n], AF.Exp, [r1n], [r1n], scale=-0.5)
                else:
                    self.ACT(r1[:, :n], r1[:, :n], AF.Exp, [r1n, "cf"], [r1n], scale=-0.5, bias=cf[:, biasc:biasc + 1])
                self.STT(dst, ps[:, :n], cf[:, nwc:nwc + 1], r1[:, :n], ALU.mult, ALU.mult, [pn, r1n, "cf"], [dname])

        def headnorm(ps, pn, n, nwc, biasc, dst, dname):
            sq, sqn = self.tB()
            self.ACT(sq[:, :n], ps[:, :n], AF.Square, [pn], [sqn])
            hn_pending.append((ps, pn, n, nwc, biasc, dst, dname, sq, sqn))

        w, wn = self.wload(0, w_in[:, 512:1024], 512)
        for p in range(4):
            for (t0, n) in blocks(TK):
                ps, pn = self.ps_new()
                for kc in range(8):
                    self.MM(ps[:, :n], w[:, kc, p * 128:(p + 1) * 128], xnT[:, kc, t0:t0 + n], kc == 0, kc == 7,
                            xr(t0, n, 0) + [wn], [pn])
                hn_finish()
                headnorm(ps, pn, n, C_KNW, None, KT[:, p, t0:t0 + n], "KT")
        w, wn = self.wload(2, w_in[:, 0:512], 512)
        for p in range(4):
            for (t0, n) in blocks(TR):
                ps, pn = self.ps_new()
                for kc in range(8):
                    self.MM(ps[:, :n], w[:, kc, p * 128:(p + 1) * 128], xnT[:, kc, 256 + t0:256 + t0 + n], kc == 0,
                            kc == 7, xr(t0, n, 256) + [wn], [pn])
                hn_finish()
                headnorm(ps, pn, n, C_QNW, C_LNQ, QT[:, p, t0:t0 + n], "QT")
        w, wn = self.wload(0, w_in[:, 1024:1536], 512)
        R.pool(lambda e: e.memset(V, 1.0), [], ["V"])
        hn_first_v = True
        for t in range(13):
            ps, pn = self.ps_new()
            for kc in range(8):
                self.MM(ps, xnT[:, kc, t * 128:(t + 1) * 128], w[:, kc, :], kc == 0, kc == 7, [f"xnT{t}", wn], [pn])
            if hn_first_v:
                hn_finish()
                hn_first_v = False
            self.ACOPY(V4[:, t, :, 0:64], ps.rearrange("p (h d) -> p h d", h=8), [pn], ["V"])
        self.dump("KT", KT, [128, 4, TK], BF16, "KT")
        self.dump("QT", QT, [128, 4, TR], BF16, "QT")
        self.dump("V", V, [128, 13 * 8 * 65], BF16, "V")
        if getattr(self, "stop_after", None) == "U2":
            self.R.dve(lambda e: e.memset(self.Sfx, 0.0), [], ["Sfx"])
            self.barrier()
            return

        onaAll = A(9 * 512).rearrange("p (i n) -> p i n", i=9)
        edge_qps = []
        if ut["top"] is not None:
            edge_qps += [(i, ut["top"][EDGE_TOP.index(i)]) for i in EDGE_TOP]
        if ut["bot"] is not None:
            edge_qps += [(i, ut["bot"][EDGE_BOT.index(i)]) for i in EDGE_BOT]
        eidx = {i: n_ for n_, (i, _) in enumerate(edge_qps)}
        kvl = {i: kv_list(i, i in eidx) for i in range(9)}
        for h in range(8):
            p, bp = h // 2, 64 * (h % 2)
            es_i = 0 if h % 2 == 0 else 2
            es, esn = self.wsl[es_i], f"wsl{es_i}"
            for n_, (i, src) in enumerate(edge_qps):
                self.LOAD("pool", es[:, n_ * 896:(n_ + 1) * 896], src[:, h * 896:(h + 1) * 896], [esn], f"{esn}_{n_}")
            OX, OXn = self.ps_new(4)
            OY, OYn = self.ps_new(5)
            zer = self.cb[:, B_ZERO:B_ZERO + 128]
            self.MM(OX[:, 0:455], zer, self.slabI[:, 0:455], True, False, ["cb", "slabI"], [OXn])
            self.MM(OY[:, 0:130], zer, self.slabI[:, 0:130], True, False, ["cb", "slabI"], [OYn])
            lastX = max(kvl[i][-1] for i in range(7))
            lastY = max(kvl[i][-1] for i in (7, 8))
            def S1(t):
                qs = [i for i in range(9) if t in kvl[i]]
                if not qs:
                    return None
                assert qs == list(range(qs[0], qs[-1] + 1)) and len(qs) <= 7
                bA, bB = (0, 1) if t % 2 == 0 else (2, 3)
                psA, pAn = self.ps_new(bA)
                psB, pBn = self.ps_new(bB)
                qA, qB = qs[:4], qs[4:]
                for (qq, ps_, pn_) in ((qA, psA, pAn), (qB, psB, pBn)):
                    if not qq:
                        continue
                    nq = len(qq)
                    c = 0
                    while c < nq:
                        i = qq[c]
                        if i in eidx:
                            c2 = c + 1
                            a_ = kvl[i].index(t)
                            blk = eidx[i] * 896 + a_ * 128
                            b_ap, b_n = es[:, blk:blk + 128], esn
                        else:
                            c2 = c
                            while c2 < nq and qq[c2] not in eidx:
                                c2 += 1
                            blk = (h * 5 + (i - t + 4)) * 128
                            b_ap, b_n = self.slabI[:, blk:blk + (c2 - c) * 128], "slabI"
                        self.MM(ps_[:, c * 128:c2 * 128], KT[bp:bp + 64, p, t * 128:(t + 1) * 128],
                                QT[bp:bp + 64, p, qq[c] * 128:(qq[c2 - 1] + 1) * 128], True, False, ["KT", "QT"], [pn_])
                        self.MM(ps_[:, c * 128:c2 * 128], self.ident, b_ap, False, True, ["cb", b_n], [pn_])
                        c = c2
                pj = self.rot("PT", 2)
                PT, PTn = self.PT[pj], f"PT{pj}"
                self.ACT(PT[:, 0:len(qA) * 128], psA[:, 0:len(qA) * 128], AF.Exp, [pAn], [PTn])
                if qB:
                    self.ACT(PT[:, 512:512 + len(qB) * 128], psB[:, 0:len(qB) * 128], AF.Exp, [pBn], [PTn])
                return (qs, PT, PTn)

            def S2(t, tok):
                qs, PT, PTn = tok
                for c, i in enumerate(qs):
                    if i < 7:
                        od, odn = OX[:, i * 65:(i + 1) * 65], OXn
                    else:
                        od, odn = OY[:, (i - 7) * 65:(i - 6) * 65], OYn
                    is_last = (c == len(qs) - 1 and t == lastY) if i >= 7 else \
                        (t == lastX and i == max(q for q in qs if q < 7))
                    self.MM(od, PT[:, c * 128:(c + 1) * 128], V4[:, t, h, :], False, is_last, [PTn, "V"], [odn])

            tok_prev = S1(0)
            for t in range(13):
                tok_next = S1(t + 1) if t + 1 < 13 else None
                if tok_prev is not None:
                    S2(t, tok_prev)
                tok_prev = tok_next
            for (O_, On_, i0, ni) in ((OX, OXn, 0, 7), (OY, OYn, 7, 2)):
                O3 = O_[:, 0:ni * 65].rearrange("p (i d) -> p i d", i=ni)
                st, sn = self.stt_()
                R.dve(lambda e, st=st, O3=O3, ni=ni: e.reciprocal(out=st[:, 0:ni], in_=O3[:, :, 64]), [On_], [sn])
                self.TT(onaAll[:, i0:i0 + ni, h * 64:(h + 1) * 64], O3[:, :, 0:64],
                        st[:, 0:ni].rearrange("p (i o) -> p i o", o=1).to_broadcast([128, ni, 64]), ALU.mult,
                        [On_, sn], ["onaAll"])
        for i in range(9):
            pt, pn = self.pt_new()
            for c in range(4):
                self.TR_(pt[:, c * 128:(c + 1) * 128], onaAll[:, i, c * 128:(c + 1) * 128], ["onaAll"], [pn])
            self.ACOPY(onaT[:, :, i * 128:(i + 1) * 128], pt[:, 0:512].rearrange("p (a b) -> p a b", a=4), [pn], ["onaT"])
        self.dump("onaT", onaT, [128, 4, TR], BF16, "onaT")
        if getattr(self, "stop_after", None) == "U3":
            self.R.dve(lambda e: e.memset(self.Sfx, 0.0), [], ["Sfx"])
            self.barrier()
            return

        self.fence(["KT", "QT", "V", "onaAll", "PT0", "PT1"], ["qgT", "kgT", "sog", "snap", "ogT"])
        self.top = m1 + 4 * TR
        ogT = A(4 * TR).rearrange("p (k n) -> p k n", k=4)
        m_gla = self.top
        qgT = A(4 * TR).rearrange("p (k n) -> p k n", k=4)
        kgT = A(4 * TR).rearrange("p (k n) -> p k n", k=4)
        lr = [self.wsl[2][:, 0:TR], self.wsl[2][:, TR:2 * TR]]
        sog = A(9 * 512).rearrange("p (t n) -> p t n", t=9)
        snap = A(9 * 512).rearrange("p (t n) -> p t n", t=9)
        w, wn = self.wload(0, w_in[:, 1536:2048], 512)
        for hh in range(4):
            for (t0, n) in blocks(TR):
                ps, pn = self.ps_new()
                for kc in range(8):
                    self.MM(ps[:, :n], w[:, kc, hh * 128:(hh + 1) * 128], xnT[:, kc, 256 + t0:256 + t0 + n], kc == 0,
                            kc == 7, xr(t0, n, 256) + [wn], [pn])
                self.AMUL(qgT[:, hh, t0:t0 + n], ps[:, :n], 128 ** -0.5, [pn], ["qgT"])
        wk, wkn = self.wload(1, w_in[:, 2048:2560], 512)
        for hh in range(4):
            for (t0, n) in blocks(TR):
                ps, pn = self.ps_new()
                for kc in range(8):
                    self.MM(ps[:, :n], wk[:, kc, hh * 128:(hh + 1) * 128], xnT[:, kc, 256 + t0:256 + t0 + n], kc == 0,
                            kc == 7, xr(t0, n, 256) + [wkn], [pn])
                self.DCOPY(kgT[:, hh, t0:t0 + n], ps[:, :n], [pn], ["kgT"])
        for d in range(2):
            for (t0, n) in blocks(TR):
                ps, pn = self.ps_new()
                for kc in range(8):
                    self.MM(ps[0:16, :n], self.wlr[:, kc, 16 * d:16 * d + 16], xnT[:, kc, 256 + t0:256 + t0 + n],
                            kc == 0, kc == 7, xr(t0, n, 256) + ["wlr"], [pn])
                self.ACOPY(lr[d][0:16, t0:t0 + n], ps[0:16, :n], [pn], [f"lr{d}", "wsl2"])
        w, wn = self.wload(0, w_in[:, 3072:3584], 512)
        for i in range(9):
            ps, pn = self.ps_new()
            for kc in range(8):
                self.MM(ps, xnT[:, kc, (i + 2) * 128:(i + 3) * 128], w[:, kc, :], kc == 0, kc == 7,
                        [f"xnT{i + 2}", wn], [pn])
            tg, tgn = self.tF()
            self.ACT(tg, ps, AF.Tanh, [pn], [tgn], scale=0.5)
            self.STT(tg, tg, 1.0, ps, ALU.add, ALU.mult, [tgn, pn], [tgn])
            self.TT(sog[:, i, :], tg, cf[:, C_GNW:C_GNW + 512], ALU.mult, [tgn, "cf"], ["sog"])
        wv, wvn = self.wload(3, w_in[:, 2560:3072], 512)
        if getattr(self, "stop_after", None) == "U5":
            self.R.dve(lambda e: e.memset(self.Sfx, 0.0), [], ["Sfx"])
            self.barrier()
            return

        if Sb_init is None:
            self.zero_state("b")
        else:
            Sb = self.S["b"]
            R.dve(lambda e, Sb=Sb, src=Sb_init[0]: e.tensor_copy(out=Sb, in_=src), [Sb_init[1]], ["Sb"])
        order6 = list(reversed(range(9)))
        self.scan_pipe("b", 9,
                       lambda n_, tok=None: (xnT[:, :, (order6[n_] + 2) * 128:(order6[n_] + 3) * 128], f"xnT{order6[n_] + 2}"),
                       wk, wkn, wv, wvn,
                       lr_of=lambda n_: (lr[1][0:16, order6[n_] * 128:(order6[n_] + 1) * 128], "lr1"),
                       snap_of=lambda n_: (snap[:, order6[n_], :], "snap"))
        self.fence(["ksb0", "ksb1", "ksb2"], [f"ded{a_}{b_}" for a_ in range(2) for b_ in range(4)])
        Sf = self.S["f"]
        ded = [[self.slabI[:, (par * 4 + q) * 512:(par * 4 + q + 1) * 512] for q in range(4)] for par in range(2)]
        m1out = {}

        def M1(i):
            par = i % 2
            xT3 = xnT[:, :, (i + 2) * 128:(i + 3) * 128]
            psk, pkn, vt, vn = self.tok_kv(xT3, f"xnT{i + 2}", wk, wkn, wv, wvn)
            qts, As = [], []
            gpf = self.gates(lr[0][0:16, i * 128:(i + 1) * 128], "lr0", 0)
            ktf, ktfn, stf, stfn = self.state_prep(gpf[0], gpf[1], psk, pkn, "f")
            for d in range(2):
                if d == 0:
                    gp, gn = gpf
                else:
                    gp, gn = self.gates(lr[d][0:16, i * 128:(i + 1) * 128], f"lr{d}", d)
                Ufm = cf[:, C_UINC:C_UINC + 128] if d == 0 else cf[:, C_ULT:C_ULT + 128]
                psp, ppn = self.ps_new()
                for hh in range(4):
                    sl = slice(hh * 128, (hh + 1) * 128)
                    self.MM(psp[:, sl], gp[:, sl], Ufm, True, True, [gn, "cf"], [ppn])
                Eq, eqn = self.tF()
                Ek, ekn = self.tF()
                self.ACT(Eq, psp, AF.Exp, [ppn], [eqn], scale=(1.0 if d == 0 else -1.0))
                self.ACT(Ek, psp, AF.Exp, [ppn], [ekn], scale=(-1.0 if d == 0 else 1.0))
                qt, qn = ded[par][d], f"ded{par}{d}"
                kt2, k2n = self.tB()
                q3 = qt.rearrange("p (h t) -> p h t", h=4)
                k3 = kt2.rearrange("p (h t) -> p h t", h=4)
                self.TT(q3, qgT[:, :, i * 128:(i + 1) * 128], Eq.rearrange("p (h t) -> p h t", h=4), ALU.mult,
                        ["qgT", eqn], [qn])
                self.TT(k3, kgT[:, :, i * 128:(i + 1) * 128], Ek.rearrange("p (h t) -> p h t", h=4), ALU.mult,
                        ["kgT", ekn], [k2n])
                psa, pan = self.ps_new()
                for hh in range(4):
                    sl = slice(hh * 128, (hh + 1) * 128)
                    self.MM(psa[:, sl], kt2[:, sl], qt[:, sl], True, True, [k2n, qn], [pan])
                Ad, adn = ded[par][2 + d], f"ded{par}{2 + d}"
                Mm = self.Mf if d == 0 else self.Mb
                self.TT(Ad.rearrange("p (h t) -> p h t", h=4), psa.rearrange("p (h t) -> p h t", h=4),
                        Mm.rearrange("p (o t) -> p o t", o=1).to_broadcast([128, 4, 128]), ALU.mult, [pan, "cb"], [adn])
                qts.append((qt, qn))
                As.append((Ad, adn))
            m1out[i] = (vt, vn, ktf, ktfn, stf, stfn, qts, As)

        def M2(i):
            vt, vn, ktf, ktfn, stf, stfn, qts, As = m1out.pop(i)
            if i == 8:
                exp_ = self.Sfx
                R.dve(lambda e, exp_=exp_, Sf=Sf: e.tensor_copy(out=exp_, in_=Sf), ["Sf"], ["Sfx"])
            self.ACOPY(self.Sbf, Sf, ["Sf"], ["Sbf"])
            pso, pon = self.ps_new()
            for hh in range(4):
                sl = slice(hh * 128, (hh + 1) * 128)
                self.MM(pso[:, sl], qts[0][0][:, sl], self.Sbf[:, sl], True, False, [qts[0][1], "Sbf"], [pon])
                self.MM(pso[:, sl], As[0][0][:, sl], vt[:, sl], False, False, [As[0][1], vn], [pon])
                self.MM(pso[:, sl], qts[1][0][:, sl], snap[:, i, sl], False, False, [qts[1][1], "snap"], [pon])
                self.MM(pso[:, sl], As[1][0][:, sl], vt[:, sl], False, True, [As[1][1], vn], [pon])
            self.state_apply(ktf, ktfn, stf, stfn, vt, vn, "f")
            st, sn = self.stt_()
            jj = self.rot("jk", 2)
            for hh in range(4):
                sl = slice(hh * 128, (hh + 1) * 128)
                self.ACT(self.jk[jj][:, sl], pso[:, sl], AF.Square, [pon], [f"jk{jj}", sn], accum=st[:, hh:hh + 1])
            self.TS(st[:, 4:8], st[:, 0:4], 1.0 / 128, EPS, ALU.mult, ALU.add, [sn], [sn])
            self.ACT(st[:, 4:8], st[:, 4:8], AF.Ln, [sn], [sn])
            self.ACT(st[:, 4:8], st[:, 4:8], AF.Exp, [sn], [sn], scale=-0.5)
            self.TS(st[:, 4:8], st[:, 4:8], 0.5, None, ALU.mult, None, [sn], [sn])
            og, ogn = self.tF()
            self.TT(og.rearrange("p (h t) -> p h t", h=4), pso.rearrange("p (h t) -> p h t", h=4),
                    st[:, 4:8].rearrange("p (h o) -> p h o", o=1).to_broadcast([128, 4, 128]), ALU.mult, [pon, sn], [ogn])
            ogb, obn = self.tB()
            self.TT(ogb, og, sog[:, i, :], ALU.mult, [ogn, "sog"], [obn])
            pt, pn = self.pt_new()
            for c in range(4):
                self.TR_(pt[:, c * 128:(c + 1) * 128], ogb[:, c * 128:(c + 1) * 128], [obn], [pn])
            self.ACOPY(ogT[:, :, i * 128:(i + 1) * 128], pt[:, 0:512].rearrange("p (a b) -> p a b", a=4), [pn], ["ogT"])

        M1(0)
        for i in range(9):
            if i + 1 < 9:
                M1(i + 1)
            M2(i)
        self.dump("ogT", ogT, [128, 4, TR], BF16, "ogT")
        if getattr(self, "stop_after", None) == "U7":
            self.R.dve(lambda e: e.memset(self.Sfx, 0.0), [], ["Sfx"])
            self.barrier()
            return

        self.fence(["qgT", "kgT", "sog", "snap", "lr0", "lr1"], ["mixT", "wsl2"] + [f"hnT{i_}" for i_ in range(9)])
        self.top = m_gla
        mixT = A(8 * TR).rearrange("p (k n) -> p k n", k=8)
        wna, wnan = self.wload(0, T["w_na_proj"][:, :], 1024, kchunks=4)
        wgl, wgln = self.wload(1, T["w_gla_proj"][:, :], 1024, kchunks=4)
        for hp in range(2):
            g1, g1n = self.wload(2, w_in[:, 3616 + 512 * hp:3616 + 512 * hp + 512], 512)
            g2, g2n = self.wload(3, w_in[:, 4640 + 512 * hp:4640 + 512 * hp + 512], 512)
            for c4 in range(4):
                c = hp * 4 + c4
                for (t0, n) in blocks(TR):
                    ps1, p1n = self.ps_new()
                    for kk in range(4):
                        self.MM(ps1[:, :n], wna[:, kk, c * 128:(c + 1) * 128], onaT[:, kk, t0:t0 + n], kk == 0, kk == 3,
                                [wnan, "onaT"], [p1n])
                    ps2, p2n = self.ps_new()
                    for kk in range(4):
                        self.MM(ps2[:, :n], wgl[:, kk, c * 128:(c + 1) * 128], ogT[:, kk, t0:t0 + n], kk == 0, kk == 3,
                                [wgln, "ogT"], [p2n])
                    ps3, p3n = self.ps_new()
                    for kc in range(8):
                        self.MM(ps3[:, :n], g1[:, kc, c4 * 128:(c4 + 1) * 128], xnT[:, kc, 256 + t0:256 + t0 + n],
                                kc == 0, kc == 7, xr(t0, n, 256) + [g1n], [p3n])
                    ps4, p4n = self.ps_new()
                    for kc in range(8):
                        self.MM(ps4[:, :n], g2[:, kc, c4 * 128:(c4 + 1) * 128], xnT[:, kc, 256 + t0:256 + t0 + n],
                                kc == 0, kc == 7, xr(t0, n, 256) + [g2n], [p4n])
                    t1, t1n = self.tF()
                    t2, t2n = self.tF()
                    self.ACT(t1[:, :n], ps3[:, :n], AF.Tanh, [p3n], [t1n], scale=0.5)
                    self.ACT(t2[:, :n], ps4[:, :n], AF.Tanh, [p4n], [t2n], scale=0.5)
                    self.STT(t1[:, :n], t1[:, :n], 1.0, ps1[:, :n], ALU.add, ALU.mult, [t1n, p1n], [t1n])
                    self.STT(t2[:, :n], t2[:, :n], 1.0, ps2[:, :n], ALU.add, ALU.mult, [t2n, p2n], [t2n])
                    self.TT(mixT[:, c, t0:t0 + n], t1[:, :n], t2[:, :n], ALU.add, [t1n, t2n], ["mixT"])
        self.dump("mixT", mixT, [128, 8, TR], BF16, "mixT")
        if getattr(self, "stop_after", None) == "U8a":
            self.R.dve(lambda e: e.memset(self.Sfx, 0.0), [], ["Sfx"])
            self.barrier()
            return
        hnT = A(8 * TR).rearrange("p (k n) -> p k n", k=8)
        wo = []
        for half in range(2):
            wo.append(self.wload(half, T["w_out"][:, half * 512:(half + 1) * 512], 512))
        prevB = None
        for i in range(10):
            curB = None
            if i < 9:
                xs, xn = self.xload(xsrc[(kv0 + 2 + i) * 128:(kv0 + 3 + i) * 128, :])
                for half in range(2):
                    ps, pn = self.ps_new()
                    for kc in range(8):
                        self.MM(ps, mixT[:, kc, i * 128:(i + 1) * 128], wo[half][0][:, kc, :], kc == 0, kc == 7,
                                ["mixT", wo[half][1]], [pn])
                    sl = slice(half * 512, (half + 1) * 512)
                    self.STT(xs[:, sl], ps, 0.5, xs[:, sl], ALU.mult, ALU.add, [pn, xn], [xn])
                self.STORE("sp", hscr[i * 128:(i + 1) * 128, :], xs, [xn], "hst_" + xn, w=[f"hscr{i}"])
                mc = None
                if i == 0:
                    mc = C_MASK + 2 * ut["mcol"]
                if i == 8:
                    mc = C_MASK + 2 * ut["mcol"] + 1
                curB = self.norm_A(xs, xn, C_WBC2, maskcol=mc) + (i,)
            if prevB is not None:
                self.norm_B(prevB[0], prevB[1], hnT[:, :, prevB[2] * 128:(prevB[2] + 1) * 128], f"hnT{prevB[2]}")
            prevB = curB
        self.dump("hnT", hnT, [128, 8, TR], BF16, "hnT8")
        if getattr(self, "stop_after", None) == "U8b":
            self.R.dve(lambda e: e.memset(self.Sfx, 0.0), [], ["Sfx"])
            self.barrier()
            return

        self.barrier()
        hn_names = [f"hnT{i}" for i in range(9)]
        self.top = self.base
        fT = A(22 * 1024).rearrange("p (j n) -> p j n", j=22)
        tmp = self.arena[:, self.tmp0:self.tmp0 + 13312]
        u = [self.slabI[:, 0:2304].bitcast(F32), self.slabI[:, 2304:4608].bitcast(F32)]
        cv = [tmp[:, 0:2048].bitcast(F32), tmp[:, 2048:4096].bitcast(F32)]
        gq = tmp[:, 4096:6144].bitcast(F32)
        wdT = tmp[:, 6144:13312].rearrange("p (j n) -> p j n", j=7)
        wdL = tmp[:, 0:6144].rearrange("p (j n) -> p j n", j=6)
        wdC = A(9 * 1024).rearrange("p (j n) -> p j n", j=9)
        assert self.top <= self.base + 8 * TK + 4 * TR + 4 * TR + 8 * TR, "fT/wdC would overlap hnT"
        wdr = T["w_down"].rearrange("(j p) n -> p j n", p=128)
        w_up = T["w_up"]
        for j in range(22):
            si_ = j % 4
            sl_, wn = self.wsl[si_], f"wsl{si_}"
            wv2 = sl_[:, 0:2048].rearrange("p (k n) -> p k n", k=8)
            self.LOAD("pool", wv2[:, :, 0:128], w_up[:, j * 128:(j + 1) * 128].rearrange("(k p) n -> p k n", p=128),
                      [wn], wn + "a")
            self.LOAD("pool", wv2[:, :, 128:256],
                      w_up[:, 2816 + j * 128:2816 + (j + 1) * 128].rearrange("(k p) n -> p k n", p=128), [wn], wn + "b")
            if j == 3:
                for g in range(0, 9, 3):
                    self.LOAD("pool", wdC[:, g:g + 3, :], wdr[:, g:g + 3, :], ["wdC"], f"wdC{g}")
                self.LOAD("pool", wdT[:, 0:4, :], wdr[:, 9:13, :], ["wdT"], "wdT0")
                self.LOAD("pool", wdT[:, 4:7, :], wdr[:, 13:16, :], ["wdT"], "wdT4")
            for ab in range(2):
                for (t0, n) in blocks(TR):
                    ps, pn = self.ps_new()
                    for kc in range(8):
                        self.MM(ps[:, :n], wv2[:, kc, ab * 128:(ab + 1) * 128], hnT[:, kc, t0:t0 + n], kc == 0, kc == 7,
                                [wn] + hn_names, [pn])
                    self.ACOPY(u[ab][:, t0:t0 + n], ps[:, :n], [pn], [f"u{ab}"])
                jj = ab * 22 + j
                cw = lambda tap, jj=jj: cf[:, C_CW + jj * 3 + tap:C_CW + jj * 3 + tap + 1]
                self.ACT(cv[ab], u[ab][:, 63:1087], AF.Identity, [f"u{ab}", "cf"], [f"cv{ab}"], scale=cw(0),
                         bias=cf[:, C_CB + jj:C_CB + jj + 1])
                self.STT(cv[ab], u[ab][:, 64:1088], cw(1), cv[ab], ALU.mult, ALU.add, [f"u{ab}", "cf", f"cv{ab}"],
                         [f"cv{ab}"])
                self.STT(cv[ab], u[ab][:, 65:1089], cw(2), cv[ab], ALU.mult, ALU.add, [f"u{ab}", "cf", f"cv{ab}"],
                         [f"cv{ab}"])
            self.ACT(gq, cv[0], AF.Square, ["cv0"], ["gq"])
            self.TS(gq, gq, 0.044715, 1.0, ALU.mult, ALU.add, ["gq"], ["gq"])
            self.TT(gq, gq, cv[0], ALU.mult, ["gq", "cv0"], ["gq"])
            self.ACT(gq, gq, AF.Tanh, ["gq"], ["gq"], scale=0.7978845608028654)
            self.STT(gq, gq, 1.0, cv[0], ALU.add, ALU.mult, ["gq", "cv0"], ["gq"])
            self.TT(fT[:, j, :], gq, cv[1], ALU.mult, ["gq", "cv1"], ["fT"])
        self.dump("fT", fT, [128, 22, 1024], BF16, "fT")
        if getattr(self, "stop_after", None) == "U10":
            self.R.dve(lambda e: e.memset(self.Sfx, 0.0), [], ["Sfx"])
            self.barrier()
            return

        self.fence(["cv0", "cv1", "gq"], ["wdL"])
        self.LOAD("pool", wdL[:, 0:3, :], wdr[:, 16:19, :], ["wdL"], "wdL0")
        self.LOAD("pool", wdL[:, 3:6, :], wdr[:, 19:22, :], ["wdL"], "wdL3")

        def wdj(j):
            if j < 9:
                return wdC[:, j], "wdC"
            if j < 16:
                return wdT[:, j - 9], "wdT"
            return wdL[:, j - 16], "wdL"
        for i8 in range(8):
            hi_ = self.rot("xs", 2)
            hs, hn_ = self.xs[hi_], f"xs{hi_}"
            self.LOAD("sp", hs, hscr[64 + i8 * 128:64 + (i8 + 1) * 128, :], [hn_], hn_,
                      r=[f"hscr{i8}", f"hscr{i8 + 1}"])
            for half in range(2):
                ps, pn = self.ps_new()
                for j in range(22):
                    self.MM(ps, fT[:, j, i8 * 128:(i8 + 1) * 128], wdj(j)[0][:, half * 512:(half + 1) * 512], j == 0,
                            j == 21, ["fT", wdj(j)[1]], [pn])
                sl = slice(half * 512, (half + 1) * 512)
                self.STT(hs[:, sl], ps, 0.5, hs[:, sl], ALU.mult, ALU.add, [pn, hn_], [hn_])
            self.stores.append(self.STORE("sp", yout[i8 * 128:(i8 + 1) * 128, :], hs, [hn_], "yst_" + hn_))
        self.barrier()

    def sequence(self, xsrc, n_units, tau0, pre, tau_max, uts, yout, hscr):
        T = self.T
        w_in = T["w_in"]
        self.zero_state("f")
        self.zero_state("b")
        post = list(range(tau_max, tau0 + 8, -1))
        saves = {}
        inits = [None] * n_units
        for m in range(n_units):
            need = tau0 + 8 * m + 9
            if need <= tau_max:
                saves[need] = (self.Ssave[m], f"Ssave{m}")
                inits[m] = (self.Ssave[m], f"Ssave{m}")
        if pre or post:
            wk, wkn = self.wload(1, w_in[:, 2048:2560], 512)
            wv, wvn = self.wload(3, w_in[:, 2560:3072], 512)
            if pre:
                self.state_scan(xsrc, pre, "f", wk, wkn, wv, wvn)
            if post:
                self.state_scan(xsrc, post, "b", wk, wkn, wv, wvn, saves=saves)
        self.barrier()
        for m in range(n_units):
            self.unit(xsrc, tau0 + 8 * m - 2, uts[m], yout[m * 1024:(m + 1) * 1024, :], hscr, inits[m], f"u{m}")
            Sf, Sfx = self.S["f"], self.Sfx
            self.R.dve(lambda e, Sf=Sf, Sfx=Sfx: e.tensor_copy(out=Sf, in_=Sfx), ["Sfx"], ["Sf"])
            self.barrier()


def build_nc(dbg=None, only_prompt_units=None, xs_tiles=225, variant=None):
    nc = bass.Bass("TRN2", target_bir_lowering=False)
    dt = lambda name, shape, kind="ExternalInput": nc.dram_tensor(name, list(shape), F32, kind=kind).ap()
    T = {
        "xp": dt("xp", [2, 21 * 128, D]),
        "xs": dt("xs", [xs_tiles * 128, D]),
        "cf": dt("cf", [128, NCF]),
        "cb": dt("cb", [128, NCB]),
        "slabI": dt("slabI", [128, 5120]),
        "sT_p": dt("sT_p", [3, 128, 7168]),
        "sB_p": dt("sB_p", [2, 128, 7168]),
        "sT_s": dt("sT_s", [3, 128, 7168]),
        "sB_s": dt("sB_s", [2, 128, 7168]),
        "w_in": dt("w_in", [D, 5664]),
        "w_a2_f": dt("w_a2_f", [16, 512]),
        "b_a_f": dt("b_a_f", [1, 512]),
        "w_a2_b": dt("w_a2_b", [16, 512]),
        "b_a_b": dt("b_a_b", [1, 512]),
        "w_na_proj": dt("w_na_proj", [512, D]),
        "w_gla_proj": dt("w_gla_proj", [512, D]),
        "w_out": dt("w_out", [D, D]),
        "w_up": dt("w_up", [D, 5632]),
        "w_down": dt("w_down", [2816, D]),
    }
    yp = dt("yp", [2, 2048, D], "ExternalOutput")
    ys = dt("ys", [4096, D], "ExternalOutput")
    hscr = dt("hscr", [TR, D], "Internal")
    with ExitStack() as es:
        k = K(nc, es, dbg)
        k.setup(T)
        k.Sfx = k.alloc(512, F32)
        k.base = k.top
        sT_p = [T["sT_p"][e] for e in range(3)]
        sB_p = [T["sB_p"][e] for e in range(2)]
        sT_s = [T["sT_s"][e] for e in range(3)]
        sB_s = [T["sB_s"][e] for e in range(2)]
        ut_p = [dict(top=sT_p, bot=None, mcol=0), dict(top=None, bot=sB_p, mcol=1)]
        ut_s = [dict(top=sT_s, bot=None, mcol=2), dict(top=None, bot=None, mcol=3),
                dict(top=None, bot=None, mcol=4), dict(top=None, bot=sB_s, mcol=5)]
        if only_prompt_units is not None:
            k.dbg_unit = f"u{only_prompt_units - 1}"
            k.sequence(T["xp"][0], only_prompt_units, 2, [], 18, ut_p, yp[0], hscr)
        elif variant == "prompts":
            for b in range(2):
                k.sequence(T["xp"][b], 2, 2, [], 18, ut_p, yp[b], hscr)
        elif variant == "scans":
            k.sequence(T["xs"], 0, 96, list(range(0, 96)), 224, ut_s, ys, hscr)
        else:
            for b in range(2):
                k.sequence(T["xp"][b], 2, 2, [], 18, ut_p, yp[b], hscr)
            k.sequence(T["xs"], 4, 96, list(range(0, 96)), 224, ut_s, ys, hscr)
        n, cnt = k.R.emit(nc, k.stores)
        k.nops = n
    return nc, k


def make_slab(rpb, i, kvs, lo, hi):
    H = rpb.shape[0]
    out = np.full((128, 7, H, 128), NEG, np.float32)
    kc = np.arange(64)
    qc = np.arange(64)
    c0 = np.clip(qc - 8, 0, 48)
    colv = (kc[:, None] >= c0[None, :]) & (kc[:, None] < c0[None, :] + 16)
    dcv = np.clip(kc[:, None] - qc[None, :] + 15, 0, 30)
    for a, t in enumerate(kvs):
        for rr in range(2):
            rk = 2 * t - 5 + rr
            for qq in range(2):
                rq = 2 * i - 1 + qq
                if lo <= rq < hi:
                    r0 = min(max(rq - 4, lo), hi - 8)
                else:
                    r0 = rq - 4
                if not (r0 <= rk < r0 + 8):
                    continue
                dr = rk - rq + 7
                blk = np.where(colv[None], rpb[:, dr][:, dcv], NEG)
                out[rr * 64:(rr + 1) * 64, a, :, qq * 64:(qq + 1) * 64] = blk.transpose(1, 0, 2)
    return out


def host_consts(norm1_w, norm2_w, gla_norm_w, qn_w, kn_w, conv_w, conv_b, masks):
    cf = np.zeros((128, NCF), np.float32)
    cf[:, C_WBC1:C_WBC1 + D] = norm1_w[None, :]
    cf[:, C_WBC2:C_WBC2 + D] = norm2_w[None, :]
    cf[:, C_GNW:C_GNW + 512] = np.tile(gla_norm_w, 4)[None, :]
    idx = np.arange(128)
    cf[:, C_UINC:C_UINC + 128] = (idx[:, None] <= idx[None, :]) * (-1.0 / 16)
    cf[:, C_ULT:C_ULT + 128] = (idx[:, None] < idx[None, :]) * (-1.0 / 16)
    cf[:, C_UGT:C_UGT + 128] = (idx[:, None] > idx[None, :]) * (-1.0 / 16)
    cw = conv_w.reshape(3, 44, 128)
    cf[:, C_CW:C_CW + 132] = cw.transpose(2, 1, 0).reshape(128, 132)
    cf[:, C_CB:C_CB + 44] = conv_b.reshape(44, 128).T
    cf[:, C_QNW] = np.tile(qn_w, 2)
    cf[:, C_KNW] = np.tile(kn_w, 2)
    cf[:, C_LNQ] = np.float32(np.log(0.125))
    cf[:, C_NEG] = -1.0 / 16
    cf[:, C_EPS] = EPS
    for m, (top_real, bot_real) in enumerate(masks):
        cf[:, C_MASK + 2 * m] = 1.0
        cf[0:64, C_MASK + 2 * m] = 1.0 if top_real else 0.0
        cf[:, C_MASK + 2 * m + 1] = 1.0
        cf[64:128, C_MASK + 2 * m + 1] = 1.0 if bot_real else 0.0
    cb = np.zeros((128, NCB), np.float32)
    cb[:, B_ID:B_ID + 128] = np.eye(128)
    cb[0:64, B_OB:B_OB + 64] = 1.0
    cb[64:128, B_OB + 64:B_OB + 128] = 1.0
    cb[:, B_ONE:B_ONE + 128] = 1.0
    cb[:, B_MF:B_MF + 128] = (idx[:, None] <= idx[None, :])
    cb[:, B_MB:B_MB + 128] = (idx[:, None] > idx[None, :])
    return cf, cb


def make_in_maps(inp):
    f = lambda k: np.asarray(inp[k], np.float32)
    x_prompt, x_sample = f("x_prompt"), f("x_sample")
    rpb = f("rpb")[0]
    w_in = np.ascontiguousarray(f("w_in")[0])
    shared = {
        "w_in": w_in,
        "w_a2_f": np.ascontiguousarray(f("w_a2_f")[0]), "b_a_f": np.ascontiguousarray(f("b_a_f")[0][None, :]),
        "w_a2_b": np.ascontiguousarray(f("w_a2_b")[0]), "b_a_b": np.ascontiguousarray(f("b_a_b")[0][None, :]),
        "w_na_proj": np.ascontiguousarray(f("w_na_proj")[0]), "w_gla_proj": np.ascontiguousarray(f("w_gla_proj")[0]),
        "w_out": np.ascontiguousarray(f("w_out")[0]), "w_up": np.ascontiguousarray(f("w_up")[0]),
        "w_down": np.ascontiguousarray(f("w_down")[0]),
    }
    flat = lambda s: np.ascontiguousarray(s.transpose(0, 2, 1, 3).reshape(128, 7168))
    slabI = make_slab(rpb, 3, kv_list(3, False), -100, 100)[:, 0:5]
    slabI = np.ascontiguousarray(slabI[:, ::-1].transpose(0, 2, 1, 3).reshape(128, 5120))
    sT_cl = np.stack([flat(make_slab(rpb, i, kv_list(i, True), 0, 32)) for i in EDGE_TOP])
    sT_un = np.stack([flat(make_slab(rpb, i, kv_list(i, True), -100, 100)) for i in EDGE_TOP])
    sB_cl = np.stack([flat(make_slab(rpb, i, kv_list(i, True), -16, 16)) for i in EDGE_BOT])
    sB_un = np.stack([flat(make_slab(rpb, i, kv_list(i, True), -100, 100)) for i in EDGE_BOT])
    shared.update({"slabI": slabI, "sT_p": sT_cl, "sB_p": sB_cl})
    maps = []
    for c in range(NCORES):
        s, j = c // 4, c % 4
        xp = np.zeros((2, 42, 64, D), np.float32)
        for b in range(2):
            xp[b, 5:37] = x_prompt[2 * c + b].reshape(32, 64, D)
        R0 = 64 * j
        xs = np.zeros((450, 64, D), np.float32)
        g0 = R0 - 193
        lo_, hi_ = max(0, g0), min(256, g0 + 450)
        xs[lo_ - g0:hi_ - g0] = x_sample[s].reshape(256, 64, D)[lo_:hi_]
        masks = [(False, True), (True, False)]
        for m in range(4):
            masks.append((R0 + 16 * m - 1 >= 0, R0 + 16 * m + 16 < 256))
        cf, cb = host_consts(f("norm1_w")[0], f("norm2_w")[0], f("gla_norm_w")[0], f("qn_w")[0], f("kn_w")[0],
                             f("conv_w")[0], f("conv_b")[0], masks)
        d = dict(shared)
        d.update({"xp": xp.reshape(2, 21 * 128, D), "xs": xs.reshape(225 * 128, D), "cf": cf, "cb": cb,
                  "sT_s": sT_cl if j == 0 else sT_un, "sB_s": sB_cl if j == 3 else sB_un})
        maps.append(d)
    return maps


def kernel(**inp):
    maps = make_in_maps(inp)
    nc, k = build_nc()
    res = run_bass_kernel_spmd(nc, maps, core_ids=list(range(NCORES)))
    y_prompt = np.empty((16, 2048, D), np.float32)
    y_sample = np.empty((2, 16384, D), np.float32)
    for c in range(NCORES):
        s, j = c // 4, c % 4
        r = res.results[c]
        y_prompt[2 * c:2 * c + 2] = np.asarray(r["yp"], np.float32)
        y_sample[s, 4096 * j:4096 * (j + 1)] = np.asarray(r["ys"], np.float32)
    return (y_prompt, y_sample)
```

```python
import numpy as np
import concourse.bass as bass
import concourse.mybir as mybir
from concourse.bass_utils import run_bass_kernel_spmd
from contextlib import ExitStack

F32 = mybir.dt.float32
BF16 = mybir.dt.bfloat16
AF = mybir.ActivationFunctionType
ALU = mybir.AluOpType
EPS = 1e-6
NCORES = 8
D = 1024
TR = 1152
TK = 1664
NEG = -30000.0
EDGE_TOP = (0, 1, 2)
EDGE_BOT = (7, 8)
C_WBC1, C_WBC2, C_GNW, C_UINC, C_ULT, C_UGT, C_CW, C_CB = 0, 1024, 2048, 2560, 2688, 2816, 2944, 3076
C_QNW, C_KNW, C_LNQ, C_NEG, C_MASK, C_EPS = 3120, 3121, 3122, 3123, 3124, 3136
NCF = 3138
B_ID, B_OB, B_ONE, B_MF, B_MB, B_ZERO = 0, 128, 256, 384, 512, 640
NCB = 768


class Buf:
    __slots__ = ("w", "rs")

    def __init__(self):
        self.w = None
        self.rs = []


class Rec:
    ENGS = ("pe", "act", "dve", "pool", "sp")

    def __init__(self):
        self.ops = []
        self.bufs = {}
        self.bar = None
        self.last = {}
        self.dmas = []

    def B(self, name):
        b = self.bufs.get(name)
        if b is None:
            b = self.bufs[name] = Buf()
        return b

    def add(self, eng, fn, r=(), w=(), dma=False, stream=None):
        deps = set()
        for n in r:
            b = self.B(n)
            if b.w is not None:
                deps.add(b.w)
        for n in w:
            b = self.B(n)
            if b.w is not None:
                deps.add(b.w)
            deps.update(b.rs)
        if self.bar is not None:
            deps.add(self.bar)
        i = len(self.ops)
        self.ops.append(dict(eng=eng, fn=fn, deps=deps, dma=dma, stream=stream, sig=False, ord=0))
        for n in r:
            self.B(n).rs.append(i)
        for n in w:
            b = self.B(n)
            b.w = i
            b.rs = []
        if dma:
            self.dmas.append(i)
        else:
            self.last[eng] = i
        return i

    def barrier(self, fn):
        deps = set(self.last.values()) | set(self.dmas)
        if self.bar is not None:
            deps.add(self.bar)
        i = len(self.ops)
        self.ops.append(dict(eng="dve", fn=fn, deps=deps, dma=False, stream=None, sig=False, ord=0))
        self.bar = i
        self.last["dve"] = i
        self.dmas = []
        return i

    def pe(self, fn, r=(), w=()):
        return self.add("pe", fn, r, w)

    def act(self, fn, r=(), w=()):
        return self.add("act", fn, r, w)

    def dve(self, fn, r=(), w=()):
        return self.add("dve", fn, r, w)

    def pool(self, fn, r=(), w=()):
        return self.add("pool", fn, r, w)

    def dma(self, q, fn, r=(), w=(), stream=None):
        return self.add(q, fn, r, w, dma=True, stream=stream)

    def emit(self, nc, final_wait_ops=()):
        ops = self.ops
        ops.append(dict(eng="sp", fn=None, deps=set(final_wait_ops), dma=False, stream=None, sig=False, ord=0))
        for o in ops:
            for d in o["deps"]:
                od = ops[d]
                if od["dma"]:
                    continue
                if od["eng"] == "pe" and o["eng"] == "pe" and not o["dma"]:
                    continue
                od["sig"] = True
        cnt = {}
        for o in ops:
            if o["dma"]:
                k = ("s", o["stream"])
                cnt[k] = cnt.get(k, 0) + 1
                o["ord"] = cnt[k]
            elif o["sig"]:
                k = ("e", o["eng"])
                cnt[k] = cnt.get(k, 0) + 1
                o["ord"] = cnt[k]
        with ExitStack() as es:
            sem = {}
            for k in cnt:
                sem[k] = es.enter_context(nc.semaphore("sem_%s_%s" % k))
            block = es.enter_context(nc.Block())
            by_eng = {e: [o for o in ops if o["eng"] == e] for e in self.ENGS}

            def run(engh, ename):
                waited = {}
                for o in by_eng[ename]:
                    need = {}
                    for d in o["deps"]:
                        od = ops[d]
                        if od["dma"]:
                            k = ("s", od["stream"])
                            v = 16 * od["ord"]
                        else:
                            if od["eng"] == "pe" and ename == "pe" and not o["dma"]:
                                continue
                            k = ("e", od["eng"])
                            v = od["ord"]
                        if v > need.get(k, 0):
                            need[k] = v
                    for k, v in need.items():
                        if waited.get(k, 0) >= v:
                            continue
                        waited[k] = v
                        engh.wait_ge(sem[k], v)
                    if o["fn"] is None:
                        continue
                    ins = o["fn"](engh)
                    if o["dma"]:
                        ins.then_inc(sem[("s", o["stream"])], 16)
                    elif o["sig"]:
                        ins.then_inc(sem[("e", ename)], 1)

            if by_eng["sp"]:
                block.sync(lambda e: run(e, "sp"))
            if by_eng["pool"]:
                block.gpsimd(lambda e: run(e, "pool"))
            if by_eng["act"]:
                block.scalar(lambda e: run(e, "act"))
            if by_eng["dve"]:
                block.vector(lambda e: run(e, "dve"))
            if by_eng["pe"]:
                block.tensor(lambda e: run(e, "pe"))
        return len(ops), cnt


def blocks(T):
    out = []
    t = 0
    while t < T:
        n = min(512, T - t)
        out.append((t, n))
        t += n
    return out


def kv_list(i, edge):
    if not edge:
        return list(range(i, i + 5))
    return {0: list(range(0, 7)), 1: list(range(1, 7)), 2: list(range(2, 7)),
            7: list(range(6, 12)), 8: list(range(6, 13))}[i]


class K:
    def __init__(self, nc, es, dbg=None):
        self.nc = nc
        self.es = es
        self.R = Rec()
        self.dbg = dbg
        self.dumps = []
        self.stores = []
        self.cnt = {}
        self.arena = es.enter_context(nc.sbuf_tensor("arena", [128, 106400], BF16))
        self.top = 0
        self.ps = [es.enter_context(nc.psum_tensor(f"ps{i}", [128, 512], F32)) for i in range(6)]
        self.pb = [es.enter_context(nc.psum_tensor(f"pb{i}", [128, 1024], BF16)) for i in range(2)]

    def alloc(self, n, dt=BF16):
        if dt == F32:
            n2 = 2 * n
        else:
            n2 = n
        self.top = (self.top + 1) // 2 * 2
        a = self.arena[:, self.top:self.top + n2]
        self.top += n2
        assert self.top <= 106400, self.top
        return a.bitcast(F32) if dt == F32 else a

    def rot(self, key, n):
        c = self.cnt.get(key, 0)
        self.cnt[key] = c + 1
        return c % n

    def ps_new(self, bank=None):
        i = self.rot("ps", 6) if bank is None else bank
        return self.ps[i][:], f"ps{i}"

    def pt_new(self):
        i = self.rot("pt", 2)
        return self.pb[i][:], f"pb{i}"

    def MM(self, out, lhsT, rhs, start, stop, r, w):
        self.R.pe(lambda e: e.matmul(out, lhsT=lhsT, rhs=rhs, start=start, stop=stop), r, w)

    def TR_(self, out, in_, r, w):
        ident = self.ident
        self.R.pe(lambda e: e.transpose(out=out, in_=in_, identity=ident), list(r) + ["cb"], w)

    def ACT(self, out, in_, func, r, w, scale=None, bias=None, accum=None):
        kw = {}
        if scale is not None:
            kw["scale"] = scale
        if bias is not None:
            kw["bias"] = bias
        if accum is not None:
            kw["accum_out"] = accum
        self.R.act(lambda e: e.activation(out=out, in_=in_, func=func, **kw), r, w)

    def ACOPY(self, out, in_, r, w):
        self.R.act(lambda e: e.copy(out=out, in_=in_), r, w)

    def AMUL(self, out, in_, c, r, w):
        self.R.act(lambda e: e.mul(out=out, in_=in_, mul=c), r, w)

    def DCOPY(self, out, in_, r, w):
        self.R.dve(lambda e: e.tensor_copy(out=out, in_=in_), r, w)

    def TS(self, out, in0, s1, s2, op0, op1, r, w):
        if op1 is None:
            self.R.dve(lambda e: e.tensor_scalar(out=out, in0=in0, scalar1=s1, scalar2=None, op0=op0), r, w)
        else:
            self.R.dve(lambda e: e.tensor_scalar(out=out, in0=in0, scalar1=s1, scalar2=s2, op0=op0, op1=op1), r, w)

    def TT(self, out, in0, in1, op, r, w):
        self.R.dve(lambda e: e.tensor_tensor(out=out, in0=in0, in1=in1, op=op), r, w)

    def STT(self, out, in0, scalar, in1, op0, op1, r, w):
        self.R.dve(lambda e: e.scalar_tensor_tensor(out=out, in0=in0, scalar=scalar, in1=in1, op0=op0, op1=op1), r, w)

    def LOAD(self, q, out, in_, w, stream, r=()):
        return self.R.dma(q, lambda e: e.dma_start(out=out, in_=in_), r=r, w=w, stream=stream)

    def STORE(self, q, out, in_, r, stream, w=()):
        i = self.R.dma(q, lambda e: e.dma_start(out=out, in_=in_), r=r, w=w, stream=stream)
        return i

    def dump(self, name, ap, shape, dt, rname):
        if self.dbg is None or name not in self.dbg or getattr(self, "cur_unit", None) != getattr(self, "dbg_unit", None):
            return
        d = self.nc.dram_tensor("dbg_" + name, list(shape), dt, kind="ExternalOutput").ap()
        self.stores.append(self.STORE("sp", d, ap, [rname], "dbg_" + name))
        self.dumps.append(name)

    def barrier(self):
        bt = self.bartile
        self.R.barrier(lambda e: e.memset(bt, 0.0))

    def fence(self, reads, writes):
        bt = self.bartile
        self.R.dve(lambda e: e.memset(bt, 0.0), [], list(reads) + list(writes) + ["bartile"])

    def setup(self, T):
        self.T = T
        A = self.alloc
        self.cf = A(NCF, F32)
        self.cb = A(NCB)
        self.ident = self.cb[:, B_ID:B_ID + 128]
        self.onesblk = self.cb[:, B_OB:B_OB + 128]
        self.ones = self.cb[:, B_ONE:B_ONE + 128]
        self.Mf = self.cb[:, B_MF:B_MF + 128]
        self.Mb = self.cb[:, B_MB:B_MB + 128]
        self.wa2 = [A(512), A(512)]
        self.ba = [A(512), A(512)]
        self.wlr = A(256).rearrange("p (k n) -> p k n", k=8)
        self.slabI = A(5120)
        self.S = {"f": A(512, F32), "b": A(512, F32)}
        self.Sbf = A(512)
        self.Ssave = [A(512, F32) for _ in range(4)]
        self.xs = [A(1024, F32) for _ in range(2)]
        self.jk = [A(1024) for _ in range(2)]
        self.xb = [A(1024) for _ in range(2)]
        self.xTt = [A(1024).rearrange("p (k n) -> p k n", k=8) for _ in range(2)]
        self.st = [A(8, F32) for _ in range(8)]
        self.wsl = [A(4096) for _ in range(4)]
        self.PT = [A(896) for _ in range(2)]
        self.ona = [A(512) for _ in range(2)]
        self.bartile = A(2, F32)
        self.tmp0 = self.top = (self.top + 1) // 2 * 2
        self.tf = [A(512, F32) for _ in range(8)]
        self.tb = [A(512) for _ in range(10)]
        assert self.top - self.tmp0 == 13312
        self.base = self.top
        L = self.LOAD
        L("sp", self.cf, T["cf"][:, :], ["cf"], "cf")
        L("pool", self.cb, T["cb"][:, :], ["cb"], "cb")
        for d, nm in enumerate(("f", "b")):
            L("pool", self.wa2[d][0:16, :], T["w_a2_" + nm][:, :], ["wa2"], "wa2" + nm)
            L("pool", self.ba[d][0:1, :], T["b_a_" + nm][:, :], ["ba"], "ba" + nm)
        L("pool", self.wlr, T["w_in"][:, 3584:3616].rearrange("(k p) n -> p k n", p=128), ["wlr"], "wlr")

    def tF(self):
        i = self.rot("tf", 8)
        return self.tf[i], f"tf{i}"

    def tB(self):
        i = self.rot("tb", 10)
        return self.tb[i], f"tb{i}"

    def stt_(self):
        i = self.rot("st", 8)
        return self.st[i], f"st{i}"

    def wload(self, si, cols_ap, ncols, kchunks=8):
        sl, nm = self.wsl[si], f"wsl{si}"
        v = sl[:, 0:kchunks * ncols].rearrange("p (k n) -> p k n", k=kchunks)
        if getattr(self, "nowload", False) and self.cnt.get("wl%d" % si, 0) > 0:
            return v, nm
        self.cnt["wl%d" % si] = 1
        self.LOAD("pool", v, cols_ap.rearrange("(k p) n -> p k n", p=128), [nm], nm)
        return v, nm

    def xload(self, src_rows):
        i = self.rot("xs", 2)
        self.LOAD("sp", self.xs[i], src_rows, [f"xs{i}"], f"xs{i}")
        return self.xs[i], f"xs{i}"

    def norm_A(self, src, sname, wbc_off, maskcol=None):
        j = self.rot("jk", 2)
        jk, jn = self.jk[j], f"jk{j}"
        xb, xn = self.xb[j], f"xb{j}"
        st, sn = self.stt_()
        cf = self.cf
        self.ACT(jk, src, AF.Square, [sname], [jn, sn], accum=st[:, 0:1])
        self.ACT(st[:, 2:3], st[:, 0:1], AF.Ln, [sn, "cf"], [sn], scale=1.0 / D, bias=cf[:, C_EPS:C_EPS + 1])
        self.ACT(st[:, 3:4], st[:, 2:3], AF.Exp, [sn], [sn], scale=-0.5)
        if maskcol is not None:
            self.TT(st[:, 3:4], st[:, 3:4], cf[:, maskcol:maskcol + 1], ALU.mult, [sn, "cf"], [sn])
        self.STT(xb, src, st[:, 3:4], cf[:, wbc_off:wbc_off + D], ALU.mult, ALU.mult, [sname, sn, "cf"], [xn])
        return xb, xn

    def norm_B(self, xb, xn, dst3, dname):
        pt, pn = self.pt_new()
        for kc in range(8):
            self.TR_(pt[:, kc * 128:(kc + 1) * 128], xb[:, kc * 128:(kc + 1) * 128], [xn], [pn])
        src3 = pt.rearrange("p (a b) -> p a b", a=8)
        self.DCOPY(dst3[:, 0:4, :], src3[:, 0:4, :], [pn], [dname])
        self.ACOPY(dst3[:, 4:8, :], src3[:, 4:8, :], [pn], [dname])

    def norm_T(self, src, sname, wbc_off, dst3, dname, maskcol=None):
        xb, xn = self.norm_A(src, sname, wbc_off, maskcol)
        self.norm_B(xb, xn, dst3, dname)

    def gates(self, lrT, lrn, d, bank=None):
        ps, pn = self.ps_new(bank)
        self.MM(ps, lrT, self.wa2[d][0:16, :], True, False, [lrn, "wa2"], [pn])
        self.MM(ps, self.ones[0:1, :], self.ba[d][0:1, :], False, True, ["cb", "ba"], [pn])
        gp, gn = self.tF()
        self.ACT(gp, ps, AF.Exp, [pn], [gn], scale=-1.0)
        self.ACT(gp, gp, AF.Ln, [gn, "cb"], [gn], bias=self.ones[:, 0:1])
        return gp, gn

    def tok_kv(self, xT3, xname, wk, wkn, wv, wvn):
        psk, pkn = self.ps_new()
        for kc in range(8):
            self.MM(psk, xT3[:, kc, :], wk[:, kc, :], kc == 0, kc == 7, [xname, wkn], [pkn])
        psv, pvn = self.ps_new()
        for kc in range(8):
            self.MM(psv, xT3[:, kc, :], wv[:, kc, :], kc == 0, kc == 7, [xname, wvn], [pvn])
        vt, vn = self.tB()
        self.ACOPY(vt, psv, [pvn], [vn])
        return psk, pkn, vt, vn

    def state_prep(self, gp, gn, psk, pkn, d, banks=(None, None)):
        cf = self.cf
        U = cf[:, C_UGT:C_UGT + 128] if d == "f" else cf[:, C_ULT:C_ULT + 128]
        psc, pcn = self.ps_new(banks[0])
        self.MM(psc, U, gp, True, True, ["cf", gn], [pcn])
        Ec, en = self.tF()
        self.ACT(Ec, psc, AF.Exp, [pcn], [en])
        kt, kn = self.tB()
        self.TT(kt, psk, Ec, ALU.mult, [pkn, en], [kn])
        pse, pen = self.ps_new(banks[1])
        for hh in range(4):
            self.MM(pse[:, hh:hh + 1], gp[:, hh * 128:(hh + 1) * 128], cf[:, C_NEG:C_NEG + 1], True, True,
                    [gn, "cf"], [pen])
        st, sn = self.stt_()
        self.ACT(st[:, 0:4], pse[:, 0:4], AF.Exp, [pen], [sn])
        return kt, kn, st, sn

    def state_apply(self, kt, kn, st, sn, vt, vn, d, snap=None, snapn=None, bank=None):
        S, Sn = self.S[d], "S" + d
        psd, pdn = self.ps_new(bank)
        for hh in range(4):
            sl = slice(hh * 128, (hh + 1) * 128)
            self.MM(psd[:, sl], kt[:, sl], vt[:, sl], True, True, [kn, vn], [pdn])
        S3 = S.rearrange("p (h v) -> p h v", h=4)
        self.TT(S3, S3, st[:, 0:4].rearrange("p (h o) -> p h o", o=1).to_broadcast([128, 4, 128]), ALU.mult,
                [Sn, sn], [Sn])
        if snap is not None:
            self.ACOPY(snap, S, [Sn], [snapn])
        self.TT(S, S, psd, ALU.add, [Sn, pdn], [Sn])

    def state_update(self, gp, gn, psk, pkn, vt, vn, d, snap=None, snapn=None):
        kt, kn, st, sn = self.state_prep(gp, gn, psk, pkn, d)
        self.state_apply(kt, kn, st, sn, vt, vn, d, snap, snapn)

    def zero_state(self, d):
        S = self.S[d]
        self.R.dve(lambda e: e.memset(S, 0.0), [], ["S" + d])

    def scan_pipe(self, d, n, p1, wk, wkn, wv, wvn, lr_of=None, snap_of=None, save_after=None, p1a=None):
        di = 0 if d == "f" else 1
        s1, s2, s3 = {}, {}, {}
        ksb_pool = [self.slabI[:, j * 1024:(j + 1) * 1024].bitcast(F32) for j in range(3)]

        def P2(i):
            xT3, xn = s1.pop(i)
            psk, pkn = self.ps_new(0)
            for kc in range(8):
                self.MM(psk, xT3[:, kc, :], wk[:, kc, :], kc == 0, kc == 7, [xn, wkn], [pkn])
            j = self.rot("ksb", 3)
            ksb, ksn = ksb_pool[j], f"ksb{j}"
            self.ACOPY(ksb, psk, [pkn], [ksn])
            psv, pvn = self.ps_new(1)
            for kc in range(8):
                self.MM(psv, xT3[:, kc, :], wv[:, kc, :], kc == 0, kc == 7, [xn, wvn], [pvn])
            vt, vn = self.tB()
            self.DCOPY(vt, psv, [pvn], [vn])
            if lr_of is None:
                psl, pln = self.ps_new(2)
                for kc in range(8):
                    self.MM(psl[0:16, 0:128], self.wlr[:, kc, 16 * di:16 * di + 16], xT3[:, kc, :], kc == 0, kc == 7,
                            ["wlr", xn], [pln])
                lrt, lrn = self.tB()
                self.DCOPY(lrt[0:16, 0:128], psl[0:16, 0:128], [pln], [lrn])
                lr = (lrt[0:16, 0:128], lrn)
            else:
                lr = lr_of(i)
            s2[i] = (ksb, ksn, vt, vn, lr)

        def P3a(i):
            ksb, ksn, vt, vn, lr = s2.pop(i)
            gp, gn = self.gates(lr[0], lr[1], di, bank=3)
            s3[i] = (ksb, ksn, vt, vn, gp, gn)

        def P3b(i):
            ksb, ksn, vt, vn, gp, gn = s3.pop(i)
            kt, kn, st, sn = self.state_prep(gp, gn, ksb, ksn, d, banks=(4, 5))
            sp = snap_of(i) if snap_of else None
            self.state_apply(kt, kn, st, sn, vt, vn, d, sp[0] if sp else None, sp[1] if sp else None, bank=5)
            if save_after and i in save_after:
                dst, dn = save_after[i]
                S = self.S[d]
                self.R.dve(lambda e, dst=dst, S=S: e.tensor_copy(out=dst, in_=S), ["S" + d], [dn])

        s0 = {}
        for it in range(n + 4):
            if 0 <= it - 4 < n:
                P3b(it - 4)
            if 0 <= it - 3 < n:
                P3a(it - 3)
            if 0 <= it - 2 < n:
                P2(it - 2)
            if 0 <= it - 1 < n:
                s1[it - 1] = p1(it - 1, s0.pop(it - 1, None))
            if it < n and p1a is not None:
                s0[it] = p1a(it)

    def state_scan(self, xsrc, taus, d, wk, wkn, wv, wvn, saves=None):
        def p1a(i):
            tau = taus[i]
            xs, xn = self.xload(xsrc[tau * 128:(tau + 1) * 128, :])
            return self.norm_A(xs, xn, C_WBC1)

        def p1(i, tok):
            j = self.rot("xTt", 2)
            xT3, xTn = self.xTt[j], f"xTt{j}"
            self.norm_B(tok[0], tok[1], xT3, xTn)
            return xT3, xTn
        sa = None
        if saves:
            sa = {i: saves[t] for i, t in enumerate(taus) if t in saves}
        self.scan_pipe(d, len(taus), p1, wk, wkn, wv, wvn, save_after=sa, p1a=p1a)

    def unit(self, xsrc, kv0, ut, yout, hscr, Sb_init, uname):
        R, T, cf = self.R, self.T, self.cf
        A = self.alloc
        self.top = self.base
        self.cur_unit = uname
        w_in = T["w_in"]
        xnT = A(8 * TK).rearrange("p (k n) -> p k n", k=8)
        self.LOAD("pool", self.slabI, T["slabI"][:, :], ["slabI"] + [f"ksb{j}" for j in range(3)], "slabI")
        prev = None
        for t in range(14):
            cur = None
            if t < 13:
                xs, xn = self.xload(xsrc[(kv0 + t) * 128:(kv0 + t + 1) * 128, :])
                cur = self.norm_A(xs, xn, C_WBC1)
            if prev is not None:
                self.norm_B(prev[0], prev[1], xnT[:, :, (t - 1) * 128:t * 128], f"xnT{t - 1}")
            prev = cur
        self.dump("xnT", xnT, [128, 8, TK], BF16, "xnT12")
        if getattr(self, "stop_after", None) == "U1":
            self.R.dve(lambda e: e.memset(self.Sfx, 0.0), [], ["Sfx"])
            self.barrier()
            return

        def xr(t0, n, off):
            a = (off + t0) // 128
            b = (off + t0 + n - 1) // 128
            return [f"xnT{t}" for t in range(a, b + 1)]

        m1 = self.top
        onaT = A(4 * TR).rearrange("p (k n) -> p k n", k=4)
        KT = A(4 * TK).rearrange("p (k n) -> p k n", k=4)
        QT = A(4 * TR).rearrange("p (k n) -> p k n", k=4)
        V = A(13 * 8 * 65)
        V4 = V.rearrange("p (t h d) -> p t h d", t=13, h=8)

        hn_pending = []

        def hn_finish():
            while hn_pending:
                (ps, pn, n, nwc, biasc, dst, dname, sq, sqn) = hn_pending.pop(0)
                ps2, p2n = self.ps_new()
                self.MM(ps2[:, :n], self.onesblk, sq[:, :n], True, True, ["cb", sqn], [p2n])
                r1, r1n = self.tF()
                self.ACT(r1[:, :n], ps2[:, :n], AF.Ln, [p2n, "cf"], [r1n], scale=1.0 / 64, bias=cf[:, C_EPS:C_EPS + 1])
                if biasc is None:
                    self.ACT(r1[:, :n], r1[:, :n], AF.Exp, [r1n], [r1n], scale=-0.5)
                else:
                    self.ACT(r1[:, :n], r1[:, :n], AF.Exp, [r1n, "cf"], [r1n], scale=-0.5, bias=cf[:, biasc:biasc + 1])
                self.STT(dst, ps[:, :n], cf[:, nwc:nwc + 1], r1[:, :n], ALU.mult, ALU.mult, [pn, r1n, "cf"], [dname])

        def headnorm(ps, pn, n, nwc, biasc, dst, dname):
            sq, sqn = self.tB()
            self.ACT(sq[:, :n], ps[:, :n], AF.Square, [pn], [sqn])
            hn_pending.append((ps, pn, n, nwc, biasc, dst, dname, sq, sqn))

        w, wn = self.wload(0, w_in[:, 512:1024], 512)
        for p in range(4):
            for (t0, n) in blocks(TK):
                ps, pn = self.ps_new()
                for kc in range(8):
                    self.MM(ps[:, :n], w[:, kc, p * 128:(p + 1) * 128], xnT[:, kc, t0:t0 + n], kc == 0, kc == 7,
                            xr(t0, n, 0) + [wn], [pn])
                hn_finish()
                headnorm(ps, pn, n, C_KNW, None, KT[:, p, t0:t0 + n], "KT")
        w, wn = self.wload(2, w_in[:, 0:512], 512)
        for p in range(4):
            for (t0, n) in blocks(TR):
                ps, pn = self.ps_new()
                for kc in range(8):
                    self.MM(ps[:, :n], w[:, kc, p * 128:(p + 1) * 128], xnT[:, kc, 256 + t0:256 + t0 + n], kc == 0,
                            kc == 7, xr(t0, n, 256) + [wn], [pn])
                hn_finish()
                headnorm(ps, pn, n, C_QNW, C_LNQ, QT[:, p, t0:t0 + n], "QT")
        w, wn = self.wload(0, w_in[:, 1024:1536], 512)
        R.pool(lambda e: e.memset(V, 1.0), [], ["V"])
        hn_first_v = True
        for t in range(13):
            ps, pn = self.ps_new()
            for kc in range(8):
                self.MM(ps, xnT[:, kc, t * 128:(t + 1) * 128], w[:, kc, :], kc == 0, kc == 7, [f"xnT{t}", wn], [pn])
            if hn_first_v:
                hn_finish()
                hn_first_v = False
            self.ACOPY(V4[:, t, :, 0:64], ps.rearrange("p (h d) -> p h d", h=8), [pn], ["V"])
        self.dump("KT", KT, [128, 4, TK], BF16, "KT")
        self.dump("QT", QT, [128, 4, TR], BF16, "QT")
        self.dump("V", V, [128, 13 * 8 * 65], BF16, "V")
        if getattr(self, "stop_after", None) == "U2":
            self.R.dve(lambda e: e.memset(self.Sfx, 0.0), [], ["Sfx"])
            self.barrier()
            return

        onaAll = A(9 * 512).rearrange("p (i n) -> p i n", i=9)
        edge_qps = []
        if ut["top"] is not None:
            edge_qps += [(i, ut["top"][EDGE_TOP.index(i)]) for i in EDGE_TOP]
        if ut["bot"] is not None:
            edge_qps += [(i, ut["bot"][EDGE_BOT.index(i)]) for i in EDGE_BOT]
        eidx = {i: n_ for n_, (i, _) in enumerate(edge_qps)}
        kvl = {i: kv_list(i, i in eidx) for i in range(9)}
        for h in range(8):
            p, bp = h // 2, 64 * (h % 2)
            es_i = 0 if h % 2 == 0 else 2
            es, esn = self.wsl[es_i], f"wsl{es_i}"
            for n_, (i, src) in enumerate(edge_qps):
                self.LOAD("pool", es[:, n_ * 896:(n_ + 1) * 896], src[:, h * 896:(h + 1) * 896], [esn], f"{esn}_{n_}")
            OX, OXn = self.ps_new(4)
            OY, OYn = self.ps_new(5)
            zer = self.cb[:, B_ZERO:B_ZERO + 128]
            self.MM(OX[:, 0:455], zer, self.slabI[:, 0:455], True, False, ["cb", "slabI"], [OXn])
            self.MM(OY[:, 0:130], zer, self.slabI[:, 0:130], True, False, ["cb", "slabI"], [OYn])
            lastX = max(kvl[i][-1] for i in range(7))
            lastY = max(kvl[i][-1] for i in (7, 8))
            def S1(t):
                qs = [i for i in range(9) if t in kvl[i]]
                if not qs:
                    return None
                assert qs == list(range(qs[0], qs[-1] + 1)) and len(qs) <= 7
                bA, bB = (0, 1) if t % 2 == 0 else (2, 3)
                psA, pAn = self.ps_new(bA)
                psB, pBn = self.ps_new(bB)
                qA, qB = qs[:4], qs[4:]
                for (qq, ps_, pn_) in ((qA, psA, pAn), (qB, psB, pBn)):
                    if not qq:
                        continue
                    nq = len(qq)
                    c = 0
                    while c < nq:
                        i = qq[c]
                        if i in eidx:
                            c2 = c + 1
                            a_ = kvl[i].index(t)
                            blk = eidx[i] * 896 + a_ * 128
                            b_ap, b_n = es[:, blk:blk + 128], esn
                        else:
                            c2 = c
                            while c2 < nq and qq[c2] not in eidx:
                                c2 += 1
                            blk = (h * 5 + (i - t + 4)) * 128
                            b_ap, b_n = self.slabI[:, blk:blk + (c2 - c) * 128], "slabI"
                        self.MM(ps_[:, c * 128:c2 * 128], KT[bp:bp + 64, p, t * 128:(t + 1) * 128],
                                QT[bp:bp + 64, p, qq[c] * 128:(qq[c2 - 1] + 1) * 128], True, False, ["KT", "QT"], [pn_])
                        self.MM(ps_[:, c * 128:c2 * 128], self.ident, b_ap, False, True, ["cb", b_n], [pn_])
                        c = c2
                pj = self.rot("PT", 2)
                PT, PTn = self.PT[pj], f"PT{pj}"
                self.ACT(PT[:, 0:len(qA) * 128], psA[:, 0:len(qA) * 128], AF.Exp, [pAn], [PTn])
                if qB:
                    self.ACT(PT[:, 512:512 + len(qB) * 128], psB[:, 0:len(qB) * 128], AF.Exp, [pBn], [PTn])
                return (qs, PT, PTn)

            def S2(t, tok):
                qs, PT, PTn = tok
                for c, i in enumerate(qs):
                    if i < 7:
                        od, odn = OX[:, i * 65:(i + 1) * 65], OXn
                    else:
                        od, odn = OY[:, (i - 7) * 65:(i - 6) * 65], OYn
                    is_last = (c == len(qs) - 1 and t == lastY) if i >= 7 else \
                        (t == lastX and i == max(q for q in qs if q < 7))
                    self.MM(od, PT[:, c * 128:(c + 1) * 128], V4[:, t, h, :], False, is_last, [PTn, "V"], [odn])

            tok_prev = S1(0)
            for t in range(13):
                tok_next = S1(t + 1) if t + 1 < 13 else None
                if tok_prev is not None:
                    S2(t, tok_prev)
                tok_prev = tok_next
            for (O_, On_, i0, ni) in ((OX, OXn, 0, 7), (OY, OYn, 7, 2)):
                O3 = O_[:, 0:ni * 65].rearrange("p (i d) -> p i d", i=ni)
                st, sn = self.stt_()
                R.dve(lambda e, st=st, O3=O3, ni=ni: e.reciprocal(out=st[:, 0:ni], in_=O3[:, :, 64]), [On_], [sn])
                self.TT(onaAll[:, i0:i0 + ni, h * 64:(h + 1) * 64], O3[:, :, 0:64],
                        st[:, 0:ni].rearrange("p (i o) -> p i o", o=1).to_broadcast([128, ni, 64]), ALU.mult,
                        [On_, sn], ["onaAll"])
        for i in range(9):
            pt, pn = self.pt_new()
            for c in range(4):
                self.TR_(pt[:, c * 128:(c + 1) * 128], onaAll[:, i, c * 128:(c + 1) * 128], ["onaAll"], [pn])
            self.ACOPY(onaT[:, :, i * 128:(i + 1) * 128], pt[:, 0:512].rearrange("p (a b) -> p a b", a=4), [pn], ["onaT"])
        self.dump("onaT", onaT, [128, 4, TR], BF16, "onaT")
        if getattr(self, "stop_after", None) == "U3":
            self.R.dve(lambda e: e.memset(self.Sfx, 0.0), [], ["Sfx"])
            self.barrier()
            return

        self.fence(["KT", "QT", "V", "onaAll", "PT0", "PT1"], ["qgT", "kgT", "sog", "snap", "ogT"])
        self.top = m1 + 4 * TR
        ogT = A(4 * TR).rearrange("p (k n) -> p k n", k=4)
        m_gla = self.top
        qgT = A(4 * TR).rearrange("p (k n) -> p k n", k=4)
        kgT = A(4 * TR).rearrange("p (k n) -> p k n", k=4)
        lr = [self.wsl[2][:, 0:TR], self.wsl[2][:, TR:2 * TR]]
        sog = A(9 * 512).rearrange("p (t n) -> p t n", t=9)
        snap = A(9 * 512).rearrange("p (t n) -> p t n", t=9)
        w, wn = self.wload(0, w_in[:, 1536:2048], 512)
        for hh in range(4):
            for (t0, n) in blocks(TR):
                ps, pn = self.ps_new()
                for kc in range(8):
                    self.MM(ps[:, :n], w[:, kc, hh * 128:(hh + 1) * 128], xnT[:, kc, 256 + t0:256 + t0 + n], kc == 0,
                            kc == 7, xr(t0, n, 256) + [wn], [pn])
                self.AMUL(qgT[:, hh, t0:t0 + n], ps[:, :n], 128 ** -0.5, [pn], ["qgT"])
        wk, wkn = self.wload(1, w_in[:, 2048:2560], 512)
        for hh in range(4):
            for (t0, n) in blocks(TR):
                ps, pn = self.ps_new()
                for kc in range(8):
                    self.MM(ps[:, :n], wk[:, kc, hh * 128:(hh + 1) * 128], xnT[:, kc, 256 + t0:256 + t0 + n], kc == 0,
                            kc == 7, xr(t0, n, 256) + [wkn], [pn])
                self.DCOPY(kgT[:, hh, t0:t0 + n], ps[:, :n], [pn], ["kgT"])
        for d in range(2):
            for (t0, n) in blocks(TR):
                ps, pn = self.ps_new()
                for kc in range(8):
                    self.MM(ps[0:16, :n], self.wlr[:, kc, 16 * d:16 * d + 16], xnT[:, kc, 256 + t0:256 + t0 + n],
                            kc == 0, kc == 7, xr(t0, n, 256) + ["wlr"], [pn])
                self.ACOPY(lr[d][0:16, t0:t0 + n], ps[0:16, :n], [pn], [f"lr{d}", "wsl2"])
        w, wn = self.wload(0, w_in[:, 3072:3584], 512)
        for i in range(9):
            ps, pn = self.ps_new()
            for kc in range(8):
                self.MM(ps, xnT[:, kc, (i + 2) * 128:(i + 3) * 128], w[:, kc, :], kc == 0, kc == 7,
                        [f"xnT{i + 2}", wn], [pn])
            tg, tgn = self.tF()
            self.ACT(tg, ps, AF.Tanh, [pn], [tgn], scale=0.5)
            self.STT(tg, tg, 1.0, ps, ALU.add, ALU.mult, [tgn, pn], [tgn])
            self.TT(sog[:, i, :], tg, cf[:, C_GNW:C_GNW + 512], ALU.mult, [tgn, "cf"], ["sog"])
        wv, wvn = self.wload(3, w_in[:, 2560:3072], 512)
        if getattr(self, "stop_after", None) == "U5":
            self.R.dve(lambda e: e.memset(self.Sfx, 0.0), [], ["Sfx"])
            self.barrier()
            return

        if Sb_init is None:
            self.zero_state("b")
        else:
            Sb = self.S["b"]
            R.dve(lambda e, Sb=Sb, src=Sb_init[0]: e.tensor_copy(out=Sb, in_=src), [Sb_init[1]], ["Sb"])
        order6 = list(reversed(range(9)))
        self.scan_pipe("b", 9,
                       lambda n_, tok=None: (xnT[:, :, (order6[n_] + 2) * 128:(order6[n_] + 3) * 128], f"xnT{order6[n_] + 2}"),
                       wk, wkn, wv, wvn,
                       lr_of=lambda n_: (lr[1][0:16, order6[n_] * 128:(order6[n_] + 1) * 128], "lr1"),
                       snap_of=lambda n_: (snap[:, order6[n_], :], "snap"))
        self.fence(["ksb0", "ksb1", "ksb2"], [f"ded{a_}{b_}" for a_ in range(2) for b_ in range(4)])
        Sf = self.S["f"]
        ded = [[self.slabI[:, (par * 4 + q) * 512:(par * 4 + q + 1) * 512] for q in range(4)] for par in range(2)]
        m1out = {}

        def M1(i):
            par = i % 2
            xT3 = xnT[:, :, (i + 2) * 128:(i + 3) * 128]
            psk, pkn, vt, vn = self.tok_kv(xT3, f"xnT{i + 2}", wk, wkn, wv, wvn)
            qts, As = [], []
            gpf = self.gates(lr[0][0:16, i * 128:(i + 1) * 128], "lr0", 0)
            ktf, ktfn, stf, stfn = self.state_prep(gpf[0], gpf[1], psk, pkn, "f")
            for d in range(2):
                if d == 0:
                    gp, gn = gpf
                else:
                    gp, gn = self.gates(lr[d][0:16, i * 128:(i + 1) * 128], f"lr{d}", d)
                Ufm = cf[:, C_UINC:C_UINC + 128] if d == 0 else cf[:, C_ULT:C_ULT + 128]
                psp, ppn = self.ps_new()
                for hh in range(4):
                    sl = slice(hh * 128, (hh + 1) * 128)
                    self.MM(psp[:, sl], gp[:, sl], Ufm, True, True, [gn, "cf"], [ppn])
                Eq, eqn = self.tF()
                Ek, ekn = self.tF()
                self.ACT(Eq, psp, AF.Exp, [ppn], [eqn], scale=(1.0 if d == 0 else -1.0))
                self.ACT(Ek, psp, AF.Exp, [ppn], [ekn], scale=(-1.0 if d == 0 else 1.0))
                qt, qn = ded[par][d], f"ded{par}{d}"
                kt2, k2n = self.tB()
                q3 = qt.rearrange("p (h t) -> p h t", h=4)
                k3 = kt2.rearrange("p (h t) -> p h t", h=4)
                self.TT(q3, qgT[:, :, i * 128:(i + 1) * 128], Eq.rearrange("p (h t) -> p h t", h=4), ALU.mult,
                        ["qgT", eqn], [qn])
                self.TT(k3, kgT[:, :, i * 128:(i + 1) * 128], Ek.rearrange("p (h t) -> p h t", h=4), ALU.mult,
                        ["kgT", ekn], [k2n])
                psa, pan = self.ps_new()
                for hh in range(4):
                    sl = slice(hh * 128, (hh + 1) * 128)
                    self.MM(psa[:, sl], kt2[:, sl], qt[:, sl], True, True, [k2n, qn], [pan])
                Ad, adn = ded[par][2 + d], f"ded{par}{2 + d}"
                Mm = self.Mf if d == 0 else self.Mb
                self.TT(Ad.rearrange("p (h t) -> p h t", h=4), psa.rearrange("p (h t) -> p h t", h=4),
                        Mm.rearrange("p (o t) -> p o t", o=1).to_broadcast([128, 4, 128]), ALU.mult, [pan, "cb"], [adn])
                qts.append((qt, qn))
                As.append((Ad, adn))
            m1out[i] = (vt, vn, ktf, ktfn, stf, stfn, qts, As)

        def M2(i):
            vt, vn, ktf, ktfn, stf, stfn, qts, As = m1out.pop(i)
            if i == 8:
                exp_ = self.Sfx
                R.dve(lambda e, exp_=exp_, Sf=Sf: e.tensor_copy(out=exp_, in_=Sf), ["Sf"], ["Sfx"])
            self.ACOPY(self.Sbf, Sf, ["Sf"], ["Sbf"])
            pso, pon = self.ps_new()
            for hh in range(4):
                sl = slice(hh * 128, (hh + 1) * 128)
                self.MM(pso[:, sl], qts[0][0][:, sl], self.Sbf[:, sl], True, False, [qts[0][1], "Sbf"], [pon])
                self.MM(pso[:, sl], As[0][0][:, sl], vt[:, sl], False, False, [As[0][1], vn], [pon])
                self.MM(pso[:, sl], qts[1][0][:, sl], snap[:, i, sl], False, False, [qts[1][1], "snap"], [pon])
                self.MM(pso[:, sl], As[1][0][:, sl], vt[:, sl], False, True, [As[1][1], vn], [pon])
            self.state_apply(ktf, ktfn, stf, stfn, vt, vn, "f")
            st, sn = self.stt_()
            jj = self.rot("jk", 2)
            for hh in range(4):
                sl = slice(hh * 128, (hh + 1) * 128)
                self.ACT(self.jk[jj][:, sl], pso[:, sl], AF.Square, [pon], [f"jk{jj}", sn], accum=st[:, hh:hh + 1])
            self.TS(st[:, 4:8], st[:, 0:4], 1.0 / 128, EPS, ALU.mult, ALU.add, [sn], [sn])
            self.ACT(st[:, 4:8], st[:, 4:8], AF.Ln, [sn], [sn])
            self.ACT(st[:, 4:8], st[:, 4:8], AF.Exp, [sn], [sn], scale=-0.5)
            self.TS(st[:, 4:8], st[:, 4:8], 0.5, None, ALU.mult, None, [sn], [sn])
            og, ogn = self.tF()
            self.TT(og.rearrange("p (h t) -> p h t", h=4), pso.rearrange("p (h t) -> p h t", h=4),
                    st[:, 4:8].rearrange("p (h o) -> p h o", o=1).to_broadcast([128, 4, 128]), ALU.mult, [pon, sn], [ogn])
            ogb, obn = self.tB()
            self.TT(ogb, og, sog[:, i, :], ALU.mult, [ogn, "sog"], [obn])
            pt, pn = self.pt_new()
            for c in range(4):
                self.TR_(pt[:, c * 128:(c + 1) * 128], ogb[:, c * 128:(c + 1) * 128], [obn], [pn])
            self.ACOPY(ogT[:, :, i * 128:(i + 1) * 128], pt[:, 0:512].rearrange("p (a b) -> p a b", a=4), [pn], ["ogT"])

        M1(0)
        for i in range(9):
            if i + 1 < 9:
                M1(i + 1)
            M2(i)
        self.dump("ogT", ogT, [128, 4, TR], BF16, "ogT")
        if getattr(self, "stop_after", None) == "U7":
            self.R.dve(lambda e: e.memset(self.Sfx, 0.0), [], ["Sfx"])
            self.barrier()
            return

        self.fence(["qgT", "kgT", "sog", "snap", "lr0", "lr1"], ["mixT", "wsl2"] + [f"hnT{i_}" for i_ in range(9)])
        self.top = m_gla
        mixT = A(8 * TR).rearrange("p (k n) -> p k n", k=8)
        wna, wnan = self.wload(0, T["w_na_proj"][:, :], 1024, kchunks=4)
        wgl, wgln = self.wload(1, T["w_gla_proj"][:, :], 1024, kchunks=4)
        for hp in range(2):
            g1, g1n = self.wload(2, w_in[:, 3616 + 512 * hp:3616 + 512 * hp + 512], 512)
            g2, g2n = self.wload(3, w_in[:, 4640 + 512 * hp:4640 + 512 * hp + 512], 512)
            for c4 in range(4):
                c = hp * 4 + c4
                for (t0, n) in blocks(TR):
                    ps1, p1n = self.ps_new()
                    for kk in range(4):
                        self.MM(ps1[:, :n], wna[:, kk, c * 128:(c + 1) * 128], onaT[:, kk, t0:t0 + n], kk == 0, kk == 3,
                                [wnan, "onaT"], [p1n])
                    ps2, p2n = self.ps_new()
                    for kk in range(4):
                        self.MM(ps2[:, :n], wgl[:, kk, c * 128:(c + 1) * 128], ogT[:, kk, t0:t0 + n], kk == 0, kk == 3,
                                [wgln, "ogT"], [p2n])
                    ps3, p3n = self.ps_new()
                    for kc in range(8):
                        self.MM(ps3[:, :n], g1[:, kc, c4 * 128:(c4 + 1) * 128], xnT[:, kc, 256 + t0:256 + t0 + n],
                                kc == 0, kc == 7, xr(t0, n, 256) + [g1n], [p3n])
                    ps4, p4n = self.ps_new()
                    for kc in range(8):
                        self.MM(ps4[:, :n], g2[:, kc, c4 * 128:(c4 + 1) * 128], xnT[:, kc, 256 + t0:256 + t0 + n],
                                kc == 0, kc == 7, xr(t0, n, 256) + [g2n], [p4n])
                    t1, t1n = self.tF()
                    t2, t2n = self.tF()
                    self.ACT(t1[:, :n], ps3[:, :n], AF.Tanh, [p3n], [t1n], scale=0.5)
                    self.ACT(t2[:, :n], ps4[:, :n], AF.Tanh, [p4n], [t2n], scale=0.5)
                    self.STT(t1[:, :n], t1[:, :n], 1.0, ps1[:, :n], ALU.add, ALU.mult, [t1n, p1n], [t1n])
                    self.STT(t2[:, :n], t2[:, :n], 1.0, ps2[:, :n], ALU.add, ALU.mult, [t2n, p2n], [t2n])
                    self.TT(mixT[:, c, t0:t0 + n], t1[:, :n], t2[:, :n], ALU.add, [t1n, t2n], ["mixT"])
        self.dump("mixT", mixT, [128, 8, TR], BF16, "mixT")
        if getattr(self, "stop_after", None) == "U8a":
            self.R.dve(lambda e: e.memset(self.Sfx, 0.0), [], ["Sfx"])
            self.barrier()
            return
        hnT = A(8 * TR).rearrange("p (k n) -> p k n", k=8)
        wo = []
        for half in range(2):
            wo.append(self.wload(half, T["w_out"][:, half * 512:(half + 1) * 512], 512))
        prevB = None
        for i in range(10):
            curB = None
            if i < 9:
                xs, xn = self.xload(xsrc[(kv0 + 2 + i) * 128:(kv0 + 3 + i) * 128, :])
                for half in range(2):
                    ps, pn = self.ps_new()
                    for kc in range(8):
                        self.MM(ps, mixT[:, kc, i * 128:(i + 1) * 128], wo[half][0][:, kc, :], kc == 0, kc == 7,
                                ["mixT", wo[half][1]], [pn])
                    sl = slice(half * 512, (half + 1) * 512)
                    self.STT(xs[:, sl], ps, 0.5, xs[:, sl], ALU.mult, ALU.add, [pn, xn], [xn])
                self.STORE("sp", hscr[i * 128:(i + 1) * 128, :], xs, [xn], "hst_" + xn, w=[f"hscr{i}"])
                mc = None
                if i == 0:
                    mc = C_MASK + 2 * ut["mcol"]
                if i == 8:
                    mc = C_MASK + 2 * ut["mcol"] + 1
                curB = self.norm_A(xs, xn, C_WBC2, maskcol=mc) + (i,)
            if prevB is not None:
                self.norm_B(prevB[0], prevB[1], hnT[:, :, prevB[2] * 128:(prevB[2] + 1) * 128], f"hnT{prevB[2]}")
            prevB = curB
        self.dump("hnT", hnT, [128, 8, TR], BF16, "hnT8")
        if getattr(self, "stop_after", None) == "U8b":
            self.R.dve(lambda e: e.memset(self.Sfx, 0.0), [], ["Sfx"])
            self.barrier()
            return

        self.fence([f"xnT{t_}" for t_ in range(13)] + ["onaT", "ogT", "mixT"]
                   + [f"ded{a_}{b_}" for a_ in range(2) for b_ in range(4)]
                   + [f"tf{t_}" for t_ in range(8)] + [f"tb{t_}" for t_ in range(10)],
                   ["fT", "u0", "u1", "cv0", "cv1", "gq", "wdC", "wdT"])
        hn_names = [f"hnT{i}" for i in range(9)]
        self.top = self.base
        fT = A(22 * 1024).rearrange("p (j n) -> p j n", j=22)
        tmp = self.arena[:, self.tmp0:self.tmp0 + 13312]
        u = [self.slabI[:, 0:2304].bitcast(F32), self.slabI[:, 2304:4608].bitcast(F32)]
        cv = [tmp[:, 0:2048].bitcast(F32), tmp[:, 2048:4096].bitcast(F32)]
        gq = tmp[:, 4096:6144].bitcast(F32)
        wdT = tmp[:, 6144:13312].rearrange("p (j n) -> p j n", j=7)
        wdL = tmp[:, 0:6144].rearrange("p (j n) -> p j n", j=6)
        wdC = A(9 * 1024).rearrange("p (j n) -> p j n", j=9)
        assert self.top <= self.base + 8 * TK + 4 * TR + 4 * TR + 8 * TR, "fT/wdC would overlap hnT"
        wdr = T["w_down"].rearrange("(j p) n -> p j n", p=128)
        w_up = T["w_up"]
        for j in range(22):
            si_ = j % 4
            sl_, wn = self.wsl[si_], f"wsl{si_}"
            wv2 = sl_[:, 0:2048].rearrange("p (k n) -> p k n", k=8)
            self.LOAD("pool", wv2[:, :, 0:128], w_up[:, j * 128:(j + 1) * 128].rearrange("(k p) n -> p k n", p=128),
                      [wn], wn + "a")
            self.LOAD("pool", wv2[:, :, 128:256],
                      w_up[:, 2816 + j * 128:2816 + (j + 1) * 128].rearrange("(k p) n -> p k n", p=128), [wn], wn + "b")
            if j == 3:
                for g in range(0, 9, 3):
                    self.LOAD("pool", wdC[:, g:g + 3, :], wdr[:, g:g + 3, :], ["wdC"], f"wdC{g}")
                self.LOAD("pool", wdT[:, 0:4, :], wdr[:, 9:13, :], ["wdT"], "wdT0")
                self.LOAD("pool", wdT[:, 4:7, :], wdr[:, 13:16, :], ["wdT"], "wdT4")
            for ab in range(2):
                for (t0, n) in blocks(TR):
                    ps, pn = self.ps_new()
                    for kc in range(8):
                        self.MM(ps[:, :n], wv2[:, kc, ab * 128:(ab + 1) * 128], hnT[:, kc, t0:t0 + n], kc == 0, kc == 7,
                                [wn] + hn_names, [pn])
                    self.ACOPY(u[ab][:, t0:t0 + n], ps[:, :n], [pn], [f"u{ab}"])
                jj = ab * 22 + j
                cw = lambda tap, jj=jj: cf[:, C_CW + jj * 3 + tap:C_CW + jj * 3 + tap + 1]
                self.ACT(cv[ab], u[ab][:, 63:1087], AF.Identity, [f"u{ab}", "cf"], [f"cv{ab}"], scale=cw(0),
                         bias=cf[:, C_CB + jj:C_CB + jj + 1])
                self.STT(cv[ab], u[ab][:, 64:1088], cw(1), cv[ab], ALU.mult, ALU.add, [f"u{ab}", "cf", f"cv{ab}"],
                         [f"cv{ab}"])
                self.STT(cv[ab], u[ab][:, 65:1089], cw(2), cv[ab], ALU.mult, ALU.add, [f"u{ab}", "cf", f"cv{ab}"],
                         [f"cv{ab}"])
            self.ACT(gq, cv[0], AF.Square, ["cv0"], ["gq"])
            self.TS(gq, gq, 0.044715, 1.0, ALU.mult, ALU.add, ["gq"], ["gq"])
            self.TT(gq, gq, cv[0], ALU.mult, ["gq", "cv0"], ["gq"])
            self.ACT(gq, gq, AF.Tanh, ["gq"], ["gq"], scale=0.7978845608028654)
            self.STT(gq, gq, 1.0, cv[0], ALU.add, ALU.mult, ["gq", "cv0"], ["gq"])
            self.TT(fT[:, j, :], gq, cv[1], ALU.mult, ["gq", "cv1"], ["fT"])
        self.dump("fT", fT, [128, 22, 1024], BF16, "fT")
        if getattr(self, "stop_after", None) == "U10":
            self.R.dve(lambda e: e.memset(self.Sfx, 0.0), [], ["Sfx"])
            self.barrier()
            return

        self.fence(["cv0", "cv1", "gq"], ["wdL"])
        self.LOAD("pool", wdL[:, 0:3, :], wdr[:, 16:19, :], ["wdL"], "wdL0")
        self.LOAD("pool", wdL[:, 3:6, :], wdr[:, 19:22, :], ["wdL"], "wdL3")

        def wdj(j):
            if j < 9:
                return wdC[:, j], "wdC"
            if j < 16:
                return wdT[:, j - 9], "wdT"
            return wdL[:, j - 16], "wdL"
        for i8 in range(8):
            hi_ = self.rot("xs", 2)
            hs, hn_ = self.xs[hi_], f"xs{hi_}"
            self.LOAD("sp", hs, hscr[64 + i8 * 128:64 + (i8 + 1) * 128, :], [hn_], hn_,
                      r=[f"hscr{i8}", f"hscr{i8 + 1}"])
            for half in range(2):
                ps, pn = self.ps_new()
                for j in range(22):
                    self.MM(ps, fT[:, j, i8 * 128:(i8 + 1) * 128], wdj(j)[0][:, half * 512:(half + 1) * 512], j == 0,
                            j == 21, ["fT", wdj(j)[1]], [pn])
                sl = slice(half * 512, (half + 1) * 512)
                self.STT(hs[:, sl], ps, 0.5, hs[:, sl], ALU.mult, ALU.add, [pn, hn_], [hn_])
            self.stores.append(self.STORE("sp", yout[i8 * 128:(i8 + 1) * 128, :], hs, [hn_], "yst_" + hn_))
        self.fence(["fT", "wdC", "wdT", "wdL", "u0", "u1", "cv0", "cv1", "gq"] + [f"hnT{t_}" for t_ in range(9)],
                   [f"xnT{t_}" for t_ in range(13)] + ["KT", "QT", "V", "onaT", "onaAll", "slabI"]
                   + [f"tf{t_}" for t_ in range(8)] + [f"tb{t_}" for t_ in range(10)]
                   + [f"ksb{t_}" for t_ in range(3)])

    def sequence(self, xsrc, n_units, tau0, pre, tau_max, uts, yout, hscr):
        T = self.T
        w_in = T["w_in"]
        self.zero_state("f")
        self.zero_state("b")
        post = list(range(tau_max, tau0 + 8, -1))
        saves = {}
        inits = [None] * n_units
        for m in range(n_units):
            need = tau0 + 8 * m + 9
            if need <= tau_max:
                saves[need] = (self.Ssave[m], f"Ssave{m}")
                inits[m] = (self.Ssave[m], f"Ssave{m}")
        if pre or post:
            wk, wkn = self.wload(1, w_in[:, 2048:2560], 512)
            wv, wvn = self.wload(3, w_in[:, 2560:3072], 512)
            if pre:
                self.state_scan(xsrc, pre, "f", wk, wkn, wv, wvn)
            if post:
                self.state_scan(xsrc, post, "b", wk, wkn, wv, wvn, saves=saves)
        self.barrier()
        for m in range(n_units):
            self.unit(xsrc, tau0 + 8 * m - 2, uts[m], yout[m * 1024:(m + 1) * 1024, :], hscr, inits[m], f"u{m}")
            Sf, Sfx = self.S["f"], self.Sfx
            self.R.dve(lambda e, Sf=Sf, Sfx=Sfx: e.tensor_copy(out=Sf, in_=Sfx), ["Sfx"], ["Sf"])


def build_nc(dbg=None, only_prompt_units=None, xs_tiles=225, variant=None):
    nc = bass.Bass("TRN2", target_bir_lowering=False)
    dt = lambda name, shape, kind="ExternalInput": nc.dram_tensor(name, list(shape), F32, kind=kind).ap()
    T = {
        "xp": dt("xp", [2, 21 * 128, D]),
        "xs": dt("xs", [xs_tiles * 128, D]),
        "cf": dt("cf", [128, NCF]),
        "cb": dt("cb", [128, NCB]),
        "slabI": dt("slabI", [128, 5120]),
        "sT_p": dt("sT_p", [3, 128, 7168]),
        "sB_p": dt("sB_p", [2, 128, 7168]),
        "sT_s": dt("sT_s", [3, 128, 7168]),
        "sB_s": dt("sB_s", [2, 128, 7168]),
        "w_in": dt("w_in", [D, 5664]),
        "w_a2_f": dt("w_a2_f", [16, 512]),
        "b_a_f": dt("b_a_f", [1, 512]),
        "w_a2_b": dt("w_a2_b", [16, 512]),
        "b_a_b": dt("b_a_b", [1, 512]),
        "w_na_proj": dt("w_na_proj", [512, D]),
        "w_gla_proj": dt("w_gla_proj", [512, D]),
        "w_out": dt("w_out", [D, D]),
        "w_up": dt("w_up", [D, 5632]),
        "w_down": dt("w_down", [2816, D]),
    }
    yp = dt("yp", [2, 2048, D], "ExternalOutput")
    ys = dt("ys", [4096, D], "ExternalOutput")
    hscr = dt("hscr", [TR, D], "Internal")
    with ExitStack() as es:
        k = K(nc, es, dbg)
        k.setup(T)
        k.Sfx = k.alloc(512, F32)
        k.base = k.top
        sT_p = [T["sT_p"][e] for e in range(3)]
        sB_p = [T["sB_p"][e] for e in range(2)]
        sT_s = [T["sT_s"][e] for e in range(3)]
        sB_s = [T["sB_s"][e] for e in range(2)]
        ut_p = [dict(top=sT_p, bot=None, mcol=0), dict(top=None, bot=sB_p, mcol=1)]
        ut_s = [dict(top=sT_s, bot=None, mcol=2), dict(top=None, bot=None, mcol=3),
                dict(top=None, bot=None, mcol=4), dict(top=None, bot=sB_s, mcol=5)]
        if only_prompt_units is not None:
            k.dbg_unit = f"u{only_prompt_units - 1}"
            k.sequence(T["xp"][0], only_prompt_units, 2, [], 18, ut_p, yp[0], hscr)
        elif variant == "prompts":
            for b in range(2):
                k.sequence(T["xp"][b], 2, 2, [], 18, ut_p, yp[b], hscr)
        elif variant == "scans":
            k.sequence(T["xs"], 0, 96, list(range(0, 96)), 224, ut_s, ys, hscr)
        else:
            for b in range(2):
                k.sequence(T["xp"][b], 2, 2, [], 18, ut_p, yp[b], hscr)
            k.sequence(T["xs"], 4, 96, list(range(0, 96)), 224, ut_s, ys, hscr)
        n, cnt = k.R.emit(nc, k.stores)
        k.nops = n
    return nc, k


def make_slab(rpb, i, kvs, lo, hi):
    H = rpb.shape[0]
    out = np.full((128, 7, H, 128), NEG, np.float32)
    kc = np.arange(64)
    qc = np.arange(64)
    c0 = np.clip(qc - 8, 0, 48)
    colv = (kc[:, None] >= c0[None, :]) & (kc[:, None] < c0[None, :] + 16)
    dcv = np.clip(kc[:, None] - qc[None, :] + 15, 0, 30)
    for a, t in enumerate(kvs):
        for rr in range(2):
            rk = 2 * t - 5 + rr
            for qq in range(2):
                rq = 2 * i - 1 + qq
                if lo <= rq < hi:
                    r0 = min(max(rq - 4, lo), hi - 8)
                else:
                    r0 = rq - 4
                if not (r0 <= rk < r0 + 8):
                    continue
                dr = rk - rq + 7
                blk = np.where(colv[None], rpb[:, dr][:, dcv], NEG)
                out[rr * 64:(rr + 1) * 64, a, :, qq * 64:(qq + 1) * 64] = blk.transpose(1, 0, 2)
    return out


def host_consts(norm1_w, norm2_w, gla_norm_w, qn_w, kn_w, conv_w, conv_b, masks):
    cf = np.zeros((128, NCF), np.float32)
    cf[:, C_WBC1:C_WBC1 + D] = norm1_w[None, :]
    cf[:, C_WBC2:C_WBC2 + D] = norm2_w[None, :]
    cf[:, C_GNW:C_GNW + 512] = np.tile(gla_norm_w, 4)[None, :]
    idx = np.arange(128)
    cf[:, C_UINC:C_UINC + 128] = (idx[:, None] <= idx[None, :]) * (-1.0 / 16)
    cf[:, C_ULT:C_ULT + 128] = (idx[:, None] < idx[None, :]) * (-1.0 / 16)
    cf[:, C_UGT:C_UGT + 128] = (idx[:, None] > idx[None, :]) * (-1.0 / 16)
    cw = conv_w.reshape(3, 44, 128)
    cf[:, C_CW:C_CW + 132] = cw.transpose(2, 1, 0).reshape(128, 132)
    cf[:, C_CB:C_CB + 44] = conv_b.reshape(44, 128).T
    cf[:, C_QNW] = np.tile(qn_w, 2)
    cf[:, C_KNW] = np.tile(kn_w, 2)
    cf[:, C_LNQ] = np.float32(np.log(0.125))
    cf[:, C_NEG] = -1.0 / 16
    cf[:, C_EPS] = EPS
    for m, (top_real, bot_real) in enumerate(masks):
        cf[:, C_MASK + 2 * m] = 1.0
        cf[0:64, C_MASK + 2 * m] = 1.0 if top_real else 0.0
        cf[:, C_MASK + 2 * m + 1] = 1.0
        cf[64:128, C_MASK + 2 * m + 1] = 1.0 if bot_real else 0.0
    cb = np.zeros((128, NCB), np.float32)
    cb[:, B_ID:B_ID + 128] = np.eye(128)
    cb[0:64, B_OB:B_OB + 64] = 1.0
    cb[64:128, B_OB + 64:B_OB + 128] = 1.0
    cb[:, B_ONE:B_ONE + 128] = 1.0
    cb[:, B_MF:B_MF + 128] = (idx[:, None] <= idx[None, :])
    cb[:, B_MB:B_MB + 128] = (idx[:, None] > idx[None, :])
    return cf, cb


def make_in_maps(inp):
    f = lambda k: np.asarray(inp[k], np.float32)
    x_prompt, x_sample = f("x_prompt"), f("x_sample")
    rpb = f("rpb")[0]
    w_in = np.ascontiguousarray(f("w_in")[0])
    shared = {
        "w_in": w_in,
        "w_a2_f": np.ascontiguousarray(f("w_a2_f")[0]), "b_a_f": np.ascontiguousarray(f("b_a_f")[0][None, :]),
        "w_a2_b": np.ascontiguousarray(f("w_a2_b")[0]), "b_a_b": np.ascontiguousarray(f("b_a_b")[0][None, :]),
        "w_na_proj": np.ascontiguousarray(f("w_na_proj")[0]), "w_gla_proj": np.ascontiguousarray(f("w_gla_proj")[0]),
        "w_out": np.ascontiguousarray(f("w_out")[0]), "w_up": np.ascontiguousarray(f("w_up")[0]),
        "w_down": np.ascontiguousarray(f("w_down")[0]),
    }
    flat = lambda s: np.ascontiguousarray(s.transpose(0, 2, 1, 3).reshape(128, 7168))
    slabI = make_slab(rpb, 3, kv_list(3, False), -100, 100)[:, 0:5]
    slabI = np.ascontiguousarray(slabI[:, ::-1].transpose(0, 2, 1, 3).reshape(128, 5120))
    sT_cl = np.stack([flat(make_slab(rpb, i, kv_list(i, True), 0, 32)) for i in EDGE_TOP])
    sT_un = np.stack([flat(make_slab(rpb, i, kv_list(i, True), -100, 100)) for i in EDGE_TOP])
    sB_cl = np.stack([flat(make_slab(rpb, i, kv_list(i, True), -16, 16)) for i in EDGE_BOT])
    sB_un = np.stack([flat(make_slab(rpb, i, kv_list(i, True), -100, 100)) for i in EDGE_BOT])
    shared.update({"slabI": slabI, "sT_p": sT_cl, "sB_p": sB_cl})
    maps = []
    for c in range(NCORES):
        s, j = c // 4, c % 4
        xp = np.zeros((2, 42, 64, D), np.float32)
        for b in range(2):
            xp[b, 5:37] = x_prompt[2 * c + b].reshape(32, 64, D)
        R0 = 64 * j
        xs = np.zeros((450, 64, D), np.float32)
        g0 = R0 - 193
        lo_, hi_ = max(0, g0), min(256, g0 + 450)
        xs[lo_ - g0:hi_ - g0] = x_sample[s].reshape(256, 64, D)[lo_:hi_]
        masks = [(False, True), (True, False)]
        for m in range(4):
            masks.append((R0 + 16 * m - 1 >= 0, R0 + 16 * m + 16 < 256))
        cf, cb = host_consts(f("norm1_w")[0], f("norm2_w")[0], f("gla_norm_w")[0], f("qn_w")[0], f("kn_w")[0],
                             f("conv_w")[0], f("conv_b")[0], masks)
        d = dict(shared)
        d.update({"xp": xp.reshape(2, 21 * 128, D), "xs": xs.reshape(225 * 128, D), "cf": cf, "cb": cb,
                  "sT_s": sT_cl if j == 0 else sT_un, "sB_s": sB_cl if j == 3 else sB_un})
        maps.append(d)
    return maps


def kernel(**inp):
    maps = make_in_maps(inp)
    nc, k = build_nc()
    res = run_bass_kernel_spmd(nc, maps, core_ids=list(range(NCORES)))
    y_prompt = np.empty((16, 2048, D), np.float32)
    y_sample = np.empty((2, 16384, D), np.float32)
    for c in range(NCORES):
        s, j = c // 4, c % 4
        r = res.results[c]
        y_prompt[2 * c:2 * c + 2] = np.asarray(r["yp"], np.float32)
        y_sample[s, 4096 * j:4096 * (j + 1)] = np.asarray(r["ys"], np.float32)
    return (y_prompt, y_sample)
```

```python
import numpy as np
import concourse.bass as bass
import concourse.mybir as mybir
from concourse.bass_utils import run_bass_kernel_spmd
from contextlib import ExitStack

F32 = mybir.dt.float32
BF16 = mybir.dt.bfloat16
AF = mybir.ActivationFunctionType
ALU = mybir.AluOpType
EPS = 1e-6
NCORES = 8
D = 1024
TR = 1152
TK = 1664
NEG = -30000.0
EDGE_TOP = (0, 1, 2)
EDGE_BOT = (7, 8)
C_WBC1, C_WBC2, C_GNW, C_UINC, C_ULT, C_UGT, C_CW, C_CB = 0, 1024, 2048, 2560, 2688, 2816, 2944, 3076
C_QNW, C_KNW, C_LNQ, C_NEG, C_MASK, C_EPS = 3120, 3121, 3122, 3123, 3124, 3136
NCF = 3138
B_ID, B_OB, B_ONE, B_MF, B_MB, B_ZERO = 0, 128, 256, 384, 512, 640
NCB = 768


class Buf:
    __slots__ = ("w", "rs")

    def __init__(self):
        self.w = None
        self.rs = []


class Rec:
    ENGS = ("pe", "act", "dve", "pool", "sp")

    def __init__(self):
        self.ops = []
        self.bufs = {}
        self.bar = None
        self.last = {}
        self.dmas = []

    def B(self, name):
        b = self.bufs.get(name)
        if b is None:
            b = self.bufs[name] = Buf()
        return b

    def add(self, eng, fn, r=(), w=(), dma=False, stream=None):
        deps = set()
        for n in r:
            b = self.B(n)
            if b.w is not None:
                deps.add(b.w)
        for n in w:
            b = self.B(n)
            if b.w is not None:
                deps.add(b.w)
            deps.update(b.rs)
        if self.bar is not None:
            deps.add(self.bar)
        i = len(self.ops)
        self.ops.append(dict(eng=eng, fn=fn, deps=deps, dma=dma, stream=stream, sig=False, ord=0))
        for n in r:
            self.B(n).rs.append(i)
        for n in w:
            b = self.B(n)
            b.w = i
            b.rs = []
        if dma:
            self.dmas.append(i)
        else:
            self.last[eng] = i
        return i

    def barrier(self, fn):
        deps = set(self.last.values()) | set(self.dmas)
        if self.bar is not None:
            deps.add(self.bar)
        i = len(self.ops)
        self.ops.append(dict(eng="dve", fn=fn, deps=deps, dma=False, stream=None, sig=False, ord=0))
        self.bar = i
        self.last["dve"] = i
        self.dmas = []
        return i

    def pe(self, fn, r=(), w=()):
        return self.add("pe", fn, r, w)

    def act(self, fn, r=(), w=()):
        return self.add("act", fn, r, w)

    def dve(self, fn, r=(), w=()):
        return self.add("dve", fn, r, w)

    def pool(self, fn, r=(), w=()):
        return self.add("pool", fn, r, w)

    def dma(self, q, fn, r=(), w=(), stream=None):
        return self.add(q, fn, r, w, dma=True, stream=stream)

    def emit(self, nc, final_wait_ops=()):
        ops = self.ops
        ops.append(dict(eng="sp", fn=None, deps=set(final_wait_ops), dma=False, stream=None, sig=False, ord=0))
        for o in ops:
            for d in o["deps"]:
                od = ops[d]
                if od["dma"]:
                    continue
                if od["eng"] == "pe" and o["eng"] == "pe" and not o["dma"]:
                    continue
                od["sig"] = True
        cnt = {}
        for o in ops:
            if o["dma"]:
                k = ("s", o["stream"])
                cnt[k] = cnt.get(k, 0) + 1
                o["ord"] = cnt[k]
            elif o["sig"]:
                k = ("e", o["eng"])
                cnt[k] = cnt.get(k, 0) + 1
                o["ord"] = cnt[k]
        with ExitStack() as es:
            sem = {}
            for k in cnt:
                sem[k] = es.enter_context(nc.semaphore("sem_%s_%s" % k))
            block = es.enter_context(nc.Block())
            by_eng = {e: [o for o in ops if o["eng"] == e] for e in self.ENGS}

            def run(engh, ename):
                waited = {}
                for o in by_eng[ename]:
                    need = {}
                    for d in o["deps"]:
                        od = ops[d]
                        if od["dma"]:
                            k = ("s", od["stream"])
                            v = 16 * od["ord"]
                        else:
                            if od["eng"] == "pe" and ename == "pe" and not o["dma"]:
                                continue
                            k = ("e", od["eng"])
                            v = od["ord"]
                        if v > need.get(k, 0):
                            need[k] = v
                    for k, v in need.items():
                        if waited.get(k, 0) >= v:
                            continue
                        waited[k] = v
                        engh.wait_ge(sem[k], v)
                    if o["fn"] is None:
                        continue
                    ins = o["fn"](engh)
                    if o["dma"]:
                        ins.then_inc(sem[("s", o["stream"])], 16)
                    elif o["sig"]:
                        ins.then_inc(sem[("e", ename)], 1)

            if by_eng["sp"]:
                block.sync(lambda e: run(e, "sp"))
            if by_eng["pool"]:
                block.gpsimd(lambda e: run(e, "pool"))
            if by_eng["act"]:
                block.scalar(lambda e: run(e, "act"))
            if by_eng["dve"]:
                block.vector(lambda e: run(e, "dve"))
            if by_eng["pe"]:
                block.tensor(lambda e: run(e, "pe"))
        return len(ops), cnt


def blocks(T):
    out = []
    t = 0
    while t < T:
        n = min(512, T - t)
        out.append((t, n))
        t += n
    return out


def kv_list(i, edge):
    if not edge:
        return list(range(i, i + 5))
    return {0: list(range(0, 7)), 1: list(range(1, 7)), 2: list(range(2, 7)),
            7: list(range(6, 12)), 8: list(range(6, 13))}[i]


class K:
    def __init__(self, nc, es, dbg=None):
        self.nc = nc
        self.es = es
        self.R = Rec()
        self.dbg = dbg
        self.dumps = []
        self.stores = []
        self.cnt = {}
        self.arena = es.enter_context(nc.sbuf_tensor("arena", [128, 106400], BF16))
        self.top = 0
        self.ps = [es.enter_context(nc.psum_tensor(f"ps{i}", [128, 512], F32)) for i in range(6)]
        self.pb = [es.enter_context(nc.psum_tensor(f"pb{i}", [128, 1024], BF16)) for i in range(2)]

    def alloc(self, n, dt=BF16):
        if dt == F32:
            n2 = 2 * n
        else:
            n2 = n
        self.top = (self.top + 1) // 2 * 2
        a = self.arena[:, self.top:self.top + n2]
        self.top += n2
        assert self.top <= 106400, self.top
        return a.bitcast(F32) if dt == F32 else a

    def rot(self, key, n):
        c = self.cnt.get(key, 0)
        self.cnt[key] = c + 1
        return c % n

    def ps_new(self, bank=None):
        i = self.rot("ps", 6) if bank is None else bank
        return self.ps[i][:], f"ps{i}"

    def pt_new(self):
        i = self.rot("pt", 2)
        return self.pb[i][:], f"pb{i}"

    def MM(self, out, lhsT, rhs, start, stop, r, w):
        self.R.pe(lambda e: e.matmul(out, lhsT=lhsT, rhs=rhs, start=start, stop=stop), r, w)

    def TR_(self, out, in_, r, w):
        ident = self.ident
        self.R.pe(lambda e: e.transpose(out=out, in_=in_, identity=ident), list(r) + ["cb"], w)

    def ACT(self, out, in_, func, r, w, scale=None, bias=None, accum=None):
        kw = {}
        if scale is not None:
            kw["scale"] = scale
        if bias is not None:
            kw["bias"] = bias
        if accum is not None:
            kw["accum_out"] = accum
        self.R.act(lambda e: e.activation(out=out, in_=in_, func=func, **kw), r, w)

    def ACOPY(self, out, in_, r, w):
        self.R.act(lambda e: e.copy(out=out, in_=in_), r, w)

    def AMUL(self, out, in_, c, r, w):
        self.R.act(lambda e: e.mul(out=out, in_=in_, mul=c), r, w)

    def DCOPY(self, out, in_, r, w):
        self.R.dve(lambda e: e.tensor_copy(out=out, in_=in_), r, w)

    def TS(self, out, in0, s1, s2, op0, op1, r, w):
        if op1 is None:
            self.R.dve(lambda e: e.tensor_scalar(out=out, in0=in0, scalar1=s1, scalar2=None, op0=op0), r, w)
        else:
            self.R.dve(lambda e: e.tensor_scalar(out=out, in0=in0, scalar1=s1, scalar2=s2, op0=op0, op1=op1), r, w)

    def TT(self, out, in0, in1, op, r, w):
        self.R.dve(lambda e: e.tensor_tensor(out=out, in0=in0, in1=in1, op=op), r, w)

    def STT(self, out, in0, scalar, in1, op0, op1, r, w):
        self.R.dve(lambda e: e.scalar_tensor_tensor(out=out, in0=in0, scalar=scalar, in1=in1, op0=op0, op1=op1), r, w)

    def LOAD(self, q, out, in_, w, stream, r=()):
        return self.R.dma(q, lambda e: e.dma_start(out=out, in_=in_), r=r, w=w, stream=stream)

    def STORE(self, q, out, in_, r, stream, w=()):
        i = self.R.dma(q, lambda e: e.dma_start(out=out, in_=in_), r=r, w=w, stream=stream)
        return i

    def dump(self, name, ap, shape, dt, rname):
        if self.dbg is None or name not in self.dbg or getattr(self, "cur_unit", None) != getattr(self, "dbg_unit", None):
            return
        d = self.nc.dram_tensor("dbg_" + name, list(shape), dt, kind="ExternalOutput").ap()
        self.stores.append(self.STORE("sp", d, ap, [rname], "dbg_" + name))
        self.dumps.append(name)

    def barrier(self):
        bt = self.bartile
        self.R.barrier(lambda e: e.memset(bt, 0.0))

    def fence(self, reads, writes):
        bt = self.bartile
        self.R.dve(lambda e: e.memset(bt, 0.0), list(reads), list(writes) + ["bartile"])

    def setup(self, T):
        self.T = T
        A = self.alloc
        self.cf = A(NCF, F32)
        self.cb = A(NCB)
        self.ident = self.cb[:, B_ID:B_ID + 128]
        self.onesblk = self.cb[:, B_OB:B_OB + 128]
        self.ones = self.cb[:, B_ONE:B_ONE + 128]
        self.Mf = self.cb[:, B_MF:B_MF + 128]
        self.Mb = self.cb[:, B_MB:B_MB + 128]
        self.wa2 = [A(512), A(512)]
        self.ba = [A(512), A(512)]
        self.wlr = A(256).rearrange("p (k n) -> p k n", k=8)
        self.slabI = A(5120)
        self.S = {"f": A(512, F32), "b": A(512, F32)}
        self.Sbf = A(512)
        self.Ssave = [A(512, F32) for _ in range(4)]
        self.xs = [A(1024, F32) for _ in range(2)]
        self.jk = [A(1024) for _ in range(2)]
        self.xb = [A(1024) for _ in range(2)]
        self.xTt = [A(1024).rearrange("p (k n) -> p k n", k=8) for _ in range(2)]
        self.st = [A(8, F32) for _ in range(8)]
        self.wsl = [A(4096) for _ in range(4)]
        self.PT = [A(896) for _ in range(2)]
        self.ona = [A(512) for _ in range(2)]
        self.bartile = A(2, F32)
        self.tmp0 = self.top = (self.top + 1) // 2 * 2
        self.tf = [A(512, F32) for _ in range(8)]
        self.tb = [A(512) for _ in range(10)]
        assert self.top - self.tmp0 == 13312
        self.base = self.top
        L = self.LOAD
        L("sp", self.cf, T["cf"][:, :], ["cf"], "cf")
        L("pool", self.cb, T["cb"][:, :], ["cb"], "cb")
        for d, nm in enumerate(("f", "b")):
            L("pool", self.wa2[d][0:16, :], T["w_a2_" + nm][:, :], ["wa2"], "wa2" + nm)
            L("pool", self.ba[d][0:1, :], T["b_a_" + nm][:, :], ["ba"], "ba" + nm)
        L("pool", self.wlr, T["w_in"][:, 3584:3616].rearrange("(k p) n -> p k n", p=128), ["wlr"], "wlr")

    def tF(self):
        i = self.rot("tf", 8)
        return self.tf[i], f"tf{i}"

    def tB(self):
        i = self.rot("tb", 10)
        return self.tb[i], f"tb{i}"

    def stt_(self):
        i = self.rot("st", 8)
        return self.st[i], f"st{i}"

    def wload(self, si, cols_ap, ncols, kchunks=8):
        sl, nm = self.wsl[si], f"wsl{si}"
        v = sl[:, 0:kchunks * ncols].rearrange("p (k n) -> p k n", k=kchunks)
        if getattr(self, "nowload", False) and self.cnt.get("wl%d" % si, 0) > 0:
            return v, nm
        self.cnt["wl%d" % si] = 1
        self.LOAD("pool", v, cols_ap.rearrange("(k p) n -> p k n", p=128), [nm], nm)
        return v, nm

    def xload(self, src_rows):
        i = self.rot("xs", 2)
        self.LOAD("sp", self.xs[i], src_rows, [f"xs{i}"], f"xs{i}")
        return self.xs[i], f"xs{i}"

    def norm_A(self, src, sname, wbc_off, maskcol=None):
        j = self.rot("jk", 2)
        jk, jn = self.jk[j], f"jk{j}"
        xb, xn = self.xb[j], f"xb{j}"
        st, sn = self.stt_()
        cf = self.cf
        self.ACT(jk, src, AF.Square, [sname], [jn, sn], accum=st[:, 0:1])
        self.ACT(st[:, 2:3], st[:, 0:1], AF.Ln, [sn, "cf"], [sn], scale=1.0 / D, bias=cf[:, C_EPS:C_EPS + 1])
        self.ACT(st[:, 3:4], st[:, 2:3], AF.Exp, [sn], [sn], scale=-0.5)
        if maskcol is not None:
            self.TT(st[:, 3:4], st[:, 3:4], cf[:, maskcol:maskcol + 1], ALU.mult, [sn, "cf"], [sn])
        self.STT(xb, src, st[:, 3:4], cf[:, wbc_off:wbc_off + D], ALU.mult, ALU.mult, [sname, sn, "cf"], [xn])
        return xb, xn

    def norm_B(self, xb, xn, dst3, dname):
        pt, pn = self.pt_new()
        for kc in range(8):
            self.TR_(pt[:, kc * 128:(kc + 1) * 128], xb[:, kc * 128:(kc + 1) * 128], [xn], [pn])
        src3 = pt.rearrange("p (a b) -> p a b", a=8)
        self.DCOPY(dst3, src3, [pn], [dname])

    def norm_T(self, src, sname, wbc_off, dst3, dname, maskcol=None):
        xb, xn = self.norm_A(src, sname, wbc_off, maskcol)
        self.norm_B(xb, xn, dst3, dname)

    def gates(self, lrT, lrn, d, bank=None):
        ps, pn = self.ps_new(bank)
        self.MM(ps, lrT, self.wa2[d][0:16, :], True, False, [lrn, "wa2"], [pn])
        self.MM(ps, self.ones[0:1, :], self.ba[d][0:1, :], False, True, ["cb", "ba"], [pn])
        gp, gn = self.tF()
        self.ACT(gp, ps, AF.Exp, [pn], [gn], scale=-1.0)
        self.ACT(gp, gp, AF.Ln, [gn, "cb"], [gn], bias=self.ones[:, 0:1])
        return gp, gn

    def tok_kv(self, xT3, xname, wk, wkn, wv, wvn):
        psk, pkn = self.ps_new()
        for kc in range(8):
            self.MM(psk, xT3[:, kc, :], wk[:, kc, :], kc == 0, kc == 7, [xname, wkn], [pkn])
        psv, pvn = self.ps_new()
        for kc in range(8):
            self.MM(psv, xT3[:, kc, :], wv[:, kc, :], kc == 0, kc == 7, [xname, wvn], [pvn])
        vt, vn = self.tB()
        self.ACOPY(vt, psv, [pvn], [vn])
        return psk, pkn, vt, vn

    def state_prep(self, gp, gn, psk, pkn, d, banks=(None, None)):
        cf = self.cf
        U = cf[:, C_UGT:C_UGT + 128] if d == "f" else cf[:, C_ULT:C_ULT + 128]
        psc, pcn = self.ps_new(banks[0])
        self.MM(psc, U, gp, True, True, ["cf", gn], [pcn])
        Ec, en = self.tF()
        self.ACT(Ec, psc, AF.Exp, [pcn], [en])
        kt, kn = self.tB()
        self.TT(kt, psk, Ec, ALU.mult, [pkn, en], [kn])
        pse, pen = self.ps_new(banks[1])
        for hh in range(4):
            self.MM(pse[:, hh:hh + 1], gp[:, hh * 128:(hh + 1) * 128], cf[:, C_NEG:C_NEG + 1], True, True,
                    [gn, "cf"], [pen])
        st, sn = self.stt_()
        self.ACT(st[:, 0:4], pse[:, 0:4], AF.Exp, [pen], [sn])
        return kt, kn, st, sn

    def state_apply(self, kt, kn, st, sn, vt, vn, d, snap=None, snapn=None, bank=None):
        S, Sn = self.S[d], "S" + d
        psd, pdn = self.ps_new(bank)
        for hh in range(4):
            sl = slice(hh * 128, (hh + 1) * 128)
            self.MM(psd[:, sl], kt[:, sl], vt[:, sl], True, True, [kn, vn], [pdn])
        S3 = S.rearrange("p (h v) -> p h v", h=4)
        self.TT(S3, S3, st[:, 0:4].rearrange("p (h o) -> p h o", o=1).to_broadcast([128, 4, 128]), ALU.mult,
                [Sn, sn], [Sn])
        if snap is not None:
            self.ACOPY(snap, S, [Sn], [snapn])
        self.TT(S, S, psd, ALU.add, [Sn, pdn], [Sn])

    def state_update(self, gp, gn, psk, pkn, vt, vn, d, snap=None, snapn=None):
        kt, kn, st, sn = self.state_prep(gp, gn, psk, pkn, d)
        self.state_apply(kt, kn, st, sn, vt, vn, d, snap, snapn)

    def zero_state(self, d):
        S = self.S[d]
        self.R.dve(lambda e: e.memset(S, 0.0), [], ["S" + d])

    def scan_pipe(self, d, n, p1, wk, wkn, wv, wvn, lr_of=None, snap_of=None, save_after=None, p1a=None):
        di = 0 if d == "f" else 1
        s1, s2, s3 = {}, {}, {}
        ksb_pool = [self.slabI[:, j * 1024:(j + 1) * 1024].bitcast(F32) for j in range(3)]

        def P2(i):
            xT3, xn = s1.pop(i)
            psk, pkn = self.ps_new(0)
            for kc in range(8):
                self.MM(psk, xT3[:, kc, :], wk[:, kc, :], kc == 0, kc == 7, [xn, wkn], [pkn])
            j = self.rot("ksb", 3)
            ksb, ksn = ksb_pool[j], f"ksb{j}"
            self.ACOPY(ksb, psk, [pkn], [ksn])
            psv, pvn = self.ps_new(1)
            for kc in range(8):
                self.MM(psv, xT3[:, kc, :], wv[:, kc, :], kc == 0, kc == 7, [xn, wvn], [pvn])
            vt, vn = self.tB()
            self.DCOPY(vt, psv, [pvn], [vn])
            if lr_of is None:
                psl, pln = self.ps_new(2)
                for kc in range(8):
                    self.MM(psl[0:16, 0:128], self.wlr[:, kc, 16 * di:16 * di + 16], xT3[:, kc, :], kc == 0, kc == 7,
                            ["wlr", xn], [pln])
                lrt, lrn = self.tB()
                self.DCOPY(lrt[0:16, 0:128], psl[0:16, 0:128], [pln], [lrn])
                lr = (lrt[0:16, 0:128], lrn)
            else:
                lr = lr_of(i)
            s2[i] = (ksb, ksn, vt, vn, lr)

        def P3a(i):
            ksb, ksn, vt, vn, lr = s2.pop(i)
            gp, gn = self.gates(lr[0], lr[1], di, bank=3)
            s3[i] = (ksb, ksn, vt, vn, gp, gn)

        def P3b(i):
            ksb, ksn, vt, vn, gp, gn = s3.pop(i)
            kt, kn, st, sn = self.state_prep(gp, gn, ksb, ksn, d, banks=(4, 5))
            sp = snap_of(i) if snap_of else None
            self.state_apply(kt, kn, st, sn, vt, vn, d, sp[0] if sp else None, sp[1] if sp else None, bank=5)
            if save_after and i in save_after:
                dst, dn = save_after[i]
                S = self.S[d]
                self.R.dve(lambda e, dst=dst, S=S: e.tensor_copy(out=dst, in_=S), ["S" + d], [dn])

        s0 = {}
        for it in range(n + 4):
            if 0 <= it - 4 < n:
                P3b(it - 4)
            if 0 <= it - 3 < n:
                P3a(it - 3)
            if 0 <= it - 2 < n:
                P2(it - 2)
            if 0 <= it - 1 < n:
                s1[it - 1] = p1(it - 1, s0.pop(it - 1, None))
            if it < n and p1a is not None:
                s0[it] = p1a(it)

    def state_scan(self, xsrc, taus, d, wk, wkn, wv, wvn, saves=None):
        def p1a(i):
            tau = taus[i]
            xs, xn = self.xload(xsrc[tau * 128:(tau + 1) * 128, :])
            return self.norm_A(xs, xn, C_WBC1)

        def p1(i, tok):
            j = self.rot("xTt", 2)
            xT3, xTn = self.xTt[j], f"xTt{j}"
            self.norm_B(tok[0], tok[1], xT3, xTn)
            return xT3, xTn
        sa = None
        if saves:
            sa = {i: saves[t] for i, t in enumerate(taus) if t in saves}
        self.scan_pipe(d, len(taus), p1, wk, wkn, wv, wvn, save_after=sa, p1a=p1a)

    def unit(self, xsrc, kv0, ut, yout, hscr, Sb_init, uname):
        R, T, cf = self.R, self.T, self.cf
        A = self.alloc
        self.top = self.base
        self.cur_unit = uname
        w_in = T["w_in"]
        xnT = A(8 * TK).rearrange("p (k n) -> p k n", k=8)
        self.LOAD("pool", self.slabI, T["slabI"][:, :], ["slabI"] + [f"ksb{j}" for j in range(3)], "slabI")
        prev = None
        for t in range(14):
            cur = None
            if t < 13:
                xs, xn = self.xload(xsrc[(kv0 + t) * 128:(kv0 + t + 1) * 128, :])
                cur = self.norm_A(xs, xn, C_WBC1)
            if prev is not None:
                self.norm_B(prev[0], prev[1], xnT[:, :, (t - 1) * 128:t * 128], f"xnT{t - 1}")
            prev = cur
        self.dump("xnT", xnT, [128, 8, TK], BF16, "xnT12")
        if getattr(self, "stop_after", None) == "U1":
            self.R.dve(lambda e: e.memset(self.Sfx, 0.0), [], ["Sfx"])
            self.barrier()
            return

        def xr(t0, n, off):
            a = (off + t0) // 128
            b = (off + t0 + n - 1) // 128
            return [f"xnT{t}" for t in range(a, b + 1)]

        m1 = self.top
        onaT = A(4 * TR).rearrange("p (k n) -> p k n", k=4)
        KT = A(4 * TK).rearrange("p (k n) -> p k n", k=4)
        QT = A(4 * TR).rearrange("p (k n) -> p k n", k=4)
        V = A(13 * 8 * 65)
        V4 = V.rearrange("p (t h d) -> p t h d", t=13, h=8)

        hn_pending = []

        def hn_finish():
            while hn_pending:
                (ps, pn, n, nwc, biasc, dst, dname, sq, sqn) = hn_pending.pop(0)
                ps2, p2n = self.ps_new()
                self.MM(ps2[:, :n], self.onesblk, sq[:, :n], True, True, ["cb", sqn], [p2n])
                r1, r1n = self.tF()
                self.ACT(r1[:, :n], ps2[:, :n], AF.Ln, [p2n, "cf"], [r1n], scale=1.0 / 64, bias=cf[:, C_EPS:C_EPS + 1])
                if biasc is None:
                    self.ACT(r1[:, :n], r1[:, :n], AF.Exp, [r1n], [r1n], scale=-0.5)
                else:
                    self.ACT(r1[:, :n], r1[:, :n], AF.Exp, [r1n, "cf"], [r1n], scale=-0.5, bias=cf[:, biasc:biasc + 1])
                self.STT(dst, ps[:, :n], cf[:, nwc:nwc + 1], r1[:, :n], ALU.mult, ALU.mult, [pn, r1n, "cf"], [dname])

        def headnorm(ps, pn, n, nwc, biasc, dst, dname):
            sq, sqn = self.tB()
            self.ACT(sq[:, :n], ps[:, :n], AF.Square, [pn], [sqn])
            hn_pending.append((ps, pn, n, nwc, biasc, dst, dname, sq, sqn))

        w, wn = self.wload(0, w_in[:, 512:1024], 512)
        for p in range(4):
            for (t0, n) in blocks(TK):
                ps, pn = self.ps_new()
                for kc in range(8):
                    self.MM(ps[:, :n], w[:, kc, p * 128:(p + 1) * 128], xnT[:, kc, t0:t0 + n], kc == 0, kc == 7,
                            xr(t0, n, 0) + [wn], [pn])
                hn_finish()
                headnorm(ps, pn, n, C_KNW, None, KT[:, p, t0:t0 + n], "KT")
        w, wn = self.wload(2, w_in[:, 0:512], 512)
        for p in range(4):
            for (t0, n) in blocks(TR):
                ps, pn = self.ps_new()
                for kc in range(8):
                    self.MM(ps[:, :n], w[:, kc, p * 128:(p + 1) * 128], xnT[:, kc, 256 + t0:256 + t0 + n], kc == 0,
                            kc == 7, xr(t0, n, 256) + [wn], [pn])
                hn_finish()
                headnorm(ps, pn, n, C_QNW, C_LNQ, QT[:, p, t0:t0 + n], "QT")
        w, wn = self.wload(0, w_in[:, 1024:1536], 512)
        R.pool(lambda e: e.memset(V, 1.0), [], ["V"])
        hn_first_v = True
        for t in range(13):
            ps, pn = self.ps_new()
            for kc in range(8):
                self.MM(ps, xnT[:, kc, t * 128:(t + 1) * 128], w[:, kc, :], kc == 0, kc == 7, [f"xnT{t}", wn], [pn])
            if hn_first_v:
                hn_finish()
                hn_first_v = False
            self.ACOPY(V4[:, t, :, 0:64], ps.rearrange("p (h d) -> p h d", h=8), [pn], ["V"])
        self.dump("KT", KT, [128, 4, TK], BF16, "KT")
        self.dump("QT", QT, [128, 4, TR], BF16, "QT")
        self.dump("V", V, [128, 13 * 8 * 65], BF16, "V")
        if getattr(self, "stop_after", None) == "U2":
            self.R.dve(lambda e: e.memset(self.Sfx, 0.0), [], ["Sfx"])
            self.barrier()
            return

        onaAll = A(9 * 512).rearrange("p (i n) -> p i n", i=9)
        edge_qps = []
        if ut["top"] is not None:
            edge_qps += [(i, ut["top"][EDGE_TOP.index(i)]) for i in EDGE_TOP]
        if ut["bot"] is not None:
            edge_qps += [(i, ut["bot"][EDGE_BOT.index(i)]) for i in EDGE_BOT]
        eidx = {i: n_ for n_, (i, _) in enumerate(edge_qps)}
        kvl = {i: kv_list(i, i in eidx) for i in range(9)}
        for h in range(8):
            p, bp = h // 2, 64 * (h % 2)
            es_i = 0 if h % 2 == 0 else 2
            es, esn = self.wsl[es_i], f"wsl{es_i}"
            for n_, (i, src) in enumerate(edge_qps):
                self.LOAD("pool", es[:, n_ * 896:(n_ + 1) * 896], src[:, h * 896:(h + 1) * 896], [esn], f"{esn}_{n_}")
            OX, OXn = self.ps_new(4)
            OY, OYn = self.ps_new(5)
            zer = self.cb[:, B_ZERO:B_ZERO + 128]
            self.MM(OX[:, 0:455], zer, self.slabI[:, 0:455], True, False, ["cb", "slabI"], [OXn])
            self.MM(OY[:, 0:130], zer, self.slabI[:, 0:130], True, False, ["cb", "slabI"], [OYn])
            lastX = max(kvl[i][-1] for i in range(7))
            lastY = max(kvl[i][-1] for i in (7, 8))
            def S1(t):
                qs = [i for i in range(9) if t in kvl[i]]
                if not qs:
                    return None
                assert qs == list(range(qs[0], qs[-1] + 1)) and len(qs) <= 7
                bA, bB = (0, 1) if t % 2 == 0 else (2, 3)
                psA, pAn = self.ps_new(bA)
                psB, pBn = self.ps_new(bB)
                qA, qB = qs[:4], qs[4:]
                for (qq, ps_, pn_) in ((qA, psA, pAn), (qB, psB, pBn)):
                    if not qq:
                        continue
                    nq = len(qq)
                    c = 0
                    while c < nq:
                        i = qq[c]
                        if i in eidx:
                            c2 = c + 1
                            a_ = kvl[i].index(t)
                            blk = eidx[i] * 896 + a_ * 128
                            b_ap, b_n = es[:, blk:blk + 128], esn
                        else:
                            c2 = c
                            while c2 < nq and qq[c2] not in eidx:
                                c2 += 1
                            blk = (h * 5 + (i - t + 4)) * 128
                            b_ap, b_n = self.slabI[:, blk:blk + (c2 - c) * 128], "slabI"
                        self.MM(ps_[:, c * 128:c2 * 128], KT[bp:bp + 64, p, t * 128:(t + 1) * 128],
                                QT[bp:bp + 64, p, qq[c] * 128:(qq[c2 - 1] + 1) * 128], True, False, ["KT", "QT"], [pn_])
                        self.MM(ps_[:, c * 128:c2 * 128], self.ident, b_ap, False, True, ["cb", b_n], [pn_])
                        c = c2
                pj = self.rot("PT", 2)
                PT, PTn = self.PT[pj], f"PT{pj}"
                self.ACT(PT[:, 0:len(qA) * 128], psA[:, 0:len(qA) * 128], AF.Exp, [pAn], [PTn])
                if qB:
                    self.ACT(PT[:, 512:512 + len(qB) * 128], psB[:, 0:len(qB) * 128], AF.Exp, [pBn], [PTn])
                return (qs, PT, PTn)

            def S2(t, tok):
                qs, PT, PTn = tok
                for c, i in enumerate(qs):
                    if i < 7:
                        od, odn = OX[:, i * 65:(i + 1) * 65], OXn
                    else:
                        od, odn = OY[:, (i - 7) * 65:(i - 6) * 65], OYn
                    is_last = (c == len(qs) - 1 and t == lastY) if i >= 7 else \
                        (t == lastX and i == max(q for q in qs if q < 7))
                    self.MM(od, PT[:, c * 128:(c + 1) * 128], V4[:, t, h, :], False, is_last, [PTn, "V"], [odn])

            tok_prev = S1(0)
            for t in range(13):
                tok_next = S1(t + 1) if t + 1 < 13 else None
                if tok_prev is not None:
                    S2(t, tok_prev)
                tok_prev = tok_next
            for (O_, On_, i0, ni) in ((OX, OXn, 0, 7), (OY, OYn, 7, 2)):
                O3 = O_[:, 0:ni * 65].rearrange("p (i d) -> p i d", i=ni)
                st, sn = self.stt_()
                R.dve(lambda e, st=st, O3=O3, ni=ni: e.reciprocal(out=st[:, 0:ni], in_=O3[:, :, 64]), [On_], [sn])
                self.TT(onaAll[:, i0:i0 + ni, h * 64:(h + 1) * 64], O3[:, :, 0:64],
                        st[:, 0:ni].rearrange("p (i o) -> p i o", o=1).to_broadcast([128, ni, 64]), ALU.mult,
                        [On_, sn], ["onaAll"])
        for i in range(9):
            pt, pn = self.pt_new()
            for c in range(4):
                self.TR_(pt[:, c * 128:(c + 1) * 128], onaAll[:, i, c * 128:(c + 1) * 128], ["onaAll"], [pn])
            self.ACOPY(onaT[:, :, i * 128:(i + 1) * 128], pt[:, 0:512].rearrange("p (a b) -> p a b", a=4), [pn], ["onaT"])
        self.dump("onaT", onaT, [128, 4, TR], BF16, "onaT")
        if getattr(self, "stop_after", None) == "U3":
            self.R.dve(lambda e: e.memset(self.Sfx, 0.0), [], ["Sfx"])
            self.barrier()
            return

        self.fence(["KT", "QT", "V", "onaAll", "PT0", "PT1"], ["qgT", "kgT", "sog", "snap", "ogT"])
        self.top = m1 + 4 * TR
        ogT = A(4 * TR).rearrange("p (k n) -> p k n", k=4)
        m_gla = self.top
        qgT = A(4 * TR).rearrange("p (k n) -> p k n", k=4)
        kgT = A(4 * TR).rearrange("p (k n) -> p k n", k=4)
        lr = [self.wsl[2][:, 0:TR], self.wsl[2][:, TR:2 * TR]]
        sog = A(9 * 512).rearrange("p (t n) -> p t n", t=9)
        snap = A(9 * 512).rearrange("p (t n) -> p t n", t=9)
        w, wn = self.wload(0, w_in[:, 1536:2048], 512)
        for hh in range(4):
            for (t0, n) in blocks(TR):
                ps, pn = self.ps_new()
                for kc in range(8):
                    self.MM(ps[:, :n], w[:, kc, hh * 128:(hh + 1) * 128], xnT[:, kc, 256 + t0:256 + t0 + n], kc == 0,
                            kc == 7, xr(t0, n, 256) + [wn], [pn])
                self.AMUL(qgT[:, hh, t0:t0 + n], ps[:, :n], 128 ** -0.5, [pn], ["qgT"])
        wk, wkn = self.wload(1, w_in[:, 2048:2560], 512)
        for hh in range(4):
            for (t0, n) in blocks(TR):
                ps, pn = self.ps_new()
                for kc in range(8):
                    self.MM(ps[:, :n], wk[:, kc, hh * 128:(hh + 1) * 128], xnT[:, kc, 256 + t0:256 + t0 + n], kc == 0,
                            kc == 7, xr(t0, n, 256) + [wkn], [pn])
                self.DCOPY(kgT[:, hh, t0:t0 + n], ps[:, :n], [pn], ["kgT"])
        for d in range(2):
            for (t0, n) in blocks(TR):
                ps, pn = self.ps_new()
                for kc in range(8):
                    self.MM(ps[0:16, :n], self.wlr[:, kc, 16 * d:16 * d + 16], xnT[:, kc, 256 + t0:256 + t0 + n],
                            kc == 0, kc == 7, xr(t0, n, 256) + ["wlr"], [pn])
                self.ACOPY(lr[d][0:16, t0:t0 + n], ps[0:16, :n], [pn], [f"lr{d}", "wsl2"])
        w, wn = self.wload(0, w_in[:, 3072:3584], 512)
        for i in range(9):
            ps, pn = self.ps_new()
            for kc in range(8):
                self.MM(ps, xnT[:, kc, (i + 2) * 128:(i + 3) * 128], w[:, kc, :], kc == 0, kc == 7,
                        [f"xnT{i + 2}", wn], [pn])
            tg, tgn = self.tF()
            self.ACT(tg, ps, AF.Tanh, [pn], [tgn], scale=0.5)
            self.STT(tg, tg, 1.0, ps, ALU.add, ALU.mult, [tgn, pn], [tgn])
            self.TT(sog[:, i, :], tg, cf[:, C_GNW:C_GNW + 512], ALU.mult, [tgn, "cf"], ["sog"])
        wv, wvn = self.wload(3, w_in[:, 2560:3072], 512)
        if getattr(self, "stop_after", None) == "U5":
            self.R.dve(lambda e: e.memset(self.Sfx, 0.0), [], ["Sfx"])
            self.barrier()
            return

        if Sb_init is None:
            self.zero_state("b")
        else:
            Sb = self.S["b"]
            R.dve(lambda e, Sb=Sb, src=Sb_init[0]: e.tensor_copy(out=Sb, in_=src), [Sb_init[1]], ["Sb"])
        order6 = list(reversed(range(9)))
        self.scan_pipe("b", 9,
                       lambda n_, tok=None: (xnT[:, :, (order6[n_] + 2) * 128:(order6[n_] + 3) * 128], f"xnT{order6[n_] + 2}"),
                       wk, wkn, wv, wvn,
                       lr_of=lambda n_: (lr[1][0:16, order6[n_] * 128:(order6[n_] + 1) * 128], "lr1"),
                       snap_of=lambda n_: (snap[:, order6[n_], :], "snap"))
        self.fence(["ksb0", "ksb1", "ksb2"], [f"ded{a_}{b_}" for a_ in range(2) for b_ in range(4)])
        Sf = self.S["f"]
        ded = [[self.slabI[:, (par * 4 + q) * 512:(par * 4 + q + 1) * 512] for q in range(4)] for par in range(2)]
        m1out = {}

        def M1(i):
            par = i % 2
            xT3 = xnT[:, :, (i + 2) * 128:(i + 3) * 128]
            psk, pkn, vt, vn = self.tok_kv(xT3, f"xnT{i + 2}", wk, wkn, wv, wvn)
            qts, As = [], []
            gpf = self.gates(lr[0][0:16, i * 128:(i + 1) * 128], "lr0", 0)
            ktf, ktfn, stf, stfn = self.state_prep(gpf[0], gpf[1], psk, pkn, "f")
            for d in range(2):
                if d == 0:
                    gp, gn = gpf
                else:
                    gp, gn = self.gates(lr[d][0:16, i * 128:(i + 1) * 128], f"lr{d}", d)
                Ufm = cf[:, C_UINC:C_UINC + 128] if d == 0 else cf[:, C_ULT:C_ULT + 128]
                psp, ppn = self.ps_new()
                for hh in range(4):
                    sl = slice(hh * 128, (hh + 1) * 128)
                    self.MM(psp[:, sl], gp[:, sl], Ufm, True, True, [gn, "cf"], [ppn])
                Eq, eqn = self.tF()
                Ek, ekn = self.tF()
                self.ACT(Eq, psp, AF.Exp, [ppn], [eqn], scale=(1.0 if d == 0 else -1.0))
                self.ACT(Ek, psp, AF.Exp, [ppn], [ekn], scale=(-1.0 if d == 0 else 1.0))
                qt, qn = ded[par][d], f"ded{par}{d}"
                kt2, k2n = self.tB()
                q3 = qt.rearrange("p (h t) -> p h t", h=4)
                k3 = kt2.rearrange("p (h t) -> p h t", h=4)
                self.TT(q3, qgT[:, :, i * 128:(i + 1) * 128], Eq.rearrange("p (h t) -> p h t", h=4), ALU.mult,
                        ["qgT", eqn], [qn])
                self.TT(k3, kgT[:, :, i * 128:(i + 1) * 128], Ek.rearrange("p (h t) -> p h t", h=4), ALU.mult,
                        ["kgT", ekn], [k2n])
                psa, pan = self.ps_new()
                for hh in range(4):
                    sl = slice(hh * 128, (hh + 1) * 128)
                    self.MM(psa[:, sl], kt2[:, sl], qt[:, sl], True, True, [k2n, qn], [pan])
                Ad, adn = ded[par][2 + d], f"ded{par}{2 + d}"
                Mm = self.Mf if d == 0 else self.Mb
                self.TT(Ad.rearrange("p (h t) -> p h t", h=4), psa.rearrange("p (h t) -> p h t", h=4),
                        Mm.rearrange("p (o t) -> p o t", o=1).to_broadcast([128, 4, 128]), ALU.mult, [pan, "cb"], [adn])
                qts.append((qt, qn))
                As.append((Ad, adn))
            m1out[i] = (vt, vn, ktf, ktfn, stf, stfn, qts, As)

        def M2(i):
            vt, vn, ktf, ktfn, stf, stfn, qts, As = m1out.pop(i)
            if i == 8:
                exp_ = self.Sfx
                R.dve(lambda e, exp_=exp_, Sf=Sf: e.tensor_copy(out=exp_, in_=Sf), ["Sf"], ["Sfx"])
            self.ACOPY(self.Sbf, Sf, ["Sf"], ["Sbf"])
            pso, pon = self.ps_new()
            for hh in range(4):
                sl = slice(hh * 128, (hh + 1) * 128)
                self.MM(pso[:, sl], qts[0][0][:, sl], self.Sbf[:, sl], True, False, [qts[0][1], "Sbf"], [pon])
                self.MM(pso[:, sl], As[0][0][:, sl], vt[:, sl], False, False, [As[0][1], vn], [pon])
                self.MM(pso[:, sl], qts[1][0][:, sl], snap[:, i, sl], False, False, [qts[1][1], "snap"], [pon])
                self.MM(pso[:, sl], As[1][0][:, sl], vt[:, sl], False, True, [As[1][1], vn], [pon])
            self.state_apply(ktf, ktfn, stf, stfn, vt, vn, "f")
            st, sn = self.stt_()
            jj = self.rot("jk", 2)
            for hh in range(4):
                sl = slice(hh * 128, (hh + 1) * 128)
                self.ACT(self.jk[jj][:, sl], pso[:, sl], AF.Square, [pon], [f"jk{jj}", sn], accum=st[:, hh:hh + 1])
            self.TS(st[:, 4:8], st[:, 0:4], 1.0 / 128, EPS, ALU.mult, ALU.add, [sn], [sn])
            self.ACT(st[:, 4:8], st[:, 4:8], AF.Ln, [sn], [sn])
            self.ACT(st[:, 4:8], st[:, 4:8], AF.Exp, [sn], [sn], scale=-0.5)
            self.TS(st[:, 4:8], st[:, 4:8], 0.5, None, ALU.mult, None, [sn], [sn])
            og, ogn = self.tF()
            self.TT(og.rearrange("p (h t) -> p h t", h=4), pso.rearrange("p (h t) -> p h t", h=4),
                    st[:, 4:8].rearrange("p (h o) -> p h o", o=1).to_broadcast([128, 4, 128]), ALU.mult, [pon, sn], [ogn])
            ogb, obn = self.tB()
            self.TT(ogb, og, sog[:, i, :], ALU.mult, [ogn, "sog"], [obn])
            pt, pn = self.pt_new()
            for c in range(4):
                self.TR_(pt[:, c * 128:(c + 1) * 128], ogb[:, c * 128:(c + 1) * 128], [obn], [pn])
            self.ACOPY(ogT[:, :, i * 128:(i + 1) * 128], pt[:, 0:512].rearrange("p (a b) -> p a b", a=4), [pn], ["ogT"])

        M1(0)
        for i in range(9):
            if i + 1 < 9:
                M1(i + 1)
            M2(i)
        self.dump("ogT", ogT, [128, 4, TR], BF16, "ogT")
        if getattr(self, "stop_after", None) == "U7":
            self.R.dve(lambda e: e.memset(self.Sfx, 0.0), [], ["Sfx"])
            self.barrier()
            return

        self.fence(["qgT", "kgT", "sog", "snap", "lr0", "lr1"], ["mixT", "wsl2"] + [f"hnT{i_}" for i_ in range(9)])
        self.top = m_gla
        mixT = A(8 * TR).rearrange("p (k n) -> p k n", k=8)
        wna, wnan = self.wload(0, T["w_na_proj"][:, :], 1024, kchunks=4)
        wgl, wgln = self.wload(1, T["w_gla_proj"][:, :], 1024, kchunks=4)
        for hp in range(2):
            g1, g1n = self.wload(2, w_in[:, 3616 + 512 * hp:3616 + 512 * hp + 512], 512)
            g2, g2n = self.wload(3, w_in[:, 4640 + 512 * hp:4640 + 512 * hp + 512], 512)
            for c4 in range(4):
                c = hp * 4 + c4
                for (t0, n) in blocks(TR):
                    ps1, p1n = self.ps_new()
                    for kk in range(4):
                        self.MM(ps1[:, :n], wna[:, kk, c * 128:(c + 1) * 128], onaT[:, kk, t0:t0 + n], kk == 0, kk == 3,
                                [wnan, "onaT"], [p1n])
                    ps2, p2n = self.ps_new()
                    for kk in range(4):
                        self.MM(ps2[:, :n], wgl[:, kk, c * 128:(c + 1) * 128], ogT[:, kk, t0:t0 + n], kk == 0, kk == 3,
                                [wgln, "ogT"], [p2n])
                    ps3, p3n = self.ps_new()
                    for kc in range(8):
                        self.MM(ps3[:, :n], g1[:, kc, c4 * 128:(c4 + 1) * 128], xnT[:, kc, 256 + t0:256 + t0 + n],
                                kc == 0, kc == 7, xr(t0, n, 256) + [g1n], [p3n])
                    ps4, p4n = self.ps_new()
                    for kc in range(8):
                        self.MM(ps4[:, :n], g2[:, kc, c4 * 128:(c4 + 1) * 128], xnT[:, kc, 256 + t0:256 + t0 + n],
                                kc == 0, kc == 7, xr(t0, n, 256) + [g2n], [p4n])
                    t1, t1n = self.tF()
                    t2, t2n = self.tF()
                    self.ACT(t1[:, :n], ps3[:, :n], AF.Tanh, [p3n], [t1n], scale=0.5)
                    self.ACT(t2[:, :n], ps4[:, :n], AF.Tanh, [p4n], [t2n], scale=0.5)
                    self.STT(t1[:, :n], t1[:, :n], 1.0, ps1[:, :n], ALU.add, ALU.mult, [t1n, p1n], [t1n])
                    self.STT(t2[:, :n], t2[:, :n], 1.0, ps2[:, :n], ALU.add, ALU.mult, [t2n, p2n], [t2n])
                    self.TT(mixT[:, c, t0:t0 + n], t1[:, :n], t2[:, :n], ALU.add, [t1n, t2n], ["mixT"])
        self.dump("mixT", mixT, [128, 8, TR], BF16, "mixT")
        if getattr(self, "stop_after", None) == "U8a":
            self.R.dve(lambda e: e.memset(self.Sfx, 0.0), [], ["Sfx"])
            self.barrier()
            return
        hnT = A(8 * TR).rearrange("p (k n) -> p k n", k=8)
        wo = []
        for half in range(2):
            wo.append(self.wload(half, T["w_out"][:, half * 512:(half + 1) * 512], 512))
        prevB = None
        for i in range(10):
            curB = None
            if i < 9:
                xs, xn = self.xload(xsrc[(kv0 + 2 + i) * 128:(kv0 + 3 + i) * 128, :])
                for half in range(2):
                    ps, pn = self.ps_new()
                    for kc in range(8):
                        self.MM(ps, mixT[:, kc, i * 128:(i + 1) * 128], wo[half][0][:, kc, :], kc == 0, kc == 7,
                                ["mixT", wo[half][1]], [pn])
                    sl = slice(half * 512, (half + 1) * 512)
                    self.STT(xs[:, sl], ps, 0.5, xs[:, sl], ALU.mult, ALU.add, [pn, xn], [xn])
                self.STORE("sp", hscr[i * 128:(i + 1) * 128, :], xs, [xn], "hst_" + xn, w=[f"hscr{i}"])
                mc = None
                if i == 0:
                    mc = C_MASK + 2 * ut["mcol"]
                if i == 8:
                    mc = C_MASK + 2 * ut["mcol"] + 1
                curB = self.norm_A(xs, xn, C_WBC2, maskcol=mc) + (i,)
            if prevB is not None:
                self.norm_B(prevB[0], prevB[1], hnT[:, :, prevB[2] * 128:(prevB[2] + 1) * 128], f"hnT{prevB[2]}")
            prevB = curB
        self.dump("hnT", hnT, [128, 8, TR], BF16, "hnT8")
        if getattr(self, "stop_after", None) == "U8b":
            self.R.dve(lambda e: e.memset(self.Sfx, 0.0), [], ["Sfx"])
            self.barrier()
            return

        self.barrier()
        hn_names = [f"hnT{i}" for i in range(9)]
        self.top = self.base
        fT = A(22 * 1024).rearrange("p (j n) -> p j n", j=22)
        tmp = self.arena[:, self.tmp0:self.tmp0 + 13312]
        u = [self.slabI[:, 0:2304].bitcast(F32), self.slabI[:, 2304:4608].bitcast(F32)]
        cv = [tmp[:, 0:2048].bitcast(F32), tmp[:, 2048:4096].bitcast(F32)]
        gq = tmp[:, 4096:6144].bitcast(F32)
        wdT = tmp[:, 6144:13312].rearrange("p (j n) -> p j n", j=7)
        wdL = tmp[:, 0:6144].rearrange("p (j n) -> p j n", j=6)
        wdC = A(9 * 1024).rearrange("p (j n) -> p j n", j=9)
        assert self.top <= self.base + 8 * TK + 4 * TR + 4 * TR + 8 * TR, "fT/wdC would overlap hnT"
        wdr = T["w_down"].rearrange("(j p) n -> p j n", p=128)
        w_up = T["w_up"]
        for j in range(22):
            si_ = j % 4
            sl_, wn = self.wsl[si_], f"wsl{si_}"
            wv2 = sl_[:, 0:2048].rearrange("p (k n) -> p k n", k=8)
            self.LOAD("pool", wv2[:, :, 0:128], w_up[:, j * 128:(j + 1) * 128].rearrange("(k p) n -> p k n", p=128),
                      [wn], wn + "a")
            self.LOAD("pool", wv2[:, :, 128:256],
                      w_up[:, 2816 + j * 128:2816 + (j + 1) * 128].rearrange("(k p) n -> p k n", p=128), [wn], wn + "b")
            if j == 3:
                for g in range(0, 9, 3):
                    self.LOAD("pool", wdC[:, g:g + 3, :], wdr[:, g:g + 3, :], ["wdC"], f"wdC{g}")
                self.LOAD("pool", wdT[:, 0:4, :], wdr[:, 9:13, :], ["wdT"], "wdT0")
                self.LOAD("pool", wdT[:, 4:7, :], wdr[:, 13:16, :], ["wdT"], "wdT4")
            for ab in range(2):
                for (t0, n) in blocks(TR):
                    ps, pn = self.ps_new()
                    for kc in range(8):
                        self.MM(ps[:, :n], wv2[:, kc, ab * 128:(ab + 1) * 128], hnT[:, kc, t0:t0 + n], kc == 0, kc == 7,
                                [wn] + hn_names, [pn])
                    self.ACOPY(u[ab][:, t0:t0 + n], ps[:, :n], [pn], [f"u{ab}"])
                jj = ab * 22 + j
                cw = lambda tap, jj=jj: cf[:, C_CW + jj * 3 + tap:C_CW + jj * 3 + tap + 1]
                self.ACT(cv[ab], u[ab][:, 63:1087], AF.Identity, [f"u{ab}", "cf"], [f"cv{ab}"], scale=cw(0),
                         bias=cf[:, C_CB + jj:C_CB + jj + 1])
                self.STT(cv[ab], u[ab][:, 64:1088], cw(1), cv[ab], ALU.mult, ALU.add, [f"u{ab}", "cf", f"cv{ab}"],
                         [f"cv{ab}"])
                self.STT(cv[ab], u[ab][:, 65:1089], cw(2), cv[ab], ALU.mult, ALU.add, [f"u{ab}", "cf", f"cv{ab}"],
                         [f"cv{ab}"])
            self.ACT(gq, cv[0], AF.Square, ["cv0"], ["gq"])
            self.TS(gq, gq, 0.044715, 1.0, ALU.mult, ALU.add, ["gq"], ["gq"])
            self.TT(gq, gq, cv[0], ALU.mult, ["gq", "cv0"], ["gq"])
            self.ACT(gq, gq, AF.Tanh, ["gq"], ["gq"], scale=0.7978845608028654)
            self.STT(gq, gq, 1.0, cv[0], ALU.add, ALU.mult, ["gq", "cv0"], ["gq"])
            self.TT(fT[:, j, :], gq, cv[1], ALU.mult, ["gq", "cv1"], ["fT"])
        self.dump("fT", fT, [128, 22, 1024], BF16, "fT")
        if getattr(self, "stop_after", None) == "U10":
            self.R.dve(lambda e: e.memset(self.Sfx, 0.0), [], ["Sfx"])
            self.barrier()
            return

        self.fence(["cv0", "cv1", "gq"], ["wdL"])
        self.LOAD("pool", wdL[:, 0:3, :], wdr[:, 16:19, :], ["wdL"], "wdL0")
        self.LOAD("pool", wdL[:, 3:6, :], wdr[:, 19:22, :], ["wdL"], "wdL3")

        def wdj(j):
            if j < 9:
                return wdC[:, j], "wdC"
            if j < 16:
                return wdT[:, j - 9], "wdT"
            return wdL[:, j - 16], "wdL"
        for i8 in range(8):
            hi_ = self.rot("xs", 2)
            hs, hn_ = self.xs[hi_], f"xs{hi_}"
            self.LOAD("sp", hs, hscr[64 + i8 * 128:64 + (i8 + 1) * 128, :], [hn_], hn_,
                      r=[f"hscr{i8}", f"hscr{i8 + 1}"])
            for half in range(2):
                ps, pn = self.ps_new()
                for j in range(22):
                    self.MM(ps, fT[:, j, i8 * 128:(i8 + 1) * 128], wdj(j)[0][:, half * 512:(half + 1) * 512], j == 0,
                            j == 21, ["fT", wdj(j)[1]], [pn])
                sl = slice(half * 512, (half + 1) * 512)
                self.STT(hs[:, sl], ps, 0.5, hs[:, sl], ALU.mult, ALU.add, [pn, hn_], [hn_])
            self.stores.append(self.STORE("sp", yout[i8 * 128:(i8 + 1) * 128, :], hs, [hn_], "yst_" + hn_))
        self.barrier()

    def sequence(self, xsrc, n_units, tau0, pre, tau_max, uts, yout, hscr):
        T = self.T
        w_in = T["w_in"]
        self.zero_state("f")
        self.zero_state("b")
        post = list(range(tau_max, tau0 + 8, -1))
        saves = {}
        inits = [None] * n_units
        for m in range(n_units):
            need = tau0 + 8 * m + 9
            if need <= tau_max:
                saves[need] = (self.Ssave[m], f"Ssave{m}")
                inits[m] = (self.Ssave[m], f"Ssave{m}")
        if pre or post:
            wk, wkn = self.wload(1, w_in[:, 2048:2560], 512)
            wv, wvn = self.wload(3, w_in[:, 2560:3072], 512)
            if pre:
                self.state_scan(xsrc, pre, "f", wk, wkn, wv, wvn)
            if post:
                self.state_scan(xsrc, post, "b", wk, wkn, wv, wvn, saves=saves)
        self.barrier()
        for m in range(n_units):
            self.unit(xsrc, tau0 + 8 * m - 2, uts[m], yout[m * 1024:(m + 1) * 1024, :], hscr, inits[m], f"u{m}")
            Sf, Sfx = self.S["f"], self.Sfx
            self.R.dve(lambda e, Sf=Sf, Sfx=Sfx: e.tensor_copy(out=Sf, in_=Sfx), ["Sfx"], ["Sf"])
            self.barrier()


def build_nc(dbg=None, only_prompt_units=None, xs_tiles=225, variant=None):
    nc = bass.Bass("TRN2", target_bir_lowering=False)
    dt = lambda name, shape, kind="ExternalInput": nc.dram_tensor(name, list(shape), F32, kind=kind).ap()
    T = {
        "xp": dt("xp", [2, 21 * 128, D]),
        "xs": dt("xs", [xs_tiles * 128, D]),
        "cf": dt("cf", [128, NCF]),
        "cb": dt("cb", [128, NCB]),
        "slabI": dt("slabI", [128, 5120]),
        "sT_p": dt("sT_p", [3, 128, 7168]),
        "sB_p": dt("sB_p", [2, 128, 7168]),
        "sT_s": dt("sT_s", [3, 128, 7168]),
        "sB_s": dt("sB_s", [2, 128, 7168]),
        "w_in": dt("w_in", [D, 5664]),
        "w_a2_f": dt("w_a2_f", [16, 512]),
        "b_a_f": dt("b_a_f", [1, 512]),
        "w_a2_b": dt("w_a2_b", [16, 512]),
        "b_a_b": dt("b_a_b", [1, 512]),
        "w_na_proj": dt("w_na_proj", [512, D]),
        "w_gla_proj": dt("w_gla_proj", [512, D]),
        "w_out": dt("w_out", [D, D]),
        "w_up": dt("w_up", [D, 5632]),
        "w_down": dt("w_down", [2816, D]),
    }
    yp = dt("yp", [2, 2048, D], "ExternalOutput")
    ys = dt("ys", [4096, D], "ExternalOutput")
    hscr = dt("hscr", [TR, D], "Internal")
    with ExitStack() as es:
        k = K(nc, es, dbg)
        k.setup(T)
        k.Sfx = k.alloc(512, F32)
        k.base = k.top
        sT_p = [T["sT_p"][e] for e in range(3)]
        sB_p = [T["sB_p"][e] for e in range(2)]
        sT_s = [T["sT_s"][e] for e in range(3)]
        sB_s = [T["sB_s"][e] for e in range(2)]
        ut_p = [dict(top=sT_p, bot=None, mcol=0), dict(top=None, bot=sB_p, mcol=1)]
        ut_s = [dict(top=sT_s, bot=None, mcol=2), dict(top=None, bot=None, mcol=3),
                dict(top=None, bot=None, mcol=4), dict(top=None, bot=sB_s, mcol=5)]
        if only_prompt_units is not None:
            k.dbg_unit = f"u{only_prompt_units - 1}"
            k.sequence(T["xp"][0], only_prompt_units, 2, [], 18, ut_p, yp[0], hscr)
        elif variant == "prompts":
            for b in range(2):
                k.sequence(T["xp"][b], 2, 2, [], 18, ut_p, yp[b], hscr)
        elif variant == "scans":
            k.sequence(T["xs"], 0, 96, list(range(0, 96)), 224, ut_s, ys, hscr)
        else:
            for b in range(2):
                k.sequence(T["xp"][b], 2, 2, [], 18, ut_p, yp[b], hscr)
            k.sequence(T["xs"], 4, 96, list(range(0, 96)), 224, ut_s, ys, hscr)
        n, cnt = k.R.emit(nc, k.stores)
        k.nops = n
    return nc, k


def make_slab(rpb, i, kvs, lo, hi):
    H = rpb.shape[0]
    out = np.full((128, 7, H, 128), NEG, np.float32)
    kc = np.arange(64)
    qc = np.arange(64)
    c0 = np.clip(qc - 8, 0, 48)
    colv = (kc[:, None] >= c0[None, :]) & (kc[:, None] < c0[None, :] + 16)
    dcv = np.clip(kc[:, None] - qc[None, :] + 15, 0, 30)
    for a, t in enumerate(kvs):
        for rr in range(2):
            rk = 2 * t - 5 + rr
            for qq in range(2):
                rq = 2 * i - 1 + qq
                if lo <= rq < hi:
                    r0 = min(max(rq - 4, lo), hi - 8)
                else:
                    r0 = rq - 4
                if not (r0 <= rk < r0 + 8):
                    continue
                dr = rk - rq + 7
                blk = np.where(colv[None], rpb[:, dr][:, dcv], NEG)
                out[rr * 64:(rr + 1) * 64, a, :, qq * 64:(qq + 1) * 64] = blk.transpose(1, 0, 2)
    return out


def host_consts(norm1_w, norm2_w, gla_norm_w, qn_w, kn_w, conv_w, conv_b, masks):
    cf = np.zeros((128, NCF), np.float32)
    cf[:, C_WBC1:C_WBC1 + D] = norm1_w[None, :]
    cf[:, C_WBC2:C_WBC2 + D] = norm2_w[None, :]
    cf[:, C_GNW:C_GNW + 512] = np.tile(gla_norm_w, 4)[None, :]
    idx = np.arange(128)
    cf[:, C_UINC:C_UINC + 128] = (idx[:, None] <= idx[None, :]) * (-1.0 / 16)
    cf[:, C_ULT:C_ULT + 128] = (idx[:, None] < idx[None, :]) * (-1.0 / 16)
    cf[:, C_UGT:C_UGT + 128] = (idx[:, None] > idx[None, :]) * (-1.0 / 16)
    cw = conv_w.reshape(3, 44, 128)
    cf[:, C_CW:C_CW + 132] = cw.transpose(2, 1, 0).reshape(128, 132)
    cf[:, C_CB:C_CB + 44] = conv_b.reshape(44, 128).T
    cf[:, C_QNW] = np.tile(qn_w, 2)
    cf[:, C_KNW] = np.tile(kn_w, 2)
    cf[:, C_LNQ] = np.float32(np.log(0.125))
    cf[:, C_NEG] = -1.0 / 16
    cf[:, C_EPS] = EPS
    for m, (top_real, bot_real) in enumerate(masks):
        cf[:, C_MASK + 2 * m] = 1.0
        cf[0:64, C_MASK + 2 * m] = 1.0 if top_real else 0.0
        cf[:, C_MASK + 2 * m + 1] = 1.0
        cf[64:128, C_MASK + 2 * m + 1] = 1.0 if bot_real else 0.0
    cb = np.zeros((128, NCB), np.float32)
    cb[:, B_ID:B_ID + 128] = np.eye(128)
    cb[0:64, B_OB:B_OB + 64] = 1.0
    cb[64:128, B_OB + 64:B_OB + 128] = 1.0
    cb[:, B_ONE:B_ONE + 128] = 1.0
    cb[:, B_MF:B_MF + 128] = (idx[:, None] <= idx[None, :])
    cb[:, B_MB:B_MB + 128] = (idx[:, None] > idx[None, :])
    return cf, cb


def make_in_maps(inp):
    f = lambda k: np.asarray(inp[k], np.float32)
    x_prompt, x_sample = f("x_prompt"), f("x_sample")
    rpb = f("rpb")[0]
    w_in = np.ascontiguousarray(f("w_in")[0])
    shared = {
        "w_in": w_in,
        "w_a2_f": np.ascontiguousarray(f("w_a2_f")[0]), "b_a_f": np.ascontiguousarray(f("b_a_f")[0][None, :]),
        "w_a2_b": np.ascontiguousarray(f("w_a2_b")[0]), "b_a_b": np.ascontiguousarray(f("b_a_b")[0][None, :]),
        "w_na_proj": np.ascontiguousarray(f("w_na_proj")[0]), "w_gla_proj": np.ascontiguousarray(f("w_gla_proj")[0]),
        "w_out": np.ascontiguousarray(f("w_out")[0]), "w_up": np.ascontiguousarray(f("w_up")[0]),
        "w_down": np.ascontiguousarray(f("w_down")[0]),
    }
    flat = lambda s: np.ascontiguousarray(s.transpose(0, 2, 1, 3).reshape(128, 7168))
    slabI = make_slab(rpb, 3, kv_list(3, False), -100, 100)[:, 0:5]
    slabI = np.ascontiguousarray(slabI[:, ::-1].transpose(0, 2, 1, 3).reshape(128, 5120))
    sT_cl = np.stack([flat(make_slab(rpb, i, kv_list(i, True), 0, 32)) for i in EDGE_TOP])
    sT_un = np.stack([flat(make_slab(rpb, i, kv_list(i, True), -100, 100)) for i in EDGE_TOP])
    sB_cl = np.stack([flat(make_slab(rpb, i, kv_list(i, True), -16, 16)) for i in EDGE_BOT])
    sB_un = np.stack([flat(make_slab(rpb, i, kv_list(i, True), -100, 100)) for i in EDGE_BOT])
    shared.update({"slabI": slabI, "sT_p": sT_cl, "sB_p": sB_cl})
    maps = []
    for c in range(NCORES):
        s, j = c // 4, c % 4
        xp = np.zeros((2, 42, 64, D), np.float32)
        for b in range(2):
            xp[b, 5:37] = x_prompt[2 * c + b].reshape(32, 64, D)
        R0 = 64 * j
        xs = np.zeros((450, 64, D), np.float32)
        g0 = R0 - 193
        lo_, hi_ = max(0, g0), min(256, g0 + 450)
        xs[lo_ - g0:hi_ - g0] = x_sample[s].reshape(256, 64, D)[lo_:hi_]
        masks = [(False, True), (True, False)]
        for m in range(4):
            masks.append((R0 + 16 * m - 1 >= 0, R0 + 16 * m + 16 < 256))
        cf, cb = host_consts(f("norm1_w")[0], f("norm2_w")[0], f("gla_norm_w")[0], f("qn_w")[0], f("kn_w")[0],
                             f("conv_w")[0], f("conv_b")[0], masks)
        d = dict(shared)
        d.update({"xp": xp.reshape(2, 21 * 128, D), "xs": xs.reshape(225 * 128, D), "cf": cf, "cb": cb,
                  "sT_s": sT_cl if j == 0 else sT_un, "sB_s": sB_cl if j == 3 else sB_un})
        maps.append(d)
    return maps


def kernel(**inp):
    maps = make_in_maps(inp)
    nc, k = build_nc()
    res = run_bass_kernel_spmd(nc, maps, core_ids=list(range(NCORES)))
    y_prompt = np.empty((16, 2048, D), np.float32)
    y_sample = np.empty((2, 16384, D), np.float32)
    for c in range(NCORES):
        s, j = c // 4, c % 4
        r = res.results[c]
        y_prompt[2 * c:2 * c + 2] = np.asarray(r["yp"], np.float32)
        y_sample[s, 4096 * j:4096 * (j + 1)] = np.asarray(r["ys"], np.float32)
    return (y_prompt, y_sample)
```

```python
import numpy as np
import concourse.bass as bass
import concourse.mybir as mybir
from concourse.bass_utils import run_bass_kernel_spmd
from contextlib import ExitStack

F32 = mybir.dt.float32
BF16 = mybir.dt.bfloat16
AF = mybir.ActivationFunctionType
ALU = mybir.AluOpType
EPS = 1e-6
NCORES = 8
D = 1024
TR = 1152
TK = 1664
NEG = -30000.0
EDGE_TOP = (0, 1, 2)
EDGE_BOT = (7, 8)
C_WBC1, C_WBC2, C_GNW, C_UINC, C_ULT, C_UGT, C_CW, C_CB = 0, 1024, 2048, 2560, 2688, 2816, 2944, 3076
C_QNW, C_KNW, C_LNQ, C_NEG, C_MASK, C_EPS = 3120, 3121, 3122, 3123, 3124, 3136
NCF = 3138
B_ID, B_OB, B_ONE, B_MF, B_MB, B_ZERO = 0, 128, 256, 384, 512, 640
NCB = 768


class Buf:
    __slots__ = ("w", "rs")

    def __init__(self):
        self.w = None
        self.rs = []


class Rec:
    ENGS = ("pe", "act", "dve", "pool", "sp")

    def __init__(self):
        self.ops = []
        self.bufs = {}
        self.bar = None
        self.last = {}
        self.dmas = []

    def B(self, name):
        b = self.bufs.get(name)
        if b is None:
            b = self.bufs[name] = Buf()
        return b

    def add(self, eng, fn, r=(), w=(), dma=False, stream=None):
        deps = set()
        for n in r:
            b = self.B(n)
            if b.w is not None:
                deps.add(b.w)
        for n in w:
            b = self.B(n)
            if b.w is not None:
                deps.add(b.w)
            deps.update(b.rs)
        if self.bar is not None:
            deps.add(self.bar)
        i = len(self.ops)
        self.ops.append(dict(eng=eng, fn=fn, deps=deps, dma=dma, stream=stream, sig=False, ord=0))
        for n in r:
            self.B(n).rs.append(i)
        for n in w:
            b = self.B(n)
            b.w = i
            b.rs = []
        if dma:
            self.dmas.append(i)
        else:
            self.last[eng] = i
        return i

    def barrier(self, fn):
        deps = set(self.last.values()) | set(self.dmas)
        if self.bar is not None:
            deps.add(self.bar)
        i = len(self.ops)
        self.ops.append(dict(eng="dve", fn=fn, deps=deps, dma=False, stream=None, sig=False, ord=0))
        self.bar = i
        self.last["dve"] = i
        self.dmas = []
        return i

    def pe(self, fn, r=(), w=()):
        return self.add("pe", fn, r, w)

    def act(self, fn, r=(), w=()):
        return self.add("act", fn, r, w)

    def dve(self, fn, r=(), w=()):
        return self.add("dve", fn, r, w)

    def pool(self, fn, r=(), w=()):
        return self.add("pool", fn, r, w)

    def dma(self, q, fn, r=(), w=(), stream=None):
        return self.add(q, fn, r, w, dma=True, stream=stream)

    def emit(self, nc, final_wait_ops=()):
        ops = self.ops
        ops.append(dict(eng="sp", fn=None, deps=set(final_wait_ops), dma=False, stream=None, sig=False, ord=0))
        for o in ops:
            for d in o["deps"]:
                od = ops[d]
                if od["dma"]:
                    continue
                if od["eng"] == "pe" and o["eng"] == "pe" and not o["dma"]:
                    continue
                od["sig"] = True
        cnt = {}
        for o in ops:
            if o["dma"]:
                k = ("s", o["stream"])
                cnt[k] = cnt.get(k, 0) + 1
                o["ord"] = cnt[k]
            elif o["sig"]:
                k = ("e", o["eng"])
                cnt[k] = cnt.get(k, 0) + 1
                o["ord"] = cnt[k]
        with ExitStack() as es:
            sem = {}
            for k in cnt:
                sem[k] = es.enter_context(nc.semaphore("sem_%s_%s" % k))
            block = es.enter_context(nc.Block())
            by_eng = {e: [o for o in ops if o["eng"] == e] for e in self.ENGS}

            def run(engh, ename):
                waited = {}
                for o in by_eng[ename]:
                    need = {}
                    for d in o["deps"]:
                        od = ops[d]
                        if od["dma"]:
                            k = ("s", od["stream"])
                            v = 16 * od["ord"]
                        else:
                            if od["eng"] == "pe" and ename == "pe" and not o["dma"]:
                                continue
                            k = ("e", od["eng"])
                            v = od["ord"]
                        if v > need.get(k, 0):
                            need[k] = v
                    for k, v in need.items():
                        if waited.get(k, 0) >= v:
                            continue
                        waited[k] = v
                        engh.wait_ge(sem[k], v)
                    if o["fn"] is None:
                        continue
                    ins = o["fn"](engh)
                    if o["dma"]:
                        ins.then_inc(sem[("s", o["stream"])], 16)
                    elif o["sig"]:
                        ins.then_inc(sem[("e", ename)], 1)

            if by_eng["sp"]:
                block.sync(lambda e: run(e, "sp"))
            if by_eng["pool"]:
                block.gpsimd(lambda e: run(e, "pool"))
            if by_eng["act"]:
                block.scalar(lambda e: run(e, "act"))
            if by_eng["dve"]:
                block.vector(lambda e: run(e, "dve"))
            if by_eng["pe"]:
                block.tensor(lambda e: run(e, "pe"))
        return len(ops), cnt


def blocks(T):
    out = []
    t = 0
    while t < T:
        n = min(512, T - t)
        out.append((t, n))
        t += n
    return out


def kv_list(i, edge):
    if not edge:
        return list(range(i, i + 5))
    return {0: list(range(0, 7)), 1: list(range(1, 7)), 2: list(range(2, 7)),
            7: list(range(6, 12)), 8: list(range(6, 13))}[i]


class K:
    def __init__(self, nc, es, dbg=None):
        self.nc = nc
        self.es = es
        self.R = Rec()
        self.dbg = dbg
        self.dumps = []
        self.stores = []
        self.cnt = {}
        self.arena = es.enter_context(nc.sbuf_tensor("arena", [128, 106400], BF16))
        self.top = 0
        self.ps = [es.enter_context(nc.psum_tensor(f"ps{i}", [128, 512], F32)) for i in range(6)]
        self.pb = [es.enter_context(nc.psum_tensor(f"pb{i}", [128, 1024], BF16)) for i in range(2)]

    def alloc(self, n, dt=BF16):
        if dt == F32:
            n2 = 2 * n
        else:
            n2 = n
        self.top = (self.top + 1) // 2 * 2
        a = self.arena[:, self.top:self.top + n2]
        self.top += n2
        assert self.top <= 106400, self.top
        return a.bitcast(F32) if dt == F32 else a

    def rot(self, key, n):
        c = self.cnt.get(key, 0)
        self.cnt[key] = c + 1
        return c % n

    def ps_new(self, bank=None):
        i = self.rot("ps", 6) if bank is None else bank
        return self.ps[i][:], f"ps{i}"

    def pt_new(self):
        i = self.rot("pt", 2)
        return self.pb[i][:], f"pb{i}"

    def MM(self, out, lhsT, rhs, start, stop, r, w):
        self.R.pe(lambda e: e.matmul(out, lhsT=lhsT, rhs=rhs, start=start, stop=stop), r, w)

    def TR_(self, out, in_, r, w):
        ident = self.ident
        self.R.pe(lambda e: e.transpose(out=out, in_=in_, identity=ident), list(r) + ["cb"], w)

    def ACT(self, out, in_, func, r, w, scale=None, bias=None, accum=None):
        kw = {}
        if scale is not None:
            kw["scale"] = scale
        if bias is not None:
            kw["bias"] = bias
        if accum is not None:
            kw["accum_out"] = accum
        self.R.act(lambda e: e.activation(out=out, in_=in_, func=func, **kw), r, w)

    def ACOPY(self, out, in_, r, w):
        self.R.act(lambda e: e.copy(out=out, in_=in_), r, w)

    def AMUL(self, out, in_, c, r, w):
        self.R.act(lambda e: e.mul(out=out, in_=in_, mul=c), r, w)

    def DCOPY(self, out, in_, r, w):
        self.R.dve(lambda e: e.tensor_copy(out=out, in_=in_), r, w)

    def TS(self, out, in0, s1, s2, op0, op1, r, w):
        if op1 is None:
            self.R.dve(lambda e: e.tensor_scalar(out=out, in0=in0, scalar1=s1, scalar2=None, op0=op0), r, w)
        else:
            self.R.dve(lambda e: e.tensor_scalar(out=out, in0=in0, scalar1=s1, scalar2=s2, op0=op0, op1=op1), r, w)

    def TT(self, out, in0, in1, op, r, w):
        self.R.dve(lambda e: e.tensor_tensor(out=out, in0=in0, in1=in1, op=op), r, w)

    def STT(self, out, in0, scalar, in1, op0, op1, r, w):
        self.R.dve(lambda e: e.scalar_tensor_tensor(out=out, in0=in0, scalar=scalar, in1=in1, op0=op0, op1=op1), r, w)

    def LOAD(self, q, out, in_, w, stream, r=()):
        return self.R.dma(q, lambda e: e.dma_start(out=out, in_=in_), r=r, w=w, stream=stream)

    def STORE(self, q, out, in_, r, stream, w=()):
        i = self.R.dma(q, lambda e: e.dma_start(out=out, in_=in_), r=r, w=w, stream=stream)
        return i

    def dump(self, name, ap, shape, dt, rname):
        if self.dbg is None or name not in self.dbg or getattr(self, "cur_unit", None) != getattr(self, "dbg_unit", None):
            return
        d = self.nc.dram_tensor("dbg_" + name, list(shape), dt, kind="ExternalOutput").ap()
        self.stores.append(self.STORE("sp", d, ap, [rname], "dbg_" + name))
        self.dumps.append(name)

    def barrier(self):
        bt = self.bartile
        self.R.barrier(lambda e: e.memset(bt, 0.0))

    def fence(self, reads, writes):
        bt = self.bartile
        self.R.dve(lambda e: e.memset(bt, 0.0), list(reads), list(writes) + ["bartile"])

    def setup(self, T):
        self.T = T
        A = self.alloc
        self.cf = A(NCF, F32)
        self.cb = A(NCB)
        self.ident = self.cb[:, B_ID:B_ID + 128]
        self.onesblk = self.cb[:, B_OB:B_OB + 128]
        self.ones = self.cb[:, B_ONE:B_ONE + 128]
        self.Mf = self.cb[:, B_MF:B_MF + 128]
        self.Mb = self.cb[:, B_MB:B_MB + 128]
        self.wa2 = [A(512), A(512)]
        self.ba = [A(512), A(512)]
        self.wlr = A(256).rearrange("p (k n) -> p k n", k=8)
        self.slabI = A(5120)
        self.S = {"f": A(512, F32), "b": A(512, F32)}
        self.Sbf = A(512)
        self.Ssave = [A(512, F32) for _ in range(4)]
        self.xs = [A(1024, F32) for _ in range(2)]
        self.jk = [A(1024) for _ in range(2)]
        self.xb = [A(1024) for _ in range(2)]
        self.xTt = [A(1024).rearrange("p (k n) -> p k n", k=8) for _ in range(2)]
        self.st = [A(8, F32) for _ in range(8)]
        self.wsl = [A(4096) for _ in range(4)]
        self.PT = [A(896) for _ in range(2)]
        self.ona = [A(512) for _ in range(2)]
        self.bartile = A(2, F32)
        self.tmp0 = self.top = (self.top + 1) // 2 * 2
        self.tf = [A(512, F32) for _ in range(8)]
        self.tb = [A(512) for _ in range(10)]
        assert self.top - self.tmp0 == 13312
        self.base = self.top
        L = self.LOAD
        L("sp", self.cf, T["cf"][:, :], ["cf"], "cf")
        L("pool", self.cb, T["cb"][:, :], ["cb"], "cb")
        for d, nm in enumerate(("f", "b")):
            L("pool", self.wa2[d][0:16, :], T["w_a2_" + nm][:, :], ["wa2"], "wa2" + nm)
            L("pool", self.ba[d][0:1, :], T["b_a_" + nm][:, :], ["ba"], "ba" + nm)
        L("pool", self.wlr, T["w_in"][:, 3584:3616].rearrange("(k p) n -> p k n", p=128), ["wlr"], "wlr")

    def tF(self):
        i = self.rot("tf", 8)
        return self.tf[i], f"tf{i}"

    def tB(self):
        i = self.rot("tb", 10)
        return self.tb[i], f"tb{i}"

    def stt_(self):
        i = self.rot("st", 8)
        return self.st[i], f"st{i}"

    def wload(self, si, cols_ap, ncols, kchunks=8):
        sl, nm = self.wsl[si], f"wsl{si}"
        v = sl[:, 0:kchunks * ncols].rearrange("p (k n) -> p k n", k=kchunks)
        if getattr(self, "nowload", False) and self.cnt.get("wl%d" % si, 0) > 0:
            return v, nm
        self.cnt["wl%d" % si] = 1
        self.LOAD("pool", v, cols_ap.rearrange("(k p) n -> p k n", p=128), [nm], nm)
        return v, nm

    def xload(self, src_rows):
        i = self.rot("xs", 2)
        self.LOAD("sp", self.xs[i], src_rows, [f"xs{i}"], f"xs{i}")
        return self.xs[i], f"xs{i}"

    def norm_A(self, src, sname, wbc_off, maskcol=None):
        j = self.rot("jk", 2)
        jk, jn = self.jk[j], f"jk{j}"
        xb, xn = self.xb[j], f"xb{j}"
        st, sn = self.stt_()
        cf = self.cf
        self.ACT(jk, src, AF.Square, [sname], [jn, sn], accum=st[:, 0:1])
        self.ACT(st[:, 2:3], st[:, 0:1], AF.Ln, [sn, "cf"], [sn], scale=1.0 / D, bias=cf[:, C_EPS:C_EPS + 1])
        self.ACT(st[:, 3:4], st[:, 2:3], AF.Exp, [sn], [sn], scale=-0.5)
        if maskcol is not None:
            self.TT(st[:, 3:4], st[:, 3:4], cf[:, maskcol:maskcol + 1], ALU.mult, [sn, "cf"], [sn])
        self.STT(xb, src, st[:, 3:4], cf[:, wbc_off:wbc_off + D], ALU.mult, ALU.mult, [sname, sn, "cf"], [xn])
        return xb, xn

    def norm_B(self, xb, xn, dst3, dname):
        pt, pn = self.pt_new()
        for kc in range(8):
            self.TR_(pt[:, kc * 128:(kc + 1) * 128], xb[:, kc * 128:(kc + 1) * 128], [xn], [pn])
        src3 = pt.rearrange("p (a b) -> p a b", a=8)
        self.DCOPY(dst3, src3, [pn], [dname])

    def norm_T(self, src, sname, wbc_off, dst3, dname, maskcol=None):
        xb, xn = self.norm_A(src, sname, wbc_off, maskcol)
        self.norm_B(xb, xn, dst3, dname)

    def gates(self, lrT, lrn, d, bank=None):
        ps, pn = self.ps_new(bank)
        self.MM(ps, lrT, self.wa2[d][0:16, :], True, False, [lrn, "wa2"], [pn])
        self.MM(ps, self.ones[0:1, :], self.ba[d][0:1, :], False, True, ["cb", "ba"], [pn])
        gp, gn = self.tF()
        self.ACT(gp, ps, AF.Exp, [pn], [gn], scale=-1.0)
        self.ACT(gp, gp, AF.Ln, [gn, "cb"], [gn], bias=self.ones[:, 0:1])
        return gp, gn

    def tok_kv(self, xT3, xname, wk, wkn, wv, wvn):
        psk, pkn = self.ps_new()
        for kc in range(8):
            self.MM(psk, xT3[:, kc, :], wk[:, kc, :], kc == 0, kc == 7, [xname, wkn], [pkn])
        psv, pvn = self.ps_new()
        for kc in range(8):
            self.MM(psv, xT3[:, kc, :], wv[:, kc, :], kc == 0, kc == 7, [xname, wvn], [pvn])
        vt, vn = self.tB()
        self.ACOPY(vt, psv, [pvn], [vn])
        return psk, pkn, vt, vn

    def state_prep(self, gp, gn, psk, pkn, d, banks=(None, None)):
        cf = self.cf
        U = cf[:, C_UGT:C_UGT + 128] if d == "f" else cf[:, C_ULT:C_ULT + 128]
        psc, pcn = self.ps_new(banks[0])
        self.MM(psc, U, gp, True, True, ["cf", gn], [pcn])
        Ec, en = self.tF()
        self.ACT(Ec, psc, AF.Exp, [pcn], [en])
        kt, kn = self.tB()
        self.TT(kt, psk, Ec, ALU.mult, [pkn, en], [kn])
        pse, pen = self.ps_new(banks[1])
        for hh in range(4):
            self.MM(pse[:, hh:hh + 1], gp[:, hh * 128:(hh + 1) * 128], cf[:, C_NEG:C_NEG + 1], True, True,
                    [gn, "cf"], [pen])
        st, sn = self.stt_()
        self.ACT(st[:, 0:4], pse[:, 0:4], AF.Exp, [pen], [sn])
        return kt, kn, st, sn

    def state_apply(self, kt, kn, st, sn, vt, vn, d, snap=None, snapn=None, bank=None):
        S, Sn = self.S[d], "S" + d
        psd, pdn = self.ps_new(bank)
        for hh in range(4):
            sl = slice(hh * 128, (hh + 1) * 128)
            self.MM(psd[:, sl], kt[:, sl], vt[:, sl], True, True, [kn, vn], [pdn])
        S3 = S.rearrange("p (h v) -> p h v", h=4)
        self.TT(S3, S3, st[:, 0:4].rearrange("p (h o) -> p h o", o=1).to_broadcast([128, 4, 128]), ALU.mult,
                [Sn, sn], [Sn])
        if snap is not None:
            self.ACOPY(snap, S, [Sn], [snapn])
        self.TT(S, S, psd, ALU.add, [Sn, pdn], [Sn])

    def state_update(self, gp, gn, psk, pkn, vt, vn, d, snap=None, snapn=None):
        kt, kn, st, sn = self.state_prep(gp, gn, psk, pkn, d)
        self.state_apply(kt, kn, st, sn, vt, vn, d, snap, snapn)

    def zero_state(self, d):
        S = self.S[d]
        self.R.dve(lambda e: e.memset(S, 0.0), [], ["S" + d])

    def scan_pipe(self, d, n, p1, wk, wkn, wv, wvn, lr_of=None, snap_of=None, save_after=None, p1a=None):
        di = 0 if d == "f" else 1
        s1, s2, s3 = {}, {}, {}
        ksb_pool = [self.slabI[:, j * 1024:(j + 1) * 1024].bitcast(F32) for j in range(3)]

        def P2(i):
            xT3, xn = s1.pop(i)
            psk, pkn = self.ps_new(0)
            for kc in range(8):
                self.MM(psk, xT3[:, kc, :], wk[:, kc, :], kc == 0, kc == 7, [xn, wkn], [pkn])
            j = self.rot("ksb", 3)
            ksb, ksn = ksb_pool[j], f"ksb{j}"
            self.DCOPY(ksb, psk, [pkn], [ksn])
            psv, pvn = self.ps_new(1)
            for kc in range(8):
                self.MM(psv, xT3[:, kc, :], wv[:, kc, :], kc == 0, kc == 7, [xn, wvn], [pvn])
            vt, vn = self.tB()
            self.DCOPY(vt, psv, [pvn], [vn])
            if lr_of is None:
                psl, pln = self.ps_new(2)
                for kc in range(8):
                    self.MM(psl[0:16, 0:128], self.wlr[:, kc, 16 * di:16 * di + 16], xT3[:, kc, :], kc == 0, kc == 7,
                            ["wlr", xn], [pln])
                lrt, lrn = self.tB()
                self.DCOPY(lrt[0:16, 0:128], psl[0:16, 0:128], [pln], [lrn])
                lr = (lrt[0:16, 0:128], lrn)
            else:
                lr = lr_of(i)
            s2[i] = (ksb, ksn, vt, vn, lr)

        def P3a(i):
            ksb, ksn, vt, vn, lr = s2.pop(i)
            gp, gn = self.gates(lr[0], lr[1], di, bank=3)
            s3[i] = (ksb, ksn, vt, vn, gp, gn)

        def P3b(i):
            ksb, ksn, vt, vn, gp, gn = s3.pop(i)
            kt, kn, st, sn = self.state_prep(gp, gn, ksb, ksn, d, banks=(4, 5))
            sp = snap_of(i) if snap_of else None
            self.state_apply(kt, kn, st, sn, vt, vn, d, sp[0] if sp else None, sp[1] if sp else None, bank=5)
            if save_after and i in save_after:
                dst, dn = save_after[i]
                S = self.S[d]
                self.R.dve(lambda e, dst=dst, S=S: e.tensor_copy(out=dst, in_=S), ["S" + d], [dn])

        s0 = {}
        for it in range(n + 4):
            if 0 <= it - 4 < n:
                P3b(it - 4)
            if 0 <= it - 3 < n:
                P3a(it - 3)
            if 0 <= it - 2 < n:
                P2(it - 2)
            if 0 <= it - 1 < n:
                s1[it - 1] = p1(it - 1, s0.pop(it - 1, None))
            if it < n and p1a is not None:
                s0[it] = p1a(it)

    def state_scan(self, xsrc, taus, d, wk, wkn, wv, wvn, saves=None):
        def p1a(i):
            tau = taus[i]
            xs, xn = self.xload(xsrc[tau * 128:(tau + 1) * 128, :])
            return self.norm_A(xs, xn, C_WBC1)

        def p1(i, tok):
            j = self.rot("xTt", 2)
            xT3, xTn = self.xTt[j], f"xTt{j}"
            self.norm_B(tok[0], tok[1], xT3, xTn)
            return xT3, xTn
        sa = None
        if saves:
            sa = {i: saves[t] for i, t in enumerate(taus) if t in saves}
        self.scan_pipe(d, len(taus), p1, wk, wkn, wv, wvn, save_after=sa, p1a=p1a)

    def unit(self, xsrc, kv0, ut, yout, hscr, Sb_init, uname):
        R, T, cf = self.R, self.T, self.cf
        A = self.alloc
        self.top = self.base
        self.cur_unit = uname
        w_in = T["w_in"]
        xnT = A(8 * TK).rearrange("p (k n) -> p k n", k=8)
        self.LOAD("pool", self.slabI, T["slabI"][:, :], ["slabI"] + [f"ksb{j}" for j in range(3)], "slabI")
        prev = None
        for t in range(14):
            cur = None
            if t < 13:
                xs, xn = self.xload(xsrc[(kv0 + t) * 128:(kv0 + t + 1) * 128, :])
                cur = self.norm_A(xs, xn, C_WBC1)
            if prev is not None:
                self.norm_B(prev[0], prev[1], xnT[:, :, (t - 1) * 128:t * 128], f"xnT{t - 1}")
            prev = cur
        self.dump("xnT", xnT, [128, 8, TK], BF16, "xnT12")
        if getattr(self, "stop_after", None) == "U1":
            self.R.dve(lambda e: e.memset(self.Sfx, 0.0), [], ["Sfx"])
            self.barrier()
            return

        def xr(t0, n, off):
            a = (off + t0) // 128
            b = (off + t0 + n - 1) // 128
            return [f"xnT{t}" for t in range(a, b + 1)]

        m1 = self.top
        onaT = A(4 * TR).rearrange("p (k n) -> p k n", k=4)
        KT = A(4 * TK).rearrange("p (k n) -> p k n", k=4)
        QT = A(4 * TR).rearrange("p (k n) -> p k n", k=4)
        V = A(13 * 8 * 65)
        V4 = V.rearrange("p (t h d) -> p t h d", t=13, h=8)

        hn_pending = []

        def hn_finish():
            while hn_pending:
                (ps, pn, n, nwc, biasc, dst, dname, sq, sqn) = hn_pending.pop(0)
                ps2, p2n = self.ps_new()
                self.MM(ps2[:, :n], self.onesblk, sq[:, :n], True, True, ["cb", sqn], [p2n])
                r1, r1n = self.tF()
                self.ACT(r1[:, :n], ps2[:, :n], AF.Ln, [p2n, "cf"], [r1n], scale=1.0 / 64, bias=cf[:, C_EPS:C_EPS + 1])
                if biasc is None:
                    self.ACT(r1[:, :n], r1[:, :n], AF.Exp, [r1n], [r1n], scale=-0.5)
                else:
                    self.ACT(r1[:, :n], r1[:, :n], AF.Exp, [r1n, "cf"], [r1n], scale=-0.5, bias=cf[:, biasc:biasc + 1])
                self.STT(dst, ps[:, :n], cf[:, nwc:nwc + 1], r1[:, :n], ALU.mult, ALU.mult, [pn, r1n, "cf"], [dname])

        def headnorm(ps, pn, n, nwc, biasc, dst, dname):
            sq, sqn = self.tB()
            self.ACT(sq[:, :n], ps[:, :n], AF.Square, [pn], [sqn])
            hn_pending.append((ps, pn, n, nwc, biasc, dst, dname, sq, sqn))

        w, wn = self.wload(0, w_in[:, 512:1024], 512)
        for p in range(4):
            for (t0, n) in blocks(TK):
                ps, pn = self.ps_new()
                for kc in range(8):
                    self.MM(ps[:, :n], w[:, kc, p * 128:(p + 1) * 128], xnT[:, kc, t0:t0 + n], kc == 0, kc == 7,
                            xr(t0, n, 0) + [wn], [pn])
                hn_finish()
                headnorm(ps, pn, n, C_KNW, None, KT[:, p, t0:t0 + n], "KT")
        w, wn = self.wload(2, w_in[:, 0:512], 512)
        for p in range(4):
            for (t0, n) in blocks(TR):
                ps, pn = self.ps_new()
                for kc in range(8):
                    self.MM(ps[:, :n], w[:, kc, p * 128:(p + 1) * 128], xnT[:, kc, 256 + t0:256 + t0 + n], kc == 0,
                            kc == 7, xr(t0, n, 256) + [wn], [pn])
                hn_finish()
                headnorm(ps, pn, n, C_QNW, C_LNQ, QT[:, p, t0:t0 + n], "QT")
        w, wn = self.wload(0, w_in[:, 1024:1536], 512)
        R.pool(lambda e: e.memset(V, 1.0), [], ["V"])
        hn_first_v = True
        for t in range(13):
            ps, pn = self.ps_new()
            for kc in range(8):
                self.MM(ps, xnT[:, kc, t * 128:(t + 1) * 128], w[:, kc, :], kc == 0, kc == 7, [f"xnT{t}", wn], [pn])
            if hn_first_v:
                hn_finish()
                hn_first_v = False
            self.ACOPY(V4[:, t, :, 0:64], ps.rearrange("p (h d) -> p h d", h=8), [pn], ["V"])
        self.dump("KT", KT, [128, 4, TK], BF16, "KT")
        self.dump("QT", QT, [128, 4, TR], BF16, "QT")
        self.dump("V", V, [128, 13 * 8 * 65], BF16, "V")
        if getattr(self, "stop_after", None) == "U2":
            self.R.dve(lambda e: e.memset(self.Sfx, 0.0), [], ["Sfx"])
            self.barrier()
            return

        onaAll = A(9 * 512).rearrange("p (i n) -> p i n", i=9)
        edge_qps = []
        if ut["top"] is not None:
            edge_qps += [(i, ut["top"][EDGE_TOP.index(i)]) for i in EDGE_TOP]
        if ut["bot"] is not None:
            edge_qps += [(i, ut["bot"][EDGE_BOT.index(i)]) for i in EDGE_BOT]
        eidx = {i: n_ for n_, (i, _) in enumerate(edge_qps)}
        kvl = {i: kv_list(i, i in eidx) for i in range(9)}
        for h in range(8):
            p, bp = h // 2, 64 * (h % 2)
            es_i = 0 if h % 2 == 0 else 2
            es, esn = self.wsl[es_i], f"wsl{es_i}"
            for n_, (i, src) in enumerate(edge_qps):
                self.LOAD("pool", es[:, n_ * 896:(n_ + 1) * 896], src[:, h * 896:(h + 1) * 896], [esn], f"{esn}_{n_}")
            OX, OXn = self.ps_new(4)
            OY, OYn = self.ps_new(5)
            zer = self.cb[:, B_ZERO:B_ZERO + 128]
            self.MM(OX[:, 0:455], zer, self.slabI[:, 0:455], True, False, ["cb", "slabI"], [OXn])
            self.MM(OY[:, 0:130], zer, self.slabI[:, 0:130], True, False, ["cb", "slabI"], [OYn])
            lastX = max(kvl[i][-1] for i in range(7))
            lastY = max(kvl[i][-1] for i in (7, 8))
            def S1(t):
                qs = [i for i in range(9) if t in kvl[i]]
                if not qs:
                    return None
                assert qs == list(range(qs[0], qs[-1] + 1)) and len(qs) <= 7
                bA, bB = (0, 1) if t % 2 == 0 else (2, 3)
                psA, pAn = self.ps_new(bA)
                psB, pBn = self.ps_new(bB)
                qA, qB = qs[:4], qs[4:]
                for (qq, ps_, pn_) in ((qA, psA, pAn), (qB, psB, pBn)):
                    if not qq:
                        continue
                    nq = len(qq)
                    c = 0
                    while c < nq:
                        i = qq[c]
                        if i in eidx:
                            c2 = c + 1
                            a_ = kvl[i].index(t)
                            blk = eidx[i] * 896 + a_ * 128
                            b_ap, b_n = es[:, blk:blk + 128], esn
                        else:
                            c2 = c
                            while c2 < nq and qq[c2] not in eidx:
                                c2 += 1
                            blk = (h * 5 + (i - t + 4)) * 128
                            b_ap, b_n = self.slabI[:, blk:blk + (c2 - c) * 128], "slabI"
                        self.MM(ps_[:, c * 128:c2 * 128], KT[bp:bp + 64, p, t * 128:(t + 1) * 128],
                                QT[bp:bp + 64, p, qq[c] * 128:(qq[c2 - 1] + 1) * 128], True, False, ["KT", "QT"], [pn_])
                        self.MM(ps_[:, c * 128:c2 * 128], self.ident, b_ap, False, True, ["cb", b_n], [pn_])
                        c = c2
                pj = self.rot("PT", 2)
                PT, PTn = self.PT[pj], f"PT{pj}"
                self.ACT(PT[:, 0:len(qA) * 128], psA[:, 0:len(qA) * 128], AF.Exp, [pAn], [PTn])
                if qB:
                    self.ACT(PT[:, 512:512 + len(qB) * 128], psB[:, 0:len(qB) * 128], AF.Exp, [pBn], [PTn])
                return (qs, PT, PTn)

            def S2(t, tok):
                qs, PT, PTn = tok
                for c, i in enumerate(qs):
                    if i < 7:
                        od, odn = OX[:, i * 65:(i + 1) * 65], OXn
                    else:
                        od, odn = OY[:, (i - 7) * 65:(i - 6) * 65], OYn
                    is_last = (c == len(qs) - 1 and t == lastY) if i >= 7 else \
                        (t == lastX and i == max(q for q in qs if q < 7))
                    self.MM(od, PT[:, c * 128:(c + 1) * 128], V4[:, t, h, :], False, is_last, [PTn, "V"], [odn])

            tok_prev = S1(0)
            for t in range(13):
                tok_next = S1(t + 1) if t + 1 < 13 else None
                if tok_prev is not None:
                    S2(t, tok_prev)
                tok_prev = tok_next
            for (O_, On_, i0, ni) in ((OX, OXn, 0, 7), (OY, OYn, 7, 2)):
                O3 = O_[:, 0:ni * 65].rearrange("p (i d) -> p i d", i=ni)
                st, sn = self.stt_()
                R.dve(lambda e, st=st, O3=O3, ni=ni: e.reciprocal(out=st[:, 0:ni], in_=O3[:, :, 64]), [On_], [sn])
                self.TT(onaAll[:, i0:i0 + ni, h * 64:(h + 1) * 64], O3[:, :, 0:64],
                        st[:, 0:ni].rearrange("p (i o) -> p i o", o=1).to_broadcast([128, ni, 64]), ALU.mult,
                        [On_, sn], ["onaAll"])
        for i in range(9):
            pt, pn = self.pt_new()
            for c in range(4):
                self.TR_(pt[:, c * 128:(c + 1) * 128], onaAll[:, i, c * 128:(c + 1) * 128], ["onaAll"], [pn])
            self.ACOPY(onaT[:, :, i * 128:(i + 1) * 128], pt[:, 0:512].rearrange("p (a b) -> p a b", a=4), [pn], ["onaT"])
        self.dump("onaT", onaT, [128, 4, TR], BF16, "onaT")
        if getattr(self, "stop_after", None) == "U3":
            self.R.dve(lambda e: e.memset(self.Sfx, 0.0), [], ["Sfx"])
            self.barrier()
            return

        self.fence(["KT", "QT", "V", "onaAll", "PT0", "PT1"], ["qgT", "kgT", "sog", "snap", "ogT"])
        self.top = m1 + 4 * TR
        ogT = A(4 * TR).rearrange("p (k n) -> p k n", k=4)
        m_gla = self.top
        qgT = A(4 * TR).rearrange("p (k n) -> p k n", k=4)
        kgT = A(4 * TR).rearrange("p (k n) -> p k n", k=4)
        lr = [self.wsl[2][:, 0:TR], self.wsl[2][:, TR:2 * TR]]
        sog = A(9 * 512).rearrange("p (t n) -> p t n", t=9)
        snap = A(9 * 512).rearrange("p (t n) -> p t n", t=9)
        w, wn = self.wload(0, w_in[:, 1536:2048], 512)
        for hh in range(4):
            for (t0, n) in blocks(TR):
                ps, pn = self.ps_new()
                for kc in range(8):
                    self.MM(ps[:, :n], w[:, kc, hh * 128:(hh + 1) * 128], xnT[:, kc, 256 + t0:256 + t0 + n], kc == 0,
                            kc == 7, xr(t0, n, 256) + [wn], [pn])
                self.AMUL(qgT[:, hh, t0:t0 + n], ps[:, :n], 128 ** -0.5, [pn], ["qgT"])
        wk, wkn = self.wload(1, w_in[:, 2048:2560], 512)
        for hh in range(4):
            for (t0, n) in blocks(TR):
                ps, pn = self.ps_new()
                for kc in range(8):
                    self.MM(ps[:, :n], wk[:, kc, hh * 128:(hh + 1) * 128], xnT[:, kc, 256 + t0:256 + t0 + n], kc == 0,
                            kc == 7, xr(t0, n, 256) + [wkn], [pn])
                self.DCOPY(kgT[:, hh, t0:t0 + n], ps[:, :n], [pn], ["kgT"])
        for d in range(2):
            for (t0, n) in blocks(TR):
                ps, pn = self.ps_new()
                for kc in range(8):
                    self.MM(ps[0:16, :n], self.wlr[:, kc, 16 * d:16 * d + 16], xnT[:, kc, 256 + t0:256 + t0 + n],
                            kc == 0, kc == 7, xr(t0, n, 256) + ["wlr"], [pn])
                self.ACOPY(lr[d][0:16, t0:t0 + n], ps[0:16, :n], [pn], [f"lr{d}", "wsl2"])
        w, wn = self.wload(0, w_in[:, 3072:3584], 512)
        for i in range(9):
            ps, pn = self.ps_new()
            for kc in range(8):
                self.MM(ps, xnT[:, kc, (i + 2) * 128:(i + 3) * 128], w[:, kc, :], kc == 0, kc == 7,
                        [f"xnT{i + 2}", wn], [pn])
            tg, tgn = self.tF()
            self.ACT(tg, ps, AF.Tanh, [pn], [tgn], scale=0.5)
            self.STT(tg, tg, 1.0, ps, ALU.add, ALU.mult, [tgn, pn], [tgn])
            self.TT(sog[:, i, :], tg, cf[:, C_GNW:C_GNW + 512], ALU.mult, [tgn, "cf"], ["sog"])
        wv, wvn = self.wload(3, w_in[:, 2560:3072], 512)
        if getattr(self, "stop_after", None) == "U5":
            self.R.dve(lambda e: e.memset(self.Sfx, 0.0), [], ["Sfx"])
            self.barrier()
            return

        if Sb_init is None:
            self.zero_state("b")
        else:
            Sb = self.S["b"]
            R.dve(lambda e, Sb=Sb, src=Sb_init[0]: e.tensor_copy(out=Sb, in_=src), [Sb_init[1]], ["Sb"])
        order6 = list(reversed(range(9)))
        self.scan_pipe("b", 9,
                       lambda n_, tok=None: (xnT[:, :, (order6[n_] + 2) * 128:(order6[n_] + 3) * 128], f"xnT{order6[n_] + 2}"),
                       wk, wkn, wv, wvn,
                       lr_of=lambda n_: (lr[1][0:16, order6[n_] * 128:(order6[n_] + 1) * 128], "lr1"),
                       snap_of=lambda n_: (snap[:, order6[n_], :], "snap"))
        self.fence(["ksb0", "ksb1", "ksb2"], [f"ded{a_}{b_}" for a_ in range(2) for b_ in range(4)])
        Sf = self.S["f"]
        ded = [[self.slabI[:, (par * 4 + q) * 512:(par * 4 + q + 1) * 512] for q in range(4)] for par in range(2)]
        m1out = {}

        def M1(i):
            par = i % 2
            xT3 = xnT[:, :, (i + 2) * 128:(i + 3) * 128]
            psk, pkn, vt, vn = self.tok_kv(xT3, f"xnT{i + 2}", wk, wkn, wv, wvn)
            qts, As = [], []
            gpf = self.gates(lr[0][0:16, i * 128:(i + 1) * 128], "lr0", 0)
            ktf, ktfn, stf, stfn = self.state_prep(gpf[0], gpf[1], psk, pkn, "f")
            for d in range(2):
                if d == 0:
                    gp, gn = gpf
                else:
                    gp, gn = self.gates(lr[d][0:16, i * 128:(i + 1) * 128], f"lr{d}", d)
                Ufm = cf[:, C_UINC:C_UINC + 128] if d == 0 else cf[:, C_ULT:C_ULT + 128]
                psp, ppn = self.ps_new()
                for hh in range(4):
                    sl = slice(hh * 128, (hh + 1) * 128)
                    self.MM(psp[:, sl], gp[:, sl], Ufm, True, True, [gn, "cf"], [ppn])
                Eq, eqn = self.tF()
                Ek, ekn = self.tF()
                self.ACT(Eq, psp, AF.Exp, [ppn], [eqn], scale=(1.0 if d == 0 else -1.0))
                self.ACT(Ek, psp, AF.Exp, [ppn], [ekn], scale=(-1.0 if d == 0 else 1.0))
                qt, qn = ded[par][d], f"ded{par}{d}"
                kt2, k2n = self.tB()
                q3 = qt.rearrange("p (h t) -> p h t", h=4)
                k3 = kt2.rearrange("p (h t) -> p h t", h=4)
                self.TT(q3, qgT[:, :, i * 128:(i + 1) * 128], Eq.rearrange("p (h t) -> p h t", h=4), ALU.mult,
                        ["qgT", eqn], [qn])
                self.TT(k3, kgT[:, :, i * 128:(i + 1) * 128], Ek.rearrange("p (h t) -> p h t", h=4), ALU.mult,
                        ["kgT", ekn], [k2n])
                psa, pan = self.ps_new()
                for hh in range(4):
                    sl = slice(hh * 128, (hh + 1) * 128)
                    self.MM(psa[:, sl], kt2[:, sl], qt[:, sl], True, True, [k2n, qn], [pan])
                Ad, adn = ded[par][2 + d], f"ded{par}{2 + d}"
                Mm = self.Mf if d == 0 else self.Mb
                self.TT(Ad.rearrange("p (h t) -> p h t", h=4), psa.rearrange("p (h t) -> p h t", h=4),
                        Mm.rearrange("p (o t) -> p o t", o=1).to_broadcast([128, 4, 128]), ALU.mult, [pan, "cb"], [adn])
                qts.append((qt, qn))
                As.append((Ad, adn))
            m1out[i] = (vt, vn, ktf, ktfn, stf, stfn, qts, As)

        def M2(i):
            vt, vn, ktf, ktfn, stf, stfn, qts, As = m1out.pop(i)
            if i == 8:
                exp_ = self.Sfx
                R.dve(lambda e, exp_=exp_, Sf=Sf: e.tensor_copy(out=exp_, in_=Sf), ["Sf"], ["Sfx"])
            self.ACOPY(self.Sbf, Sf, ["Sf"], ["Sbf"])
            pso, pon = self.ps_new()
            for hh in range(4):
                sl = slice(hh * 128, (hh + 1) * 128)
                self.MM(pso[:, sl], qts[0][0][:, sl], self.Sbf[:, sl], True, False, [qts[0][1], "Sbf"], [pon])
                self.MM(pso[:, sl], As[0][0][:, sl], vt[:, sl], False, False, [As[0][1], vn], [pon])
                self.MM(pso[:, sl], qts[1][0][:, sl], snap[:, i, sl], False, False, [qts[1][1], "snap"], [pon])
                self.MM(pso[:, sl], As[1][0][:, sl], vt[:, sl], False, True, [As[1][1], vn], [pon])
            self.state_apply(ktf, ktfn, stf, stfn, vt, vn, "f")
            st, sn = self.stt_()
            jj = self.rot("jk", 2)
            for hh in range(4):
                sl = slice(hh * 128, (hh + 1) * 128)
                self.ACT(self.jk[jj][:, sl], pso[:, sl], AF.Square, [pon], [f"jk{jj}", sn], accum=st[:, hh:hh + 1])
            self.TS(st[:, 4:8], st[:, 0:4], 1.0 / 128, EPS, ALU.mult, ALU.add, [sn], [sn])
            self.ACT(st[:, 4:8], st[:, 4:8], AF.Ln, [sn], [sn])
            self.ACT(st[:, 4:8], st[:, 4:8], AF.Exp, [sn], [sn], scale=-0.5)
            self.TS(st[:, 4:8], st[:, 4:8], 0.5, None, ALU.mult, None, [sn], [sn])
            og, ogn = self.tF()
            self.TT(og.rearrange("p (h t) -> p h t", h=4), pso.rearrange("p (h t) -> p h t", h=4),
                    st[:, 4:8].rearrange("p (h o) -> p h o", o=1).to_broadcast([128, 4, 128]), ALU.mult, [pon, sn], [ogn])
            ogb, obn = self.tB()
            self.TT(ogb, og, sog[:, i, :], ALU.mult, [ogn, "sog"], [obn])
            pt, pn = self.pt_new()
            for c in range(4):
                self.TR_(pt[:, c * 128:(c + 1) * 128], ogb[:, c * 128:(c + 1) * 128], [obn], [pn])
            self.ACOPY(ogT[:, :, i * 128:(i + 1) * 128], pt[:, 0:512].rearrange("p (a b) -> p a b", a=4), [pn], ["ogT"])

        M1(0)
        for i in range(9):
            if i + 1 < 9:
                M1(i + 1)
            M2(i)
        self.dump("ogT", ogT, [128, 4, TR], BF16, "ogT")
        if getattr(self, "stop_after", None) == "U7":
            self.R.dve(lambda e: e.memset(self.Sfx, 0.0), [], ["Sfx"])
            self.barrier()
            return

        self.fence(["qgT", "kgT", "sog", "snap", "lr0", "lr1"], ["mixT", "wsl2"] + [f"hnT{i_}" for i_ in range(9)])
        self.top = m_gla
        mixT = A(8 * TR).rearrange("p (k n) -> p k n", k=8)
        wna, wnan = self.wload(0, T["w_na_proj"][:, :], 1024, kchunks=4)
        wgl, wgln = self.wload(1, T["w_gla_proj"][:, :], 1024, kchunks=4)
        for hp in range(2):
            g1, g1n = self.wload(2, w_in[:, 3616 + 512 * hp:3616 + 512 * hp + 512], 512)
            g2, g2n = self.wload(3, w_in[:, 4640 + 512 * hp:4640 + 512 * hp + 512], 512)
            for c4 in range(4):
                c = hp * 4 + c4
                for (t0, n) in blocks(TR):
                    ps1, p1n = self.ps_new()
                    for kk in range(4):
                        self.MM(ps1[:, :n], wna[:, kk, c * 128:(c + 1) * 128], onaT[:, kk, t0:t0 + n], kk == 0, kk == 3,
                                [wnan, "onaT"], [p1n])
                    ps2, p2n = self.ps_new()
                    for kk in range(4):
                        self.MM(ps2[:, :n], wgl[:, kk, c * 128:(c + 1) * 128], ogT[:, kk, t0:t0 + n], kk == 0, kk == 3,
                                [wgln, "ogT"], [p2n])
                    ps3, p3n = self.ps_new()
                    for kc in range(8):
                        self.MM(ps3[:, :n], g1[:, kc, c4 * 128:(c4 + 1) * 128], xnT[:, kc, 256 + t0:256 + t0 + n],
                                kc == 0, kc == 7, xr(t0, n, 256) + [g1n], [p3n])
                    ps4, p4n = self.ps_new()
                    for kc in range(8):
                        self.MM(ps4[:, :n], g2[:, kc, c4 * 128:(c4 + 1) * 128], xnT[:, kc, 256 + t0:256 + t0 + n],
                                kc == 0, kc == 7, xr(t0, n, 256) + [g2n], [p4n])
                    t1, t1n = self.tF()
                    t2, t2n = self.tF()
                    self.ACT(t1[:, :n], ps3[:, :n], AF.Tanh, [p3n], [t1n], scale=0.5)
                    self.ACT(t2[:, :n], ps4[:, :n], AF.Tanh, [p4n], [t2n], scale=0.5)
                    self.STT(t1[:, :n], t1[:, :n], 1.0, ps1[:, :n], ALU.add, ALU.mult, [t1n, p1n], [t1n])
                    self.STT(t2[:, :n], t2[:, :n], 1.0, ps2[:, :n], ALU.add, ALU.mult, [t2n, p2n], [t2n])
                    self.TT(mixT[:, c, t0:t0 + n], t1[:, :n], t2[:, :n], ALU.add, [t1n, t2n], ["mixT"])
        self.dump("mixT", mixT, [128, 8, TR], BF16, "mixT")
        if getattr(self, "stop_after", None) == "U8a":
            self.R.dve(lambda e: e.memset(self.Sfx, 0.0), [], ["Sfx"])
            self.barrier()
            return
        hnT = A(8 * TR).rearrange("p (k n) -> p k n", k=8)
        wo = []
        for half in range(2):
            wo.append(self.wload(half, T["w_out"][:, half * 512:(half + 1) * 512], 512))
        prevB = None
        for i in range(10):
            curB = None
            if i < 9:
                xs, xn = self.xload(xsrc[(kv0 + 2 + i) * 128:(kv0 + 3 + i) * 128, :])
                for half in range(2):
                    ps, pn = self.ps_new()
                    for kc in range(8):
                        self.MM(ps, mixT[:, kc, i * 128:(i + 1) * 128], wo[half][0][:, kc, :], kc == 0, kc == 7,
                                ["mixT", wo[half][1]], [pn])
                    sl = slice(half * 512, (half + 1) * 512)
                    self.STT(xs[:, sl], ps, 0.5, xs[:, sl], ALU.mult, ALU.add, [pn, xn], [xn])
                self.STORE("sp", hscr[i * 128:(i + 1) * 128, :], xs, [xn], "hst_" + xn, w=[f"hscr{i}"])
                mc = None
                if i == 0:
                    mc = C_MASK + 2 * ut["mcol"]
                if i == 8:
                    mc = C_MASK + 2 * ut["mcol"] + 1
                curB = self.norm_A(xs, xn, C_WBC2, maskcol=mc) + (i,)
            if prevB is not None:
                self.norm_B(prevB[0], prevB[1], hnT[:, :, prevB[2] * 128:(prevB[2] + 1) * 128], f"hnT{prevB[2]}")
            prevB = curB
        self.dump("hnT", hnT, [128, 8, TR], BF16, "hnT8")
        if getattr(self, "stop_after", None) == "U8b":
            self.R.dve(lambda e: e.memset(self.Sfx, 0.0), [], ["Sfx"])
            self.barrier()
            return

        self.barrier()
        hn_names = [f"hnT{i}" for i in range(9)]
        self.top = self.base
        fT = A(22 * 1024).rearrange("p (j n) -> p j n", j=22)
        tmp = self.arena[:, self.tmp0:self.tmp0 + 13312]
        u = [self.slabI[:, 0:2304].bitcast(F32), self.slabI[:, 2304:4608].bitcast(F32)]
        cv = [tmp[:, 0:2048].bitcast(F32), tmp[:, 2048:4096].bitcast(F32)]
        gq = tmp[:, 4096:6144].bitcast(F32)
        wdT = tmp[:, 6144:13312].rearrange("p (j n) -> p j n", j=7)
        wdL = tmp[:, 0:6144].rearrange("p (j n) -> p j n", j=6)
        wdC = A(9 * 1024).rearrange("p (j n) -> p j n", j=9)
        assert self.top <= self.base + 8 * TK + 4 * TR + 4 * TR + 8 * TR, "fT/wdC would overlap hnT"
        wdr = T["w_down"].rearrange("(j p) n -> p j n", p=128)
        w_up = T["w_up"]
        for j in range(22):
            si_ = j % 4
            sl_, wn = self.wsl[si_], f"wsl{si_}"
            wv2 = sl_[:, 0:2048].rearrange("p (k n) -> p k n", k=8)
            self.LOAD("pool", wv2[:, :, 0:128], w_up[:, j * 128:(j + 1) * 128].rearrange("(k p) n -> p k n", p=128),
                      [wn], wn + "a")
            self.LOAD("pool", wv2[:, :, 128:256],
                      w_up[:, 2816 + j * 128:2816 + (j + 1) * 128].rearrange("(k p) n -> p k n", p=128), [wn], wn + "b")
            if j == 3:
                for g in range(0, 9, 3):
                    self.LOAD("pool", wdC[:, g:g + 3, :], wdr[:, g:g + 3, :], ["wdC"], f"wdC{g}")
                self.LOAD("pool", wdT[:, 0:4, :], wdr[:, 9:13, :], ["wdT"], "wdT0")
                self.LOAD("pool", wdT[:, 4:7, :], wdr[:, 13:16, :], ["wdT"], "wdT4")
            for ab in range(2):
                for (t0, n) in blocks(TR):
                    ps, pn = self.ps_new()
                    for kc in range(8):
                        self.MM(ps[:, :n], wv2[:, kc, ab * 128:(ab + 1) * 128], hnT[:, kc, t0:t0 + n], kc == 0, kc == 7,
                                [wn] + hn_names, [pn])
                    self.ACOPY(u[ab][:, t0:t0 + n], ps[:, :n], [pn], [f"u{ab}"])
                jj = ab * 22 + j
                cw = lambda tap, jj=jj: cf[:, C_CW + jj * 3 + tap:C_CW + jj * 3 + tap + 1]
                self.ACT(cv[ab], u[ab][:, 63:1087], AF.Identity, [f"u{ab}", "cf"], [f"cv{ab}"], scale=cw(0),
                         bias=cf[:, C_CB + jj:C_CB + jj + 1])
                self.STT(cv[ab], u[ab][:, 64:1088], cw(1), cv[ab], ALU.mult, ALU.add, [f"u{ab}", "cf", f"cv{ab}"],
                         [f"cv{ab}"])
                self.STT(cv[ab], u[ab][:, 65:1089], cw(2), cv[ab], ALU.mult, ALU.add, [f"u{ab}", "cf", f"cv{ab}"],
                         [f"cv{ab}"])
            self.ACT(gq, cv[0], AF.Square, ["cv0"], ["gq"])
            self.TS(gq, gq, 0.044715, 1.0, ALU.mult, ALU.add, ["gq"], ["gq"])
            self.TT(gq, gq, cv[0], ALU.mult, ["gq", "cv0"], ["gq"])
            self.ACT(gq, gq, AF.Tanh, ["gq"], ["gq"], scale=0.7978845608028654)
            self.STT(gq, gq, 1.0, cv[0], ALU.add, ALU.mult, ["gq", "cv0"], ["gq"])
            self.TT(fT[:, j, :], gq, cv[1], ALU.mult, ["gq", "cv1"], ["fT"])
        self.dump("fT", fT, [128, 22, 1024], BF16, "fT")
        if getattr(self, "stop_after", None) == "U10":
            self.R.dve(lambda e: e.memset(self.Sfx, 0.0), [], ["Sfx"])
            self.barrier()
            return

        self.fence(["cv0", "cv1", "gq"], ["wdL"])
        self.LOAD("pool", wdL[:, 0:3, :], wdr[:, 16:19, :], ["wdL"], "wdL0")
        self.LOAD("pool", wdL[:, 3:6, :], wdr[:, 19:22, :], ["wdL"], "wdL3")

        def wdj(j):
            if j < 9:
                return wdC[:, j], "wdC"
            if j < 16:
                return wdT[:, j - 9], "wdT"
            return wdL[:, j - 16], "wdL"
        for i8 in range(8):
            hi_ = self.rot("xs", 2)
            hs, hn_ = self.xs[hi_], f"xs{hi_}"
            self.LOAD("sp", hs, hscr[64 + i8 * 128:64 + (i8 + 1) * 128, :], [hn_], hn_,
                      r=[f"hscr{i8}", f"hscr{i8 + 1}"])
            for half in range(2):
                ps, pn = self.ps_new()
                for j in range(22):
                    self.MM(ps, fT[:, j, i8 * 128:(i8 + 1) * 128], wdj(j)[0][:, half * 512:(half + 1) * 512], j == 0,
                            j == 21, ["fT", wdj(j)[1]], [pn])
                sl = slice(half * 512, (half + 1) * 512)
                self.STT(hs[:, sl], ps, 0.5, hs[:, sl], ALU.mult, ALU.add, [pn, hn_], [hn_])
            self.stores.append(self.STORE("sp", yout[i8 * 128:(i8 + 1) * 128, :], hs, [hn_], "yst_" + hn_))
        self.barrier()

    def sequence(self, xsrc, n_units, tau0, pre, tau_max, uts, yout, hscr):
        T = self.T
        w_in = T["w_in"]
        self.zero_state("f")
        self.zero_state("b")
        post = list(range(tau_max, tau0 + 8, -1))
        saves = {}
        inits = [None] * n_units
        for m in range(n_units):
            need = tau0 + 8 * m + 9
            if need <= tau_max:
                saves[need] = (self.Ssave[m], f"Ssave{m}")
                inits[m] = (self.Ssave[m], f"Ssave{m}")
        if pre or post:
            wk, wkn = self.wload(1, w_in[:, 2048:2560], 512)
            wv, wvn = self.wload(3, w_in[:, 2560:3072], 512)
            if pre:
                self.state_scan(xsrc, pre, "f", wk, wkn, wv, wvn)
            if post:
                self.state_scan(xsrc, post, "b", wk, wkn, wv, wvn, saves=saves)
        self.barrier()
        for m in range(n_units):
            self.unit(xsrc, tau0 + 8 * m - 2, uts[m], yout[m * 1024:(m + 1) * 1024, :], hscr, inits[m], f"u{m}")
            Sf, Sfx = self.S["f"], self.Sfx
            self.R.dve(lambda e, Sf=Sf, Sfx=Sfx: e.tensor_copy(out=Sf, in_=Sfx), ["Sfx"], ["Sf"])
            self.barrier()


def build_nc(dbg=None, only_prompt_units=None, xs_tiles=225, variant=None):
    nc = bass.Bass("TRN2", target_bir_lowering=False)
    dt = lambda name, shape, kind="ExternalInput": nc.dram_tensor(name, list(shape), F32, kind=kind).ap()
    T = {
        "xp": dt("xp", [2, 21 * 128, D]),
        "xs": dt("xs", [xs_tiles * 128, D]),
        "cf": dt("cf", [128, NCF]),
        "cb": dt("cb", [128, NCB]),
        "slabI": dt("slabI", [128, 5120]),
        "sT_p": dt("sT_p", [3, 128, 7168]),
        "sB_p": dt("sB_p", [2, 128, 7168]),
        "sT_s": dt("sT_s", [3, 128, 7168]),
        "sB_s": dt("sB_s", [2, 128, 7168]),
        "w_in": dt("w_in", [D, 5664]),
        "w_a2_f": dt("w_a2_f", [16, 512]),
        "b_a_f": dt("b_a_f", [1, 512]),
        "w_a2_b": dt("w_a2_b", [16, 512]),
        "b_a_b": dt("b_a_b", [1, 512]),
        "w_na_proj": dt("w_na_proj", [512, D]),
        "w_gla_proj": dt("w_gla_proj", [512, D]),
        "w_out": dt("w_out", [D, D]),
        "w_up": dt("w_up", [D, 5632]),
        "w_down": dt("w_down", [2816, D]),
    }
    yp = dt("yp", [2, 2048, D], "ExternalOutput")
    ys = dt("ys", [4096, D], "ExternalOutput")
    hscr = dt("hscr", [TR, D], "Internal")
    with ExitStack() as es:
        k = K(nc, es, dbg)
        k.setup(T)
        k.Sfx = k.alloc(512, F32)
        k.base = k.top
        sT_p = [T["sT_p"][e] for e in range(3)]
        sB_p = [T["sB_p"][e] for e in range(2)]
        sT_s = [T["sT_s"][e] for e in range(3)]
        sB_s = [T["sB_s"][e] for e in range(2)]
        ut_p = [dict(top=sT_p, bot=None, mcol=0), dict(top=None, bot=sB_p, mcol=1)]
        ut_s = [dict(top=sT_s, bot=None, mcol=2), dict(top=None, bot=None, mcol=3),
                dict(top=None, bot=None, mcol=4), dict(top=None, bot=sB_s, mcol=5)]
        if only_prompt_units is not None:
            k.dbg_unit = f"u{only_prompt_units - 1}"
            k.sequence(T["xp"][0], only_prompt_units, 2, [], 18, ut_p, yp[0], hscr)
        elif variant == "prompts":
            for b in range(2):
                k.sequence(T["xp"][b], 2, 2, [], 18, ut_p, yp[b], hscr)
        elif variant == "scans":
            k.sequence(T["xs"], 0, 96, list(range(0, 96)), 224, ut_s, ys, hscr)
        else:
            for b in range(2):
                k.sequence(T["xp"][b], 2, 2, [], 18, ut_p, yp[b], hscr)
            k.sequence(T["xs"], 4, 96, list(range(0, 96)), 224, ut_s, ys, hscr)
        n, cnt = k.R.emit(nc, k.stores)
        k.nops = n
    return nc, k


def make_slab(rpb, i, kvs, lo, hi):
    H = rpb.shape[0]
    out = np.full((128, 7, H, 128), NEG, np.float32)
    kc = np.arange(64)
    qc = np.arange(64)
    c0 = np.clip(qc - 8, 0, 48)
    colv = (kc[:, None] >= c0[None, :]) & (kc[:, None] < c0[None, :] + 16)
    dcv = np.clip(kc[:, None] - qc[None, :] + 15, 0, 30)
    for a, t in enumerate(kvs):
        for rr in range(2):
            rk = 2 * t - 5 + rr
            for qq in range(2):
                rq = 2 * i - 1 + qq
                if lo <= rq < hi:
                    r0 = min(max(rq - 4, lo), hi - 8)
                else:
                    r0 = rq - 4
                if not (r0 <= rk < r0 + 8):
                    continue
                dr = rk - rq + 7
                blk = np.where(colv[None], rpb[:, dr][:, dcv], NEG)
                out[rr * 64:(rr + 1) * 64, a, :, qq * 64:(qq + 1) * 64] = blk.transpose(1, 0, 2)
    return out


def host_consts(norm1_w, norm2_w, gla_norm_w, qn_w, kn_w, conv_w, conv_b, masks):
    cf = np.zeros((128, NCF), np.float32)
    cf[:, C_WBC1:C_WBC1 + D] = norm1_w[None, :]
    cf[:, C_WBC2:C_WBC2 + D] = norm2_w[None, :]
    cf[:, C_GNW:C_GNW + 512] = np.tile(gla_norm_w, 4)[None, :]
    idx = np.arange(128)
    cf[:, C_UINC:C_UINC + 128] = (idx[:, None] <= idx[None, :]) * (-1.0 / 16)
    cf[:, C_ULT:C_ULT + 128] = (idx[:, None] < idx[None, :]) * (-1.0 / 16)
    cf[:, C_UGT:C_UGT + 128] = (idx[:, None] > idx[None, :]) * (-1.0 / 16)
    cw = conv_w.reshape(3, 44, 128)
    cf[:, C_CW:C_CW + 132] = cw.transpose(2, 1, 0).reshape(128, 132)
    cf[:, C_CB:C_CB + 44] = conv_b.reshape(44, 128).T
    cf[:, C_QNW] = np.tile(qn_w, 2)
    cf[:, C_KNW] = np.tile(kn_w, 2)
    cf[:, C_LNQ] = np.float32(np.log(0.125))
    cf[:, C_NEG] = -1.0 / 16
    cf[:, C_EPS] = EPS
    for m, (top_real, bot_real) in enumerate(masks):
        cf[:, C_MASK + 2 * m] = 1.0
        cf[0:64, C_MASK + 2 * m] = 1.0 if top_real else 0.0
        cf[:, C_MASK + 2 * m + 1] = 1.0
        cf[64:128, C_MASK + 2 * m + 1] = 1.0 if bot_real else 0.0
    cb = np.zeros((128, NCB), np.float32)
    cb[:, B_ID:B_ID + 128] = np.eye(128)
    cb[0:64, B_OB:B_OB + 64] = 1.0
    cb[64:128, B_OB + 64:B_OB + 128] = 1.0
    cb[:, B_ONE:B_ONE + 128] = 1.0
    cb[:, B_MF:B_MF + 128] = (idx[:, None] <= idx[None, :])
    cb[:, B_MB:B_MB + 128] = (idx[:, None] > idx[None, :])
    return cf, cb


def make_in_maps(inp):
    f = lambda k: np.asarray(inp[k], np.float32)
    x_prompt, x_sample = f("x_prompt"), f("x_sample")
    rpb = f("rpb")[0]
    w_in = np.ascontiguousarray(f("w_in")[0])
    shared = {
        "w_in": w_in,
        "w_a2_f": np.ascontiguousarray(f("w_a2_f")[0]), "b_a_f": np.ascontiguousarray(f("b_a_f")[0][None, :]),
        "w_a2_b": np.ascontiguousarray(f("w_a2_b")[0]), "b_a_b": np.ascontiguousarray(f("b_a_b")[0][None, :]),
        "w_na_proj": np.ascontiguousarray(f("w_na_proj")[0]), "w_gla_proj": np.ascontiguousarray(f("w_gla_proj")[0]),
        "w_out": np.ascontiguousarray(f("w_out")[0]), "w_up": np.ascontiguousarray(f("w_up")[0]),
        "w_down": np.ascontiguousarray(f("w_down")[0]),
    }
    flat = lambda s: np.ascontiguousarray(s.transpose(0, 2, 1, 3).reshape(128, 7168))
    slabI = make_slab(rpb, 3, kv_list(3, False), -100, 100)[:, 0:5]
    slabI = np.ascontiguousarray(slabI[:, ::-1].transpose(0, 2, 1, 3).reshape(128, 5120))
    sT_cl = np.stack([flat(make_slab(rpb, i, kv_list(i, True), 0, 32)) for i in EDGE_TOP])
    sT_un = np.stack([flat(make_slab(rpb, i, kv_list(i, True), -100, 100)) for i in EDGE_TOP])
    sB_cl = np.stack([flat(make_slab(rpb, i, kv_list(i, True), -16, 16)) for i in EDGE_BOT])
    sB_un = np.stack([flat(make_slab(rpb, i, kv_list(i, True), -100, 100)) for i in EDGE_BOT])
    shared.update({"slabI": slabI, "sT_p": sT_cl, "sB_p": sB_cl})
    maps = []
    for c in range(NCORES):
        s, j = c // 4, c % 4
        xp = np.zeros((2, 42, 64, D), np.float32)
        for b in range(2):
            xp[b, 5:37] = x_prompt[2 * c + b].reshape(32, 64, D)
        R0 = 64 * j
        xs = np.zeros((450, 64, D), np.float32)
        g0 = R0 - 193
        lo_, hi_ = max(0, g0), min(256, g0 + 450)
        xs[lo_ - g0:hi_ - g0] = x_sample[s].reshape(256, 64, D)[lo_:hi_]
        masks = [(False, True), (True, False)]
        for m in range(4):
            masks.append((R0 + 16 * m - 1 >= 0, R0 + 16 * m + 16 < 256))
        cf, cb = host_consts(f("norm1_w")[0], f("norm2_w")[0], f("gla_norm_w")[0], f("qn_w")[0], f("kn_w")[0],
                             f("conv_w")[0], f("conv_b")[0], masks)
        d = dict(shared)
        d.update({"xp": xp.reshape(2, 21 * 128, D), "xs": xs.reshape(225 * 128, D), "cf": cf, "cb": cb,
                  "sT_s": sT_cl if j == 0 else sT_un, "sB_s": sB_cl if j == 3 else sB_un})
        maps.append(d)
    return maps


def kernel(**inp):
    maps = make_in_maps(inp)
    nc, k = build_nc()
    res = run_bass_kernel_spmd(nc, maps, core_ids=list(range(NCORES)))
    y_prompt = np.empty((16, 2048, D), np.float32)
    y_sample = np.empty((2, 16384, D), np.float32)
    for c in range(NCORES):
        s, j = c // 4, c % 4
        r = res.results[c]
        y_prompt[2 * c:2 * c + 2] = np.asarray(r["yp"], np.float32)
        y_sample[s, 4096 * j:4096 * (j + 1)] = np.asarray(r["ys"], np.float32)
    return (y_prompt, y_sample)
```
